# Optimizing a Trainium2 kernel written in Bass

```python
import math
import jax, jax.numpy as jnp
from jax import lax
import numpy as np


D_MODEL = 1024
BATCH = 8
SEQ = 2048
DEPTH = 2

NORM_EPS = 1e-6
S5_WIDTH = D_MODEL // 2
S5_GROUP = 16
S5_GROUPS = S5_WIDTH // S5_GROUP
S5_STATE = 64
GLA_WIDTH = D_MODEL - S5_WIDTH
GLA_HEADS = 4
GLA_DV = GLA_WIDTH // GLA_HEADS
GLA_DK = GLA_DV // 2
GLA_QK = GLA_HEADS * GLA_DK
GLA_GATE_RANK = 16
GLA_TAU = 16.0
GLA_CHUNK = 64
IN_SPLITS = (S5_WIDTH, S5_WIDTH + GLA_QK, S5_WIDTH + 2 * GLA_QK, S5_WIDTH + 2 * GLA_QK + GLA_WIDTH, S5_WIDTH + 2 * GLA_QK + GLA_WIDTH + GLA_GATE_RANK)
IN_WIDTH = IN_SPLITS[-1] + GLA_WIDTH
RWKV_HEAD = 64
RWKV_HEADS = D_MODEL // RWKV_HEAD
RWKV_W_RANK = 64
RWKV_A_RANK = 64
RWKV_G_RANK = 128
RWKV_GN_EPS = 64e-5
PEER_HEADS = 8
PEER_NKEYS = 128
PEER_EXPERTS = PEER_NKEYS * PEER_NKEYS
PEER_DQ = 256
PEER_TOPK = 16
PEER_BLOCK = 128
N_EVEN = (DEPTH + 1) // 2
N_ODD = DEPTH // 2

kernel_name = 'hybrid_s5_gla_rwkv7_peer'


def rms_norm(x, g):
    xf = x.astype(jnp.float32)
    y = xf * lax.rsqrt(jnp.mean(xf * xf, axis=-1, keepdims=True) + NORM_EPS)
    return (y * g.astype(jnp.float32)).astype(x.dtype)


def s5_mixer(u, a_re, a_im, log_dt, b_re, b_im, c_re, c_im, d_skip, w_glu):
    bsz, L, _ = u.shape
    uf = u.astype(jnp.float32)
    ug = uf.reshape(bsz, L, S5_GROUPS, S5_GROUP)
    ar = a_re.astype(jnp.float32)
    ai = a_im.astype(jnp.float32)
    dt = jnp.exp(log_dt.astype(jnp.float32))
    mag = jnp.exp(dt * ar)
    abar_re = mag * jnp.cos(dt * ai)
    abar_im = mag * jnp.sin(dt * ai)
    nr = abar_re - 1.0
    den = ar * ar + ai * ai
    f_re = (nr * ar + abar_im * ai) / den
    f_im = (abar_im * ar - nr * ai) / den
    br = b_re.astype(jnp.float32)
    bi = b_im.astype(jnp.float32)
    bbar_re = f_re[..., None] * br - f_im[..., None] * bi
    bbar_im = f_re[..., None] * bi + f_im[..., None] * br
    bu_re = jnp.einsum('blgc,gpc->blgp', ug, bbar_re)
    bu_im = jnp.einsum('blgc,gpc->blgp', ug, bbar_im)
    aa_re = jnp.broadcast_to(abar_re, bu_re.shape)
    aa_im = jnp.broadcast_to(abar_im, bu_re.shape)

    def combine(e1, e2):
        a1r, a1i, b1r, b1i = e1
        a2r, a2i, b2r, b2i = e2
        return (a2r * a1r - a2i * a1i,
                a2r * a1i + a2i * a1r,
                a2r * b1r - a2i * b1i + b2r,
                a2r * b1i + a2i * b1r + b2i)

    _, _, s_re, s_im = lax.associative_scan(combine, (aa_re, aa_im, bu_re, bu_im), axis=1)
    y = (jnp.einsum('blgp,gcp->blgc', s_re, c_re.astype(jnp.float32))
         - jnp.einsum('blgp,gcp->blgc', s_im, c_im.astype(jnp.float32)))
    y = y.reshape(bsz, L, S5_WIDTH) + d_skip.astype(jnp.float32) * uf
    g = jax.nn.gelu(y, approximate=False)
    return g * jax.nn.sigmoid(g @ w_glu.astype(jnp.float32))


def gla_mixer(q, k, v, g_low, r, w_g2, b_g2, norm_g):
    bsz, L, _ = q.shape
    H, dk, dv, C = GLA_HEADS, GLA_DK, GLA_DV, GLA_CHUNK
    nc = L // C
    qf = q.astype(jnp.float32).reshape(bsz, nc, C, H, dk) * (dk ** -0.5)
    kf = k.astype(jnp.float32).reshape(bsz, nc, C, H, dk)
    vf = v.astype(jnp.float32).reshape(bsz, nc, C, H, dv)
    log_a = jax.nn.log_sigmoid(g_low.astype(jnp.float32) @ w_g2.astype(jnp.float32)
                               + b_g2.astype(jnp.float32)) / GLA_TAU
    log_a = log_a.reshape(bsz, nc, C, H, dk)
    b = jnp.cumsum(log_a, axis=2)
    b_last = b[:, :, -1]
    q_d = qf * jnp.exp(b)
    k_d = kf * jnp.exp(-b)
    k_end = kf * jnp.exp(b_last[:, :, None] - b)
    causal = jnp.tril(jnp.ones((C, C), dtype=bool))
    att = jnp.einsum('bnihd,bnjhd->bnhij', q_d, k_d)
    att = jnp.where(causal, att, 0.0)
    o = jnp.einsum('bnhij,bnjhv->bnihv', att, vf)
    kv = jnp.einsum('bnjhd,bnjhv->bnhdv', k_end, vf)
    decay = jnp.exp(b_last)

    def step(S, inp):
        dec, kv_n = inp
        return S * dec[..., None] + kv_n, S

    S0 = jnp.zeros((bsz, H, dk, dv), jnp.float32)
    _, S_prev = lax.scan(step, S0, (jnp.moveaxis(decay, 1, 0), jnp.moveaxis(kv, 1, 0)))
    S_prev = jnp.moveaxis(S_prev, 0, 1)
    o = o + jnp.einsum('bnihd,bnhdv->bnihv', q_d, S_prev)
    o = o.reshape(bsz, L, H, dv)
    o = o * lax.rsqrt(jnp.mean(o * o, axis=-1, keepdims=True) + NORM_EPS)
    o = o * norm_g.astype(jnp.float32).reshape(H, dv)
    return o.reshape(bsz, L, H * dv) * jax.nn.silu(r.astype(jnp.float32))


def even_mixer(xn, w_in, w_out, a_re, a_im, log_dt, b_re, b_im, c_re, c_im, d_skip, w_glu, w_g2, b_g2, norm_g):
    proj = xn @ w_in
    u, q, k, v, g_low, r = jnp.split(proj, IN_SPLITS, axis=-1)
    y_s5 = s5_mixer(u, a_re, a_im, log_dt, b_re, b_im, c_re, c_im, d_skip, w_glu)
    y_gla = gla_mixer(q, k, v, g_low, r, w_g2, b_g2, norm_g)
    y = jnp.concatenate([y_s5, y_gla], axis=-1) @ w_out.astype(jnp.float32)
    return y.astype(xn.dtype)


def rwkv7_mixer(xn, mu, w_r, w_k, w_v, w0, w_w1, w_w2, a0, w_a1, w_a2, w_g1, w_g2, k_k, k_a, r_k, lnx_g, lnx_b, w_o):
    bsz, L, D = xn.shape
    H, N = RWKV_HEADS, RWKV_HEAD
    xf = xn.astype(jnp.float32)
    mu = mu.astype(jnp.float32)
    x_prev = jnp.pad(xf, ((0, 0), (1, 0), (0, 0)))[:, :-1]
    xx = x_prev - xf
    xr, xw, xk, xv, xa, xg = [xf + xx * mu[i] for i in range(6)]
    f = lambda t: t.astype(jnp.float32)
    r = xr @ f(w_r)
    k = xk @ f(w_k)
    v = xv @ f(w_v)
    w = -jax.nn.softplus(-(f(w0) + jnp.tanh(xw @ f(w_w1)) @ f(w_w2))) - 0.5
    decay = jnp.exp(-jnp.exp(w))
    a = jax.nn.sigmoid(f(a0) + (xa @ f(w_a1)) @ f(w_a2))
    g = jax.nn.sigmoid(xg @ f(w_g1)) @ f(w_g2)
    kk = (k * f(k_k)).reshape(bsz, L, H, N)
    kk = kk / jnp.maximum(jnp.sqrt(jnp.sum(kk * kk, axis=-1, keepdims=True)), 1e-12)
    k = k * (1.0 + (a - 1.0) * f(k_a))
    hd = lambda t: t.reshape(bsz, L, H, N)
    r, decay, k, v, a = hd(r), hd(decay), hd(k), hd(v), hd(a)
    tm = lambda t: jnp.moveaxis(t, 1, 0)

    def step(S, inp):
        r_t, w_t, k_t, v_t, a_t, b_t = inp
        sa = jnp.einsum('bhvk,bhk->bhv', S, a_t)
        S = S * w_t[:, :, None, :] + sa[..., None] * b_t[:, :, None, :] + v_t[..., None] * k_t[:, :, None, :]
        return S, jnp.einsum('bhvk,bhk->bhv', S, r_t)

    S0 = jnp.zeros((bsz, H, N, N), jnp.float32)
    _, y = lax.scan(step, S0, (tm(r), tm(decay), tm(k), tm(v), tm(-kk), tm(kk * a)))
    y = jnp.moveaxis(y, 0, 1)
    mean = jnp.mean(y, axis=-1, keepdims=True)
    var = jnp.mean((y - mean) ** 2, axis=-1, keepdims=True)
    y = ((y - mean) * lax.rsqrt(var + RWKV_GN_EPS)).reshape(bsz, L, D) * f(lnx_g) + f(lnx_b)
    bonus = jnp.sum(r * k * f(r_k), axis=-1, keepdims=True) * v
    y = y + bonus.reshape(bsz, L, D)
    return ((y * g) @ f(w_o)).astype(xn.dtype)


def peer_ffn(xn, w_q, sub_keys, u_tab, v_tab):
    bsz, L, D = xn.shape
    T = bsz * L
    H, K, NK = PEER_HEADS, PEER_TOPK, PEER_NKEYS
    xt = xn.reshape(T, D)
    q = (xt.astype(jnp.float32) @ w_q.astype(jnp.float32)).reshape(T, H, 2, PEER_DQ // 2)
    s = jnp.einsum('thcd,hcnd->thcn', q, sub_keys.astype(jnp.float32))
    s1, i1 = lax.top_k(s[:, :, 0], K)
    s2, i2 = lax.top_k(s[:, :, 1], K)
    cand = (s1[..., :, None] + s2[..., None, :]).reshape(T, H, K * K)
    cand_idx = (i1[..., :, None] * NK + i2[..., None, :]).reshape(T, H, K * K)
    top_s, pos = lax.top_k(cand, K)
    idx = jnp.take_along_axis(cand_idx, pos, axis=-1)
    gate = jax.nn.softmax(top_s, axis=-1).astype(xn.dtype)
    nb = T // PEER_BLOCK

    def block(args):
        xb, ib, gb = args
        h = jnp.einsum('tkd,td->tk', u_tab[ib], xb)
        h = jax.nn.gelu(h, approximate=False) * gb
        return jnp.einsum('tk,tkd->td', h, v_tab[ib])

    out = lax.map(block, (xt.reshape(nb, PEER_BLOCK, D),
                          idx.reshape(nb, PEER_BLOCK, H * K),
                          gate.reshape(nb, PEER_BLOCK, H * K)))
    return out.reshape(bsz, L, D).astype(xn.dtype)


def setup_inputs(seed: int = 0) -> dict:
    key = jax.random.key(seed)
    ks = iter(jax.random.split(key, 64))

    def nrm(shape, scale):
        return scale * jax.random.normal(next(ks), shape, jnp.float32)

    def unif(shape, lo, hi):
        return jax.random.uniform(next(ks), shape, jnp.float32, minval=lo, maxval=hi)

    D = D_MODEL
    G, P, C = S5_GROUPS, S5_STATE, S5_GROUP
    E, O = N_EVEN, N_ODD
    return {
        'x': nrm((BATCH, SEQ, D), 1.0),
        'norm_mix_g': 1.0 + nrm((DEPTH, D), 0.02),
        'norm_ffn_g': 1.0 + nrm((DEPTH, D), 0.02),
        'final_g': 1.0 + nrm((D,), 0.02),
        'e_w_in': nrm((E, D, IN_WIDTH), D ** -0.5),
        'e_w_out': nrm((E, D, D), D ** -0.5),
        's5_a_re': -0.5 + nrm((E, G, P), 0.01),
        's5_a_im': jnp.broadcast_to(jnp.pi * jnp.arange(P, dtype=jnp.float32), (E, G, P)) + nrm((E, G, P), 0.01),
        's5_log_dt': unif((E, G, P), math.log(1e-3), math.log(1e-1)),
        's5_b_re': nrm((E, G, P, C), (2.0 * C) ** -0.5),
        's5_b_im': nrm((E, G, P, C), (2.0 * C) ** -0.5),
        's5_c_re': nrm((E, G, C, P), P ** -0.5),
        's5_c_im': nrm((E, G, C, P), P ** -0.5),
        's5_d': nrm((E, S5_WIDTH), 1.0),
        's5_w_glu': nrm((E, S5_WIDTH, S5_WIDTH), S5_WIDTH ** -0.5),
        'gla_w_g2': nrm((E, GLA_GATE_RANK, GLA_QK), GLA_GATE_RANK ** -0.5),
        'gla_b_g2': nrm((E, GLA_QK), 0.01),
        'gla_norm_g': 1.0 + nrm((E, GLA_WIDTH), 0.02),
        'o_mu': unif((O, 6, D), 0.0, 1.0),
        'o_w_r': nrm((O, D, D), D ** -0.5),
        'o_w_k': nrm((O, D, D), D ** -0.5),
        'o_w_v': nrm((O, D, D), D ** -0.5),
        'o_w0': unif((O, D), -6.0, -1.0),
        'o_w_w1': nrm((O, D, RWKV_W_RANK), D ** -0.5),
        'o_w_w2': nrm((O, RWKV_W_RANK, D), 0.1 * RWKV_W_RANK ** -0.5),
        'o_a0': nrm((O, D), 0.1),
        'o_w_a1': nrm((O, D, RWKV_A_RANK), D ** -0.5),
        'o_w_a2': nrm((O, RWKV_A_RANK, D), 0.1 * RWKV_A_RANK ** -0.5),
        'o_w_g1': nrm((O, D, RWKV_G_RANK), D ** -0.5),
        'o_w_g2': nrm((O, RWKV_G_RANK, D), RWKV_G_RANK ** -0.5),
        'o_k_k': 0.85 + nrm((O, D), 0.02),
        'o_k_a': 1.0 + nrm((O, D), 0.02),
        'o_r_k': nrm((O, RWKV_HEADS, RWKV_HEAD), 0.1),
        'o_lnx_g': 1.0 + nrm((O, D), 0.02),
        'o_lnx_b': nrm((O, D), 0.01),
        'o_w_o': nrm((O, D, D), D ** -0.5),
        'peer_w_q': nrm((DEPTH, D, PEER_HEADS * PEER_DQ), D ** -0.5),
        'peer_sub_keys': nrm((DEPTH, PEER_HEADS, 2, PEER_NKEYS, PEER_DQ // 2), (PEER_DQ // 2) ** -0.5),
        'peer_u': nrm((DEPTH, PEER_EXPERTS, D), D ** -0.5),
        'peer_v': nrm((DEPTH, PEER_EXPERTS, D), (PEER_HEADS * PEER_TOPK) ** -0.5),
    }


def reference(x, norm_mix_g, norm_ffn_g, final_g,
              e_w_in, e_w_out, s5_a_re, s5_a_im, s5_log_dt, s5_b_re, s5_b_im, s5_c_re, s5_c_im, s5_d, s5_w_glu,
              gla_w_g2, gla_b_g2, gla_norm_g,
              o_mu, o_w_r, o_w_k, o_w_v, o_w0, o_w_w1, o_w_w2, o_a0, o_w_a1, o_w_a2, o_w_g1, o_w_g2,
              o_k_k, o_k_a, o_r_k, o_lnx_g, o_lnx_b, o_w_o,
              peer_w_q, peer_sub_keys, peer_u, peer_v):
    h = x
    for layer in range(DEPTH):
        hn = rms_norm(h, norm_mix_g[layer])
        i = layer // 2
        if layer % 2 == 0:
            mix = even_mixer(hn, e_w_in[i], e_w_out[i], s5_a_re[i], s5_a_im[i], s5_log_dt[i],
                             s5_b_re[i], s5_b_im[i], s5_c_re[i], s5_c_im[i], s5_d[i], s5_w_glu[i],
                             gla_w_g2[i], gla_b_g2[i], gla_norm_g[i])
        else:
            mix = rwkv7_mixer(hn, o_mu[i], o_w_r[i], o_w_k[i], o_w_v[i], o_w0[i], o_w_w1[i], o_w_w2[i],
                              o_a0[i], o_w_a1[i], o_w_a2[i], o_w_g1[i], o_w_g2[i], o_k_k[i], o_k_a[i],
                              o_r_k[i], o_lnx_g[i], o_lnx_b[i], o_w_o[i])
        h = h + mix
        hn = rms_norm(h, norm_ffn_g[layer])
        h = h + peer_ffn(hn, peer_w_q[layer], peer_sub_keys[layer], peer_u[layer], peer_v[layer])
    return rms_norm(h, final_g)
```

```python
import numpy as np
from contextlib import ExitStack
import concourse.bass as bass
import concourse.mybir as mybir
from concourse.bass_utils import run_bass_kernel_spmd

F32 = mybir.dt.float32
BF16 = mybir.dt.bfloat16
U32 = mybir.dt.uint32
AF = mybir.ActivationFunctionType
ALU = mybir.AluOpType
AX = mybir.AxisListType

L = 2048
D = 1024
NT = L // 128
EPS = 1e-6
ENGS = ("pe", "dve", "act", "pool", "sp")


class MK:
    ROT = 1 << 30

    def __init__(self, nc, es):
        self.nc = nc
        self.es = es
        self.eng = dict(pe=nc.tensor, dve=nc.vector, act=nc.scalar, pool=nc.gpsimd, sp=nc.sync)
        self.esem = {}
        self.prev_ep = {}
        self.ecnt = {e: 0 for e in ENGS}
        self.seen = {e: {} for e in ENGS}
        self.lastw = {}
        self.readers = {}
        self.dsem = {}
        self.free_dsem = []
        self.ndsem = 0
        self.nsem = 0
        self.nins = {e: 0 for e in ENGS}
        self.spare = [self._newsem("spare%d" % i) for i in range(6)]
        for e in ENGS:
            self._rot(e)

    def _newsem(self, name):
        self.nsem += 1
        return self.es.enter_context(self.nc.semaphore(name))

    def _rot(self, e):
        if e in self.esem and self.esem[e][2] > 0:
            self.prev_ep[e] = self.esem[e][:3]
        ep = self.esem[e][3] + 1 if e in self.esem else 0
        name = "s_%s_%d" % (e, ep)
        sem = self.spare.pop() if (ep > 0 and self.spare) else self._newsem(name)
        self.esem[e] = (name, sem, 0, ep)

    def _deps(self, r, w):
        d = {}

        def add(p):
            name, sem, c = p
            if name not in d or d[name][1] < c:
                d[name] = (sem, c)

        for k in r:
            if k in self.lastw:
                add(self.lastw[k])
        for k in w:
            if k in self.lastw:
                add(self.lastw[k])
            for n, (s, c) in self.readers.get(k, {}).items():
                add((n, s, c))
        return d

    def _wait(self, e, d):
        E = self.eng[e]
        seen = self.seen[e]
        for name, (sem, c) in d.items():
            if seen.get(name, 0) >= c:
                continue
            E.wait_ge(sem, c)
            seen[name] = c

    def _record(self, p, r, w):
        name, sem, c = p
        for k in w:
            self.lastw[k] = p
            self.readers[k] = {}
        for k in r:
            rd = self.readers.setdefault(k, {})
            if name not in rd or rd[name][1] < c:
                rd[name] = (sem, c)

    def op(self, e, fn, r=(), w=()):
        w = list(w) + [x for x in r if isinstance(x, tuple) and x and x[0] == "ps" and x not in w]
        d = self._deps(r, w)
        if e == "pe":
            d = {n: v for n, v in d.items() if not n.startswith("s_pe_")}
        self._wait(e, d)
        name, sem, cnt, ep = self.esem[e]
        if cnt >= self.ROT:
            self._rot(e)
            name, sem, cnt, ep = self.esem[e]
        ins = fn(self.eng[e])
        cnt += 1
        self.esem[e] = (name, sem, cnt, ep)
        ins.then_inc(sem, 1)
        self.nins[e] += 1
        self._record((name, sem, cnt), r, w)
        return ins

    def dma(self, e, out, in_, r=(), w=(), **kw):
        d = self._deps(r, w)
        self._wait(e, d)
        key = w[0] if len(w) else r[0]
        skey = ("dma", key)
        if skey not in self.dsem:
            if self.free_dsem:
                self.dsem[skey] = self.free_dsem.pop()
            else:
                name = "d%d" % self.ndsem
                self.ndsem += 1
                self.dsem[skey] = [name, self._newsem(name), 0]
        ent = self.dsem[skey]
        ins = self.eng[e].dma_start(out=out, in_=in_, **kw)
        ent[2] += 16
        ins.then_inc(ent[1], 16)
        self.nins[e] += 1
        self._record((ent[0], ent[1], ent[2]), r, w)
        return ins

    def barrier(self):
        d = {}
        for e in ENGS:
            name, sem, cnt, ep = self.esem[e]
            if cnt > 0:
                d[name] = (sem, cnt)
            elif e in self.prev_ep:
                pn, psem, pcnt = self.prev_ep[e]
                d[pn] = (psem, pcnt)
        for ent in self.dsem.values():
            if ent[2] > 0:
                d[ent[0]] = (ent[1], ent[2])
        for e in ENGS:
            dd = d
            if e == "pe":
                dd = {n: v for n, v in d.items() if not n.startswith("s_pe_")}
            self._wait(e, dd)
        self.free_dsem.extend(self.dsem.values())
        self.dsem = {}


def tsl(tt):
    return slice(tt * 128, (tt + 1) * 128)


class Ctx:
    pass


SKIP = set()
CUT = [99.0]
GELU_FN = [AF.Gelu]
NTT = [NT]
NOINJ = [False]
INJV = [0]


def setup_common(nc, es, dbg):
    c = Ctx()
    c.nc = nc
    c.es = es
    c.dbg = dbg
    k = c.k = MK(nc, es)
    E = es.enter_context

    def dram_in(name, shape):
        return nc.dram_tensor(name, list(shape), F32, kind="ExternalInput").ap()

    c.din = dram_in
    c.x_d = dram_in("x", [L, D])
    c.out_d = nc.dram_tensor("out", [L, D], F32, kind="ExternalOutput").ap()
    c.norm_mix_g = dram_in("norm_mix_g", [2, D])
    c.norm_ffn_g = dram_in("norm_ffn_g", [2, D])
    c.final_g = dram_in("final_g", [1, D])

    used = {}

    def sb(name, shape, dt, st=None):
        n = used.get(name, 0)
        used[name] = n + 1
        nm = name if n == 0 else "%s_%d" % (name, n)
        return (st or es).enter_context(nc.sbuf_tensor(nm, list(shape), dt))

    c.sb = sb
    c.h = sb("h", [128, NT, D], F32)
    c.gbc = sb("gbc", [128, D], F32)
    c.ident = sb("ident", [128, 128], BF16)
    c.identf = sb("identf", [128, 128], F32)
    c.ones_f = sb("ones_f", [128, 128], F32)
    c.ones_b = sb("ones_b", [128, 128], BF16)
    c.onecol = sb("onecol", [128, 1], F32)
    c.ss = sb("ss", [128, NT], F32)
    c.rstd = sb("rstd", [128, NT], F32)
    c.junk = sb("junk", [128, D], BF16)
    c.xs = [sb("xs%d" % i, [128, D], BF16) for i in range(2)]
    c.ps = [E(nc.psum_tensor("ps%d" % i, [128, 512], F32)) for i in range(8)]
    c.pbf = [c.ps[i][:].bitcast(BF16) for i in range(8)]
    c.triU_f = sb("triU_f", [128, 128], F32)
    c.triU_b = sb("triU_b", [128, 128], BF16)

    k.op("pool", lambda e: e.memset(c.identf[:], 0.0), w=["identf"])
    k.op("pool", lambda e: e.affine_select(out=c.identf[:], in_=c.identf[:], pattern=[[-1, 128]],
                                            compare_op=ALU.not_equal, fill=1.0, base=0, channel_multiplier=1),
         r=["identf"], w=["identf"])
    k.op("dve", lambda e: e.tensor_copy(out=c.ident[:], in_=c.identf[:]), r=["identf"], w=["ident"])
    k.op("pool", lambda e: e.memset(c.ones_f[:], 1.0), w=["ones_f"])
    k.op("pool", lambda e: e.memset(c.ones_b[:], 1.0), w=["ones_b"])
    k.op("pool", lambda e: e.memset(c.onecol[:], 1.0), w=["onecol"])
    k.op("pool", lambda e: e.affine_select(out=c.triU_f[:], in_=c.ones_f[:], pattern=[[1, 128]],
                                            compare_op=ALU.is_ge, fill=0.0, base=0, channel_multiplier=-1),
         r=["ones_f"], w=["triU_f"])
    k.op("dve", lambda e: e.tensor_copy(out=c.triU_b[:], in_=c.triU_f[:]), r=["triU_f"], w=["triU_b"])
    for tt in range(NT):
        k.dma("sp", c.h[:, tt, :], c.x_d[tsl(tt), :], w=[("h", tt)])
    return c


def rms_stats(c):
    k = c.k
    for tt in range(NT):
        k.op("act", lambda e: e.activation(out=c.junk[:], in_=c.h[:, tt, :], func=AF.Square,
                                           accum_out=c.ss[:, tt:tt + 1]),
             r=[("h", tt)], w=["junk", ("ss", tt)])
    allss = [("ss", tt) for tt in range(NT)]
    k.op("dve", lambda e: e.tensor_scalar(out=c.rstd[:], in0=c.ss[:], scalar1=1.0 / D, scalar2=EPS,
                                          op0=ALU.mult, op1=ALU.add), r=allss, w=["rstd"])
    k.op("act", lambda e: e.activation(out=c.rstd[:], in_=c.rstd[:], func=AF.Sqrt), r=["rstd"], w=["rstd"])
    k.op("dve", lambda e: e.reciprocal(out=c.rstd[:], in_=c.rstd[:]), r=["rstd"], w=["rstd"])


def rmsnorm_T(c, g_ap, xT, tag, off=0):
    k = c.k
    k.dma("sp", c.gbc[:], g_ap.partition_broadcast(128), w=["gbc"])
    rms_stats(c)
    for tt in range(NT):
        xb = c.xs[tt % 2]
        xk = ("xs", tt % 2)
        k.op("dve", lambda e: e.scalar_tensor_tensor(out=xb[:], in0=c.h[:, tt, :], scalar=c.rstd[:, tt:tt + 1],
                                                     in1=c.gbc[:], op0=ALU.mult, op1=ALU.mult),
             r=[("h", tt), "rstd", "gbc"], w=[xk])
        b = 6 + tt % 2
        pst = c.pbf[b]
        pk = ("ps", b)
        for ch in range(8):
            k.op("pe", lambda e: e.transpose(out=pst[:, ch * 128:(ch + 1) * 128], in_=xb[:, ch * 128:(ch + 1) * 128],
                                             identity=c.ident[:]),
                 r=[xk, "ident"], w=[pk])
        k.op("act", lambda e: e.activation(out=xT[:, :, off + tt * 128:off + (tt + 1) * 128],
                                           in_=pst.rearrange("p (c t) -> p c t", c=8), func=AF.Copy),
             r=[pk], w=[(tag, tt)])


def final_norm_store(c):
    k = c.k
    k.dma("sp", c.gbc[:], c.final_g[0, :].partition_broadcast(128), w=["gbc"])
    rms_stats(c)
    for tt in range(NT):
        k.op("dve", lambda e: e.scalar_tensor_tensor(out=c.h[:, tt, :], in0=c.h[:, tt, :], scalar=c.rstd[:, tt:tt + 1],
                                                     in1=c.gbc[:], op0=ALU.mult, op1=ALU.mult),
             r=[("h", tt), "rstd", "gbc"], w=[("h", tt)])
        k.dma("sp", c.out_d[tsl(tt), :], c.h[:, tt, :], r=[("h", tt)], w=[("out", tt)])
    k._wait("sp", k._deps([("out", tt) for tt in range(NT)], []))


def cmul(k, eng_a, eng_b, o_re, o_im, a_re, a_im, b_re, b_im, t, rk, wk, tk):
    k.op(eng_a, lambda e: e.tensor_tensor(out=t[0], in0=a_re, in1=b_re, op=ALU.mult), r=rk, w=[tk + "0"])
    k.op(eng_a, lambda e: e.tensor_tensor(out=t[1], in0=a_im, in1=b_im, op=ALU.mult), r=rk, w=[tk + "1"])
    k.op(eng_b, lambda e: e.tensor_tensor(out=o_re, in0=t[0], in1=t[1], op=ALU.subtract),
         r=[tk + "0", tk + "1"], w=[wk + "_re"])
    k.op(eng_a, lambda e: e.tensor_tensor(out=t[0], in0=a_re, in1=b_im, op=ALU.mult), r=rk, w=[tk + "0"])
    k.op(eng_a, lambda e: e.tensor_tensor(out=t[1], in0=a_im, in1=b_re, op=ALU.mult), r=rk, w=[tk + "1"])
    k.op(eng_b, lambda e: e.tensor_tensor(out=o_im, in0=t[0], in1=t[1], op=ALU.add),
         r=[tk + "0", tk + "1"], w=[wk + "_im"])


def even_mixer(c, li):
    nc, k, sb, ps, pbf, h = c.nc, c.k, c.sb, c.ps, c.pbf, c.h
    din = c.din
    w_in_d = din("e_w_in", [D, 2064])
    w_out_d = din("e_w_out", [D, D])
    a_re_d = din("s5_a_re", [32, 64])
    a_im_d = din("s5_a_im", [32, 64])
    ldt_d = din("s5_log_dt", [32, 64])
    b_re_d = din("s5_b_re", [32, 64, 16])
    b_im_d = din("s5_b_im", [32, 64, 16])
    c_re_d = din("s5_c_re", [32, 16, 64])
    c_im_d = din("s5_c_im", [32, 16, 64])
    d_d = din("s5_d", [1, 512])
    wglu_d = din("s5_w_glu", [512, 512])
    wg2_d = din("gla_w_g2", [16, 256])
    bg2_d = din("gla_b_g2", [1, 256])
    gng_d = din("gla_norm_g", [1, 512])

    with ExitStack() as ph:
        xy = sb("xy", [128, 8, L], BF16, ph)
        uT = sb("uT", [128, 4, L], BF16, ph)
        xT = xy
        yT = xy
        with ExitStack() as pg:
            qkT = sb("qkT", [128, 4, L], BF16, pg)
            rT = sb("rT", [128, 4, L], BF16, pg)
            glowT = sb("glowT", [16, L], BF16, pg)
            vk = sb("vk", [128, NT, 768], BF16, pg)
            with ExitStack() as p1:
                wi = sb("wi", [128, 8, 1040], BF16, p1)
                rmsnorm_T(c, c.norm_mix_g[li, :], xT, "xT")
                allx = [("xT", tt) for tt in range(NT)]
                allw = [("wi", ch) for ch in range(8)]
                n = 0
                for piece in range(2):
                    cb = piece * 1024
                    ncol = 1024 if piece == 0 else 1040
                    for ch in range(8):
                        k.dma("pool", wi[:, ch, 0:ncol], w_in_d[ch * 128:(ch + 1) * 128, cb:cb + ncol], w=[("wi", ch)])
                    if piece == 0:
                        chunks = ([(i * 128, 128, uT, i, AF.Copy) for i in range(4)] +
                                  [(512 + i * 128, 128, qkT, i, AF.Copy) for i in range(4)])
                    else:
                        chunks = ([(1552 + i * 128, 128, rT, i, AF.Silu) for i in range(4)] +
                                  [(1536, 16, None, 0, AF.Copy)])
                    for (c0, m, dst, di, fn) in chunks:
                        for tb in range(4):
                            b = n % 4
                            n += 1
                            for ch in range(8):
                                k.op("pe", lambda e: e.matmul(ps[b][0:m, :], lhsT=wi[:, ch, c0 - cb:c0 - cb + m],
                                                              rhs=xT[:, ch, tb * 512:(tb + 1) * 512],
                                                              start=(ch == 0), stop=(ch == 7)),
                                     r=allx + allw, w=[("ps", b)])
                            if dst is None:
                                k.op("act", lambda e: e.activation(out=glowT[:, tb * 512:(tb + 1) * 512], in_=ps[b][0:16, :],
                                                                   func=AF.Copy), r=[("ps", b)], w=["glowT"])
                            else:
                                k.op("act", lambda e: e.activation(out=dst[:, di, tb * 512:(tb + 1) * 512], in_=ps[b][:, :],
                                                                   func=fn), r=[("ps", b)], w=[(dst.name, di)])
                    for tt in range(NT):
                        b0 = 4 + tt % 4
                        if piece == 0:
                            for ch in range(8):
                                k.op("pe", lambda e: e.matmul(ps[b0][:, 0:256], lhsT=xT[:, ch, tsl(tt)], rhs=wi[:, ch, 768:1024],
                                                              start=(ch == 0), stop=(ch == 7)), r=allx + allw, w=[("ps", b0)])
                            k.op("dve", lambda e: e.tensor_copy(out=vk[:, tt, 512:768], in_=ps[b0][:, 0:256]), r=[("ps", b0)],
                                 w=[("vk", tt)])
                        else:
                            for ch in range(8):
                                k.op("pe", lambda e: e.matmul(ps[b0][:, :], lhsT=xT[:, ch, tsl(tt)], rhs=wi[:, ch, 0:512],
                                                              start=(ch == 0), stop=(ch == 7)), r=allx + allw, w=[("ps", b0)])
                            k.op("dve", lambda e: e.tensor_copy(out=vk[:, tt, 0:512], in_=ps[b0][:, :]), r=[("ps", b0)],
                                 w=[("vk", tt)])
                k.barrier()
            if "uT" in c.dbg:
                for i in range(4):
                    k.dma("sp", c.dbg["uT"][i], uT[:, i, :], r=[("uT", i)], w=["dbg_uT"])
            if "vk" in c.dbg:
                k.dma("sp", c.dbg["vk"], vk[:, 3, :], r=[("vk", 3)], w=["dbg_vk"])
            if 'gla' not in SKIP:
                gla(c, pg, qkT, rT, glowT, vk, yT, wg2_d, bg2_d, gng_d)
            k.barrier()
        if 's5' not in SKIP:
            s5(c, ph, uT, yT, a_re_d, a_im_d, ldt_d, b_re_d, b_im_d, c_re_d, c_im_d, d_d, wglu_d)
        k.barrier()
        if "yT" in c.dbg:
            for i in range(8):
                k.dma("sp", c.dbg["yT"][i], yT[:, i, :], r=[("yT", i)], w=["dbg_yT"])
        with ExitStack() as p3:
            wo = sb("wo", [128, 8, D], BF16, p3)
            for ch in range(8):
                k.dma("pool", wo[:, ch, :], w_out_d[ch * 128:(ch + 1) * 128, :], w=[("wo", ch)])
            ally = [("yT", i) for i in range(8)]
            allw = [("wo", ch) for ch in range(8)]
            for tt in range(NT):
                for hf in range(2):
                    b = (tt * 2 + hf) % 4
                    for ch in range(8):
                        k.op("pe", lambda e: e.matmul(ps[b][:, :], lhsT=yT[:, ch, tsl(tt)],
                                                      rhs=wo[:, ch, hf * 512:(hf + 1) * 512],
                                                      start=(ch == 0), stop=(ch == 7)), r=ally + allw, w=[("ps", b)])
                    k.op("dve", lambda e: e.tensor_tensor(out=h[:, tt, hf * 512:(hf + 1) * 512],
                                                          in0=h[:, tt, hf * 512:(hf + 1) * 512], in1=ps[b][:, :],
                                                          op=ALU.add), r=[("ps", b), ("h", tt)], w=[("h", tt)])
            k.barrier()


def gla(c, ph, qkT, rT, glowT, vk, yT, wg2_d, bg2_d, gng_d):
    nc, k, sb, ps, pbf = c.nc, c.k, c.sb, c.ps, c.pbf
    with ExitStack() as p2:
        wg2 = sb("wg2", [16, 256], BF16, p2)
        bg2 = sb("bg2", [1, 256], BF16, p2)
        gng = sb("gng", [128, 4], F32, p2)
        triUs = sb("triUs", [128, 128], F32, p2)
        triRs = sb("triRs", [128, 128], F32, p2)
        lp = sb("lp", [128, 256], F32, p2)
        eend = sb("eend", [128, 256], F32, p2)
        ebT = sb("ebT", [128, 2, 128], F32, p2)
        enbT = sb("enbT", [128, 2, 128], F32, p2)
        kend = sb("kend", [128, 256], BF16, p2)
        qd = sb("qd", [128, 2, 128], BF16, p2)
        kd = sb("kd", [128, 2, 128], BF16, p2)
        qdz = sb("qdz", [128, 4, 128], BF16, p2)
        hmask = sb("hmask", [128, 2], F32, p2)
        attT = sb("attT", [128, 4, 128], BF16, p2)
        S = sb("S", [128, 2, 128], F32, p2)
        Sb = sb("Sb", [128, 2, 128], BF16, p2)
        ssq = sb("ssq", [128, 4], F32, p2)
        rso = sb("rso", [128, 4], F32, p2)
        on = sb("on", [128, 4, 128], BF16, p2)
        k.dma("pool", wg2[:], wg2_d[:, :], w=["wg2"])
        k.dma("pool", bg2[:], bg2_d[:, :], w=["bg2"])
        k.dma("sp", gng[:], gng_d[0, :].rearrange("(h v) -> v h", v=128), w=["gng"], allow_slow_non_contiguous=True)
        k.op("dve", lambda e: e.tensor_scalar(out=triUs[:], in0=c.triU_f[:], scalar1=-1.0 / 16, scalar2=None,
                                              op0=ALU.mult), r=["triU_f"], w=["triUs"])
        k.op("dve", lambda e: e.tensor_scalar(out=triRs[:], in0=c.triU_f[:], scalar1=1.0 / 16, scalar2=-1.0 / 16,
                                              op0=ALU.mult, op1=ALU.add), r=["triU_f"], w=["triRs"])
        k.op("pool", lambda e: e.memset(hmask[:], 0.0), w=["hmask"])
        k.op("pool", lambda e: e.memset(hmask[0:64, 0:1], 1.0), r=["hmask"], w=["hmask"])
        k.op("pool", lambda e: e.memset(hmask[64:128, 1:2], 1.0), r=["hmask"], w=["hmask"])
        k.op("pool", lambda e: e.memset(S[:], 0.0), w=["S"])
        k.op("pool", lambda e: e.memset(Sb[:], 0.0), w=["Sb"])
        for tt in range(NT):
            k.op("pe", lambda e: e.matmul(ps[0][:, 0:256], lhsT=glowT[:, tsl(tt)], rhs=wg2[:, :], start=True, stop=False),
                 r=["glowT", "wg2"], w=[("ps", 0)])
            k.op("pe", lambda e: e.matmul(ps[0][:, 0:256], lhsT=c.ones_b[0:1, :], rhs=bg2[:, :], start=False, stop=True),
                 r=["ones_b", "bg2"], w=[("ps", 0)])
            if CUT[0] <= 1:
                break
            k.op("act", lambda e: e.activation(out=lp[:], in_=ps[0][:, 0:256], func=AF.Exp, scale=-1.0),
                 r=[("ps", 0)], w=["lp"])
            k.op("act", lambda e: e.activation(out=lp[:], in_=lp[:], func=AF.Ln, bias=c.onecol[:], scale=1.0),
                 r=["lp", "onecol"], w=["lp"])
            if CUT[0] <= 2:
                break
            k.op("pe", lambda e: e.matmul(ps[1][:, 0:256], lhsT=triRs[:], rhs=lp[:], start=True, stop=True),
                 r=["triRs", "lp"], w=[("ps", 1)])
            for hf in range(2):
                k.op("pe", lambda e: e.matmul(ps[1][:, 256 + hf * 128:256 + (hf + 1) * 128],
                                              lhsT=lp[:, hf * 128:(hf + 1) * 128], rhs=triUs[:], start=True, stop=True),
                     r=["triUs", "lp"], w=[("ps", 1)])
            k.op("act", lambda e: e.activation(out=eend[:], in_=ps[1][:, 0:256], func=AF.Exp), r=[("ps", 1)], w=["eend"])
            k.op("act", lambda e: e.activation(out=ebT[:].rearrange("p a b -> p (a b)"), in_=ps[1][:, 256:512], func=AF.Exp),
                 r=[("ps", 1)], w=["ebT"])
            k.op("act", lambda e: e.activation(out=enbT[:].rearrange("p a b -> p (a b)"), in_=ps[1][:, 256:512], func=AF.Exp,
                                               scale=-1.0), r=[("ps", 1)], w=["enbT"])
            if CUT[0] <= 3:
                break
            k.op("dve", lambda e: e.tensor_tensor(out=kend[:], in0=vk[:, tt, 512:768], in1=eend[:], op=ALU.mult),
                 r=[("vk", tt), "eend"], w=["kend"])
            k.op("dve", lambda e: e.scalar_tensor_tensor(out=qd[:], in0=qkT[:, 0:2, tsl(tt)], scalar=0.125, in1=ebT[:],
                                                         op0=ALU.mult, op1=ALU.mult),
                 r=[("qkT", 0), ("qkT", 1), "ebT"], w=["qd"])
            k.op("dve", lambda e: e.tensor_tensor(out=kd[:], in0=qkT[:, 2:4, tsl(tt)], in1=enbT[:], op=ALU.mult),
                 r=[("qkT", 2), ("qkT", 3), "enbT"], w=["kd"])
            if CUT[0] <= 5:
                break
            for hd in range(4):
                pr = hd // 2
                k.op("dve", lambda e: e.tensor_scalar(out=qdz[:, hd, :], in0=qd[:, pr, :], scalar1=hmask[:, hd % 2:hd % 2 + 1],
                                                      scalar2=None, op0=ALU.mult), r=["qd", "hmask"], w=["qdz"])
            for hd in range(4):
                pr = hd // 2
                k.op("pe", lambda e: e.matmul(ps[2][:, hd * 128:(hd + 1) * 128], lhsT=kd[:, pr, :],
                                              rhs=qdz[:, hd, :], start=True, stop=True),
                     r=["kd", "qdz"], w=[("ps", 2)])
            if CUT[0] <= 5.3:
                break
            k.op("dve", lambda e: e.tensor_tensor(out=attT[:], in0=ps[2][:, :].rearrange("p (a b) -> p a b", a=4),
                                                  in1=c.triU_f[:].unsqueeze(1).to_broadcast([128, 4, 128]), op=ALU.mult),
                 r=[("ps", 2), "triU_f"], w=["attT"])
            if CUT[0] <= 5.6:
                break
            for hd in range(4):
                pr, p0 = hd // 2, (hd % 2) * 64
                k.op("pe", lambda e: e.matmul(ps[3][:, hd * 128:(hd + 1) * 128], lhsT=attT[:, hd, :],
                                              rhs=vk[:, tt, hd * 128:(hd + 1) * 128], start=True, stop=False),
                     r=["attT", ("vk", tt)], w=[("ps", 3)])
                k.op("pe", lambda e: e.matmul(ps[3][:, hd * 128:(hd + 1) * 128], lhsT=qdz[:, hd, :],
                                              rhs=Sb[:, pr, :], start=False, stop=True),
                     r=["qdz", "Sb"], w=[("ps", 3)])
            if CUT[0] <= 6:
                break
            for pr in range(2):
                k.op("pe", lambda e: e.matmul(ps[4][:, pr * 256:(pr + 1) * 256], lhsT=kend[:, pr * 128:(pr + 1) * 128],
                                              rhs=vk[:, tt, pr * 256:(pr + 1) * 256], start=True, stop=True),
                     r=["kend", ("vk", tt)], w=[("ps", 4)])
            for hd in range(4):
                pr, hf, p0 = hd // 2, hd % 2, (hd % 2) * 64
                k.op("dve", lambda e: e.scalar_tensor_tensor(
                    out=S[p0:p0 + 64, pr, :], in0=S[p0:p0 + 64, pr, :], scalar=ebT[p0:p0 + 64, pr, 127:128],
                    in1=ps[4][p0:p0 + 64, pr * 256 + hf * 128:pr * 256 + (hf + 1) * 128], op0=ALU.mult, op1=ALU.add),
                     r=["S", "ebT", ("ps", 4)], w=["S"])
            k.op("dve", lambda e: e.tensor_copy(out=Sb[:], in_=S[:]), r=["S"], w=["Sb"])
            if CUT[0] <= 7:
                break
            for hd in range(4):
                k.op("act", lambda e: e.activation(out=c.junk[:, 0:128], in_=ps[3][:, hd * 128:(hd + 1) * 128],
                                                   func=AF.Square, accum_out=ssq[:, hd:hd + 1]),
                     r=[("ps", 3)], w=["junk", "ssq"])
            k.op("dve", lambda e: e.tensor_scalar(out=rso[:], in0=ssq[:], scalar1=1.0 / 128, scalar2=EPS,
                                                  op0=ALU.mult, op1=ALU.add), r=["ssq"], w=["rso"])
            k.op("act", lambda e: e.activation(out=rso[:], in_=rso[:], func=AF.Sqrt), r=["rso"], w=["rso"])
            k.op("dve", lambda e: e.reciprocal(out=rso[:], in_=rso[:]), r=["rso"], w=["rso"])
            k.op("dve", lambda e: e.tensor_tensor(out=on[:], in0=ps[3][:, :].rearrange("p (a b) -> p a b", a=4),
                                                  in1=rso[:].unsqueeze(2).to_broadcast([128, 4, 128]), op=ALU.mult),
                 r=[("ps", 3), "rso"], w=["on"])
            for hd in range(4):
                k.op("pe", lambda e: e.transpose(out=pbf[5][:, hd * 128:(hd + 1) * 128], in_=on[:, hd, :],
                                                 identity=c.ident[:]), r=["on", "ident"], w=[("ps", 5)])
            for hd in range(4):
                k.op("dve", lambda e: e.scalar_tensor_tensor(
                    out=yT[:, 4 + hd, tsl(tt)], in0=pbf[5][:, hd * 128:(hd + 1) * 128], scalar=gng[:, hd:hd + 1],
                    in1=rT[:, hd, tsl(tt)], op0=ALU.mult, op1=ALU.mult),
                     r=[("ps", 5), "gng", ("rT", hd)], w=[("yT", 4 + hd)])


def s5(c, ph, uT, yT, a_re_d, a_im_d, ldt_d, b_re_d, b_im_d, c_re_d, c_im_d, d_d, wglu_d):
    nc, k, sb, ps, pbf = c.nc, c.k, c.sb, c.ps, c.pbf
    with ExitStack() as p2:
        wb = sb("wb", [128, 2, 4, 512], BF16, p2)
        cm = sb("cm", [128, 2, 16, 128], BF16, p2)
        with ExitStack() as pa:
            wbs = sb("wbs", [128, 2, 4, 512], F32, pa)
            cms = sb("cms", [128, 2, 16, 128], F32, pa)
            k.op("pool", lambda e: e.memset(wbs[:].rearrange("p a b c -> p (a b c)"), 0.0), w=["wbs"])
            k.op("pool", lambda e: e.memset(cms[:].rearrange("p a b c -> p (a b c)"), 0.0), w=["cms"])
            for ri, bd in enumerate((b_re_d, b_im_d)):
                for g in range(32):
                    kc, g8 = g // 8, g % 8
                    k.dma("sp", wbs[g8 * 16:(g8 + 1) * 16, ri, kc, g8 * 64:(g8 + 1) * 64],
                          bd[g].rearrange("p c -> c p"), r=[], w=["wbs"], allow_slow_non_contiguous=True)
            for ri, cd in enumerate((c_re_d, c_im_d)):
                for g in range(32):
                    ct, gl = g // 2, g % 2
                    g8 = g % 8
                    k.dma("sp", cms[gl * 64:(gl + 1) * 64, ri, ct, g8 * 16:(g8 + 1) * 16],
                          cd[g].rearrange("c p -> p c"), r=[], w=["cms"], allow_slow_non_contiguous=True)
            k.op("act", lambda e: e.activation(out=wb[:].rearrange("p a b c -> p (a b c)"),
                                               in_=wbs[:].rearrange("p a b c -> p (a b c)"), func=AF.Copy),
                 r=["wbs"], w=["wb"])
            k.op("act", lambda e: e.activation(out=cm[:, 0].rearrange("p b c -> p (b c)"),
                                               in_=cms[:, 0].rearrange("p b c -> p (b c)"), func=AF.Copy),
                 r=["cms"], w=["cm"])
            k.op("act", lambda e: e.activation(out=cm[:, 1].rearrange("p b c -> p (b c)"),
                                               in_=cms[:, 1].rearrange("p b c -> p (b c)"), func=AF.Copy, scale=-1.0),
                 r=["cms"], w=["cm"])
            k.barrier()
        if CUT[0] <= 10:
            return
        dcol = sb("dcol", [128, 4], F32, p2)
        k.dma("sp", dcol[:], d_d[0, :].rearrange("(c p) -> p c", p=128), w=["dcol"], allow_slow_non_contiguous=True)
        wglu = sb("wglu", [128, 4, 512], BF16, p2)
        for ch in range(4):
            k.dma("pool", wglu[:, ch, :], wglu_d[ch * 128:(ch + 1) * 128, :], w=["wglu"])
        ETb = sb("ETb", [128, 2, 16, 128], BF16, p2)
        EVt = sb("EVt", [128, 2, 2048], BF16, p2)
        a128 = sb("a128", [128, 2, 16], F32, p2)
        with ExitStack() as pb:
            prm = sb("prm", [16, 3, 128], F32, pb)
            for i, pd in enumerate((a_re_d, a_im_d, ldt_d)):
                k.dma("sp", prm[:, i, :], pd.rearrange("(ct gl) p -> ct (gl p)", gl=2), w=["prm"])
            for i in range(3):
                k.op("pe", lambda e: e.transpose(out=ps[0][:, i * 16:(i + 1) * 16], in_=prm[:, i, :], identity=c.identf[0:16, 0:16]),
                     r=["prm", "identf"], w=[("ps", 0)])
            if CUT[0] <= 10.5:
                return
            P = sb("P", [128, 24, 16], F32, pb)
            AR, AI, DT, MAG, TH, S_, C_, T0, T1, RM, FRE, FIM, NR, DEN, ABR, ABI, AVR, AVI = range(18)
            k.op("dve", lambda e: e.tensor_copy(out=P[:, 0:3, :], in_=ps[0][:, 0:48].rearrange("p (a b) -> p a b", a=3)),
                 r=[("ps", 0)], w=["P"])

            def tt_(o, a, b, op, eng="dve"):
                k.op(eng, lambda e: e.tensor_tensor(out=P[:, o, :], in0=P[:, a, :], in1=P[:, b, :], op=op), r=["P"], w=["P"])

            def act_(o, a, fn, scale=1.0):
                k.op("act", lambda e: e.activation(out=P[:, o, :], in_=P[:, a, :], func=fn, scale=scale), r=["P"], w=["P"])

            def ts_(o, a, s1, s2, op0, op1):
                k.op("dve", lambda e: e.tensor_scalar(out=P[:, o, :], in0=P[:, a, :], scalar1=s1, scalar2=s2, op0=op0, op1=op1),
                     r=["P"], w=["P"])

            if CUT[0] <= 11:
                return
            act_(DT, DT, AF.Exp)
            tt_(T0, DT, AR, ALU.mult)
            act_(MAG, T0, AF.Exp)
            act_(RM, T0, AF.Exp, scale=-1.0)
            tt_(TH, DT, AI, ALU.mult)
            act_(T0, TH, AF.Sin, scale=1.0 / 16)
            tt_(T0, T0, T0, ALU.mult)
            ts_(C_, T0, -2.0, 1.0, ALU.mult, ALU.add)
            act_(S_, TH, AF.Sin, scale=1.0 / 8)
            for _ in range(3):
                tt_(T0, C_, C_, ALU.mult)
                tt_(T1, S_, S_, ALU.mult)
                tt_(S_, S_, C_, ALU.mult)
                ts_(S_, S_, 2.0, None, ALU.mult, ALU.bypass)
                tt_(C_, T0, T1, ALU.subtract)
            tt_(ABR, MAG, C_, ALU.mult)
            tt_(ABI, MAG, S_, ALU.mult)
            tt_(AVR, RM, C_, ALU.mult)
            tt_(AVI, RM, S_, ALU.mult)
            ts_(AVI, AVI, -1.0, None, ALU.mult, ALU.bypass)
            ts_(NR, ABR, -1.0, None, ALU.add, ALU.bypass)
            tt_(T0, AR, AR, ALU.mult)
            tt_(T1, AI, AI, ALU.mult)
            tt_(DEN, T0, T1, ALU.add)
            k.op("dve", lambda e: e.reciprocal(out=P[:, DEN, :], in_=P[:, DEN, :]), r=["P"], w=["P"])
            tt_(T0, NR, AR, ALU.mult)
            tt_(T1, ABI, AI, ALU.mult)
            tt_(T0, T0, T1, ALU.add)
            tt_(FRE, T0, DEN, ALU.mult)
            tt_(T0, ABI, AR, ALU.mult)
            tt_(T1, NR, AI, ALU.mult)
            tt_(T0, T0, T1, ALU.subtract)
            tt_(FIM, T0, DEN, ALU.mult)
            if CUT[0] <= 12:
                return
            ET = sb("ET", [128, 2, 16, 128], F32, pb)
            EV = sb("EV", [128, 2, 16, 128], F32, pb)
            tmp = sb("s5tmp", [128, 2, 16, 64], F32, pb)
            pw = sb("s5pw", [128, 2, 16], F32, pb)
            pw2 = sb("s5pw2", [128, 2, 16], F32, pb)
            for (tab, br, bi, i0r, i0i, name) in ((ET, ABR, ABI, None, None, "ET"), (EV, AVR, AVI, FRE, FIM, "EV")):
                if i0r is None:
                    k.op("pool", lambda e: e.memset(tab[:, 0, :, 0:1], 1.0), w=[name])
                    k.op("pool", lambda e: e.memset(tab[:, 1, :, 0:1], 0.0), w=[name])
                else:
                    k.op("dve", lambda e: e.tensor_copy(out=tab[:, 0, :, 0:1], in_=P[:, i0r, :].unsqueeze(2)), r=["P"], w=[name])
                    k.op("dve", lambda e: e.tensor_copy(out=tab[:, 1, :, 0:1], in_=P[:, i0i, :].unsqueeze(2)), r=["P"], w=[name])
                k.op("dve", lambda e: e.tensor_copy(out=pw[:, 0, :], in_=P[:, br, :]), r=["P"], w=["pw"])
                k.op("dve", lambda e: e.tensor_copy(out=pw[:, 1, :], in_=P[:, bi, :]), r=["P"], w=["pw"])
                m = 1
                while m <= 128:
                    if m < 128:
                        bre = pw[:, 0, :].unsqueeze(2).to_broadcast([128, 16, m])
                        bim = pw[:, 1, :].unsqueeze(2).to_broadcast([128, 16, m])
                        cmul(k, "dve", "dve", tab[:, 0, :, m:2 * m], tab[:, 1, :, m:2 * m],
                             tab[:, 0, :, 0:m], tab[:, 1, :, 0:m], bre, bim,
                             (tmp[:, 0, :, 0:m], tmp[:, 1, :, 0:m]), [name, name + "_re", name + "_im", "pw"], name, "s5tmp")
                    elif name == "ET":
                        k.op("dve", lambda e: e.tensor_copy(out=a128[:], in_=pw[:]), r=["pw"], w=["a128"])
                    cmul(k, "dve", "dve", pw2[:, 0, :], pw2[:, 1, :], pw[:, 0, :], pw[:, 1, :], pw[:, 0, :], pw[:, 1, :],
                         (tmp[:, 0, :, 0], tmp[:, 1, :, 0]), ["pw"], "pw2", "s5tmp")
                    k.op("dve", lambda e: e.tensor_copy(out=pw[:], in_=pw2[:]), r=["pw2_re", "pw2_im"], w=["pw"])
                    m *= 2
            if CUT[0] <= 13:
                return
            n = 0
            for ri in range(2):
                for g4 in range(4):
                    b = n % 2
                    n += 1
                    for q in range(4):
                        ct = g4 * 4 + q
                        k.op("pe", lambda e: e.transpose(out=ps[b][:, q * 128:(q + 1) * 128], in_=EV[:, ri, ct, :],
                                                         identity=c.identf[:]), r=["EV", "EV_re", "EV_im", "identf"], w=[("ps", b)])
                    k.op("act", lambda e: e.activation(out=EVt[:, ri, g4 * 512:(g4 + 1) * 512], in_=ps[b][:, :], func=AF.Copy),
                         r=[("ps", b)], w=["EVt"])

            k.op("act", lambda e: e.activation(out=ETb[:].rearrange("p a b c -> p (a b c)"),
                                               in_=ET[:].rearrange("p a b c -> p (a b c)"), func=AF.Copy),
                 r=["ET", "ET_re", "ET_im"], w=["ETb"])
            k.barrier()
        if CUT[0] <= 14:
            return
        tmpc = sb("s5tmpc", [128, 2, 16], F32, p2)
        zz = sb("zz", [128, 2, 2048], BF16, p2)
        t1 = sb("s5t1", [128, 512], F32, p2)
        t2 = sb("s5t2", [128, 512], F32, p2)
        sT = sb("sT", [128, 2, 16, 128], BF16, p2)
        lastc = sb("lastc", [128, 2, 16], F32, p2)
        cz = sb("cz", [128, 2, 16], F32, p2)
        cz2 = sb("cz2", [128, 2, 16], F32, p2)
        wr = sb("s5wr", [128, 512], F32, p2)
        wi_ = sb("s5wi", [128, 512], F32, p2)
        ypre = sb("ypre", [128, 4, 128], F32, p2)
        if CUT[0] <= 15:
            return
        for tt in range(NTT[0]):
            for kc in range(4):
                for ri in range(2):
                    k.op("pe", lambda e: e.matmul(ps[ri][:, :], lhsT=uT[:, kc, tsl(tt)], rhs=wb[:, ri, kc, :],
                                                  start=True, stop=True), r=[("uT", kc), "wb"], w=[("ps", ri)])
                er = EVt[:, 0, kc * 512:(kc + 1) * 512]
                ei = EVt[:, 1, kc * 512:(kc + 1) * 512]
                k.op("dve", lambda e: e.tensor_tensor(out=t1[:], in0=ps[0][:, :], in1=er, op=ALU.mult),
                     r=[("ps", 0), "EVt"], w=["s5t1"])
                k.op("dve", lambda e: e.tensor_tensor(out=t2[:], in0=ps[1][:, :], in1=ei, op=ALU.mult),
                     r=[("ps", 1), "EVt"], w=["s5t2"])
                k.op("pool", lambda e: e.tensor_tensor(out=zz[:, 0, kc * 512:(kc + 1) * 512], in0=t1[:], in1=t2[:],
                                                       op=ALU.subtract), r=["s5t1", "s5t2"], w=["zz"])
                k.op("dve", lambda e: e.tensor_tensor(out=t1[:], in0=ps[1][:, :], in1=er, op=ALU.mult),
                     r=[("ps", 1), "EVt"], w=["s5t1"])
                k.op("dve", lambda e: e.tensor_tensor(out=t2[:], in0=ps[0][:, :], in1=ei, op=ALU.mult),
                     r=[("ps", 0), "EVt"], w=["s5t2"])
                k.op("pool", lambda e: e.tensor_tensor(out=zz[:, 1, kc * 512:(kc + 1) * 512], in0=t1[:], in1=t2[:],
                                                       op=ALU.add), r=["s5t1", "s5t2"], w=["zz"])
            if CUT[0] <= 16 and tt >= 1:
                return
            for g4 in range(4):
                for ri in range(2):
                    b = 2 + ri
                    for q in range(4):
                        ct = g4 * 4 + q
                        k.op("pe", lambda e: e.matmul(ps[b][:, q * 128:(q + 1) * 128], lhsT=zz[:, ri, ct * 128:(ct + 1) * 128],
                                                      rhs=c.triU_b[:], start=True, stop=True),
                             r=["zz", "triU_b"], w=[("ps", b)])
                cr = ps[2][:, :].rearrange("p (a b) -> p a b", a=4)
                ci = ps[3][:, :].rearrange("p (a b) -> p a b", a=4)
                etr = ETb[:, 0, g4 * 4:(g4 + 1) * 4, :]
                eti = ETb[:, 1, g4 * 4:(g4 + 1) * 4, :]
                t1v = t1[:].rearrange("p (a b) -> p a b", a=4)
                t2v = t2[:].rearrange("p (a b) -> p a b", a=4)
                wrv = wr[:].rearrange("p (a b) -> p a b", a=4)
                wiv = wi_[:].rearrange("p (a b) -> p a b", a=4)
                if tt == 0:
                    k.op("act", lambda e: e.activation(out=wrv, in_=cr, func=AF.Copy), r=[("ps", 2)], w=["wr"])
                    k.op("act", lambda e: e.activation(out=wiv, in_=ci, func=AF.Copy), r=[("ps", 3)], w=["wi_"])
                else:
                    k.op("dve", lambda e: e.tensor_tensor(out=wrv, in0=cr, in1=cz[:, 0, g4 * 4:(g4 + 1) * 4].unsqueeze(2).to_broadcast([128, 4, 128]),
                                                          op=ALU.add), r=[("ps", 2), "cz_re"], w=["wr"])
                    k.op("dve", lambda e: e.tensor_tensor(out=wiv, in0=ci, in1=cz[:, 1, g4 * 4:(g4 + 1) * 4].unsqueeze(2).to_broadcast([128, 4, 128]),
                                                          op=ALU.add), r=[("ps", 3), "cz_im"], w=["wi_"])
                k.op("act", lambda e: e.activation(out=lastc[:, 0, g4 * 4:(g4 + 1) * 4], in_=wrv[:, :, 127], func=AF.Copy),
                     r=["wr"], w=["lastc"])
                k.op("act", lambda e: e.activation(out=lastc[:, 1, g4 * 4:(g4 + 1) * 4], in_=wiv[:, :, 127], func=AF.Copy),
                     r=["wi_"], w=["lastc"])
                k.op("dve", lambda e: e.tensor_tensor(out=t1v, in0=wrv, in1=etr, op=ALU.mult), r=["wr", "ETb"], w=["s5t1"])
                k.op("pool", lambda e: e.tensor_tensor(out=t2v, in0=wiv, in1=eti, op=ALU.mult), r=["wi_", "ETb"], w=["s5t2"])
                k.op("dve", lambda e: e.tensor_tensor(out=sT[:, 0, g4 * 4:(g4 + 1) * 4, :], in0=t1v, in1=t2v, op=ALU.subtract),
                     r=["s5t1", "s5t2"], w=["sT"])
                k.op("dve", lambda e: e.tensor_tensor(out=t1v, in0=wiv, in1=etr, op=ALU.mult), r=["wi_", "ETb"], w=["s5t1"])
                k.op("pool", lambda e: e.tensor_tensor(out=t2v, in0=wrv, in1=eti, op=ALU.mult), r=["wr", "ETb"], w=["s5t2"])
                k.op("dve", lambda e: e.tensor_tensor(out=sT[:, 1, g4 * 4:(g4 + 1) * 4, :], in0=t1v, in1=t2v, op=ALU.add),
                     r=["s5t1", "s5t2"], w=["sT"])
            if tt < NT - 1:
                cmul(k, "dve", "dve", cz2[:, 0, :], cz2[:, 1, :], lastc[:, 0, :], lastc[:, 1, :], a128[:, 0, :], a128[:, 1, :],
                     (tmpc[:, 0, :], tmpc[:, 1, :]), ["lastc", "a128"], "cz2", "s5tmpc")
                k.op("dve", lambda e: e.tensor_copy(out=cz[:], in_=cz2[:]), r=["cz2_re", "cz2_im"], w=["cz_re", "cz_im"])
            if CUT[0] <= 19 and tt >= 1:
                return
            for kc in range(4):
                n = 0
                for q in range(4):
                    ct = kc * 4 + q
                    for ri in range(2):
                        k.op("pe", lambda e: e.matmul(ps[5][:, kc * 128:(kc + 1) * 128], lhsT=cm[:, ri, ct, :],
                                                      rhs=sT[:, ri, ct, :], start=(n == 0), stop=(n == 7)),
                             r=["cm", "sT"], w=[("ps", 5)])
                        n += 1
            if CUT[0] <= 19.3 and tt >= 1:
                return
            for kc in range(4):
                k.op("dve", lambda e: e.tensor_scalar(out=ypre[:, kc, :], in0=uT[:, kc, tsl(tt)], scalar1=dcol[:, kc:kc + 1],
                                                      scalar2=None, op0=ALU.mult), r=[("uT", kc), "dcol"], w=["ypre"])
                k.op("dve", lambda e: e.tensor_tensor(out=ypre[:, kc, :], in0=ypre[:, kc, :], in1=ps[5][:, kc * 128:(kc + 1) * 128],
                                                      op=ALU.add), r=["ypre", ("ps", 5)], w=["ypre"])
            if CUT[0] <= 19.6 and tt >= 1:
                return
            for kc in range(4):
                if GELU_FN[0] is None:
                    k.op("dve", lambda e: e.tensor_copy(out=yT[:, kc, tsl(tt)], in_=ypre[:, kc, :]), r=["ypre"], w=[("yT", kc)])
                else:
                    k.op("act", lambda e: e.activation(out=yT[:, kc, tsl(tt)], in_=ypre[:, kc, :], func=GELU_FN[0]), r=["ypre"],
                         w=[("yT", kc)])
        for nm, tl, kk in (("ypre", ypre, ["ypre"]), ("zz", zz, ["zz"]), ("sT", sT, ["sT"]), ("ETb", ETb, ["ETb"]), ("EVt", EVt, ["EVt"]),
                           ("wb", wb, ["wb"]), ("cm", cm, ["cm"]), ("a128", a128, ["a128"]), ("lastc", lastc, ["lastc"])):
            if nm in c.dbg:
                ap = tl[:]
                if len(ap.shape) == 3:
                    ap = ap.rearrange("p a b -> p (a b)")
                elif len(ap.shape) == 4:
                    ap = ap.rearrange("p a b c -> p (a b c)")
                k.dma("sp", c.dbg[nm], ap, r=kk, w=["dbg_" + nm])
        if CUT[0] <= 20:
            return
        sg = sb("sg", [128, 4, 512], BF16, p2)
        yk = [("yT", i) for i in range(4)]
        n = 0
        for tb in range(4):
            for c2 in range(4):
                b = 6 + n % 2
                n += 1
                for ch in range(4):
                    k.op("pe", lambda e: e.matmul(ps[b][:, :], lhsT=wglu[:, ch, c2 * 128:(c2 + 1) * 128],
                                                  rhs=yT[:, ch, tb * 512:(tb + 1) * 512], start=(ch == 0), stop=(ch == 3)),
                         r=["wglu"] + yk, w=[("ps", b)])
                k.op("act", lambda e: e.activation(out=sg[:, c2, :], in_=ps[b][:, :], func=AF.Sigmoid), r=[("ps", b)],
                     w=[("sg", c2)])
            for c2 in range(4):
                k.op("dve", lambda e: e.tensor_tensor(out=yT[:, c2, tb * 512:(tb + 1) * 512], in0=yT[:, c2, tb * 512:(tb + 1) * 512],
                                                      in1=sg[:, c2, :], op=ALU.mult), r=[("yT", c2), ("sg", c2)], w=[("yT", c2)])
        k.barrier()


def peer_inputs(c):
    c.wq_d = c.din("peer_w_q", [2, D, 2048])
    c.keys_d = c.din("peer_sub_keys", [2, 8, 2, 128, 128])
    c.u_d = c.din("peer_u", [2, 16384, D])
    c.v_d = c.din("peer_v", [2, 16384, D])


NEG = -1.0
PEER_EG = [32]
PEER_ACT_HEADS = [5]
PEER_PROD = [['act', 'pool', 'pool', 'dve', 'act', 'pool', 'pool', 'dve']]


def peer(c, li):
    nc, k, sb, ps, h = c.nc, c.k, c.sb, c.ps, c.h
    wq_d, keys_d, u_d, v_d = c.wq_d[li], c.keys_d[li], c.u_d[li], c.v_d[li]
    with ExitStack() as ph:
        hnT = sb("hnT", [128, 8, L], BF16, ph)
        rmsnorm_T(c, c.norm_ffn_g[li, :], hnT, "hnT")
        allhn = [("hnT", tt) for tt in range(NT)]
        e_all = sb("e_all", [128, 4, 16, 128], F32, ph)
        diag = sb("diag", [128, 4, 8, 128], BF16, ph)
        phi = sb("phi", [128, 4, 8], F32, ph)
        mx = sb("pmx", [128, 16], F32, ph)
        t16 = sb("t16", [128, 16, 16], F32, ph)
        tmpb = sb("tmpb", [128, 128], F32, ph)
        cand = sb("cand", [128, 256], F32, ph)
        cand2 = sb("cand2", [128, 256], F32, ph)
        c16 = sb("c16", [128, 16], F32, ph)
        zs = sb("zs", [128, 8], F32, ph)
        for tg in range(4):
            with ExitStack() as p1:
                kT = sb("kT", [128, 16, 128], BF16, p1)
                with ExitStack() as p0:
                    kst = sb("kst", [128, 16, 128], F32, p0)
                    k.dma("sp", kst[:], keys_d.rearrange("h c n d -> n (h c) d"), w=["kst"])
                    for g in range(4):
                        b = g % 2
                        for q in range(4):
                            k.op("pe", lambda e: e.transpose(out=ps[b][:, q * 128:(q + 1) * 128], in_=kst[:, g * 4 + q, :],
                                                             identity=c.identf[:]), r=["kst", "identf"], w=[("ps", b)])
                        k.op("act", lambda e: e.activation(out=kT[:, g * 4:(g + 1) * 4, :].rearrange("p a b -> p (a b)"), in_=ps[b][:, :],
                                                           func=AF.Copy), r=[("ps", b)], w=["kT"])
                    k.barrier()

                wq = sb("wq", [128, 8, 2048], BF16, p1)
                qT = sb("qT", [128, 16, 512], BF16, p1)
                for ch in range(8):
                    k.dma("pool", wq[:, ch, :], wq_d[ch * 128:(ch + 1) * 128, :], w=[("wq", ch)])
                allwq = [("wq", ch) for ch in range(8)]
                for blk in range(16):
                    b = blk % 2
                    for ch in range(8):
                        k.op("pe", lambda e: e.matmul(ps[b][:, :], lhsT=wq[:, ch, blk * 128:(blk + 1) * 128],
                                                      rhs=hnT[:, ch, tg * 512:(tg + 1) * 512], start=(ch == 0), stop=(ch == 7)),
                             r=allwq + allhn, w=[("ps", b)])
                    k.op("act", lambda e: e.activation(out=qT[:, blk, :], in_=ps[b][:, :], func=AF.Copy), r=[("ps", b)],
                         w=[("qT", blk)])
                for tt in range(4):
                    for blk in range(16):
                        b = 2 + blk // 4
                        k.op("pe", lambda e: e.matmul(ps[b][:, (blk % 4) * 128:(blk % 4 + 1) * 128], lhsT=qT[:, blk, tt * 128:(tt + 1) * 128],
                                                      rhs=kT[:, blk, :], start=True, stop=True), r=[("qT", blk), "kT"], w=[("ps", b)])
                    for b4 in range(4):
                        k.op("dve", lambda e: e.tensor_reduce(out=mx[:, b4 * 4:(b4 + 1) * 4],
                                                              in_=ps[2 + b4][:, :].rearrange("p (a n) -> p a n", a=4),
                                                              axis=AX.X, op=ALU.max), r=[("ps", 2 + b4)], w=["pmx"])
                    k.op("dve", lambda e: e.tensor_scalar(out=mx[:], in0=mx[:], scalar1=-1.0, scalar2=None, op0=ALU.mult),
                         r=["pmx"], w=["pmx"])
                    for blk in range(16):
                        b = 2 + blk // 4
                        k.op("act", lambda e: e.activation(out=e_all[:, tt, blk, :], in_=ps[b][:, (blk % 4) * 128:(blk % 4 + 1) * 128],
                                                           func=AF.Exp, bias=mx[:, blk:blk + 1], scale=1.0),
                             r=[("ps", b), "pmx"], w=[("e_all", tt, blk)])
                    for blk in range(16):
                        ek = ("e_all", tt, blk)
                        k.op("dve", lambda e: e.max(out=t16[:, blk, 0:8], in_=e_all[:, tt, blk, :]), r=[ek], w=["t16"])
                        k.op("dve", lambda e: e.match_replace(out=tmpb[:], in_to_replace=t16[:, blk, 0:8],
                                                              in_values=e_all[:, tt, blk, :], imm_value=NEG),
                             r=[ek, "t16"], w=["tmpb"])
                        k.op("dve", lambda e: e.max(out=t16[:, blk, 8:16], in_=tmpb[:]), r=["tmpb"], w=["t16"])
                    for hd in range(8):
                        k.op("dve", lambda e: e.tensor_tensor(
                            out=cand[:].rearrange("p (i j) -> p i j", i=16),
                            in0=t16[:, 2 * hd, :].unsqueeze(2).to_broadcast([128, 16, 16]),
                            in1=t16[:, 2 * hd + 1, :].unsqueeze(1).to_broadcast([128, 16, 16]), op=ALU.mult),
                             r=["t16"], w=["cand"])
                        k.op("dve", lambda e: e.max(out=c16[:, 0:8], in_=cand[:]), r=["cand"], w=["c16"])
                        k.op("dve", lambda e: e.match_replace(out=cand2[:], in_to_replace=c16[:, 0:8], in_values=cand[:],
                                                              imm_value=NEG), r=["cand", "c16"], w=["cand2"])
                        k.op("dve", lambda e: e.max(out=c16[:, 8:16], in_=cand2[:]), r=["cand2"], w=["c16"])
                        k.op("dve", lambda e: e.tensor_scalar(out=phi[:, tt, hd:hd + 1], in0=c16[:, 15:16], scalar1=1.0 - 1e-6, scalar2=None,
                                                              op0=ALU.mult), r=["c16"], w=["phi"])
                        k.op("dve", lambda e: e.tensor_reduce(out=zs[:, hd:hd + 1], in_=c16[:], axis=AX.X, op=ALU.add),
                             r=["c16"], w=["zs"])
                    k.op("dve", lambda e: e.reciprocal(out=zs[:], in_=zs[:]), r=["zs"], w=["zs"])
                    for hd in range(8):
                        k.op("dve", lambda e: e.tensor_scalar(out=diag[:, tt, hd, :], in0=c.identf[:], scalar1=zs[:, hd:hd + 1],
                                                              scalar2=None, op0=ALU.mult), r=["zs", "identf"], w=["diag"])
                k.barrier()
            with ExitStack() as p2:
                ust = sb("ust", [128, 4, D], BF16, p2)
                utsb = sb("utsb", [128, 8, 512], BF16, p2)
                vb = sb("vb", [128, 4, D], BF16, p2)
                gel = sb("gel", [128, 4, 512], BF16, p2)
                Gs = [sb("G%d" % i, [128, 8, 512], BF16, p2) for i in range(2)]
                prod = [sb("prod%d" % i, [128, 512], F32, p2) for i in range(3)]
                HsTs = [sb("HsT%d" % i, [128, 4, 128], BF16, p2) for i in range(2)]
                it = 0
                npr = 0
                for eg in range(PEER_EG[0]):
                    k.dma("pool", ust[:], u_d[eg * 512:(eg + 1) * 512, :].rearrange("(a p) d -> p a d", p=128), w=["ust"])
                    k.dma("pool", vb[:], v_d[eg * 512:(eg + 1) * 512, :].rearrange("(a p) d -> p a d", p=128), w=["vb"])
                    for dc in range(8):
                        b = 2 + dc % 2
                        for a in range(4):
                            k.op("pe", lambda e: e.transpose(out=c.pbf[b][:, a * 128:(a + 1) * 128], in_=ust[:, a, dc * 128:(dc + 1) * 128],
                                                             identity=c.ident[:]), r=["ust", "ident"], w=[("ps", b)])
                        k.op("act", lambda e: e.activation(out=utsb[:, dc, :], in_=c.pbf[b][:, 0:512], func=AF.Copy), r=[("ps", b)],
                             w=["utsb"])
                    for a in range(4):
                        b = a % 2
                        for dc in range(8):
                            k.op("pe", lambda e: e.matmul(ps[b][:, :], lhsT=utsb[:, dc, a * 128:(a + 1) * 128],
                                                          rhs=hnT[:, dc, tg * 512:(tg + 1) * 512], start=(dc == 0), stop=(dc == 7)),
                                 r=["utsb"] + allhn, w=[("ps", b)])
                        k.op("act", lambda e: e.activation(out=gel[:, a, :], in_=ps[b][:, :], func=AF.Gelu), r=[("ps", b)],
                             w=["gel"])
                    for tt in range(4):
                        G = Gs[it % 2]
                        gk = "G%d" % (it % 2)
                        HsT = HsTs[it % 2]
                        hk = "HsT%d" % (it % 2)
                        gb = 4 + it % 2
                        it += 1
                        for hd in range(8):
                            pr = prod[npr % 3]
                            pk = "prod%d" % (npr % 3)
                            npr += 1
                            pe_ = PEER_PROD[0][hd]
                            if pe_ == 'act':
                                for a in range(4):
                                    k.op("act", lambda e: e.activation(out=pr[:, a * 128:(a + 1) * 128], in_=e_all[:, tt, 2 * hd + 1, :],
                                                                       func=AF.Copy, scale=e_all[:, tt, 2 * hd, eg * 4 + a:eg * 4 + a + 1]),
                                         r=[("e_all", tt, 2 * hd), ("e_all", tt, 2 * hd + 1)], w=[pk])
                            else:
                                k.op(pe_, lambda e: e.tensor_tensor(
                                    out=pr[:].rearrange("p (a n) -> p a n", a=4),
                                    in0=e_all[:, tt, 2 * hd, eg * 4:(eg + 1) * 4].unsqueeze(2).to_broadcast([128, 4, 128]),
                                    in1=e_all[:, tt, 2 * hd + 1, :].unsqueeze(1).to_broadcast([128, 4, 128]), op=ALU.mult),
                                     r=[("e_all", tt, 2 * hd), ("e_all", tt, 2 * hd + 1)], w=[pk])
                            k.op("dve", lambda e: e.scalar_tensor_tensor(out=G[:, hd, :], in0=pr[:], scalar=phi[:, tt, hd:hd + 1],
                                                                         in1=pr[:], op0=ALU.is_ge, op1=ALU.mult),
                                 r=[pk, "phi"], w=[(gk, hd)])
                        for a in range(4):
                            for hd in range(8):
                                k.op("pe", lambda e: e.matmul(ps[gb][:, a * 128:(a + 1) * 128], lhsT=G[:, hd, a * 128:(a + 1) * 128],
                                                              rhs=diag[:, tt, hd, :], start=(hd == 0), stop=(hd == 7)),
                                     r=[(gk, hd), "diag"], w=[("ps", gb)])
                        k.op("dve", lambda e: e.tensor_tensor(out=HsT[:], in0=gel[:, :, tt * 128:(tt + 1) * 128],
                                                              in1=ps[gb][:, :].rearrange("p (a n) -> p a n", a=4), op=ALU.mult),
                             r=["gel", ("ps", gb)], w=[hk])
                        tok = tg * 4 + tt
                        for dh in range(2):
                            for a in range(4):
                                k.op("pe", lambda e: e.matmul(ps[6 + dh][:, :], lhsT=HsT[:, a, :], rhs=vb[:, a, dh * 512:(dh + 1) * 512],
                                                              start=(a == 0), stop=(a == 3)), r=[hk, "vb"], w=[("ps", 6 + dh)])
                            k.op("dve", lambda e: e.tensor_tensor(out=h[:, tok, dh * 512:(dh + 1) * 512],
                                                                  in0=h[:, tok, dh * 512:(dh + 1) * 512], in1=ps[6 + dh][:, :],
                                                                  op=ALU.add), r=[("h", tok), ("ps", 6 + dh)], w=[("h", tok)])
                k.barrier()


RW_HP = [8]
RW_TB = [4]


def rwkv_mixer(c, li):
    nc, k, sb, ps, pbf, h = c.nc, c.k, c.sb, c.ps, c.pbf, c.h
    din = c.din
    mu_d = din("o_mu", [6, D])
    wr_d, wk_d, wv_d = din("o_w_r", [D, D]), din("o_w_k", [D, D]), din("o_w_v", [D, D])
    w0_d = din("o_w0", [1, D])
    ww1_d, ww2_d = din("o_w_w1", [D, 64]), din("o_w_w2", [64, D])
    a0_d = din("o_a0", [1, D])
    wa1_d, wa2_d = din("o_w_a1", [D, 64]), din("o_w_a2", [64, D])
    wg1_d, wg2_d = din("o_w_g1", [D, 128]), din("o_w_g2", [128, D])
    kk_d, ka_d, rk_d = din("o_k_k", [1, D]), din("o_k_a", [1, D]), din("o_r_k", [1, D])
    lg_d, lb_d = din("o_lnx_g", [1, D]), din("o_lnx_b", [1, D])
    wo_d = din("o_w_o", [D, D])

    def mm(out, lhsT, rhs, r, w, start=True, stop=True):
        k.op("pe", lambda e: e.matmul(out, lhsT=lhsT, rhs=rhs, start=start, stop=stop), r=r, w=w)

    def tt_(eng, out, in0, in1, op, r, w):
        k.op(eng, lambda e: e.tensor_tensor(out=out, in0=in0, in1=in1, op=op), r=r, w=w)

    def ts_(eng, out, in0, s1, s2, op0, op1, r, w):
        k.op(eng, lambda e: e.tensor_scalar(out=out, in0=in0, scalar1=s1, scalar2=s2, op0=op0, op1=op1), r=r, w=w)

    def act(out, in_, fn, r, w, **kw):
        k.op("act", lambda e: e.activation(out=out, in_=in_, func=fn, **kw), r=r, w=w)

    with ExitStack() as ph:
        xp = sb("xp", [128, 8, L + 2], BF16, ph)
        k.op("pool", lambda e: e.memset(xp[:, :, 0:1], 0.0), w=["xp0"])
        rmsnorm_T(c, c.norm_mix_g[li, :], xp, "xp", off=1)
        allx = [("xp", tt) for tt in range(NT)] + ["xp0"]
        mjt = sb("mjt", [128, 128], F32, ph)
        mtj = sb("mtj", [128, 128], F32, ph)
        k.op("pool", lambda e: e.affine_select(out=mjt[:], in_=c.ones_f[:], pattern=[[1, 128]], compare_op=ALU.is_gt,
                                                fill=0.0, base=0, channel_multiplier=-1), r=["ones_f"], w=["mjt"])
        k.op("pool", lambda e: e.affine_select(out=mtj[:], in_=c.ones_f[:], pattern=[[-1, 128]], compare_op=ALU.is_gt,
                                                fill=0.0, base=0, channel_multiplier=1), r=["ones_f"], w=["mtj"])
        hm = sb("hm", [128, 2], F32, ph)
        bones = sb("bones", [128, 128], BF16, ph)
        cmask = sb("cmask", [128, 2, 128], F32, ph)
        dsel = sb("dsel", [128, 2, 128], F32, ph)
        k.op("pool", lambda e: e.memset(hm[:], 0.0), w=["hm"])
        k.op("pool", lambda e: e.memset(hm[0:64, 0:1], 1.0), r=["hm"], w=["hm"])
        k.op("pool", lambda e: e.memset(hm[64:128, 1:2], 1.0), r=["hm"], w=["hm"])
        k.op("pool", lambda e: e.memset(bones[:], 0.0), w=["bones"])
        k.op("pool", lambda e: e.memset(bones[0:64, 0:64], 1.0), r=["bones"], w=["bones"])
        k.op("pool", lambda e: e.memset(bones[64:128, 64:128], 1.0), r=["bones"], w=["bones"])
        k.op("pool", lambda e: e.memset(cmask[:].rearrange("p a b -> p (a b)"), 0.0), w=["cmask"])
        k.op("pool", lambda e: e.memset(cmask[:, 0, 0:64], 1.0), r=["cmask"], w=["cmask"])
        k.op("pool", lambda e: e.memset(cmask[:, 1, 64:128], 1.0), r=["cmask"], w=["cmask"])
        for hl in range(2):
            ts_("dve", dsel[:, hl, :], c.identf[:], hm[:, hl:hl + 1], None, ALU.mult, ALU.bypass, ["identf", "hm"], ["dsel"])
        chm = sb("chm", [128, 512], BF16, ph)
        k.op("pool", lambda e: e.memset(chm[:], 1.0), w=["chm"])
        k.op("pool", lambda e: e.memset(chm[:].rearrange("p (a b) -> p a b", a=4)[:, :, 0:1], 0.0), r=["chm"], w=["chm"])
        prm = sb("rprm", [128, 7, 8], F32, ph)
        for i, pd in enumerate((w0_d, a0_d, kk_d, ka_d, rk_d, lg_d, lb_d)):
            k.dma("sp", prm[:, i, :], pd[0, :].rearrange("(dc p) -> p dc", p=128), w=["rprm"], allow_slow_non_contiguous=True)
        mucol = sb("mucol", [128, 6, 8], F32, ph)
        k.dma("sp", mucol[:], mu_d.rearrange("i (dc p) -> p i dc", p=128), w=["mucol"], allow_slow_non_contiguous=True)
        MU = dict(r=0, w=1, k=2, v=3, a=4, g=5)

        def load_split(name, wd, cols, ncol, mui, st):
            w0 = sb(name + "0", [128, 8, ncol], BF16, st)
            wm = sb(name + "m", [128, 8, ncol], BF16, st)
            wp = sb(name + "p", [128, 8, ncol], BF16, st)
            k.dma("pool", w0[:], wd[:, cols].rearrange("(dc p) n -> p dc n", p=128), w=[name + "0"])
            tt_("dve", wm[:], w0[:], mucol[:, mui, :].unsqueeze(2).to_broadcast([128, 8, ncol]), ALU.mult,
                [name + "0", "mucol"], [name + "m"])
            tt_("dve", wp[:], w0[:], wm[:], ALU.subtract, [name + "0", name + "m"], [name + "p"])
            return wp, wm

        ww2 = sb("ww2", [128, D], BF16, ph)
        wa2 = sb("wa2", [128, D], BF16, ph)
        wg2 = sb("wg2r", [128, D], BF16, ph)
        k.dma("pool", ww2[0:64, :], ww2_d[:, :], w=["ww2"])
        k.dma("pool", wa2[0:64, :], wa2_d[:, :], w=["wa2"])
        k.dma("pool", wg2[:], wg2_d[:, :], w=["wg2r"])
        tw1 = sb("tw1", [128, L], BF16, ph)
        ta1 = sb("ta1", [128, L], BF16, ph)
        tg1 = sb("tg1", [128, L], BF16, ph)

        def proj(out_ps, wp, wm, c0, m, t0, n, keys):
            for dc in range(8):
                mm(out_ps, wp[:, dc, c0:c0 + m], xp[:, dc, 1 + t0:1 + t0 + n], allx + keys, [("ps", 0)], start=(dc == 0), stop=False)
            for dc in range(8):
                mm(out_ps, wm[:, dc, c0:c0 + m], xp[:, dc, t0:t0 + n], allx + keys, [("ps", 0)], start=False, stop=(dc == 7))

        with ExitStack() as p0:
            w1p, w1m = load_split("ww1", ww1_d, slice(0, 64), 64, MU["w"], p0)
            a1p, a1m = load_split("wa1", wa1_d, slice(0, 64), 64, MU["a"], p0)
            g1p, g1m = load_split("wg1", wg1_d, slice(0, 128), 128, MU["g"], p0)
            for tb in range(4):
                t0 = tb * 512
                proj(ps[0][0:64, :], w1p, w1m, 0, 64, t0, 512, ["ww1p", "ww1m"])
                act(tw1[0:64, t0:t0 + 512], ps[0][0:64, :], AF.Tanh, [("ps", 0)], ["tw1"])
                proj(ps[0][0:64, :], a1p, a1m, 0, 64, t0, 512, ["wa1p", "wa1m"])
                act(ta1[0:64, t0:t0 + 512], ps[0][0:64, :], AF.Copy, [("ps", 0)], ["ta1"])
                proj(ps[0][:, :], g1p, g1m, 0, 128, t0, 512, ["wg1p", "wg1m"])
                act(tg1[:, t0:t0 + 512], ps[0][:, :], AF.Sigmoid, [("ps", 0)], ["tg1"])
            k.barrier()
        W = 512
        f32t = lambda n, st: sb(n, [128, W], F32, st)
        b16t = lambda n, st: sb(n, [128, W], BF16, st)
        for hp in range(RW_HP[0]):
            with ExitStack() as p1:
                cols = slice(hp * 128, (hp + 1) * 128)
                wrp, wrm = load_split("wr", wr_d, cols, 128, MU["r"], p1)
                wkp, wkm = load_split("wk", wk_d, cols, 128, MU["k"], p1)
                wvp, wvm = load_split("wv", wv_d, cols, 128, MU["v"], p1)
                wo = sb("wo_hp", [128, D], BF16, p1)
                k.dma("pool", wo[:], wo_d[hp * 128:(hp + 1) * 128, :], w=["wo_hp"])
                pc = lambda i: prm[:, i, hp:hp + 1]
                Hb = sb("Hb", [128, 64], BF16, p1)
                k.op("pool", lambda e: e.memset(Hb[:], 0.0), w=["Hb"])
                r_b, k_b, v_b, g_b = b16t("r_b", p1), b16t("k_b", p1), b16t("v_b", p1), b16t("g_b", p1)
                vtok = sb("vtok", [128, 4, 128], BF16, p1)
                lw, cs, asig = f32t("lw", p1), f32t("cs", p1), f32t("asig", p1)
                e1, e2, e3, e4 = f32t("e1", p1), f32t("e2", p1), f32t("e3", p1), f32t("e4", p1)
                kkn, kmod, b_ = f32t("kkn", p1), f32t("kmod", p1), f32t("bb", p1)
                sq = b16t("sq", p1)
                rt, rz0, rz1, az0, az1 = b16t("rt", p1), b16t("rz0", p1), b16t("rz1", p1), b16t("az0", p1), b16t("az1", p1)
                at_, bt, kt, bh, kh = b16t("at", p1), b16t("bt", p1), b16t("kt", p1), b16t("bh", p1), b16t("kh", p1)
                bv = f32t("bv", p1)
                pcl = sb("pcl", [128, 4], F32, p1)
                rz, az = (rz0, rz1), (az0, az1)
                Xs = [sb("X%d" % i, [128, 256], BF16, p1) for i in range(2)]
                MNs = [[sb("MN%d_%d" % (j, i), [128, 256], BF16, p1) for i in range(2)] for j in range(2)]
                cats = [sb("cat%d" % i, [128, 256], BF16, p1) for i in range(2)]
                mrks = [sb("mrk%d" % i, [128, 128], F32, p1) for i in range(2)]
                btok = sb("btok", [128, 2, 128], BF16, p1)
                bz = sb("bz", [128, 2, 128], BF16, p1)
                kz = sb("kz", [128, 2, 128], BF16, p1)
                RGMF = [[sb("%s%d" % (n, i), [128, 128], BF16, p1) for n in ("RpT", "GT", "MpT", "FT")] for i in range(2)]
                gst = sb("gst", [128, 8], F32, p1)
                ysq = sb("ysq", [128, 128], F32, p1)
                yn = sb("yn", [128, 128], BF16, p1)
                z1 = sb("z1", [128, 128], F32, p1)
                ygT = sb("ygT", [128, 128], BF16, p1)
                for tb in range(RW_TB[0]):
                    t0 = tb * W
                    proj(ps[0][:, :], wrp, wrm, 0, 128, t0, W, ["wrp", "wrm"])
                    act(r_b[:], ps[0][:, :], AF.Copy, [("ps", 0)], ["r_b"])
                    proj(ps[0][:, :], wkp, wkm, 0, 128, t0, W, ["wkp", "wkm"])
                    act(k_b[:], ps[0][:, :], AF.Copy, [("ps", 0)], ["k_b"])
                    proj(ps[0][:, :], wvp, wvm, 0, 128, t0, W, ["wvp", "wvm"])
                    act(v_b[:], ps[0][:, :], AF.Copy, [("ps", 0)], ["v_b"])
                    for q in range(4):
                        tq = t0 + q * 128
                        for dc in range(8):
                            mm(ps[6][:, q * 128:(q + 1) * 128], xp[:, dc, 1 + tq:1 + tq + 128], wvp[:, dc, :], allx + ["wvp"],
                               [("ps", 6)], start=(dc == 0), stop=False)
                        for dc in range(8):
                            mm(ps[6][:, q * 128:(q + 1) * 128], xp[:, dc, tq:tq + 128], wvm[:, dc, :], allx + ["wvm"],
                               [("ps", 6)], start=False, stop=(dc == 7))
                    k.op("dve", lambda e: e.tensor_copy(out=vtok[:].rearrange("p a b -> p (a b)"), in_=ps[6][:, :]),
                         r=[("ps", 6)], w=["vtok"])
                    mm(ps[0][:, :], ww2[0:64, cols], tw1[0:64, t0:t0 + W], ["ww2", "tw1"], [("ps", 0)])
                    act(lw[:], ps[0][:, :], AF.Sigmoid, [("ps", 0), "rprm"], ["lw"], bias=pc(0), scale=1.0)
                    ts_("dve", lw[:], lw[:], -0.6065306597126334, None, ALU.mult, ALU.bypass, ["lw"], ["lw"])
                    mm(ps[0][:, :], wa2[0:64, cols], ta1[0:64, t0:t0 + W], ["wa2", "ta1"], [("ps", 0)])
                    act(asig[:], ps[0][:, :], AF.Sigmoid, [("ps", 0), "rprm"], ["asig"], bias=pc(1), scale=1.0)
                    mm(ps[0][:, :], wg2[:, cols], tg1[:, t0:t0 + W], ["wg2r", "tg1"], [("ps", 0)])
                    act(g_b[:], ps[0][:, :], AF.Copy, [("ps", 0)], ["g_b"])
                    k.op("dve", lambda e: e.tensor_tensor_scan(out=cs[:], data0=chm[:], data1=lw[:], initial=0.0,
                                                               op0=ALU.mult, op1=ALU.add), r=["chm", "lw"], w=["cs"])
                    cs3 = cs[:].rearrange("p (a b) -> p a b", a=4)
                    csl = cs3[:, :, 127:128].to_broadcast([128, 4, 128])
                    act(e1[:], cs[:], AF.Exp, ["cs"], ["e1"])
                    act(e2[:], cs[:], AF.Exp, ["cs"], ["e2"], scale=-1.0)
                    tt_("dve", e3[:], cs[:], lw[:], ALU.subtract, ["cs", "lw"], ["e3"])
                    act(e3[:], e3[:], AF.Exp, ["e3"], ["e3"])
                    tt_("dve", e4[:].rearrange("p (a b) -> p a b", a=4), csl, cs3, ALU.subtract, ["cs"], ["e4"])
                    act(e4[:], e4[:], AF.Exp, ["e4"], ["e4"])
                    act(pcl[:], cs3[:, :, 127], AF.Exp, ["cs"], ["pcl"])
                    ts_("dve", kkn[:], k_b[:], pc(2), None, ALU.mult, ALU.bypass, ["k_b", "rprm"], ["kkn"])
                    tt_("dve", sq[:], kkn[:], kkn[:], ALU.mult, ["kkn"], ["sq"])
                    mm(ps[0][:, :], bones[:], sq[:], ["bones", "sq"], [("ps", 0)])
                    act(kmod[:], ps[0][:, :], AF.Sqrt, [("ps", 0)], ["kmod"])
                    ts_("dve", kmod[:], kmod[:], 1e-12, None, ALU.max, ALU.bypass, ["kmod"], ["kmod"])
                    k.op("dve", lambda e: e.reciprocal(out=kmod[:], in_=kmod[:]), r=["kmod"], w=["kmod"])
                    tt_("dve", kkn[:], kkn[:], kmod[:], ALU.mult, ["kkn", "kmod"], ["kkn"])
                    ts_("dve", kmod[:], asig[:], -1.0, pc(3), ALU.add, ALU.mult, ["asig", "rprm"], ["kmod"])
                    ts_("dve", kmod[:], kmod[:], 1.0, None, ALU.add, ALU.bypass, ["kmod"], ["kmod"])
                    tt_("dve", kmod[:], kmod[:], k_b[:], ALU.mult, ["kmod", "k_b"], ["kmod"])
                    tt_("dve", b_[:], kkn[:], asig[:], ALU.mult, ["kkn", "asig"], ["bb"])
                    tt_("dve", rt[:], r_b[:], e1[:], ALU.mult, ["r_b", "e1"], ["rt"])
                    tt_("pool", kt[:], kmod[:], e2[:], ALU.mult, ["kmod", "e2"], ["kt"])
                    tt_("dve", bt[:], b_[:], e2[:], ALU.mult, ["bb", "e2"], ["bt"])
                    tt_("pool", kh[:], kmod[:], e4[:], ALU.mult, ["kmod", "e4"], ["kh"])
                    tt_("dve", bh[:], b_[:], e4[:], ALU.mult, ["bb", "e4"], ["bh"])
                    tt_("pool", e3[:], kkn[:], e3[:], ALU.mult, ["kkn", "e3"], ["e3"])
                    ts_("dve", at_[:], e3[:], -1.0, None, ALU.mult, ALU.bypass, ["e3"], ["at"])
                    for hl in range(2):
                        ts_("dve", rz[hl][:], rt[:], hm[:, hl:hl + 1], None, ALU.mult, ALU.bypass, ["rt", "hm"], ["rz%d" % hl])
                        ts_("pool", az[hl][:], at_[:], hm[:, hl:hl + 1], None, ALU.mult, ALU.bypass, ["at", "hm"], ["az%d" % hl])
                    tt_("dve", e1[:], r_b[:], kmod[:], ALU.mult, ["r_b", "kmod", "e1", "rt"], ["e1"])
                    ts_("dve", sq[:], e1[:], pc(4), None, ALU.mult, ALU.bypass, ["e1", "rprm", "sq"], ["sq"])
                    mm(ps[0][:, :], bones[:], sq[:], ["bones", "sq"], [("ps", 0)])
                    tt_("dve", bv[:], ps[0][:, :], v_b[:], ALU.mult, [("ps", 0), "v_b"], ["bv"])
                    for q in range(4):
                        csl_ = slice(q * 128, (q + 1) * 128)
                        tok = tb * 4 + q
                        k.op("pe", lambda e: e.transpose(out=pbf[1][:, 0:128], in_=bh[:, csl_], identity=c.ident[:]),
                             r=["bh", "ident"], w=[("ps", 1)])
                        k.op("pe", lambda e: e.transpose(out=pbf[1][:, 128:256], in_=kh[:, csl_], identity=c.ident[:]),
                             r=["kh", "ident"], w=[("ps", 1)])
                        for hl in range(2):
                            tt_("dve", bz[:, hl, :], pbf[1][:, 0:128], cmask[:, hl, :], ALU.mult, [("ps", 1), "cmask"], ["bz"])
                            tt_("dve", kz[:, hl, :], pbf[1][:, 128:256], cmask[:, hl, :], ALU.mult, [("ps", 1), "cmask"], ["kz"])
                        def head_seq(hl):
                            azc, rzc = az[hl][:, csl_], rz[hl][:, csl_]
                            azk, rzk = "az%d" % hl, "rz%d" % hl
                            bX, bY = 2 + 2 * hl, 3 + 2 * hl
                            kX, kY = ("ps", bX), ("ps", bY)
                            X, MN, cat, mrk = Xs[hl], MNs[hl], cats[hl], mrks[hl]
                            RpT, GT, MpT, FT = RGMF[hl]
                            xk, ck, mk = "X%d" % hl, "cat%d" % hl, "mrk%d" % hl
                            rk_, gk_, mpk, fk = ("RpT%d" % hl, "GT%d" % hl, "MpT%d" % hl, "FT%d" % hl)
                            tp = pbf[1][:, 256 + hl * 128:384 + hl * 128]
                            mm(ps[bX][:, 0:128], bt[:, csl_], azc, ["bt", azk], [kX])
                            mm(ps[bX][:, 128:256], azc, bt[:, csl_], ["bt", azk], [kX])
                            mm(ps[bX][:, 256:384], azc, kt[:, csl_], ["kt", azk], [kX])
                            k.op("pe", lambda e: e.transpose(out=tp, in_=azc, identity=c.ident[:]), r=[azk, "ident"], w=[("ps", 1)])
                            yield
                            tt_("dve", MN[0][:, 0:128], ps[bX][:, 0:128], mjt[:], ALU.mult, [kX, "mjt"], ["MN0_%d" % hl])
                            tt_("dve", MN[0][:, 128:256], ps[bX][:, 128:256], mtj[:], ALU.mult, [kX, "mtj"], ["MN0_%d" % hl])
                            tt_("dve", X[:, 128:256], ps[bX][:, 256:384], mtj[:], ALU.mult, [kX, "mtj"], [xk])
                            act(X[:, 0:128], tp, AF.Copy, [("ps", 1)], [xk])
                            yield
                            for i in range(7):
                                cur, nxt = MN[i % 2], MN[(i + 1) % 2]
                                ck_, nk = "MN%d_%d" % (i % 2, hl), "MN%d_%d" % ((i + 1) % 2, hl)
                                mm(ps[bY][:, 0:256], cur[:, 0:128], X[:], [ck_, xk], [kY])
                                if i < 6:
                                    mm(ps[bY][:, 256:384], cur[:, 128:256], cur[:, 0:128], [ck_], [kY])
                                    mm(ps[bY][:, 384:512], cur[:, 0:128], cur[:, 128:256], [ck_], [kY])
                                yield
                                if i < 6:
                                    act(nxt[:], ps[bY][:, 256:512], AF.Copy, [kY], [nk, kY])
                                tt_("dve", X[:], X[:], ps[bY][:, 0:256], ALU.add, [xk, kY], [xk, kY])
                                yield
                            mm(ps[bY][:, 0:128], bt[:, csl_], rzc, ["bt", rzk], [kY])
                            mm(ps[bY][:, 128:256], kt[:, csl_], rzc, ["kt", rzk], [kY])
                            yield
                            tt_("dve", cat[:, 0:128], ps[bY][:, 0:128], c.triU_f[:], ALU.mult, [kY, "triU_f"], [ck])
                            tt_("dve", mrk[:], ps[bY][:, 128:256], c.triU_f[:], ALU.mult, [kY, "triU_f"], [mk])
                            k.op("pool", lambda e: e.tensor_copy(out=cat[:, 128:256], in_=bz[:, hl, :]), r=["bz"], w=[ck])
                            yield
                            mm(ps[bX][:, 0:256], X[:, 0:128], cat[:], [xk, ck], [kX])
                            mm(ps[bX][:, 256:512], X[:, 128:256], cat[:], [xk, ck], [kX])
                            yield
                            tt_("dve", RpT[:], ps[bX][:, 0:128], rzc, ALU.add, [kX, rzk], [rk_])
                            k.op("dve", lambda e: e.scalar_tensor_tensor(out=GT[:], in0=dsel[:, hl, :], scalar=pcl[:, q:q + 1],
                                                                         in1=ps[bX][:, 128:256], op0=ALU.mult, op1=ALU.add),
                                 r=["dsel", "pcl", kX], w=[gk_])
                            tt_("dve", MpT[:], ps[bX][:, 256:384], mrk[:], ALU.add, [kX, mk], [mpk])
                            tt_("dve", FT[:], ps[bX][:, 384:512], kz[:, hl, :], ALU.add, [kX, "kz"], [fk])
                            yield
                            vh = vtok[:, q, hl * 64:(hl + 1) * 64]
                            mm(ps[7][:, hl * 64:(hl + 1) * 64], RpT[:], Hb[:], [rk_, "Hb"], [("ps", 7)], start=True, stop=False)
                            mm(ps[7][:, hl * 64:(hl + 1) * 64], MpT[:], vh, [mpk, "vtok"], [("ps", 7)], start=False, stop=True)
                            mm(ps[0][:, 0:64], GT[:], Hb[:], [gk_, "Hb"], [("ps", 0)], start=(hl == 0), stop=False)
                            mm(ps[0][:, 0:64], FT[:], vh, [fk, "vtok"], [("ps", 0)], start=False, stop=(hl == 1))
                            yield

                        gens = [head_seq(0), head_seq(1)]
                        alive = [True, True]
                        while any(alive):
                            for gi in range(2):
                                if alive[gi]:
                                    try:
                                        next(gens[gi])
                                    except StopIteration:
                                        alive[gi] = False
                        act(Hb[:], ps[0][:, 0:64], AF.Copy, [("ps", 0)], ["Hb"])
                        y3 = ps[7][:, 0:128].rearrange("p (a b) -> p a b", a=2)
                        k.op("dve", lambda e: e.tensor_reduce(out=gst[:, 0:2], in_=y3, axis=AX.X, op=ALU.add), r=[("ps", 7)], w=["gst"])
                        act(ysq[:], ps[7][:, 0:128], AF.Square, [("ps", 7)], ["ysq", ("ps", 7)])
                        k.op("dve", lambda e: e.tensor_reduce(out=gst[:, 2:4], in_=ysq[:].rearrange("p (a b) -> p a b", a=2),
                                                              axis=AX.X, op=ALU.add), r=["ysq"], w=["gst"])
                        ts_("dve", gst[:, 0:4], gst[:, 0:4], 1.0 / 64, None, ALU.mult, ALU.bypass, ["gst"], ["gst"])
                        tt_("dve", gst[:, 4:6], gst[:, 0:2], gst[:, 0:2], ALU.mult, ["gst"], ["gst"])
                        tt_("dve", gst[:, 4:6], gst[:, 2:4], gst[:, 4:6], ALU.subtract, ["gst"], ["gst"])
                        ts_("dve", gst[:, 4:6], gst[:, 4:6], 64e-5, None, ALU.add, ALU.bypass, ["gst"], ["gst"])
                        act(gst[:, 4:6], gst[:, 4:6], AF.Sqrt, ["gst"], ["gst"])
                        k.op("dve", lambda e: e.reciprocal(out=gst[:, 6:8], in_=gst[:, 4:6]), r=["gst"], w=["gst"])
                        for hl in range(2):
                            ts_("dve", yn[:, hl * 64:(hl + 1) * 64], ps[7][:, hl * 64:(hl + 1) * 64], gst[:, hl:hl + 1],
                                gst[:, 6 + hl:7 + hl], ALU.subtract, ALU.mult, [("ps", 7), "gst"], ["yn"])
                        k.op("pe", lambda e: e.transpose(out=pbf[1][:, 512:640], in_=yn[:], identity=c.ident[:]),
                             r=["yn", "ident"], w=[("ps", 1)])
                        ts_("dve", z1[:], pbf[1][:, 512:640], pc(5), pc(6), ALU.mult, ALU.add, [("ps", 1), "rprm"], ["z1"])
                        tt_("dve", z1[:], z1[:], bv[:, csl_], ALU.add, ["z1", "bv"], ["z1"])
                        tt_("dve", ygT[:], z1[:], g_b[:, csl_], ALU.mult, ["z1", "g_b"], ["ygT"])
                        for dh in range(2):
                            mm(ps[6][:, :], ygT[:], wo[:, dh * 512:(dh + 1) * 512], ["ygT", "wo_hp"], [("ps", 6)])
                            tt_("dve", h[:, tok, dh * 512:(dh + 1) * 512], h[:, tok, dh * 512:(dh + 1) * 512], ps[6][:, :], ALU.add,
                                [("h", tok), ("ps", 6)], [("h", tok)])
                k.barrier()
        k.barrier()


def build(nc, stages="all", dbg=None):
    es = ExitStack()
    with es:
        dbgaps = {}
        for name, shape in (dbg or {}).items():
            dbgaps[name] = nc.dram_tensor("dbg_" + name, list(shape), BF16 if name in ("uT", "yT", "vk", "zz", "sT", "ETb", "EVt", "wb", "cm") else F32,
                                          kind="ExternalOutput").ap()
        c = setup_common(nc, es, dbgaps)
        peer_inputs(c)
        st = set(stages.split(","))
        if "all" in st:
            st = {"even", "peer0", "rwkv", "peer1"}
        if "even" in st:
            even_mixer(c, 0)
        if "peer0" in st:
            peer(c, 0)
        if "rwkv" in st:
            rwkv_mixer(c, 1)
        if "peer1" in st:
            peer(c, 1)
        if "h" in dbgaps:
            for tt in range(NT):
                c.k.dma("sp", dbgaps["h"][tsl(tt), :], c.h[:, tt, :], r=[("h", tt)], w=["dbg_h"])
        final_norm_store(c)
        print("instr counts", c.k.nins, "sems", c.k.nsem)
    return nc


PARAMS = ["norm_mix_g", "norm_ffn_g", "final_g", "e_w_in", "e_w_out", "s5_a_re", "s5_a_im", "s5_log_dt", "s5_b_re", "s5_b_im",
          "s5_c_re", "s5_c_im", "s5_d", "s5_w_glu", "gla_w_g2", "gla_b_g2", "gla_norm_g",
          "peer_w_q", "peer_sub_keys", "peer_u", "peer_v",
          "o_mu", "o_w_r", "o_w_k", "o_w_v", "o_w0", "o_w_w1", "o_w_w2", "o_a0", "o_w_a1", "o_w_a2", "o_w_g1", "o_w_g2",
          "o_k_k", "o_k_a", "o_r_k", "o_lnx_g", "o_lnx_b", "o_w_o"]


def core_inputs(inputs, b):
    m = {"x": np.ascontiguousarray(inputs["x"][b])}
    for n in PARAMS:
        a = np.asarray(inputs[n])
        if n in ("norm_mix_g", "norm_ffn_g", "peer_w_q", "peer_sub_keys", "peer_u", "peer_v"):
            pass
        elif n == "final_g":
            a = a.reshape(1, D)
        elif n in ("s5_d", "gla_b_g2", "gla_norm_g", "o_w0", "o_a0", "o_k_k", "o_k_a", "o_r_k", "o_lnx_g", "o_lnx_b"):
            a = a.reshape(1, -1)
        else:
            a = a[0]
        m[n] = np.ascontiguousarray(a)
    return m


def kernel(**inputs):
    n = 8
    nc = bass.Bass("TRN2", target_bir_lowering=False)
    build(nc)
    in_maps = [core_inputs(inputs, b) for b in range(n)]
    res = run_bass_kernel_spmd(nc, in_maps, core_ids=list(range(n)))
    return np.stack([r["out"] for r in res.results], axis=0)
```

```python
import numpy as np
from contextlib import ExitStack
import concourse.bass as bass
import concourse.mybir as mybir
from concourse.bass_utils import run_bass_kernel_spmd

F32 = mybir.dt.float32
BF16 = mybir.dt.bfloat16
U32 = mybir.dt.uint32
AF = mybir.ActivationFunctionType
ALU = mybir.AluOpType
AX = mybir.AxisListType

L = 2048
D = 1024
NT = L // 128
EPS = 1e-6
ENGS = ("pe", "dve", "act", "pool", "sp")


class MK:
    ROT = 1 << 30

    def __init__(self, nc, es):
        self.nc = nc
        self.es = es
        self.eng = dict(pe=nc.tensor, dve=nc.vector, act=nc.scalar, pool=nc.gpsimd, sp=nc.sync)
        self.esem = {}
        self.prev_ep = {}
        self.ecnt = {e: 0 for e in ENGS}
        self.seen = {e: {} for e in ENGS}
        self.lastw = {}
        self.readers = {}
        self.dsem = {}
        self.free_dsem = []
        self.ndsem = 0
        self.nsem = 0
        self.nins = {e: 0 for e in ENGS}
        self.spare = [self._newsem("spare%d" % i) for i in range(6)]
        for e in ENGS:
            self._rot(e)

    def _newsem(self, name):
        self.nsem += 1
        return self.es.enter_context(self.nc.semaphore(name))

    def _rot(self, e):
        if e in self.esem and self.esem[e][2] > 0:
            self.prev_ep[e] = self.esem[e][:3]
        ep = self.esem[e][3] + 1 if e in self.esem else 0
        name = "s_%s_%d" % (e, ep)
        sem = self.spare.pop() if (ep > 0 and self.spare) else self._newsem(name)
        self.esem[e] = (name, sem, 0, ep)

    def _deps(self, r, w):
        d = {}

        def add(p):
            name, sem, c = p
            if name not in d or d[name][1] < c:
                d[name] = (sem, c)

        for k in r:
            if k in self.lastw:
                add(self.lastw[k])
        for k in w:
            if k in self.lastw:
                add(self.lastw[k])
            for n, (s, c) in self.readers.get(k, {}).items():
                add((n, s, c))
        return d

    def _wait(self, e, d):
        E = self.eng[e]
        seen = self.seen[e]
        for name, (sem, c) in d.items():
            if seen.get(name, 0) >= c:
                continue
            E.wait_ge(sem, c)
            seen[name] = c

    def _record(self, p, r, w):
        name, sem, c = p
        for k in w:
            self.lastw[k] = p
            self.readers[k] = {}
        for k in r:
            rd = self.readers.setdefault(k, {})
            if name not in rd or rd[name][1] < c:
                rd[name] = (sem, c)

    def op(self, e, fn, r=(), w=()):
        w = list(w) + [x for x in r if isinstance(x, tuple) and x and x[0] == "ps" and x not in w]
        d = self._deps(r, w)
        if e == "pe":
            d = {n: v for n, v in d.items() if not n.startswith("s_pe_")}
        self._wait(e, d)
        name, sem, cnt, ep = self.esem[e]
        if cnt >= self.ROT:
            self._rot(e)
            name, sem, cnt, ep = self.esem[e]
        ins = fn(self.eng[e])
        cnt += 1
        self.esem[e] = (name, sem, cnt, ep)
        ins.then_inc(sem, 1)
        self.nins[e] += 1
        self._record((name, sem, cnt), r, w)
        return ins

    def dma(self, e, out, in_, r=(), w=(), **kw):
        d = self._deps(r, w)
        self._wait(e, d)
        key = w[0] if len(w) else r[0]
        skey = ("dma", key)
        if skey not in self.dsem:
            if self.free_dsem:
                self.dsem[skey] = self.free_dsem.pop()
            else:
                name = "d%d" % self.ndsem
                self.ndsem += 1
                self.dsem[skey] = [name, self._newsem(name), 0]
        ent = self.dsem[skey]
        ins = self.eng[e].dma_start(out=out, in_=in_, **kw)
        ent[2] += 16
        ins.then_inc(ent[1], 16)
        self.nins[e] += 1
        self._record((ent[0], ent[1], ent[2]), r, w)
        return ins

    def barrier(self):
        d = {}
        for e in ENGS:
            name, sem, cnt, ep = self.esem[e]
            if cnt > 0:
                d[name] = (sem, cnt)
            elif e in self.prev_ep:
                pn, psem, pcnt = self.prev_ep[e]
                d[pn] = (psem, pcnt)
        for ent in self.dsem.values():
            if ent[2] > 0:
                d[ent[0]] = (ent[1], ent[2])
        for e in ENGS:
            dd = d
            if e == "pe":
                dd = {n: v for n, v in d.items() if not n.startswith("s_pe_")}
            self._wait(e, dd)
        self.free_dsem.extend(self.dsem.values())
        self.dsem = {}


def tsl(tt):
    return slice(tt * 128, (tt + 1) * 128)


class Ctx:
    pass


SKIP = set()
CUT = [99.0]
GELU_FN = [AF.Gelu]
NTT = [NT]
NOINJ = [False]
INJV = [0]


def setup_common(nc, es, dbg):
    c = Ctx()
    c.nc = nc
    c.es = es
    c.dbg = dbg
    k = c.k = MK(nc, es)
    E = es.enter_context

    def dram_in(name, shape):
        return nc.dram_tensor(name, list(shape), F32, kind="ExternalInput").ap()

    c.din = dram_in
    c.x_d = dram_in("x", [L, D])
    c.out_d = nc.dram_tensor("out", [L, D], F32, kind="ExternalOutput").ap()
    c.norm_mix_g = dram_in("norm_mix_g", [2, D])
    c.norm_ffn_g = dram_in("norm_ffn_g", [2, D])
    c.final_g = dram_in("final_g", [1, D])

    used = {}

    def sb(name, shape, dt, st=None):
        n = used.get(name, 0)
        used[name] = n + 1
        nm = name if n == 0 else "%s_%d" % (name, n)
        return (st or es).enter_context(nc.sbuf_tensor(nm, list(shape), dt))

    c.sb = sb
    c.h = sb("h", [128, NT, D], F32)
    c.gbc = sb("gbc", [128, D], F32)
    c.ident = sb("ident", [128, 128], BF16)
    c.identf = sb("identf", [128, 128], F32)
    c.ones_f = sb("ones_f", [128, 128], F32)
    c.ones_b = sb("ones_b", [128, 128], BF16)
    c.onecol = sb("onecol", [128, 1], F32)
    c.ss = sb("ss", [128, NT], F32)
    c.rstd = sb("rstd", [128, NT], F32)
    c.junk = sb("junk", [128, D], BF16)
    c.xs = [sb("xs%d" % i, [128, D], BF16) for i in range(2)]
    c.ps = [E(nc.psum_tensor("ps%d" % i, [128, 512], F32)) for i in range(8)]
    c.pbf = [c.ps[i][:].bitcast(BF16) for i in range(8)]
    c.triU_f = sb("triU_f", [128, 128], F32)
    c.triU_b = sb("triU_b", [128, 128], BF16)

    k.op("pool", lambda e: e.memset(c.identf[:], 0.0), w=["identf"])
    k.op("pool", lambda e: e.affine_select(out=c.identf[:], in_=c.identf[:], pattern=[[-1, 128]],
                                            compare_op=ALU.not_equal, fill=1.0, base=0, channel_multiplier=1),
         r=["identf"], w=["identf"])
    k.op("dve", lambda e: e.tensor_copy(out=c.ident[:], in_=c.identf[:]), r=["identf"], w=["ident"])
    k.op("pool", lambda e: e.memset(c.ones_f[:], 1.0), w=["ones_f"])
    k.op("pool", lambda e: e.memset(c.ones_b[:], 1.0), w=["ones_b"])
    k.op("pool", lambda e: e.memset(c.onecol[:], 1.0), w=["onecol"])
    k.op("pool", lambda e: e.affine_select(out=c.triU_f[:], in_=c.ones_f[:], pattern=[[1, 128]],
                                            compare_op=ALU.is_ge, fill=0.0, base=0, channel_multiplier=-1),
         r=["ones_f"], w=["triU_f"])
    k.op("dve", lambda e: e.tensor_copy(out=c.triU_b[:], in_=c.triU_f[:]), r=["triU_f"], w=["triU_b"])
    for tt in range(NT):
        k.dma("sp", c.h[:, tt, :], c.x_d[tsl(tt), :], w=[("h", tt)])
    return c


def rms_stats(c):
    k = c.k
    for tt in range(NT):
        k.op("act", lambda e: e.activation(out=c.junk[:], in_=c.h[:, tt, :], func=AF.Square,
                                           accum_out=c.ss[:, tt:tt + 1]),
             r=[("h", tt)], w=["junk", ("ss", tt)])
    allss = [("ss", tt) for tt in range(NT)]
    k.op("dve", lambda e: e.tensor_scalar(out=c.rstd[:], in0=c.ss[:], scalar1=1.0 / D, scalar2=EPS,
                                          op0=ALU.mult, op1=ALU.add), r=allss, w=["rstd"])
    k.op("act", lambda e: e.activation(out=c.rstd[:], in_=c.rstd[:], func=AF.Sqrt), r=["rstd"], w=["rstd"])
    k.op("dve", lambda e: e.reciprocal(out=c.rstd[:], in_=c.rstd[:]), r=["rstd"], w=["rstd"])


def rmsnorm_T(c, g_ap, xT, tag, off=0):
    k = c.k
    k.dma("sp", c.gbc[:], g_ap.partition_broadcast(128), w=["gbc"])
    rms_stats(c)
    for tt in range(NT):
        xb = c.xs[tt % 2]
        xk = ("xs", tt % 2)
        k.op("dve", lambda e: e.scalar_tensor_tensor(out=xb[:], in0=c.h[:, tt, :], scalar=c.rstd[:, tt:tt + 1],
                                                     in1=c.gbc[:], op0=ALU.mult, op1=ALU.mult),
             r=[("h", tt), "rstd", "gbc"], w=[xk])
        b = 6 + tt % 2
        pst = c.pbf[b]
        pk = ("ps", b)
        for ch in range(8):
            k.op("pe", lambda e: e.transpose(out=pst[:, ch * 128:(ch + 1) * 128], in_=xb[:, ch * 128:(ch + 1) * 128],
                                             identity=c.ident[:]),
                 r=[xk, "ident"], w=[pk])
        k.op("act", lambda e: e.activation(out=xT[:, :, off + tt * 128:off + (tt + 1) * 128],
                                           in_=pst.rearrange("p (c t) -> p c t", c=8), func=AF.Copy),
             r=[pk], w=[(tag, tt)])


def final_norm_store(c):
    k = c.k
    k.dma("sp", c.gbc[:], c.final_g[0, :].partition_broadcast(128), w=["gbc"])
    rms_stats(c)
    for tt in range(NT):
        k.op("dve", lambda e: e.scalar_tensor_tensor(out=c.h[:, tt, :], in0=c.h[:, tt, :], scalar=c.rstd[:, tt:tt + 1],
                                                     in1=c.gbc[:], op0=ALU.mult, op1=ALU.mult),
             r=[("h", tt), "rstd", "gbc"], w=[("h", tt)])
        k.dma("sp", c.out_d[tsl(tt), :], c.h[:, tt, :], r=[("h", tt)], w=[("out", tt)])
    k._wait("sp", k._deps([("out", tt) for tt in range(NT)], []))


def cmul(k, eng_a, eng_b, o_re, o_im, a_re, a_im, b_re, b_im, t, rk, wk, tk):
    k.op(eng_a, lambda e: e.tensor_tensor(out=t[0], in0=a_re, in1=b_re, op=ALU.mult), r=rk, w=[tk + "0"])
    k.op(eng_a, lambda e: e.tensor_tensor(out=t[1], in0=a_im, in1=b_im, op=ALU.mult), r=rk, w=[tk + "1"])
    k.op(eng_b, lambda e: e.tensor_tensor(out=o_re, in0=t[0], in1=t[1], op=ALU.subtract),
         r=[tk + "0", tk + "1"], w=[wk + "_re"])
    k.op(eng_a, lambda e: e.tensor_tensor(out=t[0], in0=a_re, in1=b_im, op=ALU.mult), r=rk, w=[tk + "0"])
    k.op(eng_a, lambda e: e.tensor_tensor(out=t[1], in0=a_im, in1=b_re, op=ALU.mult), r=rk, w=[tk + "1"])
    k.op(eng_b, lambda e: e.tensor_tensor(out=o_im, in0=t[0], in1=t[1], op=ALU.add),
         r=[tk + "0", tk + "1"], w=[wk + "_im"])


def even_mixer(c, li):
    nc, k, sb, ps, pbf, h = c.nc, c.k, c.sb, c.ps, c.pbf, c.h
    din = c.din
    w_in_d = din("e_w_in", [D, 2064])
    w_out_d = din("e_w_out", [D, D])
    a_re_d = din("s5_a_re", [32, 64])
    a_im_d = din("s5_a_im", [32, 64])
    ldt_d = din("s5_log_dt", [32, 64])
    b_re_d = din("s5_b_re", [32, 64, 16])
    b_im_d = din("s5_b_im", [32, 64, 16])
    c_re_d = din("s5_c_re", [32, 16, 64])
    c_im_d = din("s5_c_im", [32, 16, 64])
    d_d = din("s5_d", [1, 512])
    wglu_d = din("s5_w_glu", [512, 512])
    wg2_d = din("gla_w_g2", [16, 256])
    bg2_d = din("gla_b_g2", [1, 256])
    gng_d = din("gla_norm_g", [1, 512])

    with ExitStack() as ph:
        xy = sb("xy", [128, 8, L], BF16, ph)
        uT = sb("uT", [128, 4, L], BF16, ph)
        xT = xy
        yT = xy
        with ExitStack() as pg:
            qkT = sb("qkT", [128, 4, L], BF16, pg)
            rT = sb("rT", [128, 4, L], BF16, pg)
            glowT = sb("glowT", [16, L], BF16, pg)
            vk = sb("vk", [128, NT, 768], BF16, pg)
            with ExitStack() as p1:
                wi = sb("wi", [128, 8, 1040], BF16, p1)
                rmsnorm_T(c, c.norm_mix_g[li, :], xT, "xT")
                allx = [("xT", tt) for tt in range(NT)]
                allw = [("wi", ch) for ch in range(8)]
                n = 0
                for piece in range(2):
                    cb = piece * 1024
                    ncol = 1024 if piece == 0 else 1040
                    for ch in range(8):
                        k.dma("pool", wi[:, ch, 0:ncol], w_in_d[ch * 128:(ch + 1) * 128, cb:cb + ncol], w=[("wi", ch)])
                    if piece == 0:
                        chunks = ([(i * 128, 128, uT, i, AF.Copy) for i in range(4)] +
                                  [(512 + i * 128, 128, qkT, i, AF.Copy) for i in range(4)])
                    else:
                        chunks = ([(1552 + i * 128, 128, rT, i, AF.Silu) for i in range(4)] +
                                  [(1536, 16, None, 0, AF.Copy)])
                    for (c0, m, dst, di, fn) in chunks:
                        for tb in range(4):
                            b = n % 4
                            n += 1
                            for ch in range(8):
                                k.op("pe", lambda e: e.matmul(ps[b][0:m, :], lhsT=wi[:, ch, c0 - cb:c0 - cb + m],
                                                              rhs=xT[:, ch, tb * 512:(tb + 1) * 512],
                                                              start=(ch == 0), stop=(ch == 7)),
                                     r=allx + allw, w=[("ps", b)])
                            if dst is None:
                                k.op("act", lambda e: e.activation(out=glowT[:, tb * 512:(tb + 1) * 512], in_=ps[b][0:16, :],
                                                                   func=AF.Copy), r=[("ps", b)], w=["glowT"])
                            else:
                                k.op("act", lambda e: e.activation(out=dst[:, di, tb * 512:(tb + 1) * 512], in_=ps[b][:, :],
                                                                   func=fn), r=[("ps", b)], w=[(dst.name, di)])
                    for tt in range(NT):
                        b0 = 4 + tt % 4
                        if piece == 0:
                            for ch in range(8):
                                k.op("pe", lambda e: e.matmul(ps[b0][:, 0:256], lhsT=xT[:, ch, tsl(tt)], rhs=wi[:, ch, 768:1024],
                                                              start=(ch == 0), stop=(ch == 7)), r=allx + allw, w=[("ps", b0)])
                            k.op("dve", lambda e: e.tensor_copy(out=vk[:, tt, 512:768], in_=ps[b0][:, 0:256]), r=[("ps", b0)],
                                 w=[("vk", tt)])
                        else:
                            for ch in range(8):
                                k.op("pe", lambda e: e.matmul(ps[b0][:, :], lhsT=xT[:, ch, tsl(tt)], rhs=wi[:, ch, 0:512],
                                                              start=(ch == 0), stop=(ch == 7)), r=allx + allw, w=[("ps", b0)])
                            k.op("dve", lambda e: e.tensor_copy(out=vk[:, tt, 0:512], in_=ps[b0][:, :]), r=[("ps", b0)],
                                 w=[("vk", tt)])
                k.barrier()
            if "uT" in c.dbg:
                for i in range(4):
                    k.dma("sp", c.dbg["uT"][i], uT[:, i, :], r=[("uT", i)], w=["dbg_uT"])
            if "vk" in c.dbg:
                k.dma("sp", c.dbg["vk"], vk[:, 3, :], r=[("vk", 3)], w=["dbg_vk"])
            if 'gla' not in SKIP:
                gla(c, pg, qkT, rT, glowT, vk, yT, wg2_d, bg2_d, gng_d)
            k.barrier()
        if 's5' not in SKIP:
            s5(c, ph, uT, yT, a_re_d, a_im_d, ldt_d, b_re_d, b_im_d, c_re_d, c_im_d, d_d, wglu_d)
        k.barrier()
        if "yT" in c.dbg:
            for i in range(8):
                k.dma("sp", c.dbg["yT"][i], yT[:, i, :], r=[("yT", i)], w=["dbg_yT"])
        with ExitStack() as p3:
            wo = sb("wo", [128, 8, D], BF16, p3)
            for ch in range(8):
                k.dma("pool", wo[:, ch, :], w_out_d[ch * 128:(ch + 1) * 128, :], w=[("wo", ch)])
            ally = [("yT", i) for i in range(8)]
            allw = [("wo", ch) for ch in range(8)]
            for tt in range(NT):
                for hf in range(2):
                    b = (tt * 2 + hf) % 4
                    for ch in range(8):
                        k.op("pe", lambda e: e.matmul(ps[b][:, :], lhsT=yT[:, ch, tsl(tt)],
                                                      rhs=wo[:, ch, hf * 512:(hf + 1) * 512],
                                                      start=(ch == 0), stop=(ch == 7)), r=ally + allw, w=[("ps", b)])
                    k.op("dve", lambda e: e.tensor_tensor(out=h[:, tt, hf * 512:(hf + 1) * 512],
                                                          in0=h[:, tt, hf * 512:(hf + 1) * 512], in1=ps[b][:, :],
                                                          op=ALU.add), r=[("ps", b), ("h", tt)], w=[("h", tt)])
            k.barrier()


def gla(c, ph, qkT, rT, glowT, vk, yT, wg2_d, bg2_d, gng_d):
    nc, k, sb, ps, pbf = c.nc, c.k, c.sb, c.ps, c.pbf
    with ExitStack() as p2:
        wg2 = sb("wg2", [16, 256], BF16, p2)
        bg2 = sb("bg2", [1, 256], BF16, p2)
        gng = sb("gng", [128, 4], F32, p2)
        triUs = sb("triUs", [128, 128], F32, p2)
        triRs = sb("triRs", [128, 128], F32, p2)
        lp = sb("lp", [128, 256], F32, p2)
        eend = sb("eend", [128, 256], F32, p2)
        ebT = sb("ebT", [128, 2, 128], F32, p2)
        enbT = sb("enbT", [128, 2, 128], F32, p2)
        kend = sb("kend", [128, 256], BF16, p2)
        qd = sb("qd", [128, 2, 128], BF16, p2)
        kd = sb("kd", [128, 2, 128], BF16, p2)
        qdz = sb("qdz", [128, 4, 128], BF16, p2)
        hmask = sb("hmask", [128, 2], F32, p2)
        attT = sb("attT", [128, 4, 128], BF16, p2)
        S = sb("S", [128, 2, 128], F32, p2)
        Sb = sb("Sb", [128, 2, 128], BF16, p2)
        ssq = sb("ssq", [128, 4], F32, p2)
        rso = sb("rso", [128, 4], F32, p2)
        on = sb("on", [128, 4, 128], BF16, p2)
        k.dma("pool", wg2[:], wg2_d[:, :], w=["wg2"])
        k.dma("pool", bg2[:], bg2_d[:, :], w=["bg2"])
        k.dma("sp", gng[:], gng_d[0, :].rearrange("(h v) -> v h", v=128), w=["gng"], allow_slow_non_contiguous=True)
        k.op("dve", lambda e: e.tensor_scalar(out=triUs[:], in0=c.triU_f[:], scalar1=-1.0 / 16, scalar2=None,
                                              op0=ALU.mult), r=["triU_f"], w=["triUs"])
        k.op("dve", lambda e: e.tensor_scalar(out=triRs[:], in0=c.triU_f[:], scalar1=1.0 / 16, scalar2=-1.0 / 16,
                                              op0=ALU.mult, op1=ALU.add), r=["triU_f"], w=["triRs"])
        k.op("pool", lambda e: e.memset(hmask[:], 0.0), w=["hmask"])
        k.op("pool", lambda e: e.memset(hmask[0:64, 0:1], 1.0), r=["hmask"], w=["hmask"])
        k.op("pool", lambda e: e.memset(hmask[64:128, 1:2], 1.0), r=["hmask"], w=["hmask"])
        k.op("pool", lambda e: e.memset(S[:], 0.0), w=["S"])
        k.op("pool", lambda e: e.memset(Sb[:], 0.0), w=["Sb"])
        for tt in range(NT):
            k.op("pe", lambda e: e.matmul(ps[0][:, 0:256], lhsT=glowT[:, tsl(tt)], rhs=wg2[:, :], start=True, stop=False),
                 r=["glowT", "wg2"], w=[("ps", 0)])
            k.op("pe", lambda e: e.matmul(ps[0][:, 0:256], lhsT=c.ones_b[0:1, :], rhs=bg2[:, :], start=False, stop=True),
                 r=["ones_b", "bg2"], w=[("ps", 0)])
            if CUT[0] <= 1:
                break
            k.op("act", lambda e: e.activation(out=lp[:], in_=ps[0][:, 0:256], func=AF.Exp, scale=-1.0),
                 r=[("ps", 0)], w=["lp"])
            k.op("act", lambda e: e.activation(out=lp[:], in_=lp[:], func=AF.Ln, bias=c.onecol[:], scale=1.0),
                 r=["lp", "onecol"], w=["lp"])
            if CUT[0] <= 2:
                break
            k.op("pe", lambda e: e.matmul(ps[1][:, 0:256], lhsT=triRs[:], rhs=lp[:], start=True, stop=True),
                 r=["triRs", "lp"], w=[("ps", 1)])
            for hf in range(2):
                k.op("pe", lambda e: e.matmul(ps[1][:, 256 + hf * 128:256 + (hf + 1) * 128],
                                              lhsT=lp[:, hf * 128:(hf + 1) * 128], rhs=triUs[:], start=True, stop=True),
                     r=["triUs", "lp"], w=[("ps", 1)])
            k.op("act", lambda e: e.activation(out=eend[:], in_=ps[1][:, 0:256], func=AF.Exp), r=[("ps", 1)], w=["eend"])
            k.op("act", lambda e: e.activation(out=ebT[:].rearrange("p a b -> p (a b)"), in_=ps[1][:, 256:512], func=AF.Exp),
                 r=[("ps", 1)], w=["ebT"])
            k.op("act", lambda e: e.activation(out=enbT[:].rearrange("p a b -> p (a b)"), in_=ps[1][:, 256:512], func=AF.Exp,
                                               scale=-1.0), r=[("ps", 1)], w=["enbT"])
            if CUT[0] <= 3:
                break
            k.op("dve", lambda e: e.tensor_tensor(out=kend[:], in0=vk[:, tt, 512:768], in1=eend[:], op=ALU.mult),
                 r=[("vk", tt), "eend"], w=["kend"])
            k.op("dve", lambda e: e.scalar_tensor_tensor(out=qd[:], in0=qkT[:, 0:2, tsl(tt)], scalar=0.125, in1=ebT[:],
                                                         op0=ALU.mult, op1=ALU.mult),
                 r=[("qkT", 0), ("qkT", 1), "ebT"], w=["qd"])
            k.op("dve", lambda e: e.tensor_tensor(out=kd[:], in0=qkT[:, 2:4, tsl(tt)], in1=enbT[:], op=ALU.mult),
                 r=[("qkT", 2), ("qkT", 3), "enbT"], w=["kd"])
            if CUT[0] <= 5:
                break
            for hd in range(4):
                pr = hd // 2
                k.op("dve", lambda e: e.tensor_scalar(out=qdz[:, hd, :], in0=qd[:, pr, :], scalar1=hmask[:, hd % 2:hd % 2 + 1],
                                                      scalar2=None, op0=ALU.mult), r=["qd", "hmask"], w=["qdz"])
            for hd in range(4):
                pr = hd // 2
                k.op("pe", lambda e: e.matmul(ps[2][:, hd * 128:(hd + 1) * 128], lhsT=kd[:, pr, :],
                                              rhs=qdz[:, hd, :], start=True, stop=True),
                     r=["kd", "qdz"], w=[("ps", 2)])
            if CUT[0] <= 5.3:
                break
            k.op("dve", lambda e: e.tensor_tensor(out=attT[:], in0=ps[2][:, :].rearrange("p (a b) -> p a b", a=4),
                                                  in1=c.triU_f[:].unsqueeze(1).to_broadcast([128, 4, 128]), op=ALU.mult),
                 r=[("ps", 2), "triU_f"], w=["attT"])
            if CUT[0] <= 5.6:
                break
            for hd in range(4):
                pr, p0 = hd // 2, (hd % 2) * 64
                k.op("pe", lambda e: e.matmul(ps[3][:, hd * 128:(hd + 1) * 128], lhsT=attT[:, hd, :],
                                              rhs=vk[:, tt, hd * 128:(hd + 1) * 128], start=True, stop=False),
                     r=["attT", ("vk", tt)], w=[("ps", 3)])
                k.op("pe", lambda e: e.matmul(ps[3][:, hd * 128:(hd + 1) * 128], lhsT=qdz[:, hd, :],
                                              rhs=Sb[:, pr, :], start=False, stop=True),
                     r=["qdz", "Sb"], w=[("ps", 3)])
            if CUT[0] <= 6:
                break
            for pr in range(2):
                k.op("pe", lambda e: e.matmul(ps[4][:, pr * 256:(pr + 1) * 256], lhsT=kend[:, pr * 128:(pr + 1) * 128],
                                              rhs=vk[:, tt, pr * 256:(pr + 1) * 256], start=True, stop=True),
                     r=["kend", ("vk", tt)], w=[("ps", 4)])
            for hd in range(4):
                pr, hf, p0 = hd // 2, hd % 2, (hd % 2) * 64
                k.op("dve", lambda e: e.scalar_tensor_tensor(
                    out=S[p0:p0 + 64, pr, :], in0=S[p0:p0 + 64, pr, :], scalar=ebT[p0:p0 + 64, pr, 127:128],
                    in1=ps[4][p0:p0 + 64, pr * 256 + hf * 128:pr * 256 + (hf + 1) * 128], op0=ALU.mult, op1=ALU.add),
                     r=["S", "ebT", ("ps", 4)], w=["S"])
            k.op("dve", lambda e: e.tensor_copy(out=Sb[:], in_=S[:]), r=["S"], w=["Sb"])
            if CUT[0] <= 7:
                break
            for hd in range(4):
                k.op("act", lambda e: e.activation(out=c.junk[:, 0:128], in_=ps[3][:, hd * 128:(hd + 1) * 128],
                                                   func=AF.Square, accum_out=ssq[:, hd:hd + 1]),
                     r=[("ps", 3)], w=["junk", "ssq"])
            k.op("dve", lambda e: e.tensor_scalar(out=rso[:], in0=ssq[:], scalar1=1.0 / 128, scalar2=EPS,
                                                  op0=ALU.mult, op1=ALU.add), r=["ssq"], w=["rso"])
            k.op("act", lambda e: e.activation(out=rso[:], in_=rso[:], func=AF.Sqrt), r=["rso"], w=["rso"])
            k.op("dve", lambda e: e.reciprocal(out=rso[:], in_=rso[:]), r=["rso"], w=["rso"])
            k.op("dve", lambda e: e.tensor_tensor(out=on[:], in0=ps[3][:, :].rearrange("p (a b) -> p a b", a=4),
                                                  in1=rso[:].unsqueeze(2).to_broadcast([128, 4, 128]), op=ALU.mult),
                 r=[("ps", 3), "rso"], w=["on"])
            for hd in range(4):
                k.op("pe", lambda e: e.transpose(out=pbf[5][:, hd * 128:(hd + 1) * 128], in_=on[:, hd, :],
                                                 identity=c.ident[:]), r=["on", "ident"], w=[("ps", 5)])
            for hd in range(4):
                k.op("dve", lambda e: e.scalar_tensor_tensor(
                    out=yT[:, 4 + hd, tsl(tt)], in0=pbf[5][:, hd * 128:(hd + 1) * 128], scalar=gng[:, hd:hd + 1],
                    in1=rT[:, hd, tsl(tt)], op0=ALU.mult, op1=ALU.mult),
                     r=[("ps", 5), "gng", ("rT", hd)], w=[("yT", 4 + hd)])


def s5(c, ph, uT, yT, a_re_d, a_im_d, ldt_d, b_re_d, b_im_d, c_re_d, c_im_d, d_d, wglu_d):
    nc, k, sb, ps, pbf = c.nc, c.k, c.sb, c.ps, c.pbf
    with ExitStack() as p2:
        wb = sb("wb", [128, 2, 4, 512], BF16, p2)
        cm = sb("cm", [128, 2, 16, 128], BF16, p2)
        with ExitStack() as pa:
            wbs = sb("wbs", [128, 2, 4, 512], F32, pa)
            cms = sb("cms", [128, 2, 16, 128], F32, pa)
            k.op("pool", lambda e: e.memset(wbs[:].rearrange("p a b c -> p (a b c)"), 0.0), w=["wbs"])
            k.op("pool", lambda e: e.memset(cms[:].rearrange("p a b c -> p (a b c)"), 0.0), w=["cms"])
            for ri, bd in enumerate((b_re_d, b_im_d)):
                for g in range(32):
                    kc, g8 = g // 8, g % 8
                    k.dma("sp", wbs[g8 * 16:(g8 + 1) * 16, ri, kc, g8 * 64:(g8 + 1) * 64],
                          bd[g].rearrange("p c -> c p"), r=[], w=["wbs"], allow_slow_non_contiguous=True)
            for ri, cd in enumerate((c_re_d, c_im_d)):
                for g in range(32):
                    ct, gl = g // 2, g % 2
                    g8 = g % 8
                    k.dma("sp", cms[gl * 64:(gl + 1) * 64, ri, ct, g8 * 16:(g8 + 1) * 16],
                          cd[g].rearrange("c p -> p c"), r=[], w=["cms"], allow_slow_non_contiguous=True)
            k.op("act", lambda e: e.activation(out=wb[:].rearrange("p a b c -> p (a b c)"),
                                               in_=wbs[:].rearrange("p a b c -> p (a b c)"), func=AF.Copy),
                 r=["wbs"], w=["wb"])
            k.op("act", lambda e: e.activation(out=cm[:, 0].rearrange("p b c -> p (b c)"),
                                               in_=cms[:, 0].rearrange("p b c -> p (b c)"), func=AF.Copy),
                 r=["cms"], w=["cm"])
            k.op("act", lambda e: e.activation(out=cm[:, 1].rearrange("p b c -> p (b c)"),
                                               in_=cms[:, 1].rearrange("p b c -> p (b c)"), func=AF.Copy, scale=-1.0),
                 r=["cms"], w=["cm"])
            k.barrier()
        if CUT[0] <= 10:
            return
        dcol = sb("dcol", [128, 4], F32, p2)
        k.dma("sp", dcol[:], d_d[0, :].rearrange("(c p) -> p c", p=128), w=["dcol"], allow_slow_non_contiguous=True)
        wglu = sb("wglu", [128, 4, 512], BF16, p2)
        for ch in range(4):
            k.dma("pool", wglu[:, ch, :], wglu_d[ch * 128:(ch + 1) * 128, :], w=["wglu"])
        ETb = sb("ETb", [128, 2, 16, 128], BF16, p2)
        EVt = sb("EVt", [128, 2, 2048], BF16, p2)
        a128 = sb("a128", [128, 2, 16], F32, p2)
        with ExitStack() as pb:
            prm = sb("prm", [16, 3, 128], F32, pb)
            for i, pd in enumerate((a_re_d, a_im_d, ldt_d)):
                k.dma("sp", prm[:, i, :], pd.rearrange("(ct gl) p -> ct (gl p)", gl=2), w=["prm"])
            for i in range(3):
                k.op("pe", lambda e: e.transpose(out=ps[0][:, i * 16:(i + 1) * 16], in_=prm[:, i, :], identity=c.identf[0:16, 0:16]),
                     r=["prm", "identf"], w=[("ps", 0)])
            if CUT[0] <= 10.5:
                return
            P = sb("P", [128, 24, 16], F32, pb)
            AR, AI, DT, MAG, TH, S_, C_, T0, T1, RM, FRE, FIM, NR, DEN, ABR, ABI, AVR, AVI = range(18)
            k.op("dve", lambda e: e.tensor_copy(out=P[:, 0:3, :], in_=ps[0][:, 0:48].rearrange("p (a b) -> p a b", a=3)),
                 r=[("ps", 0)], w=["P"])

            def tt_(o, a, b, op, eng="dve"):
                k.op(eng, lambda e: e.tensor_tensor(out=P[:, o, :], in0=P[:, a, :], in1=P[:, b, :], op=op), r=["P"], w=["P"])

            def act_(o, a, fn, scale=1.0):
                k.op("act", lambda e: e.activation(out=P[:, o, :], in_=P[:, a, :], func=fn, scale=scale), r=["P"], w=["P"])

            def ts_(o, a, s1, s2, op0, op1):
                k.op("dve", lambda e: e.tensor_scalar(out=P[:, o, :], in0=P[:, a, :], scalar1=s1, scalar2=s2, op0=op0, op1=op1),
                     r=["P"], w=["P"])

            if CUT[0] <= 11:
                return
            act_(DT, DT, AF.Exp)
            tt_(T0, DT, AR, ALU.mult)
            act_(MAG, T0, AF.Exp)
            act_(RM, T0, AF.Exp, scale=-1.0)
            tt_(TH, DT, AI, ALU.mult)
            act_(T0, TH, AF.Sin, scale=1.0 / 16)
            tt_(T0, T0, T0, ALU.mult)
            ts_(C_, T0, -2.0, 1.0, ALU.mult, ALU.add)
            act_(S_, TH, AF.Sin, scale=1.0 / 8)
            for _ in range(3):
                tt_(T0, C_, C_, ALU.mult)
                tt_(T1, S_, S_, ALU.mult)
                tt_(S_, S_, C_, ALU.mult)
                ts_(S_, S_, 2.0, None, ALU.mult, ALU.bypass)
                tt_(C_, T0, T1, ALU.subtract)
            tt_(ABR, MAG, C_, ALU.mult)
            tt_(ABI, MAG, S_, ALU.mult)
            tt_(AVR, RM, C_, ALU.mult)
            tt_(AVI, RM, S_, ALU.mult)
            ts_(AVI, AVI, -1.0, None, ALU.mult, ALU.bypass)
            ts_(NR, ABR, -1.0, None, ALU.add, ALU.bypass)
            tt_(T0, AR, AR, ALU.mult)
            tt_(T1, AI, AI, ALU.mult)
            tt_(DEN, T0, T1, ALU.add)
            k.op("dve", lambda e: e.reciprocal(out=P[:, DEN, :], in_=P[:, DEN, :]), r=["P"], w=["P"])
            tt_(T0, NR, AR, ALU.mult)
            tt_(T1, ABI, AI, ALU.mult)
            tt_(T0, T0, T1, ALU.add)
            tt_(FRE, T0, DEN, ALU.mult)
            tt_(T0, ABI, AR, ALU.mult)
            tt_(T1, NR, AI, ALU.mult)
            tt_(T0, T0, T1, ALU.subtract)
            tt_(FIM, T0, DEN, ALU.mult)
            if CUT[0] <= 12:
                return
            ET = sb("ET", [128, 2, 16, 128], F32, pb)
            EV = sb("EV", [128, 2, 16, 128], F32, pb)
            tmp = sb("s5tmp", [128, 2, 16, 64], F32, pb)
            pw = sb("s5pw", [128, 2, 16], F32, pb)
            pw2 = sb("s5pw2", [128, 2, 16], F32, pb)
            for (tab, br, bi, i0r, i0i, name) in ((ET, ABR, ABI, None, None, "ET"), (EV, AVR, AVI, FRE, FIM, "EV")):
                if i0r is None:
                    k.op("pool", lambda e: e.memset(tab[:, 0, :, 0:1], 1.0), w=[name])
                    k.op("pool", lambda e: e.memset(tab[:, 1, :, 0:1], 0.0), w=[name])
                else:
                    k.op("dve", lambda e: e.tensor_copy(out=tab[:, 0, :, 0:1], in_=P[:, i0r, :].unsqueeze(2)), r=["P"], w=[name])
                    k.op("dve", lambda e: e.tensor_copy(out=tab[:, 1, :, 0:1], in_=P[:, i0i, :].unsqueeze(2)), r=["P"], w=[name])
                k.op("dve", lambda e: e.tensor_copy(out=pw[:, 0, :], in_=P[:, br, :]), r=["P"], w=["pw"])
                k.op("dve", lambda e: e.tensor_copy(out=pw[:, 1, :], in_=P[:, bi, :]), r=["P"], w=["pw"])
                m = 1
                while m <= 128:
                    if m < 128:
                        bre = pw[:, 0, :].unsqueeze(2).to_broadcast([128, 16, m])
                        bim = pw[:, 1, :].unsqueeze(2).to_broadcast([128, 16, m])
                        cmul(k, "dve", "dve", tab[:, 0, :, m:2 * m], tab[:, 1, :, m:2 * m],
                             tab[:, 0, :, 0:m], tab[:, 1, :, 0:m], bre, bim,
                             (tmp[:, 0, :, 0:m], tmp[:, 1, :, 0:m]), [name, name + "_re", name + "_im", "pw"], name, "s5tmp")
                    elif name == "ET":
                        k.op("dve", lambda e: e.tensor_copy(out=a128[:], in_=pw[:]), r=["pw"], w=["a128"])
                    cmul(k, "dve", "dve", pw2[:, 0, :], pw2[:, 1, :], pw[:, 0, :], pw[:, 1, :], pw[:, 0, :], pw[:, 1, :],
                         (tmp[:, 0, :, 0], tmp[:, 1, :, 0]), ["pw"], "pw2", "s5tmp")
                    k.op("dve", lambda e: e.tensor_copy(out=pw[:], in_=pw2[:]), r=["pw2_re", "pw2_im"], w=["pw"])
                    m *= 2
            if CUT[0] <= 13:
                return
            n = 0
            for ri in range(2):
                for g4 in range(4):
                    b = n % 2
                    n += 1
                    for q in range(4):
                        ct = g4 * 4 + q
                        k.op("pe", lambda e: e.transpose(out=ps[b][:, q * 128:(q + 1) * 128], in_=EV[:, ri, ct, :],
                                                         identity=c.identf[:]), r=["EV", "EV_re", "EV_im", "identf"], w=[("ps", b)])
                    k.op("act", lambda e: e.activation(out=EVt[:, ri, g4 * 512:(g4 + 1) * 512], in_=ps[b][:, :], func=AF.Copy),
                         r=[("ps", b)], w=["EVt"])

            k.op("act", lambda e: e.activation(out=ETb[:].rearrange("p a b c -> p (a b c)"),
                                               in_=ET[:].rearrange("p a b c -> p (a b c)"), func=AF.Copy),
                 r=["ET", "ET_re", "ET_im"], w=["ETb"])
            k.barrier()
        if CUT[0] <= 14:
            return
        tmpc = sb("s5tmpc", [128, 2, 16], F32, p2)
        zz = sb("zz", [128, 2, 2048], BF16, p2)
        t1 = sb("s5t1", [128, 512], F32, p2)
        t2 = sb("s5t2", [128, 512], F32, p2)
        sT = sb("sT", [128, 2, 16, 128], BF16, p2)
        lastc = sb("lastc", [128, 2, 16], F32, p2)
        cz = sb("cz", [128, 2, 16], F32, p2)
        cz2 = sb("cz2", [128, 2, 16], F32, p2)
        wr = sb("s5wr", [128, 512], F32, p2)
        wi_ = sb("s5wi", [128, 512], F32, p2)
        ypre = sb("ypre", [128, 4, 128], F32, p2)
        if CUT[0] <= 15:
            return
        for tt in range(NTT[0]):
            for kc in range(4):
                for ri in range(2):
                    k.op("pe", lambda e: e.matmul(ps[ri][:, :], lhsT=uT[:, kc, tsl(tt)], rhs=wb[:, ri, kc, :],
                                                  start=True, stop=True), r=[("uT", kc), "wb"], w=[("ps", ri)])
                er = EVt[:, 0, kc * 512:(kc + 1) * 512]
                ei = EVt[:, 1, kc * 512:(kc + 1) * 512]
                k.op("dve", lambda e: e.tensor_tensor(out=t1[:], in0=ps[0][:, :], in1=er, op=ALU.mult),
                     r=[("ps", 0), "EVt"], w=["s5t1"])
                k.op("dve", lambda e: e.tensor_tensor(out=t2[:], in0=ps[1][:, :], in1=ei, op=ALU.mult),
                     r=[("ps", 1), "EVt"], w=["s5t2"])
                k.op("pool", lambda e: e.tensor_tensor(out=zz[:, 0, kc * 512:(kc + 1) * 512], in0=t1[:], in1=t2[:],
                                                       op=ALU.subtract), r=["s5t1", "s5t2"], w=["zz"])
                k.op("dve", lambda e: e.tensor_tensor(out=t1[:], in0=ps[1][:, :], in1=er, op=ALU.mult),
                     r=[("ps", 1), "EVt"], w=["s5t1"])
                k.op("dve", lambda e: e.tensor_tensor(out=t2[:], in0=ps[0][:, :], in1=ei, op=ALU.mult),
                     r=[("ps", 0), "EVt"], w=["s5t2"])
                k.op("pool", lambda e: e.tensor_tensor(out=zz[:, 1, kc * 512:(kc + 1) * 512], in0=t1[:], in1=t2[:],
                                                       op=ALU.add), r=["s5t1", "s5t2"], w=["zz"])
            if CUT[0] <= 16 and tt >= 1:
                return
            for g4 in range(4):
                for ri in range(2):
                    b = 2 + ri
                    for q in range(4):
                        ct = g4 * 4 + q
                        k.op("pe", lambda e: e.matmul(ps[b][:, q * 128:(q + 1) * 128], lhsT=zz[:, ri, ct * 128:(ct + 1) * 128],
                                                      rhs=c.triU_b[:], start=True, stop=True),
                             r=["zz", "triU_b"], w=[("ps", b)])
                cr = ps[2][:, :].rearrange("p (a b) -> p a b", a=4)
                ci = ps[3][:, :].rearrange("p (a b) -> p a b", a=4)
                etr = ETb[:, 0, g4 * 4:(g4 + 1) * 4, :]
                eti = ETb[:, 1, g4 * 4:(g4 + 1) * 4, :]
                t1v = t1[:].rearrange("p (a b) -> p a b", a=4)
                t2v = t2[:].rearrange("p (a b) -> p a b", a=4)
                wrv = wr[:].rearrange("p (a b) -> p a b", a=4)
                wiv = wi_[:].rearrange("p (a b) -> p a b", a=4)
                if tt == 0:
                    k.op("act", lambda e: e.activation(out=wrv, in_=cr, func=AF.Copy), r=[("ps", 2)], w=["wr"])
                    k.op("act", lambda e: e.activation(out=wiv, in_=ci, func=AF.Copy), r=[("ps", 3)], w=["wi_"])
                else:
                    k.op("dve", lambda e: e.tensor_tensor(out=wrv, in0=cr, in1=cz[:, 0, g4 * 4:(g4 + 1) * 4].unsqueeze(2).to_broadcast([128, 4, 128]),
                                                          op=ALU.add), r=[("ps", 2), "cz_re"], w=["wr"])
                    k.op("dve", lambda e: e.tensor_tensor(out=wiv, in0=ci, in1=cz[:, 1, g4 * 4:(g4 + 1) * 4].unsqueeze(2).to_broadcast([128, 4, 128]),
                                                          op=ALU.add), r=[("ps", 3), "cz_im"], w=["wi_"])
                k.op("act", lambda e: e.activation(out=lastc[:, 0, g4 * 4:(g4 + 1) * 4], in_=wrv[:, :, 127], func=AF.Copy),
                     r=["wr"], w=["lastc"])
                k.op("act", lambda e: e.activation(out=lastc[:, 1, g4 * 4:(g4 + 1) * 4], in_=wiv[:, :, 127], func=AF.Copy),
                     r=["wi_"], w=["lastc"])
                k.op("dve", lambda e: e.tensor_tensor(out=t1v, in0=wrv, in1=etr, op=ALU.mult), r=["wr", "ETb"], w=["s5t1"])
                k.op("pool", lambda e: e.tensor_tensor(out=t2v, in0=wiv, in1=eti, op=ALU.mult), r=["wi_", "ETb"], w=["s5t2"])
                k.op("dve", lambda e: e.tensor_tensor(out=sT[:, 0, g4 * 4:(g4 + 1) * 4, :], in0=t1v, in1=t2v, op=ALU.subtract),
                     r=["s5t1", "s5t2"], w=["sT"])
                k.op("dve", lambda e: e.tensor_tensor(out=t1v, in0=wiv, in1=etr, op=ALU.mult), r=["wi_", "ETb"], w=["s5t1"])
                k.op("pool", lambda e: e.tensor_tensor(out=t2v, in0=wrv, in1=eti, op=ALU.mult), r=["wr", "ETb"], w=["s5t2"])
                k.op("dve", lambda e: e.tensor_tensor(out=sT[:, 1, g4 * 4:(g4 + 1) * 4, :], in0=t1v, in1=t2v, op=ALU.add),
                     r=["s5t1", "s5t2"], w=["sT"])
            if tt < NT - 1:
                cmul(k, "dve", "dve", cz2[:, 0, :], cz2[:, 1, :], lastc[:, 0, :], lastc[:, 1, :], a128[:, 0, :], a128[:, 1, :],
                     (tmpc[:, 0, :], tmpc[:, 1, :]), ["lastc", "a128"], "cz2", "s5tmpc")
                k.op("dve", lambda e: e.tensor_copy(out=cz[:], in_=cz2[:]), r=["cz2_re", "cz2_im"], w=["cz_re", "cz_im"])
            if CUT[0] <= 19 and tt >= 1:
                return
            for kc in range(4):
                n = 0
                for q in range(4):
                    ct = kc * 4 + q
                    for ri in range(2):
                        k.op("pe", lambda e: e.matmul(ps[5][:, kc * 128:(kc + 1) * 128], lhsT=cm[:, ri, ct, :],
                                                      rhs=sT[:, ri, ct, :], start=(n == 0), stop=(n == 7)),
                             r=["cm", "sT"], w=[("ps", 5)])
                        n += 1
            if CUT[0] <= 19.3 and tt >= 1:
                return
            for kc in range(4):
                k.op("dve", lambda e: e.tensor_scalar(out=ypre[:, kc, :], in0=uT[:, kc, tsl(tt)], scalar1=dcol[:, kc:kc + 1],
                                                      scalar2=None, op0=ALU.mult), r=[("uT", kc), "dcol"], w=["ypre"])
                k.op("dve", lambda e: e.tensor_tensor(out=ypre[:, kc, :], in0=ypre[:, kc, :], in1=ps[5][:, kc * 128:(kc + 1) * 128],
                                                      op=ALU.add), r=["ypre", ("ps", 5)], w=["ypre"])
            if CUT[0] <= 19.6 and tt >= 1:
                return
            for kc in range(4):
                if GELU_FN[0] is None:
                    k.op("dve", lambda e: e.tensor_copy(out=yT[:, kc, tsl(tt)], in_=ypre[:, kc, :]), r=["ypre"], w=[("yT", kc)])
                else:
                    k.op("act", lambda e: e.activation(out=yT[:, kc, tsl(tt)], in_=ypre[:, kc, :], func=GELU_FN[0]), r=["ypre"],
                         w=[("yT", kc)])
        for nm, tl, kk in (("ypre", ypre, ["ypre"]), ("zz", zz, ["zz"]), ("sT", sT, ["sT"]), ("ETb", ETb, ["ETb"]), ("EVt", EVt, ["EVt"]),
                           ("wb", wb, ["wb"]), ("cm", cm, ["cm"]), ("a128", a128, ["a128"]), ("lastc", lastc, ["lastc"])):
            if nm in c.dbg:
                ap = tl[:]
                if len(ap.shape) == 3:
                    ap = ap.rearrange("p a b -> p (a b)")
                elif len(ap.shape) == 4:
                    ap = ap.rearrange("p a b c -> p (a b c)")
                k.dma("sp", c.dbg[nm], ap, r=kk, w=["dbg_" + nm])
        if CUT[0] <= 20:
            return
        sg = sb("sg", [128, 4, 512], BF16, p2)
        yk = [("yT", i) for i in range(4)]
        n = 0
        for tb in range(4):
            for c2 in range(4):
                b = 6 + n % 2
                n += 1
                for ch in range(4):
                    k.op("pe", lambda e: e.matmul(ps[b][:, :], lhsT=wglu[:, ch, c2 * 128:(c2 + 1) * 128],
                                                  rhs=yT[:, ch, tb * 512:(tb + 1) * 512], start=(ch == 0), stop=(ch == 3)),
                         r=["wglu"] + yk, w=[("ps", b)])
                k.op("act", lambda e: e.activation(out=sg[:, c2, :], in_=ps[b][:, :], func=AF.Sigmoid), r=[("ps", b)],
                     w=[("sg", c2)])
            for c2 in range(4):
                k.op("dve", lambda e: e.tensor_tensor(out=yT[:, c2, tb * 512:(tb + 1) * 512], in0=yT[:, c2, tb * 512:(tb + 1) * 512],
                                                      in1=sg[:, c2, :], op=ALU.mult), r=[("yT", c2), ("sg", c2)], w=[("yT", c2)])
        k.barrier()


def peer_inputs(c):
    c.wq_d = c.din("peer_w_q", [2, D, 2048])
    c.keys_d = c.din("peer_sub_keys", [2, 8, 2, 128, 128])
    c.u_d = c.din("peer_u", [2, 16384, D])
    c.v_d = c.din("peer_v", [2, 16384, D])
    c.ut_scr = c.nc.dram_tensor("ut_scr", [8, 128, 16384], BF16, kind="Internal").ap()
    c.vb_scr = c.nc.dram_tensor("vb_scr", [16384, D], BF16, kind="Internal").ap()


NEG = -1.0
PEER_EG = [32]
PEER_ACT_HEADS = [5]
PEER_PROD = [['act', 'pool', 'pool', 'act', 'pool', 'pool', 'act', 'pool']]


def peer(c, li):
    nc, k, sb, ps, h = c.nc, c.k, c.sb, c.ps, c.h
    wq_d, keys_d, u_d, v_d = c.wq_d[li], c.keys_d[li], c.u_d[li], c.v_d[li]
    utv = c.ut_scr.rearrange("dc d e -> d dc e")
    with ExitStack() as pp:
        usts = [sb("pust%d" % i, [128, 4, D], BF16, pp) for i in range(2)]
        uts = [sb("puts%d" % i, [128, 8, 512], BF16, pp) for i in range(2)]
        vbs_ = [sb("pvb%d" % i, [128, 4, D], BF16, pp) for i in range(2)]
        for eg in range(32):
            i = eg % 2
            uk, tk, vk_ = "pust%d" % i, "puts%d" % i, "pvb%d" % i
            k.dma("pool", usts[i][:], u_d[eg * 512:(eg + 1) * 512, :].rearrange("(a p) d -> p a d", p=128), w=[uk])
            k.dma("pool", vbs_[i][:], v_d[eg * 512:(eg + 1) * 512, :].rearrange("(a p) d -> p a d", p=128), w=[vk_])
            for dc in range(8):
                b = (eg * 8 + dc) % 4
                for a in range(4):
                    k.op("pe", lambda e: e.transpose(out=c.pbf[b][:, a * 128:(a + 1) * 128], in_=usts[i][:, a, dc * 128:(dc + 1) * 128],
                                                     identity=c.ident[:]), r=[uk, "ident"], w=[("ps", b)])
                k.op("act", lambda e: e.activation(out=uts[i][:, dc, :], in_=c.pbf[b][:, 0:512], func=AF.Copy), r=[("ps", b)],
                     w=[tk])
            k.dma("sp", utv[:, :, eg * 512:(eg + 1) * 512], uts[i][:], r=[tk], w=[("utscr", eg)])
            k.dma("sp", c.vb_scr[eg * 512:(eg + 1) * 512, :].rearrange("(a p) d -> p a d", p=128), vbs_[i][:], r=[vk_],
                  w=[("vbscr", eg)])
        k.barrier()
    with ExitStack() as ph:
        hnT = sb("hnT", [128, 8, L], BF16, ph)
        rmsnorm_T(c, c.norm_ffn_g[li, :], hnT, "hnT")
        allhn = [("hnT", tt) for tt in range(NT)]
        e_all = sb("e_all", [128, 4, 16, 128], F32, ph)
        diag = sb("diag", [128, 4, 8, 128], BF16, ph)
        phi = sb("phi", [128, 4, 8], F32, ph)
        mx = sb("pmx", [128, 16], F32, ph)
        t16 = sb("t16", [128, 16, 16], F32, ph)
        tmpb = sb("tmpb", [128, 128], F32, ph)
        cand = sb("cand", [128, 256], F32, ph)
        cand2 = sb("cand2", [128, 256], F32, ph)
        c16 = sb("c16", [128, 16], F32, ph)
        zs = sb("zs", [128, 8], F32, ph)
        for tg in range(4):
            with ExitStack() as p1:
                kT = sb("kT", [128, 16, 128], BF16, p1)
                with ExitStack() as p0:
                    kst = sb("kst", [128, 16, 128], F32, p0)
                    k.dma("sp", kst[:], keys_d.rearrange("h c n d -> n (h c) d"), w=["kst"])
                    for g in range(4):
                        b = g % 2
                        for q in range(4):
                            k.op("pe", lambda e: e.transpose(out=ps[b][:, q * 128:(q + 1) * 128], in_=kst[:, g * 4 + q, :],
                                                             identity=c.identf[:]), r=["kst", "identf"], w=[("ps", b)])
                        k.op("act", lambda e: e.activation(out=kT[:, g * 4:(g + 1) * 4, :].rearrange("p a b -> p (a b)"), in_=ps[b][:, :],
                                                           func=AF.Copy), r=[("ps", b)], w=["kT"])
                    k.barrier()

                wq = sb("wq", [128, 8, 2048], BF16, p1)
                qT = sb("qT", [128, 16, 512], BF16, p1)
                for ch in range(8):
                    k.dma("pool", wq[:, ch, :], wq_d[ch * 128:(ch + 1) * 128, :], w=[("wq", ch)])
                allwq = [("wq", ch) for ch in range(8)]
                for blk in range(16):
                    b = blk % 2
                    for ch in range(8):
                        k.op("pe", lambda e: e.matmul(ps[b][:, :], lhsT=wq[:, ch, blk * 128:(blk + 1) * 128],
                                                      rhs=hnT[:, ch, tg * 512:(tg + 1) * 512], start=(ch == 0), stop=(ch == 7)),
                             r=allwq + allhn, w=[("ps", b)])
                    k.op("act", lambda e: e.activation(out=qT[:, blk, :], in_=ps[b][:, :], func=AF.Copy), r=[("ps", b)],
                         w=[("qT", blk)])
                for tt in range(4):
                    for blk in range(16):
                        b = 2 + blk // 4
                        k.op("pe", lambda e: e.matmul(ps[b][:, (blk % 4) * 128:(blk % 4 + 1) * 128], lhsT=qT[:, blk, tt * 128:(tt + 1) * 128],
                                                      rhs=kT[:, blk, :], start=True, stop=True), r=[("qT", blk), "kT"], w=[("ps", b)])
                    for b4 in range(4):
                        k.op("dve", lambda e: e.tensor_reduce(out=mx[:, b4 * 4:(b4 + 1) * 4],
                                                              in_=ps[2 + b4][:, :].rearrange("p (a n) -> p a n", a=4),
                                                              axis=AX.X, op=ALU.max), r=[("ps", 2 + b4)], w=["pmx"])
                    k.op("dve", lambda e: e.tensor_scalar(out=mx[:], in0=mx[:], scalar1=-1.0, scalar2=None, op0=ALU.mult),
                         r=["pmx"], w=["pmx"])
                    for blk in range(16):
                        b = 2 + blk // 4
                        k.op("act", lambda e: e.activation(out=e_all[:, tt, blk, :], in_=ps[b][:, (blk % 4) * 128:(blk % 4 + 1) * 128],
                                                           func=AF.Exp, bias=mx[:, blk:blk + 1], scale=1.0),
                             r=[("ps", b), "pmx"], w=[("e_all", tt, blk)])
                    for blk in range(16):
                        ek = ("e_all", tt, blk)
                        k.op("dve", lambda e: e.max(out=t16[:, blk, 0:8], in_=e_all[:, tt, blk, :]), r=[ek], w=["t16"])
                        k.op("dve", lambda e: e.match_replace(out=tmpb[:], in_to_replace=t16[:, blk, 0:8],
                                                              in_values=e_all[:, tt, blk, :], imm_value=NEG),
                             r=[ek, "t16"], w=["tmpb"])
                        k.op("dve", lambda e: e.max(out=t16[:, blk, 8:16], in_=tmpb[:]), r=["tmpb"], w=["t16"])
                    for hd in range(8):
                        k.op("dve", lambda e: e.tensor_tensor(
                            out=cand[:].rearrange("p (i j) -> p i j", i=16),
                            in0=t16[:, 2 * hd, :].unsqueeze(2).to_broadcast([128, 16, 16]),
                            in1=t16[:, 2 * hd + 1, :].unsqueeze(1).to_broadcast([128, 16, 16]), op=ALU.mult),
                             r=["t16"], w=["cand"])
                        k.op("dve", lambda e: e.max(out=c16[:, 0:8], in_=cand[:]), r=["cand"], w=["c16"])
                        k.op("dve", lambda e: e.match_replace(out=cand2[:], in_to_replace=c16[:, 0:8], in_values=cand[:],
                                                              imm_value=NEG), r=["cand", "c16"], w=["cand2"])
                        k.op("dve", lambda e: e.max(out=c16[:, 8:16], in_=cand2[:]), r=["cand2"], w=["c16"])
                        k.op("dve", lambda e: e.tensor_scalar(out=phi[:, tt, hd:hd + 1], in0=c16[:, 15:16], scalar1=1.0 - 1e-6, scalar2=None,
                                                              op0=ALU.mult), r=["c16"], w=["phi"])
                        k.op("dve", lambda e: e.tensor_reduce(out=zs[:, hd:hd + 1], in_=c16[:], axis=AX.X, op=ALU.add),
                             r=["c16"], w=["zs"])
                    k.op("dve", lambda e: e.reciprocal(out=zs[:], in_=zs[:]), r=["zs"], w=["zs"])
                    for hd in range(8):
                        k.op("dve", lambda e: e.tensor_scalar(out=diag[:, tt, hd, :], in0=c.identf[:], scalar1=zs[:, hd:hd + 1],
                                                              scalar2=None, op0=ALU.mult), r=["zs", "identf"], w=["diag"])
                k.barrier()
            with ExitStack() as p2:
                utsb = sb("utsb", [128, 8, 512], BF16, p2)
                vbs = [sb("vb%d" % i, [128, 4, D], BF16, p2) for i in range(2)]
                gels = [sb("gel%d" % i, [128, 4, 512], BF16, p2) for i in range(2)]
                Gs = [sb("G%d" % i, [128, 8, 512], BF16, p2) for i in range(2)]
                prod = [sb("prod%d" % i, [128, 512], F32, p2) for i in range(2)]
                HsTs = [sb("HsT%d" % i, [128, 4, 128], BF16, p2) for i in range(2)]
                NEG_ = PEER_EG[0]
                steps = [(eg, tt) for eg in range(NEG_) for tt in range(4)]
                npr = [0]

                def load(eg):
                    k.dma("sp", utsb[:], utv[:, :, eg * 512:(eg + 1) * 512], r=[("utscr", eg)], w=["utsb"])
                    k.dma("sp", vbs[eg % 2][:], c.vb_scr[eg * 512:(eg + 1) * 512, :].rearrange("(a p) d -> p a d", p=128),
                          r=[("vbscr", eg)], w=["vb%d" % (eg % 2)])

                def hpart(eg, a):
                    b = a % 2
                    for dc in range(8):
                        k.op("pe", lambda e: e.matmul(ps[b][:, :], lhsT=utsb[:, dc, a * 128:(a + 1) * 128],
                                                      rhs=hnT[:, dc, tg * 512:(tg + 1) * 512], start=(dc == 0), stop=(dc == 7)),
                             r=["utsb"] + allhn, w=[("ps", b)])
                    k.op("act", lambda e: e.activation(out=gels[eg % 2][:, a, :], in_=ps[b][:, :], func=AF.Gelu), r=[("ps", b)],
                         w=[("gel%d" % (eg % 2), a)])

                def stage_a(s_):
                    eg, tt = steps[s_]
                    G, gk = Gs[s_ % 2], "G%d" % (s_ % 2)
                    for hd in range(8):
                        pr = prod[npr[0] % 2]
                        pk = "prod%d" % (npr[0] % 2)
                        npr[0] += 1
                        pe_ = PEER_PROD[0][hd]
                        if pe_ == 'act':
                            for a in range(4):
                                k.op("act", lambda e: e.activation(out=pr[:, a * 128:(a + 1) * 128], in_=e_all[:, tt, 2 * hd + 1, :],
                                                                   func=AF.Copy, scale=e_all[:, tt, 2 * hd, eg * 4 + a:eg * 4 + a + 1]),
                                     r=[("e_all", tt, 2 * hd), ("e_all", tt, 2 * hd + 1)], w=[pk])
                        else:
                            k.op(pe_, lambda e: e.tensor_tensor(
                                out=pr[:].rearrange("p (a n) -> p a n", a=4),
                                in0=e_all[:, tt, 2 * hd, eg * 4:(eg + 1) * 4].unsqueeze(2).to_broadcast([128, 4, 128]),
                                in1=e_all[:, tt, 2 * hd + 1, :].unsqueeze(1).to_broadcast([128, 4, 128]), op=ALU.mult),
                                 r=[("e_all", tt, 2 * hd), ("e_all", tt, 2 * hd + 1)], w=[pk])
                        k.op("dve", lambda e: e.scalar_tensor_tensor(out=G[:, hd, :], in0=pr[:], scalar=phi[:, tt, hd:hd + 1],
                                                                     in1=pr[:], op0=ALU.is_ge, op1=ALU.mult),
                             r=[pk, "phi"], w=[(gk, hd)])

                def stage_b(s_):
                    eg, tt = steps[s_]
                    G, gk, gb = Gs[s_ % 2], "G%d" % (s_ % 2), 4 + s_ % 2
                    for a in range(4):
                        for hd in range(8):
                            k.op("pe", lambda e: e.matmul(ps[gb][:, a * 128:(a + 1) * 128], lhsT=G[:, hd, a * 128:(a + 1) * 128],
                                                          rhs=diag[:, tt, hd, :], start=(hd == 0), stop=(hd == 7)),
                                 r=[(gk, hd), "diag"], w=[("ps", gb)])

                def stage_c(s_):
                    eg, tt = steps[s_]
                    gb = 4 + s_ % 2
                    k.op("dve", lambda e: e.tensor_tensor(out=HsTs[s_ % 2][:], in0=gels[eg % 2][:, :, tt * 128:(tt + 1) * 128],
                                                          in1=ps[gb][:, :].rearrange("p (a n) -> p a n", a=4), op=ALU.mult),
                         r=[("gel%d" % (eg % 2), a) for a in range(4)] + [("ps", gb)], w=["HsT%d" % (s_ % 2)])

                def obank(s_, dh):
                    return (2 + dh) if s_ % 2 == 0 else (6 + dh)

                def stage_d(s_):
                    eg, tt = steps[s_]
                    for dh in range(2):
                        ob = obank(s_, dh)
                        for a in range(4):
                            k.op("pe", lambda e: e.matmul(ps[ob][:, :], lhsT=HsTs[s_ % 2][:, a, :], rhs=vbs[eg % 2][:, a, dh * 512:(dh + 1) * 512],
                                                          start=(a == 0), stop=(a == 3)), r=["HsT%d" % (s_ % 2), "vb%d" % (eg % 2)],
                                 w=[("ps", ob)])

                def stage_e(s_):
                    eg, tt = steps[s_]
                    tok = tg * 4 + tt
                    for dh in range(2):
                        ob = obank(s_, dh)
                        k.op("dve", lambda e: e.tensor_tensor(out=h[:, tok, dh * 512:(dh + 1) * 512],
                                                              in0=h[:, tok, dh * 512:(dh + 1) * 512], in1=ps[ob][:, :],
                                                              op=ALU.add), r=[("h", tok), ("ps", ob)], w=[("h", tok)])

                if steps:
                    load(0)
                    for a in range(4):
                        hpart(0, a)
                    stage_a(0)
                for s_ in range(len(steps)):
                    eg, tt = steps[s_]
                    if tt == 0 and eg + 1 < NEG_:
                        load(eg + 1)
                    if s_ + 1 < len(steps):
                        stage_a(s_ + 1)
                    stage_b(s_)
                    stage_c(s_)
                    if eg + 1 < NEG_:
                        hpart(eg + 1, tt)
                    stage_d(s_)
                    if s_ >= 1:
                        stage_e(s_ - 1)
                if steps:
                    stage_e(len(steps) - 1)
                k.barrier()


RW_HP = [8]
RW_TB = [4]


def rwkv_mixer(c, li):
    nc, k, sb, ps, pbf, h = c.nc, c.k, c.sb, c.ps, c.pbf, c.h
    din = c.din
    mu_d = din("o_mu", [6, D])
    wr_d, wk_d, wv_d = din("o_w_r", [D, D]), din("o_w_k", [D, D]), din("o_w_v", [D, D])
    w0_d = din("o_w0", [1, D])
    ww1_d, ww2_d = din("o_w_w1", [D, 64]), din("o_w_w2", [64, D])
    a0_d = din("o_a0", [1, D])
    wa1_d, wa2_d = din("o_w_a1", [D, 64]), din("o_w_a2", [64, D])
    wg1_d, wg2_d = din("o_w_g1", [D, 128]), din("o_w_g2", [128, D])
    kk_d, ka_d, rk_d = din("o_k_k", [1, D]), din("o_k_a", [1, D]), din("o_r_k", [1, D])
    lg_d, lb_d = din("o_lnx_g", [1, D]), din("o_lnx_b", [1, D])
    wo_d = din("o_w_o", [D, D])

    def mm(out, lhsT, rhs, r, w, start=True, stop=True):
        k.op("pe", lambda e: e.matmul(out, lhsT=lhsT, rhs=rhs, start=start, stop=stop), r=r, w=w)

    def tt_(eng, out, in0, in1, op, r, w):
        k.op(eng, lambda e: e.tensor_tensor(out=out, in0=in0, in1=in1, op=op), r=r, w=w)

    def ts_(eng, out, in0, s1, s2, op0, op1, r, w):
        k.op(eng, lambda e: e.tensor_scalar(out=out, in0=in0, scalar1=s1, scalar2=s2, op0=op0, op1=op1), r=r, w=w)

    def act(out, in_, fn, r, w, **kw):
        k.op("act", lambda e: e.activation(out=out, in_=in_, func=fn, **kw), r=r, w=w)

    with ExitStack() as ph:
        xp = sb("xp", [128, 8, L + 2], BF16, ph)
        k.op("pool", lambda e: e.memset(xp[:, :, 0:1], 0.0), w=["xp0"])
        rmsnorm_T(c, c.norm_mix_g[li, :], xp, "xp", off=1)
        allx = [("xp", tt) for tt in range(NT)] + ["xp0"]
        mjt = sb("mjt", [128, 128], F32, ph)
        mtj = sb("mtj", [128, 128], F32, ph)
        k.op("pool", lambda e: e.affine_select(out=mjt[:], in_=c.ones_f[:], pattern=[[1, 128]], compare_op=ALU.is_gt,
                                                fill=0.0, base=0, channel_multiplier=-1), r=["ones_f"], w=["mjt"])
        k.op("pool", lambda e: e.affine_select(out=mtj[:], in_=c.ones_f[:], pattern=[[-1, 128]], compare_op=ALU.is_gt,
                                                fill=0.0, base=0, channel_multiplier=1), r=["ones_f"], w=["mtj"])
        hm = sb("hm", [128, 2], F32, ph)
        bones = sb("bones", [128, 128], BF16, ph)
        cmask = sb("cmask", [128, 2, 128], F32, ph)
        dsel = sb("dsel", [128, 2, 128], F32, ph)
        k.op("pool", lambda e: e.memset(hm[:], 0.0), w=["hm"])
        k.op("pool", lambda e: e.memset(hm[0:64, 0:1], 1.0), r=["hm"], w=["hm"])
        k.op("pool", lambda e: e.memset(hm[64:128, 1:2], 1.0), r=["hm"], w=["hm"])
        k.op("pool", lambda e: e.memset(bones[:], 0.0), w=["bones"])
        k.op("pool", lambda e: e.memset(bones[0:64, 0:64], 1.0), r=["bones"], w=["bones"])
        k.op("pool", lambda e: e.memset(bones[64:128, 64:128], 1.0), r=["bones"], w=["bones"])
        k.op("pool", lambda e: e.memset(cmask[:].rearrange("p a b -> p (a b)"), 0.0), w=["cmask"])
        k.op("pool", lambda e: e.memset(cmask[:, 0, 0:64], 1.0), r=["cmask"], w=["cmask"])
        k.op("pool", lambda e: e.memset(cmask[:, 1, 64:128], 1.0), r=["cmask"], w=["cmask"])
        for hl in range(2):
            ts_("dve", dsel[:, hl, :], c.identf[:], hm[:, hl:hl + 1], None, ALU.mult, ALU.bypass, ["identf", "hm"], ["dsel"])
        chm = sb("chm", [128, 512], BF16, ph)
        k.op("pool", lambda e: e.memset(chm[:], 1.0), w=["chm"])
        k.op("pool", lambda e: e.memset(chm[:].rearrange("p (a b) -> p a b", a=4)[:, :, 0:1], 0.0), r=["chm"], w=["chm"])
        prm = sb("rprm", [128, 7, 8], F32, ph)
        for i, pd in enumerate((w0_d, a0_d, kk_d, ka_d, rk_d, lg_d, lb_d)):
            k.dma("sp", prm[:, i, :], pd[0, :].rearrange("(dc p) -> p dc", p=128), w=["rprm"], allow_slow_non_contiguous=True)
        mucol = sb("mucol", [128, 6, 8], F32, ph)
        k.dma("sp", mucol[:], mu_d.rearrange("i (dc p) -> p i dc", p=128), w=["mucol"], allow_slow_non_contiguous=True)
        MU = dict(r=0, w=1, k=2, v=3, a=4, g=5)

        def load_split(name, wd, cols, ncol, mui, st):
            w0 = sb(name + "0", [128, 8, ncol], BF16, st)
            wm = sb(name + "m", [128, 8, ncol], BF16, st)
            wp = sb(name + "p", [128, 8, ncol], BF16, st)
            k.dma("pool", w0[:], wd[:, cols].rearrange("(dc p) n -> p dc n", p=128), w=[name + "0"])
            tt_("dve", wm[:], w0[:], mucol[:, mui, :].unsqueeze(2).to_broadcast([128, 8, ncol]), ALU.mult,
                [name + "0", "mucol"], [name + "m"])
            tt_("dve", wp[:], w0[:], wm[:], ALU.subtract, [name + "0", name + "m"], [name + "p"])
            return wp, wm

        ww2 = sb("ww2", [128, D], BF16, ph)
        wa2 = sb("wa2", [128, D], BF16, ph)
        wg2 = sb("wg2r", [128, D], BF16, ph)
        k.dma("pool", ww2[0:64, :], ww2_d[:, :], w=["ww2"])
        k.dma("pool", wa2[0:64, :], wa2_d[:, :], w=["wa2"])
        k.dma("pool", wg2[:], wg2_d[:, :], w=["wg2r"])
        tw1 = sb("tw1", [128, L], BF16, ph)
        ta1 = sb("ta1", [128, L], BF16, ph)
        tg1 = sb("tg1", [128, L], BF16, ph)

        def proj(out_ps, wp, wm, c0, m, t0, n, keys):
            for dc in range(8):
                mm(out_ps, wp[:, dc, c0:c0 + m], xp[:, dc, 1 + t0:1 + t0 + n], allx + keys, [("ps", 0)], start=(dc == 0), stop=False)
            for dc in range(8):
                mm(out_ps, wm[:, dc, c0:c0 + m], xp[:, dc, t0:t0 + n], allx + keys, [("ps", 0)], start=False, stop=(dc == 7))

        with ExitStack() as p0:
            w1p, w1m = load_split("ww1", ww1_d, slice(0, 64), 64, MU["w"], p0)
            a1p, a1m = load_split("wa1", wa1_d, slice(0, 64), 64, MU["a"], p0)
            g1p, g1m = load_split("wg1", wg1_d, slice(0, 128), 128, MU["g"], p0)
            for tb in range(4):
                t0 = tb * 512
                proj(ps[0][0:64, :], w1p, w1m, 0, 64, t0, 512, ["ww1p", "ww1m"])
                act(tw1[0:64, t0:t0 + 512], ps[0][0:64, :], AF.Tanh, [("ps", 0)], ["tw1"])
                proj(ps[0][0:64, :], a1p, a1m, 0, 64, t0, 512, ["wa1p", "wa1m"])
                act(ta1[0:64, t0:t0 + 512], ps[0][0:64, :], AF.Copy, [("ps", 0)], ["ta1"])
                proj(ps[0][:, :], g1p, g1m, 0, 128, t0, 512, ["wg1p", "wg1m"])
                act(tg1[:, t0:t0 + 512], ps[0][:, :], AF.Sigmoid, [("ps", 0)], ["tg1"])
            k.barrier()
        W = 512
        f32t = lambda n, st: sb(n, [128, W], F32, st)
        b16t = lambda n, st: sb(n, [128, W], BF16, st)
        for hp in range(RW_HP[0]):
            with ExitStack() as p1:
                cols = slice(hp * 128, (hp + 1) * 128)
                wrp, wrm = load_split("wr", wr_d, cols, 128, MU["r"], p1)
                wkp, wkm = load_split("wk", wk_d, cols, 128, MU["k"], p1)
                wvp, wvm = load_split("wv", wv_d, cols, 128, MU["v"], p1)
                wo = sb("wo_hp", [128, D], BF16, p1)
                k.dma("pool", wo[:], wo_d[hp * 128:(hp + 1) * 128, :], w=["wo_hp"])
                pc = lambda i: prm[:, i, hp:hp + 1]
                Hb = sb("Hb", [128, 64], BF16, p1)
                k.op("pool", lambda e: e.memset(Hb[:], 0.0), w=["Hb"])
                r_b, k_b, v_b, g_b = b16t("r_b", p1), b16t("k_b", p1), b16t("v_b", p1), b16t("g_b", p1)
                vtok = sb("vtok", [128, 4, 128], BF16, p1)
                lw, cs, asig = f32t("lw", p1), f32t("cs", p1), f32t("asig", p1)
                e1, e2, e3, e4 = f32t("e1", p1), f32t("e2", p1), f32t("e3", p1), f32t("e4", p1)
                kkn, kmod, b_ = f32t("kkn", p1), f32t("kmod", p1), f32t("bb", p1)
                sq = b16t("sq", p1)
                rt, rz0, rz1, az0, az1 = b16t("rt", p1), b16t("rz0", p1), b16t("rz1", p1), b16t("az0", p1), b16t("az1", p1)
                at_, bt, kt, bh, kh = b16t("at", p1), b16t("bt", p1), b16t("kt", p1), b16t("bh", p1), b16t("kh", p1)
                bv = f32t("bv", p1)
                pcl = sb("pcl", [128, 4], F32, p1)
                rz, az = (rz0, rz1), (az0, az1)
                Xs = [sb("X%d" % i, [128, 256], BF16, p1) for i in range(2)]
                MNs = [[sb("MN%d_%d" % (j, i), [128, 256], BF16, p1) for i in range(2)] for j in range(2)]
                cats = [sb("cat%d" % i, [128, 256], BF16, p1) for i in range(2)]
                mrks = [sb("mrk%d" % i, [128, 128], F32, p1) for i in range(2)]
                btok = sb("btok", [128, 2, 128], BF16, p1)
                bz = sb("bz", [128, 2, 128], BF16, p1)
                kz = sb("kz", [128, 2, 128], BF16, p1)
                RGMF = [[sb("%s%d" % (n, i), [128, 128], BF16, p1) for n in ("RpT", "GT", "MpT", "FT")] for i in range(2)]
                gst = sb("gst", [128, 8], F32, p1)
                ysq = sb("ysq", [128, 128], F32, p1)
                yn = sb("yn", [128, 128], BF16, p1)
                z1 = sb("z1", [128, 128], F32, p1)
                ygT = sb("ygT", [128, 128], BF16, p1)
                for tb in range(RW_TB[0]):
                    t0 = tb * W
                    proj(ps[0][:, :], wrp, wrm, 0, 128, t0, W, ["wrp", "wrm"])
                    act(r_b[:], ps[0][:, :], AF.Copy, [("ps", 0)], ["r_b"])
                    proj(ps[0][:, :], wkp, wkm, 0, 128, t0, W, ["wkp", "wkm"])
                    act(k_b[:], ps[0][:, :], AF.Copy, [("ps", 0)], ["k_b"])
                    proj(ps[0][:, :], wvp, wvm, 0, 128, t0, W, ["wvp", "wvm"])
                    act(v_b[:], ps[0][:, :], AF.Copy, [("ps", 0)], ["v_b"])
                    for q in range(4):
                        tq = t0 + q * 128
                        for dc in range(8):
                            mm(ps[6][:, q * 128:(q + 1) * 128], xp[:, dc, 1 + tq:1 + tq + 128], wvp[:, dc, :], allx + ["wvp"],
                               [("ps", 6)], start=(dc == 0), stop=False)
                        for dc in range(8):
                            mm(ps[6][:, q * 128:(q + 1) * 128], xp[:, dc, tq:tq + 128], wvm[:, dc, :], allx + ["wvm"],
                               [("ps", 6)], start=False, stop=(dc == 7))
                    k.op("dve", lambda e: e.tensor_copy(out=vtok[:].rearrange("p a b -> p (a b)"), in_=ps[6][:, :]),
                         r=[("ps", 6)], w=["vtok"])
                    mm(ps[0][:, :], ww2[0:64, cols], tw1[0:64, t0:t0 + W], ["ww2", "tw1"], [("ps", 0)])
                    act(lw[:], ps[0][:, :], AF.Sigmoid, [("ps", 0), "rprm"], ["lw"], bias=pc(0), scale=1.0)
                    ts_("dve", lw[:], lw[:], -0.6065306597126334, None, ALU.mult, ALU.bypass, ["lw"], ["lw"])
                    mm(ps[0][:, :], wa2[0:64, cols], ta1[0:64, t0:t0 + W], ["wa2", "ta1"], [("ps", 0)])
                    act(asig[:], ps[0][:, :], AF.Sigmoid, [("ps", 0), "rprm"], ["asig"], bias=pc(1), scale=1.0)
                    mm(ps[0][:, :], wg2[:, cols], tg1[:, t0:t0 + W], ["wg2r", "tg1"], [("ps", 0)])
                    act(g_b[:], ps[0][:, :], AF.Copy, [("ps", 0)], ["g_b"])
                    k.op("dve", lambda e: e.tensor_tensor_scan(out=cs[:], data0=chm[:], data1=lw[:], initial=0.0,
                                                               op0=ALU.mult, op1=ALU.add), r=["chm", "lw"], w=["cs"])
                    cs3 = cs[:].rearrange("p (a b) -> p a b", a=4)
                    csl = cs3[:, :, 127:128].to_broadcast([128, 4, 128])
                    act(e1[:], cs[:], AF.Exp, ["cs"], ["e1"])
                    act(e2[:], cs[:], AF.Exp, ["cs"], ["e2"], scale=-1.0)
                    tt_("dve", e3[:], cs[:], lw[:], ALU.subtract, ["cs", "lw"], ["e3"])
                    act(e3[:], e3[:], AF.Exp, ["e3"], ["e3"])
                    tt_("dve", e4[:].rearrange("p (a b) -> p a b", a=4), csl, cs3, ALU.subtract, ["cs"], ["e4"])
                    act(e4[:], e4[:], AF.Exp, ["e4"], ["e4"])
                    act(pcl[:], cs3[:, :, 127], AF.Exp, ["cs"], ["pcl"])
                    ts_("dve", kkn[:], k_b[:], pc(2), None, ALU.mult, ALU.bypass, ["k_b", "rprm"], ["kkn"])
                    tt_("dve", sq[:], kkn[:], kkn[:], ALU.mult, ["kkn"], ["sq"])
                    mm(ps[0][:, :], bones[:], sq[:], ["bones", "sq"], [("ps", 0)])
                    act(kmod[:], ps[0][:, :], AF.Sqrt, [("ps", 0)], ["kmod"])
                    ts_("dve", kmod[:], kmod[:], 1e-12, None, ALU.max, ALU.bypass, ["kmod"], ["kmod"])
                    k.op("dve", lambda e: e.reciprocal(out=kmod[:], in_=kmod[:]), r=["kmod"], w=["kmod"])
                    tt_("dve", kkn[:], kkn[:], kmod[:], ALU.mult, ["kkn", "kmod"], ["kkn"])
                    ts_("dve", kmod[:], asig[:], -1.0, pc(3), ALU.add, ALU.mult, ["asig", "rprm"], ["kmod"])
                    ts_("dve", kmod[:], kmod[:], 1.0, None, ALU.add, ALU.bypass, ["kmod"], ["kmod"])
                    tt_("dve", kmod[:], kmod[:], k_b[:], ALU.mult, ["kmod", "k_b"], ["kmod"])
                    tt_("dve", b_[:], kkn[:], asig[:], ALU.mult, ["kkn", "asig"], ["bb"])
                    tt_("dve", rt[:], r_b[:], e1[:], ALU.mult, ["r_b", "e1"], ["rt"])
                    tt_("pool", kt[:], kmod[:], e2[:], ALU.mult, ["kmod", "e2"], ["kt"])
                    tt_("dve", bt[:], b_[:], e2[:], ALU.mult, ["bb", "e2"], ["bt"])
                    tt_("pool", kh[:], kmod[:], e4[:], ALU.mult, ["kmod", "e4"], ["kh"])
                    tt_("dve", bh[:], b_[:], e4[:], ALU.mult, ["bb", "e4"], ["bh"])
                    tt_("pool", e3[:], kkn[:], e3[:], ALU.mult, ["kkn", "e3"], ["e3"])
                    ts_("dve", at_[:], e3[:], -1.0, None, ALU.mult, ALU.bypass, ["e3"], ["at"])
                    for hl in range(2):
                        ts_("dve", rz[hl][:], rt[:], hm[:, hl:hl + 1], None, ALU.mult, ALU.bypass, ["rt", "hm"], ["rz%d" % hl])
                        ts_("pool", az[hl][:], at_[:], hm[:, hl:hl + 1], None, ALU.mult, ALU.bypass, ["at", "hm"], ["az%d" % hl])
                    tt_("dve", e1[:], r_b[:], kmod[:], ALU.mult, ["r_b", "kmod", "e1", "rt"], ["e1"])
                    ts_("dve", sq[:], e1[:], pc(4), None, ALU.mult, ALU.bypass, ["e1", "rprm", "sq"], ["sq"])
                    mm(ps[0][:, :], bones[:], sq[:], ["bones", "sq"], [("ps", 0)])
                    tt_("dve", bv[:], ps[0][:, :], v_b[:], ALU.mult, [("ps", 0), "v_b"], ["bv"])
                    for q in range(4):
                        csl_ = slice(q * 128, (q + 1) * 128)
                        tok = tb * 4 + q
                        k.op("pe", lambda e: e.transpose(out=pbf[1][:, 0:128], in_=bh[:, csl_], identity=c.ident[:]),
                             r=["bh", "ident"], w=[("ps", 1)])
                        k.op("pe", lambda e: e.transpose(out=pbf[1][:, 128:256], in_=kh[:, csl_], identity=c.ident[:]),
                             r=["kh", "ident"], w=[("ps", 1)])
                        for hl in range(2):
                            tt_("dve", bz[:, hl, :], pbf[1][:, 0:128], cmask[:, hl, :], ALU.mult, [("ps", 1), "cmask"], ["bz"])
                            tt_("dve", kz[:, hl, :], pbf[1][:, 128:256], cmask[:, hl, :], ALU.mult, [("ps", 1), "cmask"], ["kz"])
                        def head_seq(hl):
                            azc, rzc = az[hl][:, csl_], rz[hl][:, csl_]
                            azk, rzk = "az%d" % hl, "rz%d" % hl
                            bX, bY = 2 + 2 * hl, 3 + 2 * hl
                            kX, kY = ("ps", bX), ("ps", bY)
                            X, MN, cat, mrk = Xs[hl], MNs[hl], cats[hl], mrks[hl]
                            RpT, GT, MpT, FT = RGMF[hl]
                            xk, ck, mk = "X%d" % hl, "cat%d" % hl, "mrk%d" % hl
                            rk_, gk_, mpk, fk = ("RpT%d" % hl, "GT%d" % hl, "MpT%d" % hl, "FT%d" % hl)
                            tp = pbf[1][:, 256 + hl * 128:384 + hl * 128]
                            mm(ps[bX][:, 0:128], bt[:, csl_], azc, ["bt", azk], [kX])
                            mm(ps[bX][:, 128:256], azc, bt[:, csl_], ["bt", azk], [kX])
                            mm(ps[bX][:, 256:384], azc, kt[:, csl_], ["kt", azk], [kX])
                            k.op("pe", lambda e: e.transpose(out=tp, in_=azc, identity=c.ident[:]), r=[azk, "ident"], w=[("ps", 1)])
                            yield
                            tt_("dve", MN[0][:, 0:128], ps[bX][:, 0:128], mjt[:], ALU.mult, [kX, "mjt"], ["MN0_%d" % hl])
                            tt_("dve", MN[0][:, 128:256], ps[bX][:, 128:256], mtj[:], ALU.mult, [kX, "mtj"], ["MN0_%d" % hl])
                            tt_("dve", X[:, 128:256], ps[bX][:, 256:384], mtj[:], ALU.mult, [kX, "mtj"], [xk])
                            act(X[:, 0:128], tp, AF.Copy, [("ps", 1)], [xk])
                            yield
                            for i in range(7):
                                cur, nxt = MN[i % 2], MN[(i + 1) % 2]
                                ck_, nk = "MN%d_%d" % (i % 2, hl), "MN%d_%d" % ((i + 1) % 2, hl)
                                mm(ps[bY][:, 0:256], cur[:, 0:128], X[:], [ck_, xk], [kY])
                                if i < 6:
                                    mm(ps[bY][:, 256:384], cur[:, 128:256], cur[:, 0:128], [ck_], [kY])
                                    mm(ps[bY][:, 384:512], cur[:, 0:128], cur[:, 128:256], [ck_], [kY])
                                yield
                                if i < 6:
                                    act(nxt[:], ps[bY][:, 256:512], AF.Copy, [kY], [nk, kY])
                                tt_("dve", X[:], X[:], ps[bY][:, 0:256], ALU.add, [xk, kY], [xk, kY])
                                yield
                            mm(ps[bY][:, 0:128], bt[:, csl_], rzc, ["bt", rzk], [kY])
                            mm(ps[bY][:, 128:256], kt[:, csl_], rzc, ["kt", rzk], [kY])
                            yield
                            tt_("dve", cat[:, 0:128], ps[bY][:, 0:128], c.triU_f[:], ALU.mult, [kY, "triU_f"], [ck])
                            tt_("dve", mrk[:], ps[bY][:, 128:256], c.triU_f[:], ALU.mult, [kY, "triU_f"], [mk])
                            k.op("pool", lambda e: e.tensor_copy(out=cat[:, 128:256], in_=bz[:, hl, :]), r=["bz"], w=[ck])
                            yield
                            mm(ps[bX][:, 0:256], X[:, 0:128], cat[:], [xk, ck], [kX])
                            mm(ps[bX][:, 256:512], X[:, 128:256], cat[:], [xk, ck], [kX])
                            yield
                            tt_("dve", RpT[:], ps[bX][:, 0:128], rzc, ALU.add, [kX, rzk], [rk_])
                            k.op("dve", lambda e: e.scalar_tensor_tensor(out=GT[:], in0=dsel[:, hl, :], scalar=pcl[:, q:q + 1],
                                                                         in1=ps[bX][:, 128:256], op0=ALU.mult, op1=ALU.add),
                                 r=["dsel", "pcl", kX], w=[gk_])
                            tt_("dve", MpT[:], ps[bX][:, 256:384], mrk[:], ALU.add, [kX, mk], [mpk])
                            tt_("dve", FT[:], ps[bX][:, 384:512], kz[:, hl, :], ALU.add, [kX, "kz"], [fk])
                            yield
                            vh = vtok[:, q, hl * 64:(hl + 1) * 64]
                            mm(ps[7][:, hl * 64:(hl + 1) * 64], RpT[:], Hb[:], [rk_, "Hb"], [("ps", 7)], start=True, stop=False)
                            mm(ps[7][:, hl * 64:(hl + 1) * 64], MpT[:], vh, [mpk, "vtok"], [("ps", 7)], start=False, stop=True)
                            mm(ps[0][:, 0:64], GT[:], Hb[:], [gk_, "Hb"], [("ps", 0)], start=(hl == 0), stop=False)
                            mm(ps[0][:, 0:64], FT[:], vh, [fk, "vtok"], [("ps", 0)], start=False, stop=(hl == 1))
                            yield

                        gens = [head_seq(0), head_seq(1)]
                        alive = [True, True]
                        while any(alive):
                            for gi in range(2):
                                if alive[gi]:
                                    try:
                                        next(gens[gi])
                                    except StopIteration:
                                        alive[gi] = False
                        act(Hb[:], ps[0][:, 0:64], AF.Copy, [("ps", 0)], ["Hb"])
                        y3 = ps[7][:, 0:128].rearrange("p (a b) -> p a b", a=2)
                        k.op("dve", lambda e: e.tensor_reduce(out=gst[:, 0:2], in_=y3, axis=AX.X, op=ALU.add), r=[("ps", 7)], w=["gst"])
                        act(ysq[:], ps[7][:, 0:128], AF.Square, [("ps", 7)], ["ysq", ("ps", 7)])
                        k.op("dve", lambda e: e.tensor_reduce(out=gst[:, 2:4], in_=ysq[:].rearrange("p (a b) -> p a b", a=2),
                                                              axis=AX.X, op=ALU.add), r=["ysq"], w=["gst"])
                        ts_("dve", gst[:, 0:4], gst[:, 0:4], 1.0 / 64, None, ALU.mult, ALU.bypass, ["gst"], ["gst"])
                        tt_("dve", gst[:, 4:6], gst[:, 0:2], gst[:, 0:2], ALU.mult, ["gst"], ["gst"])
                        tt_("dve", gst[:, 4:6], gst[:, 2:4], gst[:, 4:6], ALU.subtract, ["gst"], ["gst"])
                        ts_("dve", gst[:, 4:6], gst[:, 4:6], 64e-5, None, ALU.add, ALU.bypass, ["gst"], ["gst"])
                        act(gst[:, 4:6], gst[:, 4:6], AF.Sqrt, ["gst"], ["gst"])
                        k.op("dve", lambda e: e.reciprocal(out=gst[:, 6:8], in_=gst[:, 4:6]), r=["gst"], w=["gst"])
                        for hl in range(2):
                            ts_("dve", yn[:, hl * 64:(hl + 1) * 64], ps[7][:, hl * 64:(hl + 1) * 64], gst[:, hl:hl + 1],
                                gst[:, 6 + hl:7 + hl], ALU.subtract, ALU.mult, [("ps", 7), "gst"], ["yn"])
                        k.op("pe", lambda e: e.transpose(out=pbf[1][:, 512:640], in_=yn[:], identity=c.ident[:]),
                             r=["yn", "ident"], w=[("ps", 1)])
                        ts_("dve", z1[:], pbf[1][:, 512:640], pc(5), pc(6), ALU.mult, ALU.add, [("ps", 1), "rprm"], ["z1"])
                        tt_("dve", z1[:], z1[:], bv[:, csl_], ALU.add, ["z1", "bv"], ["z1"])
                        tt_("dve", ygT[:], z1[:], g_b[:, csl_], ALU.mult, ["z1", "g_b"], ["ygT"])
                        for dh in range(2):
                            mm(ps[6][:, :], ygT[:], wo[:, dh * 512:(dh + 1) * 512], ["ygT", "wo_hp"], [("ps", 6)])
                            tt_("dve", h[:, tok, dh * 512:(dh + 1) * 512], h[:, tok, dh * 512:(dh + 1) * 512], ps[6][:, :], ALU.add,
                                [("h", tok), ("ps", 6)], [("h", tok)])
                k.barrier()
        k.barrier()


def build(nc, stages="all", dbg=None):
    es = ExitStack()
    with es:
        dbgaps = {}
        for name, shape in (dbg or {}).items():
            dbgaps[name] = nc.dram_tensor("dbg_" + name, list(shape), BF16 if name in ("uT", "yT", "vk", "zz", "sT", "ETb", "EVt", "wb", "cm") else F32,
                                          kind="ExternalOutput").ap()
        c = setup_common(nc, es, dbgaps)
        peer_inputs(c)
        st = set(stages.split(","))
        if "all" in st:
            st = {"even", "peer0", "rwkv", "peer1"}
        if "even" in st:
            even_mixer(c, 0)
        if "peer0" in st:
            peer(c, 0)
        if "rwkv" in st:
            rwkv_mixer(c, 1)
        if "peer1" in st:
            peer(c, 1)
        if "h" in dbgaps:
            for tt in range(NT):
                c.k.dma("sp", dbgaps["h"][tsl(tt), :], c.h[:, tt, :], r=[("h", tt)], w=["dbg_h"])
        final_norm_store(c)
        print("instr counts", c.k.nins, "sems", c.k.nsem)
    return nc


PARAMS = ["norm_mix_g", "norm_ffn_g", "final_g", "e_w_in", "e_w_out", "s5_a_re", "s5_a_im", "s5_log_dt", "s5_b_re", "s5_b_im",
          "s5_c_re", "s5_c_im", "s5_d", "s5_w_glu", "gla_w_g2", "gla_b_g2", "gla_norm_g",
          "peer_w_q", "peer_sub_keys", "peer_u", "peer_v",
          "o_mu", "o_w_r", "o_w_k", "o_w_v", "o_w0", "o_w_w1", "o_w_w2", "o_a0", "o_w_a1", "o_w_a2", "o_w_g1", "o_w_g2",
          "o_k_k", "o_k_a", "o_r_k", "o_lnx_g", "o_lnx_b", "o_w_o"]


def core_inputs(inputs, b):
    m = {"x": np.ascontiguousarray(inputs["x"][b])}
    for n in PARAMS:
        a = np.asarray(inputs[n])
        if n in ("norm_mix_g", "norm_ffn_g", "peer_w_q", "peer_sub_keys", "peer_u", "peer_v"):
            pass
        elif n == "final_g":
            a = a.reshape(1, D)
        elif n in ("s5_d", "gla_b_g2", "gla_norm_g", "o_w0", "o_a0", "o_k_k", "o_k_a", "o_r_k", "o_lnx_g", "o_lnx_b"):
            a = a.reshape(1, -1)
        else:
            a = a[0]
        m[n] = np.ascontiguousarray(a)
    return m


def kernel(**inputs):
    n = 8
    nc = bass.Bass("TRN2", target_bir_lowering=False)
    build(nc)
    in_maps = [core_inputs(inputs, b) for b in range(n)]
    res = run_bass_kernel_spmd(nc, in_maps, core_ids=list(range(n)))
    return np.stack([r["out"] for r in res.results], axis=0)
```

```python
import numpy as np
from contextlib import ExitStack
import concourse.bass as bass
import concourse.mybir as mybir
from concourse.bass_utils import run_bass_kernel_spmd

F32 = mybir.dt.float32
BF16 = mybir.dt.bfloat16
U32 = mybir.dt.uint32
AF = mybir.ActivationFunctionType
ALU = mybir.AluOpType
AX = mybir.AxisListType

L = 2048
D = 1024
NT = L // 128
EPS = 1e-6
ENGS = ("pe", "dve", "act", "pool", "sp")


class MK:
    ROT = 1 << 30

    def __init__(self, nc, es):
        self.nc = nc
        self.es = es
        self.eng = dict(pe=nc.tensor, dve=nc.vector, act=nc.scalar, pool=nc.gpsimd, sp=nc.sync)
        self.esem = {}
        self.prev_ep = {}
        self.ecnt = {e: 0 for e in ENGS}
        self.seen = {e: {} for e in ENGS}
        self.lastw = {}
        self.readers = {}
        self.dsem = {}
        self.free_dsem = []
        self.ndsem = 0
        self.nsem = 0
        self.nins = {e: 0 for e in ENGS}
        self.spare = [self._newsem("spare%d" % i) for i in range(6)]
        for e in ENGS:
            self._rot(e)

    def _newsem(self, name):
        self.nsem += 1
        return self.es.enter_context(self.nc.semaphore(name))

    def _rot(self, e):
        if e in self.esem and self.esem[e][2] > 0:
            self.prev_ep[e] = self.esem[e][:3]
        ep = self.esem[e][3] + 1 if e in self.esem else 0
        name = "s_%s_%d" % (e, ep)
        sem = self.spare.pop() if (ep > 0 and self.spare) else self._newsem(name)
        self.esem[e] = (name, sem, 0, ep)

    def _deps(self, r, w):
        d = {}

        def add(p):
            name, sem, c = p
            if name not in d or d[name][1] < c:
                d[name] = (sem, c)

        for k in r:
            if k in self.lastw:
                add(self.lastw[k])
        for k in w:
            if k in self.lastw:
                add(self.lastw[k])
            for n, (s, c) in self.readers.get(k, {}).items():
                add((n, s, c))
        return d

    def _wait(self, e, d):
        E = self.eng[e]
        seen = self.seen[e]
        for name, (sem, c) in d.items():
            if seen.get(name, 0) >= c:
                continue
            E.wait_ge(sem, c)
            seen[name] = c

    def _record(self, p, r, w):
        name, sem, c = p
        for k in w:
            self.lastw[k] = p
            self.readers[k] = {}
        for k in r:
            rd = self.readers.setdefault(k, {})
            if name not in rd or rd[name][1] < c:
                rd[name] = (sem, c)

    def op(self, e, fn, r=(), w=()):
        w = list(w) + [x for x in r if isinstance(x, tuple) and x and x[0] == "ps" and x not in w]
        d = self._deps(r, w)
        if e == "pe":
            d = {n: v for n, v in d.items() if not n.startswith("s_pe_")}
        self._wait(e, d)
        name, sem, cnt, ep = self.esem[e]
        if cnt >= self.ROT:
            self._rot(e)
            name, sem, cnt, ep = self.esem[e]
        ins = fn(self.eng[e])
        cnt += 1
        self.esem[e] = (name, sem, cnt, ep)
        ins.then_inc(sem, 1)
        self.nins[e] += 1
        self._record((name, sem, cnt), r, w)
        return ins

    def dma(self, e, out, in_, r=(), w=(), **kw):
        d = self._deps(r, w)
        self._wait(e, d)
        key = w[0] if len(w) else r[0]
        skey = ("dma", key)
        if skey not in self.dsem:
            if self.free_dsem:
                self.dsem[skey] = self.free_dsem.pop()
            else:
                name = "d%d" % self.ndsem
                self.ndsem += 1
                self.dsem[skey] = [name, self._newsem(name), 0]
        ent = self.dsem[skey]
        ins = self.eng[e].dma_start(out=out, in_=in_, **kw)
        ent[2] += 16
        ins.then_inc(ent[1], 16)
        self.nins[e] += 1
        self._record((ent[0], ent[1], ent[2]), r, w)
        return ins

    def barrier(self):
        d = {}
        for e in ENGS:
            name, sem, cnt, ep = self.esem[e]
            if cnt > 0:
                d[name] = (sem, cnt)
            elif e in self.prev_ep:
                pn, psem, pcnt = self.prev_ep[e]
                d[pn] = (psem, pcnt)
        for ent in self.dsem.values():
            if ent[2] > 0:
                d[ent[0]] = (ent[1], ent[2])
        for e in ENGS:
            dd = d
            if e == "pe":
                dd = {n: v for n, v in d.items() if not n.startswith("s_pe_")}
            self._wait(e, dd)
        self.free_dsem.extend(self.dsem.values())
        self.dsem = {}


def tsl(tt):
    return slice(tt * 128, (tt + 1) * 128)


class Ctx:
    pass


SKIP = set()
CUT = [99.0]
GELU_FN = [AF.Gelu]
NTT = [NT]
NOINJ = [False]
INJV = [0]


def setup_common(nc, es, dbg):
    c = Ctx()
    c.nc = nc
    c.es = es
    c.dbg = dbg
    k = c.k = MK(nc, es)
    E = es.enter_context

    def dram_in(name, shape):
        return nc.dram_tensor(name, list(shape), F32, kind="ExternalInput").ap()

    c.din = dram_in
    c.x_d = dram_in("x", [L, D])
    c.out_d = nc.dram_tensor("out", [L, D], F32, kind="ExternalOutput").ap()
    c.norm_mix_g = dram_in("norm_mix_g", [2, D])
    c.norm_ffn_g = dram_in("norm_ffn_g", [2, D])
    c.final_g = dram_in("final_g", [1, D])

    used = {}

    def sb(name, shape, dt, st=None):
        n = used.get(name, 0)
        used[name] = n + 1
        nm = name if n == 0 else "%s_%d" % (name, n)
        return (st or es).enter_context(nc.sbuf_tensor(nm, list(shape), dt))

    c.sb = sb
    c.h = sb("h", [128, NT, D], F32)
    c.gbc = sb("gbc", [128, D], F32)
    c.ident = sb("ident", [128, 128], BF16)
    c.identf = sb("identf", [128, 128], F32)
    c.ones_f = sb("ones_f", [128, 128], F32)
    c.ones_b = sb("ones_b", [128, 128], BF16)
    c.onecol = sb("onecol", [128, 1], F32)
    c.ss = sb("ss", [128, NT], F32)
    c.rstd = sb("rstd", [128, NT], F32)
    c.junk = sb("junk", [128, D], BF16)
    c.xs = [sb("xs%d" % i, [128, D], BF16) for i in range(2)]
    c.ps = [E(nc.psum_tensor("ps%d" % i, [128, 512], F32)) for i in range(8)]
    c.pbf = [c.ps[i][:].bitcast(BF16) for i in range(8)]
    c.triU_f = sb("triU_f", [128, 128], F32)
    c.triU_b = sb("triU_b", [128, 128], BF16)

    k.op("pool", lambda e: e.memset(c.identf[:], 0.0), w=["identf"])
    k.op("pool", lambda e: e.affine_select(out=c.identf[:], in_=c.identf[:], pattern=[[-1, 128]],
                                            compare_op=ALU.not_equal, fill=1.0, base=0, channel_multiplier=1),
         r=["identf"], w=["identf"])
    k.op("dve", lambda e: e.tensor_copy(out=c.ident[:], in_=c.identf[:]), r=["identf"], w=["ident"])
    k.op("pool", lambda e: e.memset(c.ones_f[:], 1.0), w=["ones_f"])
    k.op("pool", lambda e: e.memset(c.ones_b[:], 1.0), w=["ones_b"])
    k.op("pool", lambda e: e.memset(c.onecol[:], 1.0), w=["onecol"])
    k.op("pool", lambda e: e.affine_select(out=c.triU_f[:], in_=c.ones_f[:], pattern=[[1, 128]],
                                            compare_op=ALU.is_ge, fill=0.0, base=0, channel_multiplier=-1),
         r=["ones_f"], w=["triU_f"])
    k.op("dve", lambda e: e.tensor_copy(out=c.triU_b[:], in_=c.triU_f[:]), r=["triU_f"], w=["triU_b"])
    for tt in range(NT):
        k.dma("sp", c.h[:, tt, :], c.x_d[tsl(tt), :], w=[("h", tt)])
    return c


def rms_stats(c):
    k = c.k
    for tt in range(NT):
        k.op("act", lambda e: e.activation(out=c.junk[:], in_=c.h[:, tt, :], func=AF.Square,
                                           accum_out=c.ss[:, tt:tt + 1]),
             r=[("h", tt)], w=["junk", ("ss", tt)])
    allss = [("ss", tt) for tt in range(NT)]
    k.op("dve", lambda e: e.tensor_scalar(out=c.rstd[:], in0=c.ss[:], scalar1=1.0 / D, scalar2=EPS,
                                          op0=ALU.mult, op1=ALU.add), r=allss, w=["rstd"])
    k.op("act", lambda e: e.activation(out=c.rstd[:], in_=c.rstd[:], func=AF.Sqrt), r=["rstd"], w=["rstd"])
    k.op("dve", lambda e: e.reciprocal(out=c.rstd[:], in_=c.rstd[:]), r=["rstd"], w=["rstd"])


def rmsnorm_T(c, g_ap, xT, tag, off=0):
    k = c.k
    k.dma("sp", c.gbc[:], g_ap.partition_broadcast(128), w=["gbc"])
    rms_stats(c)
    for tt in range(NT):
        xb = c.xs[tt % 2]
        xk = ("xs", tt % 2)
        k.op("dve", lambda e: e.scalar_tensor_tensor(out=xb[:], in0=c.h[:, tt, :], scalar=c.rstd[:, tt:tt + 1],
                                                     in1=c.gbc[:], op0=ALU.mult, op1=ALU.mult),
             r=[("h", tt), "rstd", "gbc"], w=[xk])
        b = 6 + tt % 2
        pst = c.pbf[b]
        pk = ("ps", b)
        for ch in range(8):
            k.op("pe", lambda e: e.transpose(out=pst[:, ch * 128:(ch + 1) * 128], in_=xb[:, ch * 128:(ch + 1) * 128],
                                             identity=c.ident[:]),
                 r=[xk, "ident"], w=[pk])
        k.op("act", lambda e: e.activation(out=xT[:, :, off + tt * 128:off + (tt + 1) * 128],
                                           in_=pst.rearrange("p (c t) -> p c t", c=8), func=AF.Copy),
             r=[pk], w=[(tag, tt)])


def final_norm_store(c):
    k = c.k
    k.dma("sp", c.gbc[:], c.final_g[0, :].partition_broadcast(128), w=["gbc"])
    rms_stats(c)
    for tt in range(NT):
        k.op("dve", lambda e: e.scalar_tensor_tensor(out=c.h[:, tt, :], in0=c.h[:, tt, :], scalar=c.rstd[:, tt:tt + 1],
                                                     in1=c.gbc[:], op0=ALU.mult, op1=ALU.mult),
             r=[("h", tt), "rstd", "gbc"], w=[("h", tt)])
        k.dma("sp", c.out_d[tsl(tt), :], c.h[:, tt, :], r=[("h", tt)], w=[("out", tt)])
    k._wait("sp", k._deps([("out", tt) for tt in range(NT)], []))


def cmul(k, eng_a, eng_b, o_re, o_im, a_re, a_im, b_re, b_im, t, rk, wk, tk):
    k.op(eng_a, lambda e: e.tensor_tensor(out=t[0], in0=a_re, in1=b_re, op=ALU.mult), r=rk, w=[tk + "0"])
    k.op(eng_a, lambda e: e.tensor_tensor(out=t[1], in0=a_im, in1=b_im, op=ALU.mult), r=rk, w=[tk + "1"])
    k.op(eng_b, lambda e: e.tensor_tensor(out=o_re, in0=t[0], in1=t[1], op=ALU.subtract),
         r=[tk + "0", tk + "1"], w=[wk + "_re"])
    k.op(eng_a, lambda e: e.tensor_tensor(out=t[0], in0=a_re, in1=b_im, op=ALU.mult), r=rk, w=[tk + "0"])
    k.op(eng_a, lambda e: e.tensor_tensor(out=t[1], in0=a_im, in1=b_re, op=ALU.mult), r=rk, w=[tk + "1"])
    k.op(eng_b, lambda e: e.tensor_tensor(out=o_im, in0=t[0], in1=t[1], op=ALU.add),
         r=[tk + "0", tk + "1"], w=[wk + "_im"])


def even_mixer(c, li):
    nc, k, sb, ps, pbf, h = c.nc, c.k, c.sb, c.ps, c.pbf, c.h
    din = c.din
    w_in_d = din("e_w_in", [D, 2064])
    w_out_d = din("e_w_out", [D, D])
    a_re_d = din("s5_a_re", [32, 64])
    a_im_d = din("s5_a_im", [32, 64])
    ldt_d = din("s5_log_dt", [32, 64])
    b_re_d = din("s5_b_re", [32, 64, 16])
    b_im_d = din("s5_b_im", [32, 64, 16])
    c_re_d = din("s5_c_re", [32, 16, 64])
    c_im_d = din("s5_c_im", [32, 16, 64])
    d_d = din("s5_d", [1, 512])
    wglu_d = din("s5_w_glu", [512, 512])
    wg2_d = din("gla_w_g2", [16, 256])
    bg2_d = din("gla_b_g2", [1, 256])
    gng_d = din("gla_norm_g", [1, 512])

    with ExitStack() as ph:
        xy = sb("xy", [128, 8, L], BF16, ph)
        uT = sb("uT", [128, 4, L], BF16, ph)
        xT = xy
        yT = xy
        with ExitStack() as pg:
            qkT = sb("qkT", [128, 4, L], BF16, pg)
            rT = sb("rT", [128, 4, L], BF16, pg)
            glowT = sb("glowT", [16, L], BF16, pg)
            vk = sb("vk", [128, NT, 768], BF16, pg)
            with ExitStack() as p1:
                wi = sb("wi", [128, 8, 1040], BF16, p1)
                rmsnorm_T(c, c.norm_mix_g[li, :], xT, "xT")
                allx = [("xT", tt) for tt in range(NT)]
                allw = [("wi", ch) for ch in range(8)]
                n = 0
                for piece in range(2):
                    cb = piece * 1024
                    ncol = 1024 if piece == 0 else 1040
                    for ch in range(8):
                        k.dma("pool", wi[:, ch, 0:ncol], w_in_d[ch * 128:(ch + 1) * 128, cb:cb + ncol], w=[("wi", ch)])
                    if piece == 0:
                        chunks = ([(i * 128, 128, uT, i, AF.Copy) for i in range(4)] +
                                  [(512 + i * 128, 128, qkT, i, AF.Copy) for i in range(4)])
                    else:
                        chunks = ([(1552 + i * 128, 128, rT, i, AF.Silu) for i in range(4)] +
                                  [(1536, 16, None, 0, AF.Copy)])
                    for (c0, m, dst, di, fn) in chunks:
                        for tb in range(4):
                            b = n % 4
                            n += 1
                            for ch in range(8):
                                k.op("pe", lambda e: e.matmul(ps[b][0:m, :], lhsT=wi[:, ch, c0 - cb:c0 - cb + m],
                                                              rhs=xT[:, ch, tb * 512:(tb + 1) * 512],
                                                              start=(ch == 0), stop=(ch == 7)),
                                     r=allx + allw, w=[("ps", b)])
                            if dst is None:
                                k.op("act", lambda e: e.activation(out=glowT[:, tb * 512:(tb + 1) * 512], in_=ps[b][0:16, :],
                                                                   func=AF.Copy), r=[("ps", b)], w=["glowT"])
                            else:
                                k.op("act", lambda e: e.activation(out=dst[:, di, tb * 512:(tb + 1) * 512], in_=ps[b][:, :],
                                                                   func=fn), r=[("ps", b)], w=[(dst.name, di)])
                    for tt in range(NT):
                        b0 = 4 + tt % 4
                        if piece == 0:
                            for ch in range(8):
                                k.op("pe", lambda e: e.matmul(ps[b0][:, 0:256], lhsT=xT[:, ch, tsl(tt)], rhs=wi[:, ch, 768:1024],
                                                              start=(ch == 0), stop=(ch == 7)), r=allx + allw, w=[("ps", b0)])
                            k.op("dve", lambda e: e.tensor_copy(out=vk[:, tt, 512:768], in_=ps[b0][:, 0:256]), r=[("ps", b0)],
                                 w=[("vk", tt)])
                        else:
                            for ch in range(8):
                                k.op("pe", lambda e: e.matmul(ps[b0][:, :], lhsT=xT[:, ch, tsl(tt)], rhs=wi[:, ch, 0:512],
                                                              start=(ch == 0), stop=(ch == 7)), r=allx + allw, w=[("ps", b0)])
                            k.op("dve", lambda e: e.tensor_copy(out=vk[:, tt, 0:512], in_=ps[b0][:, :]), r=[("ps", b0)],
                                 w=[("vk", tt)])
                k.barrier()
            if "uT" in c.dbg:
                for i in range(4):
                    k.dma("sp", c.dbg["uT"][i], uT[:, i, :], r=[("uT", i)], w=["dbg_uT"])
            if "vk" in c.dbg:
                k.dma("sp", c.dbg["vk"], vk[:, 3, :], r=[("vk", 3)], w=["dbg_vk"])
            if 'gla' not in SKIP:
                gla(c, pg, qkT, rT, glowT, vk, yT, wg2_d, bg2_d, gng_d)
            k.barrier()
        if 's5' not in SKIP:
            s5(c, ph, uT, yT, a_re_d, a_im_d, ldt_d, b_re_d, b_im_d, c_re_d, c_im_d, d_d, wglu_d)
        k.barrier()
        if "yT" in c.dbg:
            for i in range(8):
                k.dma("sp", c.dbg["yT"][i], yT[:, i, :], r=[("yT", i)], w=["dbg_yT"])
        with ExitStack() as p3:
            wo = sb("wo", [128, 8, D], BF16, p3)
            for ch in range(8):
                k.dma("pool", wo[:, ch, :], w_out_d[ch * 128:(ch + 1) * 128, :], w=[("wo", ch)])
            ally = [("yT", i) for i in range(8)]
            allw = [("wo", ch) for ch in range(8)]
            for tt in range(NT):
                for hf in range(2):
                    b = (tt * 2 + hf) % 4
                    for ch in range(8):
                        k.op("pe", lambda e: e.matmul(ps[b][:, :], lhsT=yT[:, ch, tsl(tt)],
                                                      rhs=wo[:, ch, hf * 512:(hf + 1) * 512],
                                                      start=(ch == 0), stop=(ch == 7)), r=ally + allw, w=[("ps", b)])
                    k.op("dve", lambda e: e.tensor_tensor(out=h[:, tt, hf * 512:(hf + 1) * 512],
                                                          in0=h[:, tt, hf * 512:(hf + 1) * 512], in1=ps[b][:, :],
                                                          op=ALU.add), r=[("ps", b), ("h", tt)], w=[("h", tt)])
            k.barrier()


def gla(c, ph, qkT, rT, glowT, vk, yT, wg2_d, bg2_d, gng_d):
    nc, k, sb, ps, pbf = c.nc, c.k, c.sb, c.ps, c.pbf
    with ExitStack() as p2:
        wg2 = sb("wg2", [16, 256], BF16, p2)
        bg2 = sb("bg2", [1, 256], BF16, p2)
        gng = sb("gng", [128, 4], F32, p2)
        triUs = sb("triUs", [128, 128], F32, p2)
        triRs = sb("triRs", [128, 128], F32, p2)
        lp = sb("lp", [128, 256], F32, p2)
        eend = sb("eend", [128, 256], F32, p2)
        ebT = sb("ebT", [128, 2, 128], F32, p2)
        enbT = sb("enbT", [128, 2, 128], F32, p2)
        kend = sb("kend", [128, 256], BF16, p2)
        qd = sb("qd", [128, 2, 128], BF16, p2)
        kd = sb("kd", [128, 2, 128], BF16, p2)
        qdz = sb("qdz", [128, 4, 128], BF16, p2)
        hmask = sb("hmask", [128, 2], F32, p2)
        attT = sb("attT", [128, 4, 128], BF16, p2)
        S = sb("S", [128, 2, 128], F32, p2)
        Sb = sb("Sb", [128, 2, 128], BF16, p2)
        ssq = sb("ssq", [128, 4], F32, p2)
        rso = sb("rso", [128, 4], F32, p2)
        on = sb("on", [128, 4, 128], BF16, p2)
        k.dma("pool", wg2[:], wg2_d[:, :], w=["wg2"])
        k.dma("pool", bg2[:], bg2_d[:, :], w=["bg2"])
        k.dma("sp", gng[:], gng_d[0, :].rearrange("(h v) -> v h", v=128), w=["gng"], allow_slow_non_contiguous=True)
        k.op("dve", lambda e: e.tensor_scalar(out=triUs[:], in0=c.triU_f[:], scalar1=-1.0 / 16, scalar2=None,
                                              op0=ALU.mult), r=["triU_f"], w=["triUs"])
        k.op("dve", lambda e: e.tensor_scalar(out=triRs[:], in0=c.triU_f[:], scalar1=1.0 / 16, scalar2=-1.0 / 16,
                                              op0=ALU.mult, op1=ALU.add), r=["triU_f"], w=["triRs"])
        k.op("pool", lambda e: e.memset(hmask[:], 0.0), w=["hmask"])
        k.op("pool", lambda e: e.memset(hmask[0:64, 0:1], 1.0), r=["hmask"], w=["hmask"])
        k.op("pool", lambda e: e.memset(hmask[64:128, 1:2], 1.0), r=["hmask"], w=["hmask"])
        k.op("pool", lambda e: e.memset(S[:], 0.0), w=["S"])
        k.op("pool", lambda e: e.memset(Sb[:], 0.0), w=["Sb"])
        for tt in range(NT):
            k.op("pe", lambda e: e.matmul(ps[0][:, 0:256], lhsT=glowT[:, tsl(tt)], rhs=wg2[:, :], start=True, stop=False),
                 r=["glowT", "wg2"], w=[("ps", 0)])
            k.op("pe", lambda e: e.matmul(ps[0][:, 0:256], lhsT=c.ones_b[0:1, :], rhs=bg2[:, :], start=False, stop=True),
                 r=["ones_b", "bg2"], w=[("ps", 0)])
            if CUT[0] <= 1:
                break
            k.op("act", lambda e: e.activation(out=lp[:], in_=ps[0][:, 0:256], func=AF.Exp, scale=-1.0),
                 r=[("ps", 0)], w=["lp"])
            k.op("act", lambda e: e.activation(out=lp[:], in_=lp[:], func=AF.Ln, bias=c.onecol[:], scale=1.0),
                 r=["lp", "onecol"], w=["lp"])
            if CUT[0] <= 2:
                break
            k.op("pe", lambda e: e.matmul(ps[1][:, 0:256], lhsT=triRs[:], rhs=lp[:], start=True, stop=True),
                 r=["triRs", "lp"], w=[("ps", 1)])
            for hf in range(2):
                k.op("pe", lambda e: e.matmul(ps[1][:, 256 + hf * 128:256 + (hf + 1) * 128],
                                              lhsT=lp[:, hf * 128:(hf + 1) * 128], rhs=triUs[:], start=True, stop=True),
                     r=["triUs", "lp"], w=[("ps", 1)])
            k.op("act", lambda e: e.activation(out=eend[:], in_=ps[1][:, 0:256], func=AF.Exp), r=[("ps", 1)], w=["eend"])
            k.op("act", lambda e: e.activation(out=ebT[:].rearrange("p a b -> p (a b)"), in_=ps[1][:, 256:512], func=AF.Exp),
                 r=[("ps", 1)], w=["ebT"])
            k.op("act", lambda e: e.activation(out=enbT[:].rearrange("p a b -> p (a b)"), in_=ps[1][:, 256:512], func=AF.Exp,
                                               scale=-1.0), r=[("ps", 1)], w=["enbT"])
            if CUT[0] <= 3:
                break
            k.op("dve", lambda e: e.tensor_tensor(out=kend[:], in0=vk[:, tt, 512:768], in1=eend[:], op=ALU.mult),
                 r=[("vk", tt), "eend"], w=["kend"])
            k.op("dve", lambda e: e.scalar_tensor_tensor(out=qd[:], in0=qkT[:, 0:2, tsl(tt)], scalar=0.125, in1=ebT[:],
                                                         op0=ALU.mult, op1=ALU.mult),
                 r=[("qkT", 0), ("qkT", 1), "ebT"], w=["qd"])
            k.op("dve", lambda e: e.tensor_tensor(out=kd[:], in0=qkT[:, 2:4, tsl(tt)], in1=enbT[:], op=ALU.mult),
                 r=[("qkT", 2), ("qkT", 3), "enbT"], w=["kd"])
            if CUT[0] <= 5:
                break
            for hd in range(4):
                pr = hd // 2
                k.op("dve", lambda e: e.tensor_scalar(out=qdz[:, hd, :], in0=qd[:, pr, :], scalar1=hmask[:, hd % 2:hd % 2 + 1],
                                                      scalar2=None, op0=ALU.mult), r=["qd", "hmask"], w=["qdz"])
            for hd in range(4):
                pr = hd // 2
                k.op("pe", lambda e: e.matmul(ps[2][:, hd * 128:(hd + 1) * 128], lhsT=kd[:, pr, :],
                                              rhs=qdz[:, hd, :], start=True, stop=True),
                     r=["kd", "qdz"], w=[("ps", 2)])
            if CUT[0] <= 5.3:
                break
            k.op("dve", lambda e: e.tensor_tensor(out=attT[:], in0=ps[2][:, :].rearrange("p (a b) -> p a b", a=4),
                                                  in1=c.triU_f[:].unsqueeze(1).to_broadcast([128, 4, 128]), op=ALU.mult),
                 r=[("ps", 2), "triU_f"], w=["attT"])
            if CUT[0] <= 5.6:
                break
            for hd in range(4):
                pr, p0 = hd // 2, (hd % 2) * 64
                k.op("pe", lambda e: e.matmul(ps[3][:, hd * 128:(hd + 1) * 128], lhsT=attT[:, hd, :],
                                              rhs=vk[:, tt, hd * 128:(hd + 1) * 128], start=True, stop=False),
                     r=["attT", ("vk", tt)], w=[("ps", 3)])
                k.op("pe", lambda e: e.matmul(ps[3][:, hd * 128:(hd + 1) * 128], lhsT=qdz[:, hd, :],
                                              rhs=Sb[:, pr, :], start=False, stop=True),
                     r=["qdz", "Sb"], w=[("ps", 3)])
            if CUT[0] <= 6:
                break
            for pr in range(2):
                k.op("pe", lambda e: e.matmul(ps[4][:, pr * 256:(pr + 1) * 256], lhsT=kend[:, pr * 128:(pr + 1) * 128],
                                              rhs=vk[:, tt, pr * 256:(pr + 1) * 256], start=True, stop=True),
                     r=["kend", ("vk", tt)], w=[("ps", 4)])
            for hd in range(4):
                pr, hf, p0 = hd // 2, hd % 2, (hd % 2) * 64
                k.op("dve", lambda e: e.scalar_tensor_tensor(
                    out=S[p0:p0 + 64, pr, :], in0=S[p0:p0 + 64, pr, :], scalar=ebT[p0:p0 + 64, pr, 127:128],
                    in1=ps[4][p0:p0 + 64, pr * 256 + hf * 128:pr * 256 + (hf + 1) * 128], op0=ALU.mult, op1=ALU.add),
                     r=["S", "ebT", ("ps", 4)], w=["S"])
            k.op("dve", lambda e: e.tensor_copy(out=Sb[:], in_=S[:]), r=["S"], w=["Sb"])
            if CUT[0] <= 7:
                break
            for hd in range(4):
                k.op("act", lambda e: e.activation(out=c.junk[:, 0:128], in_=ps[3][:, hd * 128:(hd + 1) * 128],
                                                   func=AF.Square, accum_out=ssq[:, hd:hd + 1]),
                     r=[("ps", 3)], w=["junk", "ssq"])
            k.op("dve", lambda e: e.tensor_scalar(out=rso[:], in0=ssq[:], scalar1=1.0 / 128, scalar2=EPS,
                                                  op0=ALU.mult, op1=ALU.add), r=["ssq"], w=["rso"])
            k.op("act", lambda e: e.activation(out=rso[:], in_=rso[:], func=AF.Sqrt), r=["rso"], w=["rso"])
            k.op("dve", lambda e: e.reciprocal(out=rso[:], in_=rso[:]), r=["rso"], w=["rso"])
            k.op("dve", lambda e: e.tensor_tensor(out=on[:], in0=ps[3][:, :].rearrange("p (a b) -> p a b", a=4),
                                                  in1=rso[:].unsqueeze(2).to_broadcast([128, 4, 128]), op=ALU.mult),
                 r=[("ps", 3), "rso"], w=["on"])
            for hd in range(4):
                k.op("pe", lambda e: e.transpose(out=pbf[5][:, hd * 128:(hd + 1) * 128], in_=on[:, hd, :],
                                                 identity=c.ident[:]), r=["on", "ident"], w=[("ps", 5)])
            for hd in range(4):
                k.op("dve", lambda e: e.scalar_tensor_tensor(
                    out=yT[:, 4 + hd, tsl(tt)], in0=pbf[5][:, hd * 128:(hd + 1) * 128], scalar=gng[:, hd:hd + 1],
                    in1=rT[:, hd, tsl(tt)], op0=ALU.mult, op1=ALU.mult),
                     r=[("ps", 5), "gng", ("rT", hd)], w=[("yT", 4 + hd)])


def s5(c, ph, uT, yT, a_re_d, a_im_d, ldt_d, b_re_d, b_im_d, c_re_d, c_im_d, d_d, wglu_d):
    nc, k, sb, ps, pbf = c.nc, c.k, c.sb, c.ps, c.pbf
    with ExitStack() as p2:
        wb = sb("wb", [128, 2, 4, 512], BF16, p2)
        cm = sb("cm", [128, 2, 16, 128], BF16, p2)
        with ExitStack() as pa:
            wbs = sb("wbs", [128, 2, 4, 512], F32, pa)
            cms = sb("cms", [128, 2, 16, 128], F32, pa)
            k.op("pool", lambda e: e.memset(wbs[:].rearrange("p a b c -> p (a b c)"), 0.0), w=["wbs"])
            k.op("pool", lambda e: e.memset(cms[:].rearrange("p a b c -> p (a b c)"), 0.0), w=["cms"])
            for ri, bd in enumerate((b_re_d, b_im_d)):
                for g in range(32):
                    kc, g8 = g // 8, g % 8
                    k.dma("sp", wbs[g8 * 16:(g8 + 1) * 16, ri, kc, g8 * 64:(g8 + 1) * 64],
                          bd[g].rearrange("p c -> c p"), r=[], w=["wbs"], allow_slow_non_contiguous=True)
            for ri, cd in enumerate((c_re_d, c_im_d)):
                for g in range(32):
                    ct, gl = g // 2, g % 2
                    g8 = g % 8
                    k.dma("sp", cms[gl * 64:(gl + 1) * 64, ri, ct, g8 * 16:(g8 + 1) * 16],
                          cd[g].rearrange("c p -> p c"), r=[], w=["cms"], allow_slow_non_contiguous=True)
            k.op("act", lambda e: e.activation(out=wb[:].rearrange("p a b c -> p (a b c)"),
                                               in_=wbs[:].rearrange("p a b c -> p (a b c)"), func=AF.Copy),
                 r=["wbs"], w=["wb"])
            k.op("act", lambda e: e.activation(out=cm[:, 0].rearrange("p b c -> p (b c)"),
                                               in_=cms[:, 0].rearrange("p b c -> p (b c)"), func=AF.Copy),
                 r=["cms"], w=["cm"])
            k.op("act", lambda e: e.activation(out=cm[:, 1].rearrange("p b c -> p (b c)"),
                                               in_=cms[:, 1].rearrange("p b c -> p (b c)"), func=AF.Copy, scale=-1.0),
                 r=["cms"], w=["cm"])
            k.barrier()
        if CUT[0] <= 10:
            return
        dcol = sb("dcol", [128, 4], F32, p2)
        k.dma("sp", dcol[:], d_d[0, :].rearrange("(c p) -> p c", p=128), w=["dcol"], allow_slow_non_contiguous=True)
        wglu = sb("wglu", [128, 4, 512], BF16, p2)
        for ch in range(4):
            k.dma("pool", wglu[:, ch, :], wglu_d[ch * 128:(ch + 1) * 128, :], w=["wglu"])
        ETb = sb("ETb", [128, 2, 16, 128], BF16, p2)
        EVt = sb("EVt", [128, 2, 2048], BF16, p2)
        a128 = sb("a128", [128, 2, 16], F32, p2)
        with ExitStack() as pb:
            prm = sb("prm", [16, 3, 128], F32, pb)
            for i, pd in enumerate((a_re_d, a_im_d, ldt_d)):
                k.dma("sp", prm[:, i, :], pd.rearrange("(ct gl) p -> ct (gl p)", gl=2), w=["prm"])
            for i in range(3):
                k.op("pe", lambda e: e.transpose(out=ps[0][:, i * 16:(i + 1) * 16], in_=prm[:, i, :], identity=c.identf[0:16, 0:16]),
                     r=["prm", "identf"], w=[("ps", 0)])
            if CUT[0] <= 10.5:
                return
            P = sb("P", [128, 24, 16], F32, pb)
            AR, AI, DT, MAG, TH, S_, C_, T0, T1, RM, FRE, FIM, NR, DEN, ABR, ABI, AVR, AVI = range(18)
            k.op("dve", lambda e: e.tensor_copy(out=P[:, 0:3, :], in_=ps[0][:, 0:48].rearrange("p (a b) -> p a b", a=3)),
                 r=[("ps", 0)], w=["P"])

            def tt_(o, a, b, op, eng="dve"):
                k.op(eng, lambda e: e.tensor_tensor(out=P[:, o, :], in0=P[:, a, :], in1=P[:, b, :], op=op), r=["P"], w=["P"])

            def act_(o, a, fn, scale=1.0):
                k.op("act", lambda e: e.activation(out=P[:, o, :], in_=P[:, a, :], func=fn, scale=scale), r=["P"], w=["P"])

            def ts_(o, a, s1, s2, op0, op1):
                k.op("dve", lambda e: e.tensor_scalar(out=P[:, o, :], in0=P[:, a, :], scalar1=s1, scalar2=s2, op0=op0, op1=op1),
                     r=["P"], w=["P"])

            if CUT[0] <= 11:
                return
            act_(DT, DT, AF.Exp)
            tt_(T0, DT, AR, ALU.mult)
            act_(MAG, T0, AF.Exp)
            act_(RM, T0, AF.Exp, scale=-1.0)
            tt_(TH, DT, AI, ALU.mult)
            act_(T0, TH, AF.Sin, scale=1.0 / 16)
            tt_(T0, T0, T0, ALU.mult)
            ts_(C_, T0, -2.0, 1.0, ALU.mult, ALU.add)
            act_(S_, TH, AF.Sin, scale=1.0 / 8)
            for _ in range(3):
                tt_(T0, C_, C_, ALU.mult)
                tt_(T1, S_, S_, ALU.mult)
                tt_(S_, S_, C_, ALU.mult)
                ts_(S_, S_, 2.0, None, ALU.mult, ALU.bypass)
                tt_(C_, T0, T1, ALU.subtract)
            tt_(ABR, MAG, C_, ALU.mult)
            tt_(ABI, MAG, S_, ALU.mult)
            tt_(AVR, RM, C_, ALU.mult)
            tt_(AVI, RM, S_, ALU.mult)
            ts_(AVI, AVI, -1.0, None, ALU.mult, ALU.bypass)
            ts_(NR, ABR, -1.0, None, ALU.add, ALU.bypass)
            tt_(T0, AR, AR, ALU.mult)
            tt_(T1, AI, AI, ALU.mult)
            tt_(DEN, T0, T1, ALU.add)
            k.op("dve", lambda e: e.reciprocal(out=P[:, DEN, :], in_=P[:, DEN, :]), r=["P"], w=["P"])
            tt_(T0, NR, AR, ALU.mult)
            tt_(T1, ABI, AI, ALU.mult)
            tt_(T0, T0, T1, ALU.add)
            tt_(FRE, T0, DEN, ALU.mult)
            tt_(T0, ABI, AR, ALU.mult)
            tt_(T1, NR, AI, ALU.mult)
            tt_(T0, T0, T1, ALU.subtract)
            tt_(FIM, T0, DEN, ALU.mult)
            if CUT[0] <= 12:
                return
            ET = sb("ET", [128, 2, 16, 128], F32, pb)
            EV = sb("EV", [128, 2, 16, 128], F32, pb)
            tmp = sb("s5tmp", [128, 2, 16, 64], F32, pb)
            pw = sb("s5pw", [128, 2, 16], F32, pb)
            pw2 = sb("s5pw2", [128, 2, 16], F32, pb)
            for (tab, br, bi, i0r, i0i, name) in ((ET, ABR, ABI, None, None, "ET"), (EV, AVR, AVI, FRE, FIM, "EV")):
                if i0r is None:
                    k.op("pool", lambda e: e.memset(tab[:, 0, :, 0:1], 1.0), w=[name])
                    k.op("pool", lambda e: e.memset(tab[:, 1, :, 0:1], 0.0), w=[name])
                else:
                    k.op("dve", lambda e: e.tensor_copy(out=tab[:, 0, :, 0:1], in_=P[:, i0r, :].unsqueeze(2)), r=["P"], w=[name])
                    k.op("dve", lambda e: e.tensor_copy(out=tab[:, 1, :, 0:1], in_=P[:, i0i, :].unsqueeze(2)), r=["P"], w=[name])
                k.op("dve", lambda e: e.tensor_copy(out=pw[:, 0, :], in_=P[:, br, :]), r=["P"], w=["pw"])
                k.op("dve", lambda e: e.tensor_copy(out=pw[:, 1, :], in_=P[:, bi, :]), r=["P"], w=["pw"])
                m = 1
                while m <= 128:
                    if m < 128:
                        bre = pw[:, 0, :].unsqueeze(2).to_broadcast([128, 16, m])
                        bim = pw[:, 1, :].unsqueeze(2).to_broadcast([128, 16, m])
                        cmul(k, "dve", "dve", tab[:, 0, :, m:2 * m], tab[:, 1, :, m:2 * m],
                             tab[:, 0, :, 0:m], tab[:, 1, :, 0:m], bre, bim,
                             (tmp[:, 0, :, 0:m], tmp[:, 1, :, 0:m]), [name, name + "_re", name + "_im", "pw"], name, "s5tmp")
                    elif name == "ET":
                        k.op("dve", lambda e: e.tensor_copy(out=a128[:], in_=pw[:]), r=["pw"], w=["a128"])
                    cmul(k, "dve", "dve", pw2[:, 0, :], pw2[:, 1, :], pw[:, 0, :], pw[:, 1, :], pw[:, 0, :], pw[:, 1, :],
                         (tmp[:, 0, :, 0], tmp[:, 1, :, 0]), ["pw"], "pw2", "s5tmp")
                    k.op("dve", lambda e: e.tensor_copy(out=pw[:], in_=pw2[:]), r=["pw2_re", "pw2_im"], w=["pw"])
                    m *= 2
            if CUT[0] <= 13:
                return
            n = 0
            for ri in range(2):
                for g4 in range(4):
                    b = n % 2
                    n += 1
                    for q in range(4):
                        ct = g4 * 4 + q
                        k.op("pe", lambda e: e.transpose(out=ps[b][:, q * 128:(q + 1) * 128], in_=EV[:, ri, ct, :],
                                                         identity=c.identf[:]), r=["EV", "EV_re", "EV_im", "identf"], w=[("ps", b)])
                    k.op("act", lambda e: e.activation(out=EVt[:, ri, g4 * 512:(g4 + 1) * 512], in_=ps[b][:, :], func=AF.Copy),
                         r=[("ps", b)], w=["EVt"])

            k.op("act", lambda e: e.activation(out=ETb[:].rearrange("p a b c -> p (a b c)"),
                                               in_=ET[:].rearrange("p a b c -> p (a b c)"), func=AF.Copy),
                 r=["ET", "ET_re", "ET_im"], w=["ETb"])
            k.barrier()
        if CUT[0] <= 14:
            return
        tmpc = sb("s5tmpc", [128, 2, 16], F32, p2)
        zz = sb("zz", [128, 2, 2048], BF16, p2)
        t1 = sb("s5t1", [128, 512], F32, p2)
        t2 = sb("s5t2", [128, 512], F32, p2)
        sT = sb("sT", [128, 2, 16, 128], BF16, p2)
        lastc = sb("lastc", [128, 2, 16], F32, p2)
        cz = sb("cz", [128, 2, 16], F32, p2)
        cz2 = sb("cz2", [128, 2, 16], F32, p2)
        wr = sb("s5wr", [128, 512], F32, p2)
        wi_ = sb("s5wi", [128, 512], F32, p2)
        ypre = sb("ypre", [128, 4, 128], F32, p2)
        if CUT[0] <= 15:
            return
        for tt in range(NTT[0]):
            for kc in range(4):
                for ri in range(2):
                    k.op("pe", lambda e: e.matmul(ps[ri][:, :], lhsT=uT[:, kc, tsl(tt)], rhs=wb[:, ri, kc, :],
                                                  start=True, stop=True), r=[("uT", kc), "wb"], w=[("ps", ri)])
                er = EVt[:, 0, kc * 512:(kc + 1) * 512]
                ei = EVt[:, 1, kc * 512:(kc + 1) * 512]
                k.op("dve", lambda e: e.tensor_tensor(out=t1[:], in0=ps[0][:, :], in1=er, op=ALU.mult),
                     r=[("ps", 0), "EVt"], w=["s5t1"])
                k.op("dve", lambda e: e.tensor_tensor(out=t2[:], in0=ps[1][:, :], in1=ei, op=ALU.mult),
                     r=[("ps", 1), "EVt"], w=["s5t2"])
                k.op("pool", lambda e: e.tensor_tensor(out=zz[:, 0, kc * 512:(kc + 1) * 512], in0=t1[:], in1=t2[:],
                                                       op=ALU.subtract), r=["s5t1", "s5t2"], w=["zz"])
                k.op("dve", lambda e: e.tensor_tensor(out=t1[:], in0=ps[1][:, :], in1=er, op=ALU.mult),
                     r=[("ps", 1), "EVt"], w=["s5t1"])
                k.op("dve", lambda e: e.tensor_tensor(out=t2[:], in0=ps[0][:, :], in1=ei, op=ALU.mult),
                     r=[("ps", 0), "EVt"], w=["s5t2"])
                k.op("pool", lambda e: e.tensor_tensor(out=zz[:, 1, kc * 512:(kc + 1) * 512], in0=t1[:], in1=t2[:],
                                                       op=ALU.add), r=["s5t1", "s5t2"], w=["zz"])
            if CUT[0] <= 16 and tt >= 1:
                return
            for g4 in range(4):
                for ri in range(2):
                    b = 2 + ri
                    for q in range(4):
                        ct = g4 * 4 + q
                        k.op("pe", lambda e: e.matmul(ps[b][:, q * 128:(q + 1) * 128], lhsT=zz[:, ri, ct * 128:(ct + 1) * 128],
                                                      rhs=c.triU_b[:], start=True, stop=True),
                             r=["zz", "triU_b"], w=[("ps", b)])
                cr = ps[2][:, :].rearrange("p (a b) -> p a b", a=4)
                ci = ps[3][:, :].rearrange("p (a b) -> p a b", a=4)
                etr = ETb[:, 0, g4 * 4:(g4 + 1) * 4, :]
                eti = ETb[:, 1, g4 * 4:(g4 + 1) * 4, :]
                t1v = t1[:].rearrange("p (a b) -> p a b", a=4)
                t2v = t2[:].rearrange("p (a b) -> p a b", a=4)
                wrv = wr[:].rearrange("p (a b) -> p a b", a=4)
                wiv = wi_[:].rearrange("p (a b) -> p a b", a=4)
                if tt == 0:
                    k.op("act", lambda e: e.activation(out=wrv, in_=cr, func=AF.Copy), r=[("ps", 2)], w=["wr"])
                    k.op("act", lambda e: e.activation(out=wiv, in_=ci, func=AF.Copy), r=[("ps", 3)], w=["wi_"])
                else:
                    k.op("dve", lambda e: e.tensor_tensor(out=wrv, in0=cr, in1=cz[:, 0, g4 * 4:(g4 + 1) * 4].unsqueeze(2).to_broadcast([128, 4, 128]),
                                                          op=ALU.add), r=[("ps", 2), "cz_re"], w=["wr"])
                    k.op("dve", lambda e: e.tensor_tensor(out=wiv, in0=ci, in1=cz[:, 1, g4 * 4:(g4 + 1) * 4].unsqueeze(2).to_broadcast([128, 4, 128]),
                                                          op=ALU.add), r=[("ps", 3), "cz_im"], w=["wi_"])
                k.op("act", lambda e: e.activation(out=lastc[:, 0, g4 * 4:(g4 + 1) * 4], in_=wrv[:, :, 127], func=AF.Copy),
                     r=["wr"], w=["lastc"])
                k.op("act", lambda e: e.activation(out=lastc[:, 1, g4 * 4:(g4 + 1) * 4], in_=wiv[:, :, 127], func=AF.Copy),
                     r=["wi_"], w=["lastc"])
                k.op("dve", lambda e: e.tensor_tensor(out=t1v, in0=wrv, in1=etr, op=ALU.mult), r=["wr", "ETb"], w=["s5t1"])
                k.op("pool", lambda e: e.tensor_tensor(out=t2v, in0=wiv, in1=eti, op=ALU.mult), r=["wi_", "ETb"], w=["s5t2"])
                k.op("dve", lambda e: e.tensor_tensor(out=sT[:, 0, g4 * 4:(g4 + 1) * 4, :], in0=t1v, in1=t2v, op=ALU.subtract),
                     r=["s5t1", "s5t2"], w=["sT"])
                k.op("dve", lambda e: e.tensor_tensor(out=t1v, in0=wiv, in1=etr, op=ALU.mult), r=["wi_", "ETb"], w=["s5t1"])
                k.op("pool", lambda e: e.tensor_tensor(out=t2v, in0=wrv, in1=eti, op=ALU.mult), r=["wr", "ETb"], w=["s5t2"])
                k.op("dve", lambda e: e.tensor_tensor(out=sT[:, 1, g4 * 4:(g4 + 1) * 4, :], in0=t1v, in1=t2v, op=ALU.add),
                     r=["s5t1", "s5t2"], w=["sT"])
            if tt < NT - 1:
                cmul(k, "dve", "dve", cz2[:, 0, :], cz2[:, 1, :], lastc[:, 0, :], lastc[:, 1, :], a128[:, 0, :], a128[:, 1, :],
                     (tmpc[:, 0, :], tmpc[:, 1, :]), ["lastc", "a128"], "cz2", "s5tmpc")
                k.op("dve", lambda e: e.tensor_copy(out=cz[:], in_=cz2[:]), r=["cz2_re", "cz2_im"], w=["cz_re", "cz_im"])
            if CUT[0] <= 19 and tt >= 1:
                return
            for kc in range(4):
                n = 0
                for q in range(4):
                    ct = kc * 4 + q
                    for ri in range(2):
                        k.op("pe", lambda e: e.matmul(ps[5][:, kc * 128:(kc + 1) * 128], lhsT=cm[:, ri, ct, :],
                                                      rhs=sT[:, ri, ct, :], start=(n == 0), stop=(n == 7)),
                             r=["cm", "sT"], w=[("ps", 5)])
                        n += 1
            if CUT[0] <= 19.3 and tt >= 1:
                return
            for kc in range(4):
                k.op("dve", lambda e: e.tensor_scalar(out=ypre[:, kc, :], in0=uT[:, kc, tsl(tt)], scalar1=dcol[:, kc:kc + 1],
                                                      scalar2=None, op0=ALU.mult), r=[("uT", kc), "dcol"], w=["ypre"])
                k.op("dve", lambda e: e.tensor_tensor(out=ypre[:, kc, :], in0=ypre[:, kc, :], in1=ps[5][:, kc * 128:(kc + 1) * 128],
                                                      op=ALU.add), r=["ypre", ("ps", 5)], w=["ypre"])
            if CUT[0] <= 19.6 and tt >= 1:
                return
            for kc in range(4):
                if GELU_FN[0] is None:
                    k.op("dve", lambda e: e.tensor_copy(out=yT[:, kc, tsl(tt)], in_=ypre[:, kc, :]), r=["ypre"], w=[("yT", kc)])
                else:
                    k.op("act", lambda e: e.activation(out=yT[:, kc, tsl(tt)], in_=ypre[:, kc, :], func=GELU_FN[0]), r=["ypre"],
                         w=[("yT", kc)])
        for nm, tl, kk in (("ypre", ypre, ["ypre"]), ("zz", zz, ["zz"]), ("sT", sT, ["sT"]), ("ETb", ETb, ["ETb"]), ("EVt", EVt, ["EVt"]),
                           ("wb", wb, ["wb"]), ("cm", cm, ["cm"]), ("a128", a128, ["a128"]), ("lastc", lastc, ["lastc"])):
            if nm in c.dbg:
                ap = tl[:]
                if len(ap.shape) == 3:
                    ap = ap.rearrange("p a b -> p (a b)")
                elif len(ap.shape) == 4:
                    ap = ap.rearrange("p a b c -> p (a b c)")
                k.dma("sp", c.dbg[nm], ap, r=kk, w=["dbg_" + nm])
        if CUT[0] <= 20:
            return
        sg = sb("sg", [128, 4, 512], BF16, p2)
        yk = [("yT", i) for i in range(4)]
        n = 0
        for tb in range(4):
            for c2 in range(4):
                b = 6 + n % 2
                n += 1
                for ch in range(4):
                    k.op("pe", lambda e: e.matmul(ps[b][:, :], lhsT=wglu[:, ch, c2 * 128:(c2 + 1) * 128],
                                                  rhs=yT[:, ch, tb * 512:(tb + 1) * 512], start=(ch == 0), stop=(ch == 3)),
                         r=["wglu"] + yk, w=[("ps", b)])
                k.op("act", lambda e: e.activation(out=sg[:, c2, :], in_=ps[b][:, :], func=AF.Sigmoid), r=[("ps", b)],
                     w=[("sg", c2)])
            for c2 in range(4):
                k.op("dve", lambda e: e.tensor_tensor(out=yT[:, c2, tb * 512:(tb + 1) * 512], in0=yT[:, c2, tb * 512:(tb + 1) * 512],
                                                      in1=sg[:, c2, :], op=ALU.mult), r=[("yT", c2), ("sg", c2)], w=[("yT", c2)])
        k.barrier()


def peer_inputs(c):
    c.wq_d = c.din("peer_w_q", [2, D, 2048])
    c.keys_d = c.din("peer_sub_keys", [2, 8, 2, 128, 128])
    c.u_d = c.din("peer_u", [2, 16384, D])
    c.v_d = c.din("peer_v", [2, 16384, D])
    c.ut_scr = c.nc.dram_tensor("ut_scr", [8, 128, 16384], BF16, kind="Internal").ap()
    c.vb_scr = c.nc.dram_tensor("vb_scr", [16384, D], BF16, kind="Internal").ap()


NEG = -1.0
PEER_EG = [32]
NPROD = 6
PEER_ACT_HEADS = [5]
PEER_PROD = [['act', 'pool', 'pool', 'act', 'pool', 'pool', 'act', 'pool']]


def peer(c, li):
    nc, k, sb, ps, h = c.nc, c.k, c.sb, c.ps, c.h
    wq_d, keys_d, u_d, v_d = c.wq_d[li], c.keys_d[li], c.u_d[li], c.v_d[li]
    utv = c.ut_scr.rearrange("dc d e -> d dc e")
    with ExitStack() as pp:
        usts = [sb("pust%d" % i, [128, 4, D], BF16, pp) for i in range(2)]
        uts = [sb("puts%d" % i, [128, 8, 512], BF16, pp) for i in range(2)]
        vbs_ = [sb("pvb%d" % i, [128, 4, D], BF16, pp) for i in range(2)]
        for eg in range(32):
            i = eg % 2
            uk, tk, vk_ = "pust%d" % i, "puts%d" % i, "pvb%d" % i
            k.dma("pool", usts[i][:], u_d[eg * 512:(eg + 1) * 512, :].rearrange("(a p) d -> p a d", p=128), w=[uk])
            k.dma("pool", vbs_[i][:], v_d[eg * 512:(eg + 1) * 512, :].rearrange("(a p) d -> p a d", p=128), w=[vk_])
            for dc in range(8):
                b = (eg * 8 + dc) % 4
                for a in range(4):
                    k.op("pe", lambda e: e.transpose(out=c.pbf[b][:, a * 128:(a + 1) * 128], in_=usts[i][:, a, dc * 128:(dc + 1) * 128],
                                                     identity=c.ident[:]), r=[uk, "ident"], w=[("ps", b)])
                k.op("act", lambda e: e.activation(out=uts[i][:, dc, :], in_=c.pbf[b][:, 0:512], func=AF.Copy), r=[("ps", b)],
                     w=[tk])
            k.dma("sp", utv[:, :, eg * 512:(eg + 1) * 512], uts[i][:], r=[tk], w=[("utscr", eg)])
            k.dma("sp", c.vb_scr[eg * 512:(eg + 1) * 512, :].rearrange("(a p) d -> p a d", p=128), vbs_[i][:], r=[vk_],
                  w=[("vbscr", eg)])
        k.barrier()
    with ExitStack() as ph:
        hnT = sb("hnT", [128, 8, L], BF16, ph)
        rmsnorm_T(c, c.norm_ffn_g[li, :], hnT, "hnT")
        allhn = [("hnT", tt) for tt in range(NT)]
        e_all = sb("e_all", [128, 4, 16, 128], F32, ph)
        phi = sb("phi", [128, 4, 8], F32, ph)
        mx = sb("pmx", [128, 16], F32, ph)
        t16 = sb("t16", [128, 16, 16], F32, ph)
        tmpb = sb("tmpb", [128, 128], F32, ph)
        cand = sb("cand", [128, 256], F32, ph)
        cand2 = sb("cand2", [128, 256], F32, ph)
        c16 = sb("c16", [128, 16], F32, ph)
        zs = sb("zs", [128, 8], F32, ph)
        for tg in range(4):
            with ExitStack() as p1:
                kT = sb("kT", [128, 16, 128], BF16, p1)
                with ExitStack() as p0:
                    kst = sb("kst", [128, 16, 128], F32, p0)
                    k.dma("sp", kst[:], keys_d.rearrange("h c n d -> n (h c) d"), w=["kst"])
                    for g in range(4):
                        b = g % 2
                        for q in range(4):
                            k.op("pe", lambda e: e.transpose(out=ps[b][:, q * 128:(q + 1) * 128], in_=kst[:, g * 4 + q, :],
                                                             identity=c.identf[:]), r=["kst", "identf"], w=[("ps", b)])
                        k.op("act", lambda e: e.activation(out=kT[:, g * 4:(g + 1) * 4, :].rearrange("p a b -> p (a b)"), in_=ps[b][:, :],
                                                           func=AF.Copy), r=[("ps", b)], w=["kT"])
                    k.barrier()

                wq = sb("wq", [128, 8, 2048], BF16, p1)
                qT = sb("qT", [128, 16, 512], BF16, p1)
                for ch in range(8):
                    k.dma("pool", wq[:, ch, :], wq_d[ch * 128:(ch + 1) * 128, :], w=[("wq", ch)])
                allwq = [("wq", ch) for ch in range(8)]
                for blk in range(16):
                    b = blk % 2
                    for ch in range(8):
                        k.op("pe", lambda e: e.matmul(ps[b][:, :], lhsT=wq[:, ch, blk * 128:(blk + 1) * 128],
                                                      rhs=hnT[:, ch, tg * 512:(tg + 1) * 512], start=(ch == 0), stop=(ch == 7)),
                             r=allwq + allhn, w=[("ps", b)])
                    k.op("act", lambda e: e.activation(out=qT[:, blk, :], in_=ps[b][:, :], func=AF.Copy), r=[("ps", b)],
                         w=[("qT", blk)])
                for tt in range(4):
                    for blk in range(16):
                        b = 2 + blk // 4
                        k.op("pe", lambda e: e.matmul(ps[b][:, (blk % 4) * 128:(blk % 4 + 1) * 128], lhsT=qT[:, blk, tt * 128:(tt + 1) * 128],
                                                      rhs=kT[:, blk, :], start=True, stop=True), r=[("qT", blk), "kT"], w=[("ps", b)])
                    for b4 in range(4):
                        k.op("dve", lambda e: e.tensor_reduce(out=mx[:, b4 * 4:(b4 + 1) * 4],
                                                              in_=ps[2 + b4][:, :].rearrange("p (a n) -> p a n", a=4),
                                                              axis=AX.X, op=ALU.max), r=[("ps", 2 + b4)], w=["pmx"])
                    k.op("dve", lambda e: e.tensor_scalar(out=mx[:], in0=mx[:], scalar1=-1.0, scalar2=None, op0=ALU.mult),
                         r=["pmx"], w=["pmx"])
                    for blk in range(16):
                        b = 2 + blk // 4
                        k.op("act", lambda e: e.activation(out=e_all[:, tt, blk, :], in_=ps[b][:, (blk % 4) * 128:(blk % 4 + 1) * 128],
                                                           func=AF.Exp, bias=mx[:, blk:blk + 1], scale=1.0),
                             r=[("ps", b), "pmx"], w=[("e_all", tt, blk)])
                    for blk in range(16):
                        ek = ("e_all", tt, blk)
                        k.op("dve", lambda e: e.max(out=t16[:, blk, 0:8], in_=e_all[:, tt, blk, :]), r=[ek], w=["t16"])
                        k.op("dve", lambda e: e.match_replace(out=tmpb[:], in_to_replace=t16[:, blk, 0:8],
                                                              in_values=e_all[:, tt, blk, :], imm_value=NEG),
                             r=[ek, "t16"], w=["tmpb"])
                        k.op("dve", lambda e: e.max(out=t16[:, blk, 8:16], in_=tmpb[:]), r=["tmpb"], w=["t16"])
                    for hd in range(8):
                        k.op("dve", lambda e: e.tensor_tensor(
                            out=cand[:].rearrange("p (i j) -> p i j", i=16),
                            in0=t16[:, 2 * hd, :].unsqueeze(2).to_broadcast([128, 16, 16]),
                            in1=t16[:, 2 * hd + 1, :].unsqueeze(1).to_broadcast([128, 16, 16]), op=ALU.mult),
                             r=["t16"], w=["cand"])
                        k.op("dve", lambda e: e.max(out=c16[:, 0:8], in_=cand[:]), r=["cand"], w=["c16"])
                        k.op("dve", lambda e: e.match_replace(out=cand2[:], in_to_replace=c16[:, 0:8], in_values=cand[:],
                                                              imm_value=NEG), r=["cand", "c16"], w=["cand2"])
                        k.op("dve", lambda e: e.max(out=c16[:, 8:16], in_=cand2[:]), r=["cand2"], w=["c16"])
                        k.op("dve", lambda e: e.tensor_scalar(out=phi[:, tt, hd:hd + 1], in0=c16[:, 15:16], scalar1=1.0 - 2e-6, scalar2=None,
                                                              op0=ALU.mult), r=["c16"], w=["phi"])
                        k.op("dve", lambda e: e.tensor_reduce(out=zs[:, hd:hd + 1], in_=c16[:], axis=AX.X, op=ALU.add),
                             r=["c16"], w=["zs"])
                    k.op("dve", lambda e: e.reciprocal(out=zs[:], in_=zs[:]), r=["zs"], w=["zs"])
                    k.op("dve", lambda e: e.tensor_tensor(out=phi[:, tt, :], in0=phi[:, tt, :], in1=zs[:], op=ALU.mult),
                         r=["phi", "zs"], w=["phi"])
                    for hd in range(8):
                        k.op("dve", lambda e: e.tensor_scalar(out=e_all[:, tt, 2 * hd, :], in0=e_all[:, tt, 2 * hd, :],
                                                              scalar1=zs[:, hd:hd + 1], scalar2=None, op0=ALU.mult),
                             r=["zs", ("e_all", tt, 2 * hd)], w=[("e_all", tt, 2 * hd)])
                k.barrier()
            with ExitStack() as p2:
                utsb = sb("utsb", [128, 8, 512], BF16, p2)
                vbs = [sb("vb%d" % i, [128, 4, D], BF16, p2) for i in range(2)]
                gels = [sb("gel%d" % i, [128, 4, 512], BF16, p2) for i in range(2)]
                Gs = [sb("G%d" % i, [128, 8, 512], BF16, p2) for i in range(2)]
                prod = [sb("prod%d" % i, [128, 512], F32, p2) for i in range(NPROD)]
                HsTs = [sb("HsT%d" % i, [128, 4, 128], BF16, p2) for i in range(2)]
                NEG_ = PEER_EG[0]
                steps = [(eg, tt) for eg in range(NEG_) for tt in range(4)]
                npr = [0]

                def load(eg):
                    k.dma("sp", utsb[:], utv[:, :, eg * 512:(eg + 1) * 512], r=[("utscr", eg)], w=["utsb"])
                    k.dma("sp", vbs[eg % 2][:], c.vb_scr[eg * 512:(eg + 1) * 512, :].rearrange("(a p) d -> p a d", p=128),
                          r=[("vbscr", eg)], w=["vb%d" % (eg % 2)])

                def hpart(eg, a):
                    b = a % 2
                    for dc in range(8):
                        k.op("pe", lambda e: e.matmul(ps[b][:, :], lhsT=utsb[:, dc, a * 128:(a + 1) * 128],
                                                      rhs=hnT[:, dc, tg * 512:(tg + 1) * 512], start=(dc == 0), stop=(dc == 7)),
                             r=["utsb"] + allhn, w=[("ps", b)])
                    k.op("act", lambda e: e.activation(out=gels[eg % 2][:, a, :], in_=ps[b][:, :], func=AF.Gelu), r=[("ps", b)],
                         w=[("gel%d" % (eg % 2), a)])

                def stage_a(s_):
                    eg, tt = steps[s_]
                    G, gk = Gs[s_ % 2], "G%d" % (s_ % 2)
                    for hd in range(8):
                        pr = prod[npr[0] % NPROD]
                        pk = "prod%d" % (npr[0] % NPROD)
                        npr[0] += 1
                        pe_ = PEER_PROD[0][hd]
                        if pe_ == 'act':
                            for a in range(4):
                                k.op("act", lambda e: e.activation(out=pr[:, a * 128:(a + 1) * 128], in_=e_all[:, tt, 2 * hd + 1, :],
                                                                   func=AF.Copy, scale=e_all[:, tt, 2 * hd, eg * 4 + a:eg * 4 + a + 1]),
                                     r=[("e_all", tt, 2 * hd), ("e_all", tt, 2 * hd + 1)], w=[pk])
                        else:
                            k.op(pe_, lambda e: e.tensor_tensor(
                                out=pr[:].rearrange("p (a n) -> p a n", a=4),
                                in0=e_all[:, tt, 2 * hd, eg * 4:(eg + 1) * 4].unsqueeze(2).to_broadcast([128, 4, 128]),
                                in1=e_all[:, tt, 2 * hd + 1, :].unsqueeze(1).to_broadcast([128, 4, 128]), op=ALU.mult),
                                 r=[("e_all", tt, 2 * hd), ("e_all", tt, 2 * hd + 1)], w=[pk])
                        k.op("dve", lambda e: e.scalar_tensor_tensor(out=G[:, hd, :], in0=pr[:], scalar=phi[:, tt, hd:hd + 1],
                                                                     in1=pr[:], op0=ALU.is_ge, op1=ALU.mult),
                             r=[pk, "phi"], w=[(gk, hd)])

                def stage_b(s_):
                    eg, tt = steps[s_]
                    G, gk, gb = Gs[s_ % 2], "G%d" % (s_ % 2), 4 + s_ % 2
                    for a in range(4):
                        for hd in range(8):
                            k.op("pe", lambda e: e.matmul(ps[gb][:, a * 128:(a + 1) * 128], lhsT=G[:, hd, a * 128:(a + 1) * 128],
                                                          rhs=c.ident[:], start=(hd == 0), stop=(hd == 7)),
                                 r=[(gk, hd), "ident"], w=[("ps", gb)])

                def stage_c(s_):
                    eg, tt = steps[s_]
                    gb = 4 + s_ % 2
                    k.op("dve", lambda e: e.tensor_tensor(out=HsTs[s_ % 2][:], in0=gels[eg % 2][:, :, tt * 128:(tt + 1) * 128],
                                                          in1=ps[gb][:, :].rearrange("p (a n) -> p a n", a=4), op=ALU.mult),
                         r=[("gel%d" % (eg % 2), a) for a in range(4)] + [("ps", gb)], w=["HsT%d" % (s_ % 2)])

                def obank(s_, dh):
                    return (2 + dh) if s_ % 2 == 0 else (6 + dh)

                def stage_d(s_):
                    eg, tt = steps[s_]
                    for dh in range(2):
                        ob = obank(s_, dh)
                        for a in range(4):
                            k.op("pe", lambda e: e.matmul(ps[ob][:, :], lhsT=HsTs[s_ % 2][:, a, :], rhs=vbs[eg % 2][:, a, dh * 512:(dh + 1) * 512],
                                                          start=(a == 0), stop=(a == 3)), r=["HsT%d" % (s_ % 2), "vb%d" % (eg % 2)],
                                 w=[("ps", ob)])

                def stage_e(s_):
                    eg, tt = steps[s_]
                    tok = tg * 4 + tt
                    for dh in range(2):
                        ob = obank(s_, dh)
                        k.op("dve", lambda e: e.tensor_tensor(out=h[:, tok, dh * 512:(dh + 1) * 512],
                                                              in0=h[:, tok, dh * 512:(dh + 1) * 512], in1=ps[ob][:, :],
                                                              op=ALU.add), r=[("h", tok), ("ps", ob)], w=[("h", tok)])

                if steps:
                    load(0)
                    for a in range(4):
                        hpart(0, a)
                    stage_a(0)
                for s_ in range(len(steps)):
                    eg, tt = steps[s_]
                    if tt == 0 and eg + 1 < NEG_:
                        load(eg + 1)
                    if s_ + 1 < len(steps):
                        stage_a(s_ + 1)
                    stage_b(s_)
                    stage_c(s_)
                    if eg + 1 < NEG_:
                        hpart(eg + 1, tt)
                    stage_d(s_)
                    if s_ >= 1:
                        stage_e(s_ - 1)
                if steps:
                    stage_e(len(steps) - 1)
                k.barrier()


RW_HP = [8]
RW_TB = [4]


def rwkv_mixer(c, li):
    nc, k, sb, ps, pbf, h = c.nc, c.k, c.sb, c.ps, c.pbf, c.h
    din = c.din
    mu_d = din("o_mu", [6, D])
    wr_d, wk_d, wv_d = din("o_w_r", [D, D]), din("o_w_k", [D, D]), din("o_w_v", [D, D])
    w0_d = din("o_w0", [1, D])
    ww1_d, ww2_d = din("o_w_w1", [D, 64]), din("o_w_w2", [64, D])
    a0_d = din("o_a0", [1, D])
    wa1_d, wa2_d = din("o_w_a1", [D, 64]), din("o_w_a2", [64, D])
    wg1_d, wg2_d = din("o_w_g1", [D, 128]), din("o_w_g2", [128, D])
    kk_d, ka_d, rk_d = din("o_k_k", [1, D]), din("o_k_a", [1, D]), din("o_r_k", [1, D])
    lg_d, lb_d = din("o_lnx_g", [1, D]), din("o_lnx_b", [1, D])
    wo_d = din("o_w_o", [D, D])

    def mm(out, lhsT, rhs, r, w, start=True, stop=True):
        k.op("pe", lambda e: e.matmul(out, lhsT=lhsT, rhs=rhs, start=start, stop=stop), r=r, w=w)

    def tt_(eng, out, in0, in1, op, r, w):
        k.op(eng, lambda e: e.tensor_tensor(out=out, in0=in0, in1=in1, op=op), r=r, w=w)

    def ts_(eng, out, in0, s1, s2, op0, op1, r, w):
        k.op(eng, lambda e: e.tensor_scalar(out=out, in0=in0, scalar1=s1, scalar2=s2, op0=op0, op1=op1), r=r, w=w)

    def act(out, in_, fn, r, w, **kw):
        k.op("act", lambda e: e.activation(out=out, in_=in_, func=fn, **kw), r=r, w=w)

    with ExitStack() as ph:
        xp = sb("xp", [128, 8, L + 2], BF16, ph)
        k.op("pool", lambda e: e.memset(xp[:, :, 0:1], 0.0), w=["xp0"])
        rmsnorm_T(c, c.norm_mix_g[li, :], xp, "xp", off=1)
        allx = [("xp", tt) for tt in range(NT)] + ["xp0"]
        mjt = sb("mjt", [128, 128], F32, ph)
        mtj = sb("mtj", [128, 128], F32, ph)
        k.op("pool", lambda e: e.affine_select(out=mjt[:], in_=c.ones_f[:], pattern=[[1, 128]], compare_op=ALU.is_gt,
                                                fill=0.0, base=0, channel_multiplier=-1), r=["ones_f"], w=["mjt"])
        k.op("pool", lambda e: e.affine_select(out=mtj[:], in_=c.ones_f[:], pattern=[[-1, 128]], compare_op=ALU.is_gt,
                                                fill=0.0, base=0, channel_multiplier=1), r=["ones_f"], w=["mtj"])
        hm = sb("hm", [128, 2], F32, ph)
        bones = sb("bones", [128, 128], BF16, ph)
        cmask = sb("cmask", [128, 2, 128], F32, ph)
        dsel = sb("dsel", [128, 2, 128], F32, ph)
        k.op("pool", lambda e: e.memset(hm[:], 0.0), w=["hm"])
        k.op("pool", lambda e: e.memset(hm[0:64, 0:1], 1.0), r=["hm"], w=["hm"])
        k.op("pool", lambda e: e.memset(hm[64:128, 1:2], 1.0), r=["hm"], w=["hm"])
        k.op("pool", lambda e: e.memset(bones[:], 0.0), w=["bones"])
        k.op("pool", lambda e: e.memset(bones[0:64, 0:64], 1.0), r=["bones"], w=["bones"])
        k.op("pool", lambda e: e.memset(bones[64:128, 64:128], 1.0), r=["bones"], w=["bones"])
        k.op("pool", lambda e: e.memset(cmask[:].rearrange("p a b -> p (a b)"), 0.0), w=["cmask"])
        k.op("pool", lambda e: e.memset(cmask[:, 0, 0:64], 1.0), r=["cmask"], w=["cmask"])
        k.op("pool", lambda e: e.memset(cmask[:, 1, 64:128], 1.0), r=["cmask"], w=["cmask"])
        for hl in range(2):
            ts_("dve", dsel[:, hl, :], c.identf[:], hm[:, hl:hl + 1], None, ALU.mult, ALU.bypass, ["identf", "hm"], ["dsel"])
        chm = sb("chm", [128, 512], BF16, ph)
        k.op("pool", lambda e: e.memset(chm[:], 1.0), w=["chm"])
        k.op("pool", lambda e: e.memset(chm[:].rearrange("p (a b) -> p a b", a=4)[:, :, 0:1], 0.0), r=["chm"], w=["chm"])
        prm = sb("rprm", [128, 7, 8], F32, ph)
        for i, pd in enumerate((w0_d, a0_d, kk_d, ka_d, rk_d, lg_d, lb_d)):
            k.dma("sp", prm[:, i, :], pd[0, :].rearrange("(dc p) -> p dc", p=128), w=["rprm"], allow_slow_non_contiguous=True)
        mucol = sb("mucol", [128, 6, 8], F32, ph)
        k.dma("sp", mucol[:], mu_d.rearrange("i (dc p) -> p i dc", p=128), w=["mucol"], allow_slow_non_contiguous=True)
        MU = dict(r=0, w=1, k=2, v=3, a=4, g=5)

        def load_split(name, wd, cols, ncol, mui, st):
            w0 = sb(name + "0", [128, 8, ncol], BF16, st)
            wm = sb(name + "m", [128, 8, ncol], BF16, st)
            wp = sb(name + "p", [128, 8, ncol], BF16, st)
            k.dma("pool", w0[:], wd[:, cols].rearrange("(dc p) n -> p dc n", p=128), w=[name + "0"])
            tt_("dve", wm[:], w0[:], mucol[:, mui, :].unsqueeze(2).to_broadcast([128, 8, ncol]), ALU.mult,
                [name + "0", "mucol"], [name + "m"])
            tt_("dve", wp[:], w0[:], wm[:], ALU.subtract, [name + "0", name + "m"], [name + "p"])
            return wp, wm

        ww2 = sb("ww2", [128, D], BF16, ph)
        wa2 = sb("wa2", [128, D], BF16, ph)
        wg2 = sb("wg2r", [128, D], BF16, ph)
        k.dma("pool", ww2[0:64, :], ww2_d[:, :], w=["ww2"])
        k.dma("pool", wa2[0:64, :], wa2_d[:, :], w=["wa2"])
        k.dma("pool", wg2[:], wg2_d[:, :], w=["wg2r"])
        tw1 = sb("tw1", [128, L], BF16, ph)
        ta1 = sb("ta1", [128, L], BF16, ph)
        tg1 = sb("tg1", [128, L], BF16, ph)

        def proj(out_ps, wp, wm, c0, m, t0, n, keys):
            for dc in range(8):
                mm(out_ps, wp[:, dc, c0:c0 + m], xp[:, dc, 1 + t0:1 + t0 + n], allx + keys, [("ps", 0)], start=(dc == 0), stop=False)
            for dc in range(8):
                mm(out_ps, wm[:, dc, c0:c0 + m], xp[:, dc, t0:t0 + n], allx + keys, [("ps", 0)], start=False, stop=(dc == 7))

        with ExitStack() as p0:
            w1p, w1m = load_split("ww1", ww1_d, slice(0, 64), 64, MU["w"], p0)
            a1p, a1m = load_split("wa1", wa1_d, slice(0, 64), 64, MU["a"], p0)
            g1p, g1m = load_split("wg1", wg1_d, slice(0, 128), 128, MU["g"], p0)
            for tb in range(4):
                t0 = tb * 512
                proj(ps[0][0:64, :], w1p, w1m, 0, 64, t0, 512, ["ww1p", "ww1m"])
                act(tw1[0:64, t0:t0 + 512], ps[0][0:64, :], AF.Tanh, [("ps", 0)], ["tw1"])
                proj(ps[0][0:64, :], a1p, a1m, 0, 64, t0, 512, ["wa1p", "wa1m"])
                act(ta1[0:64, t0:t0 + 512], ps[0][0:64, :], AF.Copy, [("ps", 0)], ["ta1"])
                proj(ps[0][:, :], g1p, g1m, 0, 128, t0, 512, ["wg1p", "wg1m"])
                act(tg1[:, t0:t0 + 512], ps[0][:, :], AF.Sigmoid, [("ps", 0)], ["tg1"])
            k.barrier()
        W = 512
        f32t = lambda n, st: sb(n, [128, W], F32, st)
        b16t = lambda n, st: sb(n, [128, W], BF16, st)
        for hp in range(RW_HP[0]):
            with ExitStack() as p1:
                cols = slice(hp * 128, (hp + 1) * 128)
                wrp, wrm = load_split("wr", wr_d, cols, 128, MU["r"], p1)
                wkp, wkm = load_split("wk", wk_d, cols, 128, MU["k"], p1)
                wvp, wvm = load_split("wv", wv_d, cols, 128, MU["v"], p1)
                wo = sb("wo_hp", [128, D], BF16, p1)
                k.dma("pool", wo[:], wo_d[hp * 128:(hp + 1) * 128, :], w=["wo_hp"])
                pc = lambda i: prm[:, i, hp:hp + 1]
                Hb = sb("Hb", [128, 64], BF16, p1)
                k.op("pool", lambda e: e.memset(Hb[:], 0.0), w=["Hb"])
                r_b, k_b, v_b, g_b = b16t("r_b", p1), b16t("k_b", p1), b16t("v_b", p1), b16t("g_b", p1)
                vtok = sb("vtok", [128, 4, 128], BF16, p1)
                lw, cs, asig = f32t("lw", p1), f32t("cs", p1), f32t("asig", p1)
                e1, e2, e3, e4 = f32t("e1", p1), f32t("e2", p1), f32t("e3", p1), f32t("e4", p1)
                kkn, kmod, b_ = f32t("kkn", p1), f32t("kmod", p1), f32t("bb", p1)
                sq = b16t("sq", p1)
                rt, rz0, rz1, az0, az1 = b16t("rt", p1), b16t("rz0", p1), b16t("rz1", p1), b16t("az0", p1), b16t("az1", p1)
                at_, bt, kt, bh, kh = b16t("at", p1), b16t("bt", p1), b16t("kt", p1), b16t("bh", p1), b16t("kh", p1)
                bv = f32t("bv", p1)
                pcl = sb("pcl", [128, 4], F32, p1)
                rz, az = (rz0, rz1), (az0, az1)
                Xs = [sb("X%d" % i, [128, 256], BF16, p1) for i in range(2)]
                MNs = [[sb("MN%d_%d" % (j, i), [128, 256], BF16, p1) for i in range(2)] for j in range(2)]
                cats = [sb("cat%d" % i, [128, 256], BF16, p1) for i in range(2)]
                mrks = [sb("mrk%d" % i, [128, 128], F32, p1) for i in range(2)]
                btok = sb("btok", [128, 2, 128], BF16, p1)
                bz = sb("bz", [128, 2, 128], BF16, p1)
                kz = sb("kz", [128, 2, 128], BF16, p1)
                RGMF = [[sb("%s%d" % (n, i), [128, 128], BF16, p1) for n in ("RpT", "GT", "MpT", "FT")] for i in range(2)]
                gst = sb("gst", [128, 8], F32, p1)
                ysq = sb("ysq", [128, 128], F32, p1)
                yn = sb("yn", [128, 128], BF16, p1)
                z1 = sb("z1", [128, 128], F32, p1)
                ygT = sb("ygT", [128, 128], BF16, p1)
                for tb in range(RW_TB[0]):
                    t0 = tb * W
                    proj(ps[0][:, :], wrp, wrm, 0, 128, t0, W, ["wrp", "wrm"])
                    act(r_b[:], ps[0][:, :], AF.Copy, [("ps", 0)], ["r_b"])
                    proj(ps[0][:, :], wkp, wkm, 0, 128, t0, W, ["wkp", "wkm"])
                    act(k_b[:], ps[0][:, :], AF.Copy, [("ps", 0)], ["k_b"])
                    proj(ps[0][:, :], wvp, wvm, 0, 128, t0, W, ["wvp", "wvm"])
                    act(v_b[:], ps[0][:, :], AF.Copy, [("ps", 0)], ["v_b"])
                    for q in range(4):
                        tq = t0 + q * 128
                        for dc in range(8):
                            mm(ps[6][:, q * 128:(q + 1) * 128], xp[:, dc, 1 + tq:1 + tq + 128], wvp[:, dc, :], allx + ["wvp"],
                               [("ps", 6)], start=(dc == 0), stop=False)
                        for dc in range(8):
                            mm(ps[6][:, q * 128:(q + 1) * 128], xp[:, dc, tq:tq + 128], wvm[:, dc, :], allx + ["wvm"],
                               [("ps", 6)], start=False, stop=(dc == 7))
                    k.op("dve", lambda e: e.tensor_copy(out=vtok[:].rearrange("p a b -> p (a b)"), in_=ps[6][:, :]),
                         r=[("ps", 6)], w=["vtok"])
                    mm(ps[0][:, :], ww2[0:64, cols], tw1[0:64, t0:t0 + W], ["ww2", "tw1"], [("ps", 0)])
                    act(lw[:], ps[0][:, :], AF.Sigmoid, [("ps", 0), "rprm"], ["lw"], bias=pc(0), scale=1.0)
                    ts_("dve", lw[:], lw[:], -0.6065306597126334, None, ALU.mult, ALU.bypass, ["lw"], ["lw"])
                    mm(ps[0][:, :], wa2[0:64, cols], ta1[0:64, t0:t0 + W], ["wa2", "ta1"], [("ps", 0)])
                    act(asig[:], ps[0][:, :], AF.Sigmoid, [("ps", 0), "rprm"], ["asig"], bias=pc(1), scale=1.0)
                    mm(ps[0][:, :], wg2[:, cols], tg1[:, t0:t0 + W], ["wg2r", "tg1"], [("ps", 0)])
                    act(g_b[:], ps[0][:, :], AF.Copy, [("ps", 0)], ["g_b"])
                    k.op("dve", lambda e: e.tensor_tensor_scan(out=cs[:], data0=chm[:], data1=lw[:], initial=0.0,
                                                               op0=ALU.mult, op1=ALU.add), r=["chm", "lw"], w=["cs"])
                    cs3 = cs[:].rearrange("p (a b) -> p a b", a=4)
                    csl = cs3[:, :, 127:128].to_broadcast([128, 4, 128])
                    act(e1[:], cs[:], AF.Exp, ["cs"], ["e1"])
                    act(e2[:], cs[:], AF.Exp, ["cs"], ["e2"], scale=-1.0)
                    tt_("dve", e3[:], cs[:], lw[:], ALU.subtract, ["cs", "lw"], ["e3"])
                    act(e3[:], e3[:], AF.Exp, ["e3"], ["e3"])
                    tt_("dve", e4[:].rearrange("p (a b) -> p a b", a=4), csl, cs3, ALU.subtract, ["cs"], ["e4"])
                    act(e4[:], e4[:], AF.Exp, ["e4"], ["e4"])
                    act(pcl[:], cs3[:, :, 127], AF.Exp, ["cs"], ["pcl"])
                    ts_("dve", kkn[:], k_b[:], pc(2), None, ALU.mult, ALU.bypass, ["k_b", "rprm"], ["kkn"])
                    tt_("dve", sq[:], kkn[:], kkn[:], ALU.mult, ["kkn"], ["sq"])
                    mm(ps[0][:, :], bones[:], sq[:], ["bones", "sq"], [("ps", 0)])
                    act(kmod[:], ps[0][:, :], AF.Sqrt, [("ps", 0)], ["kmod"])
                    ts_("dve", kmod[:], kmod[:], 1e-12, None, ALU.max, ALU.bypass, ["kmod"], ["kmod"])
                    k.op("dve", lambda e: e.reciprocal(out=kmod[:], in_=kmod[:]), r=["kmod"], w=["kmod"])
                    tt_("dve", kkn[:], kkn[:], kmod[:], ALU.mult, ["kkn", "kmod"], ["kkn"])
                    ts_("dve", kmod[:], asig[:], -1.0, pc(3), ALU.add, ALU.mult, ["asig", "rprm"], ["kmod"])
                    ts_("dve", kmod[:], kmod[:], 1.0, None, ALU.add, ALU.bypass, ["kmod"], ["kmod"])
                    tt_("dve", kmod[:], kmod[:], k_b[:], ALU.mult, ["kmod", "k_b"], ["kmod"])
                    tt_("dve", b_[:], kkn[:], asig[:], ALU.mult, ["kkn", "asig"], ["bb"])
                    tt_("dve", rt[:], r_b[:], e1[:], ALU.mult, ["r_b", "e1"], ["rt"])
                    tt_("pool", kt[:], kmod[:], e2[:], ALU.mult, ["kmod", "e2"], ["kt"])
                    tt_("dve", bt[:], b_[:], e2[:], ALU.mult, ["bb", "e2"], ["bt"])
                    tt_("pool", kh[:], kmod[:], e4[:], ALU.mult, ["kmod", "e4"], ["kh"])
                    tt_("dve", bh[:], b_[:], e4[:], ALU.mult, ["bb", "e4"], ["bh"])
                    tt_("pool", e3[:], kkn[:], e3[:], ALU.mult, ["kkn", "e3"], ["e3"])
                    ts_("dve", at_[:], e3[:], -1.0, None, ALU.mult, ALU.bypass, ["e3"], ["at"])
                    for hl in range(2):
                        ts_("dve", rz[hl][:], rt[:], hm[:, hl:hl + 1], None, ALU.mult, ALU.bypass, ["rt", "hm"], ["rz%d" % hl])
                        ts_("pool", az[hl][:], at_[:], hm[:, hl:hl + 1], None, ALU.mult, ALU.bypass, ["at", "hm"], ["az%d" % hl])
                    tt_("dve", e1[:], r_b[:], kmod[:], ALU.mult, ["r_b", "kmod", "e1", "rt"], ["e1"])
                    ts_("dve", sq[:], e1[:], pc(4), None, ALU.mult, ALU.bypass, ["e1", "rprm", "sq"], ["sq"])
                    mm(ps[0][:, :], bones[:], sq[:], ["bones", "sq"], [("ps", 0)])
                    tt_("dve", bv[:], ps[0][:, :], v_b[:], ALU.mult, [("ps", 0), "v_b"], ["bv"])
                    for q in range(4):
                        csl_ = slice(q * 128, (q + 1) * 128)
                        tok = tb * 4 + q
                        k.op("pe", lambda e: e.transpose(out=pbf[1][:, 0:128], in_=bh[:, csl_], identity=c.ident[:]),
                             r=["bh", "ident"], w=[("ps", 1)])
                        k.op("pe", lambda e: e.transpose(out=pbf[1][:, 128:256], in_=kh[:, csl_], identity=c.ident[:]),
                             r=["kh", "ident"], w=[("ps", 1)])
                        for hl in range(2):
                            tt_("dve", bz[:, hl, :], pbf[1][:, 0:128], cmask[:, hl, :], ALU.mult, [("ps", 1), "cmask"], ["bz"])
                            tt_("dve", kz[:, hl, :], pbf[1][:, 128:256], cmask[:, hl, :], ALU.mult, [("ps", 1), "cmask"], ["kz"])
                        def head_seq(hl):
                            azc, rzc = az[hl][:, csl_], rz[hl][:, csl_]
                            azk, rzk = "az%d" % hl, "rz%d" % hl
                            bX, bY = 2 + 2 * hl, 3 + 2 * hl
                            kX, kY = ("ps", bX), ("ps", bY)
                            X, MN, cat, mrk = Xs[hl], MNs[hl], cats[hl], mrks[hl]
                            RpT, GT, MpT, FT = RGMF[hl]
                            xk, ck, mk = "X%d" % hl, "cat%d" % hl, "mrk%d" % hl
                            rk_, gk_, mpk, fk = ("RpT%d" % hl, "GT%d" % hl, "MpT%d" % hl, "FT%d" % hl)
                            tp = pbf[1][:, 256 + hl * 128:384 + hl * 128]
                            mm(ps[bX][:, 0:128], bt[:, csl_], azc, ["bt", azk], [kX])
                            mm(ps[bX][:, 128:256], azc, bt[:, csl_], ["bt", azk], [kX])
                            mm(ps[bX][:, 256:384], azc, kt[:, csl_], ["kt", azk], [kX])
                            k.op("pe", lambda e: e.transpose(out=tp, in_=azc, identity=c.ident[:]), r=[azk, "ident"], w=[("ps", 1)])
                            yield
                            tt_("dve", MN[0][:, 0:128], ps[bX][:, 0:128], mjt[:], ALU.mult, [kX, "mjt"], ["MN0_%d" % hl])
                            tt_("dve", MN[0][:, 128:256], ps[bX][:, 128:256], mtj[:], ALU.mult, [kX, "mtj"], ["MN0_%d" % hl])
                            tt_("dve", X[:, 128:256], ps[bX][:, 256:384], mtj[:], ALU.mult, [kX, "mtj"], [xk])
                            act(X[:, 0:128], tp, AF.Copy, [("ps", 1)], [xk])
                            yield
                            for i in range(7):
                                cur, nxt = MN[i % 2], MN[(i + 1) % 2]
                                ck_, nk = "MN%d_%d" % (i % 2, hl), "MN%d_%d" % ((i + 1) % 2, hl)
                                mm(ps[bY][:, 0:256], cur[:, 0:128], X[:], [ck_, xk], [kY])
                                if i < 6:
                                    mm(ps[bY][:, 256:384], cur[:, 128:256], cur[:, 0:128], [ck_], [kY])
                                    mm(ps[bY][:, 384:512], cur[:, 0:128], cur[:, 128:256], [ck_], [kY])
                                yield
                                if i < 6:
                                    act(nxt[:], ps[bY][:, 256:512], AF.Copy, [kY], [nk, kY])
                                tt_("dve", X[:], X[:], ps[bY][:, 0:256], ALU.add, [xk, kY], [xk, kY])
                                yield
                            mm(ps[bY][:, 0:128], bt[:, csl_], rzc, ["bt", rzk], [kY])
                            mm(ps[bY][:, 128:256], kt[:, csl_], rzc, ["kt", rzk], [kY])
                            yield
                            tt_("dve", cat[:, 0:128], ps[bY][:, 0:128], c.triU_f[:], ALU.mult, [kY, "triU_f"], [ck])
                            tt_("dve", mrk[:], ps[bY][:, 128:256], c.triU_f[:], ALU.mult, [kY, "triU_f"], [mk])
                            k.op("pool", lambda e: e.tensor_copy(out=cat[:, 128:256], in_=bz[:, hl, :]), r=["bz"], w=[ck])
                            yield
                            mm(ps[bX][:, 0:256], X[:, 0:128], cat[:], [xk, ck], [kX])
                            mm(ps[bX][:, 256:512], X[:, 128:256], cat[:], [xk, ck], [kX])
                            yield
                            tt_("dve", RpT[:], ps[bX][:, 0:128], rzc, ALU.add, [kX, rzk], [rk_])
                            k.op("dve", lambda e: e.scalar_tensor_tensor(out=GT[:], in0=dsel[:, hl, :], scalar=pcl[:, q:q + 1],
                                                                         in1=ps[bX][:, 128:256], op0=ALU.mult, op1=ALU.add),
                                 r=["dsel", "pcl", kX], w=[gk_])
                            tt_("dve", MpT[:], ps[bX][:, 256:384], mrk[:], ALU.add, [kX, mk], [mpk])
                            tt_("dve", FT[:], ps[bX][:, 384:512], kz[:, hl, :], ALU.add, [kX, "kz"], [fk])
                            yield
                            vh = vtok[:, q, hl * 64:(hl + 1) * 64]
                            mm(ps[7][:, hl * 64:(hl + 1) * 64], RpT[:], Hb[:], [rk_, "Hb"], [("ps", 7)], start=True, stop=False)
                            mm(ps[7][:, hl * 64:(hl + 1) * 64], MpT[:], vh, [mpk, "vtok"], [("ps", 7)], start=False, stop=True)
                            mm(ps[0][:, 0:64], GT[:], Hb[:], [gk_, "Hb"], [("ps", 0)], start=(hl == 0), stop=False)
                            mm(ps[0][:, 0:64], FT[:], vh, [fk, "vtok"], [("ps", 0)], start=False, stop=(hl == 1))
                            yield

                        gens = [head_seq(0), head_seq(1)]
                        alive = [True, True]
                        while any(alive):
                            for gi in range(2):
                                if alive[gi]:
                                    try:
                                        next(gens[gi])
                                    except StopIteration:
                                        alive[gi] = False
                        act(Hb[:], ps[0][:, 0:64], AF.Copy, [("ps", 0)], ["Hb"])
                        y3 = ps[7][:, 0:128].rearrange("p (a b) -> p a b", a=2)
                        k.op("dve", lambda e: e.tensor_reduce(out=gst[:, 0:2], in_=y3, axis=AX.X, op=ALU.add), r=[("ps", 7)], w=["gst"])
                        act(ysq[:], ps[7][:, 0:128], AF.Square, [("ps", 7)], ["ysq", ("ps", 7)])
                        k.op("dve", lambda e: e.tensor_reduce(out=gst[:, 2:4], in_=ysq[:].rearrange("p (a b) -> p a b", a=2),
                                                              axis=AX.X, op=ALU.add), r=["ysq"], w=["gst"])
                        ts_("dve", gst[:, 0:4], gst[:, 0:4], 1.0 / 64, None, ALU.mult, ALU.bypass, ["gst"], ["gst"])
                        tt_("dve", gst[:, 4:6], gst[:, 0:2], gst[:, 0:2], ALU.mult, ["gst"], ["gst"])
                        tt_("dve", gst[:, 4:6], gst[:, 2:4], gst[:, 4:6], ALU.subtract, ["gst"], ["gst"])
                        ts_("dve", gst[:, 4:6], gst[:, 4:6], 64e-5, None, ALU.add, ALU.bypass, ["gst"], ["gst"])
                        act(gst[:, 4:6], gst[:, 4:6], AF.Sqrt, ["gst"], ["gst"])
                        k.op("dve", lambda e: e.reciprocal(out=gst[:, 6:8], in_=gst[:, 4:6]), r=["gst"], w=["gst"])
                        for hl in range(2):
                            ts_("dve", yn[:, hl * 64:(hl + 1) * 64], ps[7][:, hl * 64:(hl + 1) * 64], gst[:, hl:hl + 1],
                                gst[:, 6 + hl:7 + hl], ALU.subtract, ALU.mult, [("ps", 7), "gst"], ["yn"])
                        k.op("pe", lambda e: e.transpose(out=pbf[1][:, 512:640], in_=yn[:], identity=c.ident[:]),
                             r=["yn", "ident"], w=[("ps", 1)])
                        ts_("dve", z1[:], pbf[1][:, 512:640], pc(5), pc(6), ALU.mult, ALU.add, [("ps", 1), "rprm"], ["z1"])
                        tt_("dve", z1[:], z1[:], bv[:, csl_], ALU.add, ["z1", "bv"], ["z1"])
                        tt_("dve", ygT[:], z1[:], g_b[:, csl_], ALU.mult, ["z1", "g_b"], ["ygT"])
                        for dh in range(2):
                            mm(ps[6][:, :], ygT[:], wo[:, dh * 512:(dh + 1) * 512], ["ygT", "wo_hp"], [("ps", 6)])
                            tt_("dve", h[:, tok, dh * 512:(dh + 1) * 512], h[:, tok, dh * 512:(dh + 1) * 512], ps[6][:, :], ALU.add,
                                [("h", tok), ("ps", 6)], [("h", tok)])
                k.barrier()
        k.barrier()


def build(nc, stages="all", dbg=None):
    es = ExitStack()
    with es:
        dbgaps = {}
        for name, shape in (dbg or {}).items():
            dbgaps[name] = nc.dram_tensor("dbg_" + name, list(shape), BF16 if name in ("uT", "yT", "vk", "zz", "sT", "ETb", "EVt", "wb", "cm") else F32,
                                          kind="ExternalOutput").ap()
        c = setup_common(nc, es, dbgaps)
        peer_inputs(c)
        st = set(stages.split(","))
        if "all" in st:
            st = {"even", "peer0", "rwkv", "peer1"}
        if "even" in st:
            even_mixer(c, 0)
        if "peer0" in st:
            peer(c, 0)
        if "rwkv" in st:
            rwkv_mixer(c, 1)
        if "peer1" in st:
            peer(c, 1)
        if "h" in dbgaps:
            for tt in range(NT):
                c.k.dma("sp", dbgaps["h"][tsl(tt), :], c.h[:, tt, :], r=[("h", tt)], w=["dbg_h"])
        final_norm_store(c)
        print("instr counts", c.k.nins, "sems", c.k.nsem)
    return nc


PARAMS = ["norm_mix_g", "norm_ffn_g", "final_g", "e_w_in", "e_w_out", "s5_a_re", "s5_a_im", "s5_log_dt", "s5_b_re", "s5_b_im",
          "s5_c_re", "s5_c_im", "s5_d", "s5_w_glu", "gla_w_g2", "gla_b_g2", "gla_norm_g",
          "peer_w_q", "peer_sub_keys", "peer_u", "peer_v",
          "o_mu", "o_w_r", "o_w_k", "o_w_v", "o_w0", "o_w_w1", "o_w_w2", "o_a0", "o_w_a1", "o_w_a2", "o_w_g1", "o_w_g2",
          "o_k_k", "o_k_a", "o_r_k", "o_lnx_g", "o_lnx_b", "o_w_o"]


def core_inputs(inputs, b):
    m = {"x": np.ascontiguousarray(inputs["x"][b])}
    for n in PARAMS:
        a = np.asarray(inputs[n])
        if n in ("norm_mix_g", "norm_ffn_g", "peer_w_q", "peer_sub_keys", "peer_u", "peer_v"):
            pass
        elif n == "final_g":
            a = a.reshape(1, D)
        elif n in ("s5_d", "gla_b_g2", "gla_norm_g", "o_w0", "o_a0", "o_k_k", "o_k_a", "o_r_k", "o_lnx_g", "o_lnx_b"):
            a = a.reshape(1, -1)
        else:
            a = a[0]
        m[n] = np.ascontiguousarray(a)
    return m


def kernel(**inputs):
    n = 8
    nc = bass.Bass("TRN2", target_bir_lowering=False)
    build(nc)
    in_maps = [core_inputs(inputs, b) for b in range(n)]
    res = run_bass_kernel_spmd(nc, in_maps, core_ids=list(range(n)))
    return np.stack([r["out"] for r in res.results], axis=0)
```

```python
import numpy as np
from contextlib import ExitStack
import concourse.bass as bass
import concourse.mybir as mybir
from concourse.bass_utils import run_bass_kernel_spmd

F32 = mybir.dt.float32
BF16 = mybir.dt.bfloat16
U32 = mybir.dt.uint32
AF = mybir.ActivationFunctionType
ALU = mybir.AluOpType
AX = mybir.AxisListType

L = 2048
D = 1024
NT = L // 128
EPS = 1e-6
ENGS = ("pe", "dve", "act", "pool", "sp")


class MK:
    ROT = 1 << 30

    def __init__(self, nc, es):
        self.nc = nc
        self.es = es
        self.eng = dict(pe=nc.tensor, dve=nc.vector, act=nc.scalar, pool=nc.gpsimd, sp=nc.sync)
        self.esem = {}
        self.prev_ep = {}
        self.ecnt = {e: 0 for e in ENGS}
        self.seen = {e: {} for e in ENGS}
        self.lastw = {}
        self.readers = {}
        self.dsem = {}
        self.free_dsem = {}
        self.ndsem = 0
        self.nsem = 0
        self.nins = {e: 0 for e in ENGS}
        self.spare = [self._newsem("spare%d" % i) for i in range(6)]
        for e in ENGS:
            self._rot(e)

    def _newsem(self, name):
        self.nsem += 1
        return self.es.enter_context(self.nc.semaphore(name))

    def _rot(self, e):
        if e in self.esem and self.esem[e][2] > 0:
            self.prev_ep[e] = self.esem[e][:3]
        ep = self.esem[e][3] + 1 if e in self.esem else 0
        name = "s_%s_%d" % (e, ep)
        sem = self.spare.pop() if (ep > 0 and self.spare) else self._newsem(name)
        self.esem[e] = (name, sem, 0, ep)

    def _deps(self, r, w):
        d = {}

        def add(p):
            name, sem, c = p
            if name not in d or d[name][1] < c:
                d[name] = (sem, c)

        for k in r:
            if k in self.lastw:
                add(self.lastw[k])
        for k in w:
            if k in self.lastw:
                add(self.lastw[k])
            for n, (s, c) in self.readers.get(k, {}).items():
                add((n, s, c))
        return d

    def _wait(self, e, d):
        E = self.eng[e]
        seen = self.seen[e]
        for name, (sem, c) in d.items():
            if seen.get(name, 0) >= c:
                continue
            E.wait_ge(sem, c)
            seen[name] = c

    def _record(self, p, r, w):
        name, sem, c = p
        for k in w:
            self.lastw[k] = p
            self.readers[k] = {}
        for k in r:
            rd = self.readers.setdefault(k, {})
            if name not in rd or rd[name][1] < c:
                rd[name] = (sem, c)

    def op(self, e, fn, r=(), w=()):
        w = list(w) + [x for x in r if isinstance(x, tuple) and x and x[0] == "ps" and x not in w]
        d = self._deps(r, w)
        if e == "pe":
            d = {n: v for n, v in d.items() if not n.startswith("s_pe_")}
        self._wait(e, d)
        name, sem, cnt, ep = self.esem[e]
        if cnt >= self.ROT:
            self._rot(e)
            name, sem, cnt, ep = self.esem[e]
        ins = fn(self.eng[e])
        cnt += 1
        self.esem[e] = (name, sem, cnt, ep)
        ins.then_inc(sem, 1)
        self.nins[e] += 1
        self._record((name, sem, cnt), r, w)
        return ins

    def dma(self, e, out, in_, r=(), w=(), **kw):
        d = self._deps(r, w)
        self._wait(e, d)
        key = w[0] if len(w) else r[0]
        skey = ("dma", e, key)
        if skey not in self.dsem:
            if self.free_dsem.get(e):
                self.dsem[skey] = self.free_dsem[e].pop()
            else:
                name = "d%d" % self.ndsem
                self.ndsem += 1
                self.dsem[skey] = [name, self._newsem(name), 0]
        ent = self.dsem[skey]
        ins = self.eng[e].dma_start(out=out, in_=in_, **kw)
        ent[2] += 16
        ins.then_inc(ent[1], 16)
        self.nins[e] += 1
        self._record((ent[0], ent[1], ent[2]), r, w)
        return ins

    def barrier(self):
        d = {}
        for e in ENGS:
            name, sem, cnt, ep = self.esem[e]
            if cnt > 0:
                d[name] = (sem, cnt)
            elif e in self.prev_ep:
                pn, psem, pcnt = self.prev_ep[e]
                d[pn] = (psem, pcnt)
        for ent in self.dsem.values():
            if ent[2] > 0:
                d[ent[0]] = (ent[1], ent[2])
        for e in ENGS:
            dd = d
            if e == "pe":
                dd = {n: v for n, v in d.items() if not n.startswith("s_pe_")}
            self._wait(e, dd)
        for skey, ent in self.dsem.items():
            self.free_dsem.setdefault(skey[1], []).append(ent)
        self.dsem = {}


def tsl(tt):
    return slice(tt * 128, (tt + 1) * 128)


class Ctx:
    pass


SKIP = set()
CUT = [99.0]
GELU_FN = [AF.Gelu]
NTT = [NT]
NOINJ = [False]
INJV = [0]


def setup_common(nc, es, dbg):
    c = Ctx()
    c.nc = nc
    c.es = es
    c.dbg = dbg
    k = c.k = MK(nc, es)
    E = es.enter_context

    def dram_in(name, shape):
        return nc.dram_tensor(name, list(shape), F32, kind="ExternalInput").ap()

    c.din = dram_in
    c.x_d = dram_in("x", [L, D])
    c.out_d = nc.dram_tensor("out", [L, D], F32, kind="ExternalOutput").ap()
    c.norm_mix_g = dram_in("norm_mix_g", [2, D])
    c.norm_ffn_g = dram_in("norm_ffn_g", [2, D])
    c.final_g = dram_in("final_g", [1, D])

    used = {}

    def sb(name, shape, dt, st=None):
        n = used.get(name, 0)
        used[name] = n + 1
        nm = name if n == 0 else "%s_%d" % (name, n)
        return (st or es).enter_context(nc.sbuf_tensor(nm, list(shape), dt))

    c.sb = sb
    c.h = sb("h", [128, NT, D], F32)
    c.gbc = sb("gbc", [128, D], F32)
    c.ident = sb("ident", [128, 128], BF16)
    c.identf = sb("identf", [128, 128], F32)
    c.ones_f = sb("ones_f", [128, 128], F32)
    c.ones_b = sb("ones_b", [128, 128], BF16)
    c.onecol = sb("onecol", [128, 1], F32)
    c.ss = sb("ss", [128, NT], F32)
    c.rstd = sb("rstd", [128, NT], F32)
    c.junk = sb("junk", [128, D], BF16)
    c.xs = [sb("xs%d" % i, [128, D], BF16) for i in range(2)]
    c.ps = [E(nc.psum_tensor("ps%d" % i, [128, 512], F32)) for i in range(8)]
    c.pbf = [c.ps[i][:].bitcast(BF16) for i in range(8)]
    c.triU_f = sb("triU_f", [128, 128], F32)
    c.triU_b = sb("triU_b", [128, 128], BF16)

    k.op("pool", lambda e: e.memset(c.identf[:], 0.0), w=["identf"])
    k.op("pool", lambda e: e.affine_select(out=c.identf[:], in_=c.identf[:], pattern=[[-1, 128]],
                                            compare_op=ALU.not_equal, fill=1.0, base=0, channel_multiplier=1),
         r=["identf"], w=["identf"])
    k.op("dve", lambda e: e.tensor_copy(out=c.ident[:], in_=c.identf[:]), r=["identf"], w=["ident"])
    k.op("pool", lambda e: e.memset(c.ones_f[:], 1.0), w=["ones_f"])
    k.op("pool", lambda e: e.memset(c.ones_b[:], 1.0), w=["ones_b"])
    k.op("pool", lambda e: e.memset(c.onecol[:], 1.0), w=["onecol"])
    k.op("pool", lambda e: e.affine_select(out=c.triU_f[:], in_=c.ones_f[:], pattern=[[1, 128]],
                                            compare_op=ALU.is_ge, fill=0.0, base=0, channel_multiplier=-1),
         r=["ones_f"], w=["triU_f"])
    k.op("dve", lambda e: e.tensor_copy(out=c.triU_b[:], in_=c.triU_f[:]), r=["triU_f"], w=["triU_b"])
    for tt in range(NT):
        k.dma("sp", c.h[:, tt, :], c.x_d[tsl(tt), :], w=[("h", tt)])
    return c


def rms_stats(c):
    k = c.k
    for tt in range(NT):
        k.op("act", lambda e: e.activation(out=c.junk[:], in_=c.h[:, tt, :], func=AF.Square,
                                           accum_out=c.ss[:, tt:tt + 1]),
             r=[("h", tt)], w=["junk", ("ss", tt)])
    allss = [("ss", tt) for tt in range(NT)]
    k.op("dve", lambda e: e.tensor_scalar(out=c.rstd[:], in0=c.ss[:], scalar1=1.0 / D, scalar2=EPS,
                                          op0=ALU.mult, op1=ALU.add), r=allss, w=["rstd"])
    k.op("act", lambda e: e.activation(out=c.rstd[:], in_=c.rstd[:], func=AF.Sqrt), r=["rstd"], w=["rstd"])
    k.op("dve", lambda e: e.reciprocal(out=c.rstd[:], in_=c.rstd[:]), r=["rstd"], w=["rstd"])


def rmsnorm_T(c, g_ap, xT, tag, off=0):
    k = c.k
    k.dma("sp", c.gbc[:], g_ap.partition_broadcast(128), w=["gbc"])
    rms_stats(c)
    for tt in range(NT):
        xb = c.xs[tt % 2]
        xk = ("xs", tt % 2)
        k.op("dve", lambda e: e.scalar_tensor_tensor(out=xb[:], in0=c.h[:, tt, :], scalar=c.rstd[:, tt:tt + 1],
                                                     in1=c.gbc[:], op0=ALU.mult, op1=ALU.mult),
             r=[("h", tt), "rstd", "gbc"], w=[xk])
        b = 6 + tt % 2
        pst = c.pbf[b]
        pk = ("ps", b)
        for ch in range(8):
            k.op("pe", lambda e: e.transpose(out=pst[:, ch * 128:(ch + 1) * 128], in_=xb[:, ch * 128:(ch + 1) * 128],
                                             identity=c.ident[:]),
                 r=[xk, "ident"], w=[pk])
        k.op("act", lambda e: e.activation(out=xT[:, :, off + tt * 128:off + (tt + 1) * 128],
                                           in_=pst.rearrange("p (c t) -> p c t", c=8), func=AF.Copy),
             r=[pk], w=[(tag, tt)])


def final_norm_store(c):
    k = c.k
    k.dma("sp", c.gbc[:], c.final_g[0, :].partition_broadcast(128), w=["gbc"])
    rms_stats(c)
    for tt in range(NT):
        k.op("dve", lambda e: e.scalar_tensor_tensor(out=c.h[:, tt, :], in0=c.h[:, tt, :], scalar=c.rstd[:, tt:tt + 1],
                                                     in1=c.gbc[:], op0=ALU.mult, op1=ALU.mult),
             r=[("h", tt), "rstd", "gbc"], w=[("h", tt)])
        k.dma("sp", c.out_d[tsl(tt), :], c.h[:, tt, :], r=[("h", tt)], w=[("out", tt)])
    k._wait("sp", k._deps([("out", tt) for tt in range(NT)], []))


def cmul(k, eng_a, eng_b, o_re, o_im, a_re, a_im, b_re, b_im, t, rk, wk, tk):
    k.op(eng_a, lambda e: e.tensor_tensor(out=t[0], in0=a_re, in1=b_re, op=ALU.mult), r=rk, w=[tk + "0"])
    k.op(eng_a, lambda e: e.tensor_tensor(out=t[1], in0=a_im, in1=b_im, op=ALU.mult), r=rk, w=[tk + "1"])
    k.op(eng_b, lambda e: e.tensor_tensor(out=o_re, in0=t[0], in1=t[1], op=ALU.subtract),
         r=[tk + "0", tk + "1"], w=[wk + "_re"])
    k.op(eng_a, lambda e: e.tensor_tensor(out=t[0], in0=a_re, in1=b_im, op=ALU.mult), r=rk, w=[tk + "0"])
    k.op(eng_a, lambda e: e.tensor_tensor(out=t[1], in0=a_im, in1=b_re, op=ALU.mult), r=rk, w=[tk + "1"])
    k.op(eng_b, lambda e: e.tensor_tensor(out=o_im, in0=t[0], in1=t[1], op=ALU.add),
         r=[tk + "0", tk + "1"], w=[wk + "_im"])


def even_mixer(c, li):
    nc, k, sb, ps, pbf, h = c.nc, c.k, c.sb, c.ps, c.pbf, c.h
    din = c.din
    w_in_d = din("e_w_in", [D, 2064])
    w_out_d = din("e_w_out", [D, D])
    a_re_d = din("s5_a_re", [32, 64])
    a_im_d = din("s5_a_im", [32, 64])
    ldt_d = din("s5_log_dt", [32, 64])
    b_re_d = din("s5_b_re", [32, 64, 16])
    b_im_d = din("s5_b_im", [32, 64, 16])
    c_re_d = din("s5_c_re", [32, 16, 64])
    c_im_d = din("s5_c_im", [32, 16, 64])
    d_d = din("s5_d", [1, 512])
    wglu_d = din("s5_w_glu", [512, 512])
    wg2_d = din("gla_w_g2", [16, 256])
    bg2_d = din("gla_b_g2", [1, 256])
    gng_d = din("gla_norm_g", [1, 512])

    with ExitStack() as ph:
        xy = sb("xy", [128, 8, L], BF16, ph)
        uT = sb("uT", [128, 4, L], BF16, ph)
        xT = xy
        yT = xy
        with ExitStack() as pg:
            qkT = sb("qkT", [128, 4, L], BF16, pg)
            rT = sb("rT", [128, 4, L], BF16, pg)
            glowT = sb("glowT", [16, L], BF16, pg)
            vk = sb("vk", [128, NT, 768], BF16, pg)
            with ExitStack() as p1:
                wi = sb("wi", [128, 8, 1040], BF16, p1)
                rmsnorm_T(c, c.norm_mix_g[li, :], xT, "xT")
                allx = [("xT", tt) for tt in range(NT)]
                allw = [("wi", ch) for ch in range(8)]
                n = 0
                for piece in range(2):
                    cb = piece * 1024
                    ncol = 1024 if piece == 0 else 1040
                    for ch in range(8):
                        k.dma("pool", wi[:, ch, 0:ncol], w_in_d[ch * 128:(ch + 1) * 128, cb:cb + ncol], w=[("wi", ch)])
                    if piece == 0:
                        chunks = ([(i * 128, 128, uT, i, AF.Copy) for i in range(4)] +
                                  [(512 + i * 128, 128, qkT, i, AF.Copy) for i in range(4)])
                    else:
                        chunks = ([(1552 + i * 128, 128, rT, i, AF.Silu) for i in range(4)] +
                                  [(1536, 16, None, 0, AF.Copy)])
                    for (c0, m, dst, di, fn) in chunks:
                        for tb in range(4):
                            b = n % 4
                            n += 1
                            for ch in range(8):
                                k.op("pe", lambda e: e.matmul(ps[b][0:m, :], lhsT=wi[:, ch, c0 - cb:c0 - cb + m],
                                                              rhs=xT[:, ch, tb * 512:(tb + 1) * 512],
                                                              start=(ch == 0), stop=(ch == 7)),
                                     r=allx + allw, w=[("ps", b)])
                            if dst is None:
                                k.op("act", lambda e: e.activation(out=glowT[:, tb * 512:(tb + 1) * 512], in_=ps[b][0:16, :],
                                                                   func=AF.Copy), r=[("ps", b)], w=["glowT"])
                            else:
                                k.op("act", lambda e: e.activation(out=dst[:, di, tb * 512:(tb + 1) * 512], in_=ps[b][:, :],
                                                                   func=fn), r=[("ps", b)], w=[(dst.name, di)])
                    for tt in range(NT):
                        b0 = 4 + tt % 4
                        if piece == 0:
                            for ch in range(8):
                                k.op("pe", lambda e: e.matmul(ps[b0][:, 0:256], lhsT=xT[:, ch, tsl(tt)], rhs=wi[:, ch, 768:1024],
                                                              start=(ch == 0), stop=(ch == 7)), r=allx + allw, w=[("ps", b0)])
                            k.op("dve", lambda e: e.tensor_copy(out=vk[:, tt, 512:768], in_=ps[b0][:, 0:256]), r=[("ps", b0)],
                                 w=[("vk", tt)])
                        else:
                            for ch in range(8):
                                k.op("pe", lambda e: e.matmul(ps[b0][:, :], lhsT=xT[:, ch, tsl(tt)], rhs=wi[:, ch, 0:512],
                                                              start=(ch == 0), stop=(ch == 7)), r=allx + allw, w=[("ps", b0)])
                            k.op("dve", lambda e: e.tensor_copy(out=vk[:, tt, 0:512], in_=ps[b0][:, :]), r=[("ps", b0)],
                                 w=[("vk", tt)])
                k.barrier()
            if "uT" in c.dbg:
                for i in range(4):
                    k.dma("sp", c.dbg["uT"][i], uT[:, i, :], r=[("uT", i)], w=["dbg_uT"])
            if "vk" in c.dbg:
                k.dma("sp", c.dbg["vk"], vk[:, 3, :], r=[("vk", 3)], w=["dbg_vk"])
            if 'gla' not in SKIP:
                gla(c, pg, qkT, rT, glowT, vk, yT, wg2_d, bg2_d, gng_d)
            k.barrier()
        if 's5' not in SKIP:
            s5(c, ph, uT, yT, a_re_d, a_im_d, ldt_d, b_re_d, b_im_d, c_re_d, c_im_d, d_d, wglu_d)
        k.barrier()
        if "yT" in c.dbg:
            for i in range(8):
                k.dma("sp", c.dbg["yT"][i], yT[:, i, :], r=[("yT", i)], w=["dbg_yT"])
        with ExitStack() as p3:
            wo = sb("wo", [128, 8, D], BF16, p3)
            for ch in range(8):
                k.dma("pool", wo[:, ch, :], w_out_d[ch * 128:(ch + 1) * 128, :], w=[("wo", ch)])
            ally = [("yT", i) for i in range(8)]
            allw = [("wo", ch) for ch in range(8)]
            for tt in range(NT):
                for hf in range(2):
                    b = (tt * 2 + hf) % 4
                    for ch in range(8):
                        k.op("pe", lambda e: e.matmul(ps[b][:, :], lhsT=yT[:, ch, tsl(tt)],
                                                      rhs=wo[:, ch, hf * 512:(hf + 1) * 512],
                                                      start=(ch == 0), stop=(ch == 7)), r=ally + allw, w=[("ps", b)])
                    k.op("dve", lambda e: e.tensor_tensor(out=h[:, tt, hf * 512:(hf + 1) * 512],
                                                          in0=h[:, tt, hf * 512:(hf + 1) * 512], in1=ps[b][:, :],
                                                          op=ALU.add), r=[("ps", b), ("h", tt)], w=[("h", tt)])
            k.barrier()


def gla(c, ph, qkT, rT, glowT, vk, yT, wg2_d, bg2_d, gng_d):
    nc, k, sb, ps, pbf = c.nc, c.k, c.sb, c.ps, c.pbf
    with ExitStack() as p2:
        wg2 = sb("wg2", [16, 256], BF16, p2)
        bg2 = sb("bg2", [1, 256], BF16, p2)
        gng = sb("gng", [128, 4], F32, p2)
        triUs = sb("triUs", [128, 128], F32, p2)
        triRs = sb("triRs", [128, 128], F32, p2)
        lp = sb("lp", [128, 256], F32, p2)
        eend = sb("eend", [128, 256], F32, p2)
        ebT = sb("ebT", [128, 2, 128], F32, p2)
        enbT = sb("enbT", [128, 2, 128], F32, p2)
        kend = sb("kend", [128, 256], BF16, p2)
        qd = sb("qd", [128, 2, 128], BF16, p2)
        kd = sb("kd", [128, 2, 128], BF16, p2)
        qdz = sb("qdz", [128, 4, 128], BF16, p2)
        hmask = sb("hmask", [128, 2], F32, p2)
        attT = sb("attT", [128, 4, 128], BF16, p2)
        S = sb("S", [128, 2, 128], F32, p2)
        Sb = sb("Sb", [128, 2, 128], BF16, p2)
        ssq = sb("ssq", [128, 4], F32, p2)
        rso = sb("rso", [128, 4], F32, p2)
        on = sb("on", [128, 4, 128], BF16, p2)
        k.dma("pool", wg2[:], wg2_d[:, :], w=["wg2"])
        k.dma("pool", bg2[:], bg2_d[:, :], w=["bg2"])
        k.dma("sp", gng[:], gng_d[0, :].rearrange("(h v) -> v h", v=128), w=["gng"], allow_slow_non_contiguous=True)
        k.op("dve", lambda e: e.tensor_scalar(out=triUs[:], in0=c.triU_f[:], scalar1=-1.0 / 16, scalar2=None,
                                              op0=ALU.mult), r=["triU_f"], w=["triUs"])
        k.op("dve", lambda e: e.tensor_scalar(out=triRs[:], in0=c.triU_f[:], scalar1=1.0 / 16, scalar2=-1.0 / 16,
                                              op0=ALU.mult, op1=ALU.add), r=["triU_f"], w=["triRs"])
        k.op("pool", lambda e: e.memset(hmask[:], 0.0), w=["hmask"])
        k.op("pool", lambda e: e.memset(hmask[0:64, 0:1], 1.0), r=["hmask"], w=["hmask"])
        k.op("pool", lambda e: e.memset(hmask[64:128, 1:2], 1.0), r=["hmask"], w=["hmask"])
        k.op("pool", lambda e: e.memset(S[:], 0.0), w=["S"])
        k.op("pool", lambda e: e.memset(Sb[:], 0.0), w=["Sb"])
        for tt in range(NT):
            k.op("pe", lambda e: e.matmul(ps[0][:, 0:256], lhsT=glowT[:, tsl(tt)], rhs=wg2[:, :], start=True, stop=False),
                 r=["glowT", "wg2"], w=[("ps", 0)])
            k.op("pe", lambda e: e.matmul(ps[0][:, 0:256], lhsT=c.ones_b[0:1, :], rhs=bg2[:, :], start=False, stop=True),
                 r=["ones_b", "bg2"], w=[("ps", 0)])
            if CUT[0] <= 1:
                break
            k.op("act", lambda e: e.activation(out=lp[:], in_=ps[0][:, 0:256], func=AF.Exp, scale=-1.0),
                 r=[("ps", 0)], w=["lp"])
            k.op("act", lambda e: e.activation(out=lp[:], in_=lp[:], func=AF.Ln, bias=c.onecol[:], scale=1.0),
                 r=["lp", "onecol"], w=["lp"])
            if CUT[0] <= 2:
                break
            k.op("pe", lambda e: e.matmul(ps[1][:, 0:256], lhsT=triRs[:], rhs=lp[:], start=True, stop=True),
                 r=["triRs", "lp"], w=[("ps", 1)])
            for hf in range(2):
                k.op("pe", lambda e: e.matmul(ps[1][:, 256 + hf * 128:256 + (hf + 1) * 128],
                                              lhsT=lp[:, hf * 128:(hf + 1) * 128], rhs=triUs[:], start=True, stop=True),
                     r=["triUs", "lp"], w=[("ps", 1)])
            k.op("act", lambda e: e.activation(out=eend[:], in_=ps[1][:, 0:256], func=AF.Exp), r=[("ps", 1)], w=["eend"])
            k.op("act", lambda e: e.activation(out=ebT[:].rearrange("p a b -> p (a b)"), in_=ps[1][:, 256:512], func=AF.Exp),
                 r=[("ps", 1)], w=["ebT"])
            k.op("act", lambda e: e.activation(out=enbT[:].rearrange("p a b -> p (a b)"), in_=ps[1][:, 256:512], func=AF.Exp,
                                               scale=-1.0), r=[("ps", 1)], w=["enbT"])
            if CUT[0] <= 3:
                break
            k.op("dve", lambda e: e.tensor_tensor(out=kend[:], in0=vk[:, tt, 512:768], in1=eend[:], op=ALU.mult),
                 r=[("vk", tt), "eend"], w=["kend"])
            k.op("dve", lambda e: e.scalar_tensor_tensor(out=qd[:], in0=qkT[:, 0:2, tsl(tt)], scalar=0.125, in1=ebT[:],
                                                         op0=ALU.mult, op1=ALU.mult),
                 r=[("qkT", 0), ("qkT", 1), "ebT"], w=["qd"])
            k.op("dve", lambda e: e.tensor_tensor(out=kd[:], in0=qkT[:, 2:4, tsl(tt)], in1=enbT[:], op=ALU.mult),
                 r=[("qkT", 2), ("qkT", 3), "enbT"], w=["kd"])
            if CUT[0] <= 5:
                break
            for hd in range(4):
                pr = hd // 2
                k.op("dve", lambda e: e.tensor_scalar(out=qdz[:, hd, :], in0=qd[:, pr, :], scalar1=hmask[:, hd % 2:hd % 2 + 1],
                                                      scalar2=None, op0=ALU.mult), r=["qd", "hmask"], w=["qdz"])
            for hd in range(4):
                pr = hd // 2
                k.op("pe", lambda e: e.matmul(ps[2][:, hd * 128:(hd + 1) * 128], lhsT=kd[:, pr, :],
                                              rhs=qdz[:, hd, :], start=True, stop=True),
                     r=["kd", "qdz"], w=[("ps", 2)])
            if CUT[0] <= 5.3:
                break
            k.op("dve", lambda e: e.tensor_tensor(out=attT[:], in0=ps[2][:, :].rearrange("p (a b) -> p a b", a=4),
                                                  in1=c.triU_f[:].unsqueeze(1).to_broadcast([128, 4, 128]), op=ALU.mult),
                 r=[("ps", 2), "triU_f"], w=["attT"])
            if CUT[0] <= 5.6:
                break
            for hd in range(4):
                pr, p0 = hd // 2, (hd % 2) * 64
                k.op("pe", lambda e: e.matmul(ps[3][:, hd * 128:(hd + 1) * 128], lhsT=attT[:, hd, :],
                                              rhs=vk[:, tt, hd * 128:(hd + 1) * 128], start=True, stop=False),
                     r=["attT", ("vk", tt)], w=[("ps", 3)])
                k.op("pe", lambda e: e.matmul(ps[3][:, hd * 128:(hd + 1) * 128], lhsT=qdz[:, hd, :],
                                              rhs=Sb[:, pr, :], start=False, stop=True),
                     r=["qdz", "Sb"], w=[("ps", 3)])
            if CUT[0] <= 6:
                break
            for pr in range(2):
                k.op("pe", lambda e: e.matmul(ps[4][:, pr * 256:(pr + 1) * 256], lhsT=kend[:, pr * 128:(pr + 1) * 128],
                                              rhs=vk[:, tt, pr * 256:(pr + 1) * 256], start=True, stop=True),
                     r=["kend", ("vk", tt)], w=[("ps", 4)])
            for hd in range(4):
                pr, hf, p0 = hd // 2, hd % 2, (hd % 2) * 64
                k.op("dve", lambda e: e.scalar_tensor_tensor(
                    out=S[p0:p0 + 64, pr, :], in0=S[p0:p0 + 64, pr, :], scalar=ebT[p0:p0 + 64, pr, 127:128],
                    in1=ps[4][p0:p0 + 64, pr * 256 + hf * 128:pr * 256 + (hf + 1) * 128], op0=ALU.mult, op1=ALU.add),
                     r=["S", "ebT", ("ps", 4)], w=["S"])
            k.op("dve", lambda e: e.tensor_copy(out=Sb[:], in_=S[:]), r=["S"], w=["Sb"])
            if CUT[0] <= 7:
                break
            for hd in range(4):
                k.op("act", lambda e: e.activation(out=c.junk[:, 0:128], in_=ps[3][:, hd * 128:(hd + 1) * 128],
                                                   func=AF.Square, accum_out=ssq[:, hd:hd + 1]),
                     r=[("ps", 3)], w=["junk", "ssq"])
            k.op("dve", lambda e: e.tensor_scalar(out=rso[:], in0=ssq[:], scalar1=1.0 / 128, scalar2=EPS,
                                                  op0=ALU.mult, op1=ALU.add), r=["ssq"], w=["rso"])
            k.op("act", lambda e: e.activation(out=rso[:], in_=rso[:], func=AF.Sqrt), r=["rso"], w=["rso"])
            k.op("dve", lambda e: e.reciprocal(out=rso[:], in_=rso[:]), r=["rso"], w=["rso"])
            k.op("dve", lambda e: e.tensor_tensor(out=on[:], in0=ps[3][:, :].rearrange("p (a b) -> p a b", a=4),
                                                  in1=rso[:].unsqueeze(2).to_broadcast([128, 4, 128]), op=ALU.mult),
                 r=[("ps", 3), "rso"], w=["on"])
            for hd in range(4):
                k.op("pe", lambda e: e.transpose(out=pbf[5][:, hd * 128:(hd + 1) * 128], in_=on[:, hd, :],
                                                 identity=c.ident[:]), r=["on", "ident"], w=[("ps", 5)])
            for hd in range(4):
                k.op("dve", lambda e: e.scalar_tensor_tensor(
                    out=yT[:, 4 + hd, tsl(tt)], in0=pbf[5][:, hd * 128:(hd + 1) * 128], scalar=gng[:, hd:hd + 1],
                    in1=rT[:, hd, tsl(tt)], op0=ALU.mult, op1=ALU.mult),
                     r=[("ps", 5), "gng", ("rT", hd)], w=[("yT", 4 + hd)])


def s5(c, ph, uT, yT, a_re_d, a_im_d, ldt_d, b_re_d, b_im_d, c_re_d, c_im_d, d_d, wglu_d):
    nc, k, sb, ps, pbf = c.nc, c.k, c.sb, c.ps, c.pbf
    with ExitStack() as p2:
        wb = sb("wb", [128, 2, 4, 512], BF16, p2)
        cm = sb("cm", [128, 2, 16, 128], BF16, p2)
        with ExitStack() as pa:
            wbs = sb("wbs", [128, 2, 4, 512], F32, pa)
            cms = sb("cms", [128, 2, 16, 128], F32, pa)
            k.op("pool", lambda e: e.memset(wbs[:].rearrange("p a b c -> p (a b c)"), 0.0), w=["wbs"])
            k.op("pool", lambda e: e.memset(cms[:].rearrange("p a b c -> p (a b c)"), 0.0), w=["cms"])
            for ri, bd in enumerate((b_re_d, b_im_d)):
                for g in range(32):
                    kc, g8 = g // 8, g % 8
                    k.dma("sp", wbs[g8 * 16:(g8 + 1) * 16, ri, kc, g8 * 64:(g8 + 1) * 64],
                          bd[g].rearrange("p c -> c p"), r=[], w=["wbs"], allow_slow_non_contiguous=True)
            for ri, cd in enumerate((c_re_d, c_im_d)):
                for g in range(32):
                    ct, gl = g // 2, g % 2
                    g8 = g % 8
                    k.dma("sp", cms[gl * 64:(gl + 1) * 64, ri, ct, g8 * 16:(g8 + 1) * 16],
                          cd[g].rearrange("c p -> p c"), r=[], w=["cms"], allow_slow_non_contiguous=True)
            k.op("act", lambda e: e.activation(out=wb[:].rearrange("p a b c -> p (a b c)"),
                                               in_=wbs[:].rearrange("p a b c -> p (a b c)"), func=AF.Copy),
                 r=["wbs"], w=["wb"])
            k.op("act", lambda e: e.activation(out=cm[:, 0].rearrange("p b c -> p (b c)"),
                                               in_=cms[:, 0].rearrange("p b c -> p (b c)"), func=AF.Copy),
                 r=["cms"], w=["cm"])
            k.op("act", lambda e: e.activation(out=cm[:, 1].rearrange("p b c -> p (b c)"),
                                               in_=cms[:, 1].rearrange("p b c -> p (b c)"), func=AF.Copy, scale=-1.0),
                 r=["cms"], w=["cm"])
            k.barrier()
        if CUT[0] <= 10:
            return
        dcol = sb("dcol", [128, 4], F32, p2)
        k.dma("sp", dcol[:], d_d[0, :].rearrange("(c p) -> p c", p=128), w=["dcol"], allow_slow_non_contiguous=True)
        wglu = sb("wglu", [128, 4, 512], BF16, p2)
        for ch in range(4):
            k.dma("pool", wglu[:, ch, :], wglu_d[ch * 128:(ch + 1) * 128, :], w=["wglu"])
        ETb = sb("ETb", [128, 2, 16, 128], BF16, p2)
        EVt = sb("EVt", [128, 2, 2048], BF16, p2)
        a128 = sb("a128", [128, 2, 16], F32, p2)
        with ExitStack() as pb:
            prm = sb("prm", [16, 3, 128], F32, pb)
            for i, pd in enumerate((a_re_d, a_im_d, ldt_d)):
                k.dma("sp", prm[:, i, :], pd.rearrange("(ct gl) p -> ct (gl p)", gl=2), w=["prm"])
            for i in range(3):
                k.op("pe", lambda e: e.transpose(out=ps[0][:, i * 16:(i + 1) * 16], in_=prm[:, i, :], identity=c.identf[0:16, 0:16]),
                     r=["prm", "identf"], w=[("ps", 0)])
            if CUT[0] <= 10.5:
                return
            P = sb("P", [128, 24, 16], F32, pb)
            AR, AI, DT, MAG, TH, S_, C_, T0, T1, RM, FRE, FIM, NR, DEN, ABR, ABI, AVR, AVI = range(18)
            k.op("dve", lambda e: e.tensor_copy(out=P[:, 0:3, :], in_=ps[0][:, 0:48].rearrange("p (a b) -> p a b", a=3)),
                 r=[("ps", 0)], w=["P"])

            def tt_(o, a, b, op, eng="dve"):
                k.op(eng, lambda e: e.tensor_tensor(out=P[:, o, :], in0=P[:, a, :], in1=P[:, b, :], op=op), r=["P"], w=["P"])

            def act_(o, a, fn, scale=1.0):
                k.op("act", lambda e: e.activation(out=P[:, o, :], in_=P[:, a, :], func=fn, scale=scale), r=["P"], w=["P"])

            def ts_(o, a, s1, s2, op0, op1):
                k.op("dve", lambda e: e.tensor_scalar(out=P[:, o, :], in0=P[:, a, :], scalar1=s1, scalar2=s2, op0=op0, op1=op1),
                     r=["P"], w=["P"])

            if CUT[0] <= 11:
                return
            act_(DT, DT, AF.Exp)
            tt_(T0, DT, AR, ALU.mult)
            act_(MAG, T0, AF.Exp)
            act_(RM, T0, AF.Exp, scale=-1.0)
            tt_(TH, DT, AI, ALU.mult)
            act_(T0, TH, AF.Sin, scale=1.0 / 16)
            tt_(T0, T0, T0, ALU.mult)
            ts_(C_, T0, -2.0, 1.0, ALU.mult, ALU.add)
            act_(S_, TH, AF.Sin, scale=1.0 / 8)
            for _ in range(3):
                tt_(T0, C_, C_, ALU.mult)
                tt_(T1, S_, S_, ALU.mult)
                tt_(S_, S_, C_, ALU.mult)
                ts_(S_, S_, 2.0, None, ALU.mult, ALU.bypass)
                tt_(C_, T0, T1, ALU.subtract)
            tt_(ABR, MAG, C_, ALU.mult)
            tt_(ABI, MAG, S_, ALU.mult)
            tt_(AVR, RM, C_, ALU.mult)
            tt_(AVI, RM, S_, ALU.mult)
            ts_(AVI, AVI, -1.0, None, ALU.mult, ALU.bypass)
            ts_(NR, ABR, -1.0, None, ALU.add, ALU.bypass)
            tt_(T0, AR, AR, ALU.mult)
            tt_(T1, AI, AI, ALU.mult)
            tt_(DEN, T0, T1, ALU.add)
            k.op("dve", lambda e: e.reciprocal(out=P[:, DEN, :], in_=P[:, DEN, :]), r=["P"], w=["P"])
            tt_(T0, NR, AR, ALU.mult)
            tt_(T1, ABI, AI, ALU.mult)
            tt_(T0, T0, T1, ALU.add)
            tt_(FRE, T0, DEN, ALU.mult)
            tt_(T0, ABI, AR, ALU.mult)
            tt_(T1, NR, AI, ALU.mult)
            tt_(T0, T0, T1, ALU.subtract)
            tt_(FIM, T0, DEN, ALU.mult)
            if CUT[0] <= 12:
                return
            ET = sb("ET", [128, 2, 16, 128], F32, pb)
            EV = sb("EV", [128, 2, 16, 128], F32, pb)
            tmp = sb("s5tmp", [128, 2, 16, 64], F32, pb)
            pw = sb("s5pw", [128, 2, 16], F32, pb)
            pw2 = sb("s5pw2", [128, 2, 16], F32, pb)
            for (tab, br, bi, i0r, i0i, name) in ((ET, ABR, ABI, None, None, "ET"), (EV, AVR, AVI, FRE, FIM, "EV")):
                if i0r is None:
                    k.op("pool", lambda e: e.memset(tab[:, 0, :, 0:1], 1.0), w=[name])
                    k.op("pool", lambda e: e.memset(tab[:, 1, :, 0:1], 0.0), w=[name])
                else:
                    k.op("dve", lambda e: e.tensor_copy(out=tab[:, 0, :, 0:1], in_=P[:, i0r, :].unsqueeze(2)), r=["P"], w=[name])
                    k.op("dve", lambda e: e.tensor_copy(out=tab[:, 1, :, 0:1], in_=P[:, i0i, :].unsqueeze(2)), r=["P"], w=[name])
                k.op("dve", lambda e: e.tensor_copy(out=pw[:, 0, :], in_=P[:, br, :]), r=["P"], w=["pw"])
                k.op("dve", lambda e: e.tensor_copy(out=pw[:, 1, :], in_=P[:, bi, :]), r=["P"], w=["pw"])
                m = 1
                while m <= 128:
                    if m < 128:
                        bre = pw[:, 0, :].unsqueeze(2).to_broadcast([128, 16, m])
                        bim = pw[:, 1, :].unsqueeze(2).to_broadcast([128, 16, m])
                        cmul(k, "dve", "dve", tab[:, 0, :, m:2 * m], tab[:, 1, :, m:2 * m],
                             tab[:, 0, :, 0:m], tab[:, 1, :, 0:m], bre, bim,
                             (tmp[:, 0, :, 0:m], tmp[:, 1, :, 0:m]), [name, name + "_re", name + "_im", "pw"], name, "s5tmp")
                    elif name == "ET":
                        k.op("dve", lambda e: e.tensor_copy(out=a128[:], in_=pw[:]), r=["pw"], w=["a128"])
                    cmul(k, "dve", "dve", pw2[:, 0, :], pw2[:, 1, :], pw[:, 0, :], pw[:, 1, :], pw[:, 0, :], pw[:, 1, :],
                         (tmp[:, 0, :, 0], tmp[:, 1, :, 0]), ["pw"], "pw2", "s5tmp")
                    k.op("dve", lambda e: e.tensor_copy(out=pw[:], in_=pw2[:]), r=["pw2_re", "pw2_im"], w=["pw"])
                    m *= 2
            if CUT[0] <= 13:
                return
            n = 0
            for ri in range(2):
                for g4 in range(4):
                    b = n % 2
                    n += 1
                    for q in range(4):
                        ct = g4 * 4 + q
                        k.op("pe", lambda e: e.transpose(out=ps[b][:, q * 128:(q + 1) * 128], in_=EV[:, ri, ct, :],
                                                         identity=c.identf[:]), r=["EV", "EV_re", "EV_im", "identf"], w=[("ps", b)])
                    k.op("act", lambda e: e.activation(out=EVt[:, ri, g4 * 512:(g4 + 1) * 512], in_=ps[b][:, :], func=AF.Copy),
                         r=[("ps", b)], w=["EVt"])

            k.op("act", lambda e: e.activation(out=ETb[:].rearrange("p a b c -> p (a b c)"),
                                               in_=ET[:].rearrange("p a b c -> p (a b c)"), func=AF.Copy),
                 r=["ET", "ET_re", "ET_im"], w=["ETb"])
            k.barrier()
        if CUT[0] <= 14:
            return
        tmpc = sb("s5tmpc", [128, 2, 16], F32, p2)
        zz = sb("zz", [128, 2, 2048], BF16, p2)
        t1 = sb("s5t1", [128, 512], F32, p2)
        t2 = sb("s5t2", [128, 512], F32, p2)
        sT = sb("sT", [128, 2, 16, 128], BF16, p2)
        lastc = sb("lastc", [128, 2, 16], F32, p2)
        cz = sb("cz", [128, 2, 16], F32, p2)
        cz2 = sb("cz2", [128, 2, 16], F32, p2)
        wr = sb("s5wr", [128, 512], F32, p2)
        wi_ = sb("s5wi", [128, 512], F32, p2)
        ypre = sb("ypre", [128, 4, 128], F32, p2)
        if CUT[0] <= 15:
            return
        for tt in range(NTT[0]):
            for kc in range(4):
                for ri in range(2):
                    k.op("pe", lambda e: e.matmul(ps[ri][:, :], lhsT=uT[:, kc, tsl(tt)], rhs=wb[:, ri, kc, :],
                                                  start=True, stop=True), r=[("uT", kc), "wb"], w=[("ps", ri)])
                er = EVt[:, 0, kc * 512:(kc + 1) * 512]
                ei = EVt[:, 1, kc * 512:(kc + 1) * 512]
                k.op("dve", lambda e: e.tensor_tensor(out=t1[:], in0=ps[0][:, :], in1=er, op=ALU.mult),
                     r=[("ps", 0), "EVt"], w=["s5t1"])
                k.op("dve", lambda e: e.tensor_tensor(out=t2[:], in0=ps[1][:, :], in1=ei, op=ALU.mult),
                     r=[("ps", 1), "EVt"], w=["s5t2"])
                k.op("pool", lambda e: e.tensor_tensor(out=zz[:, 0, kc * 512:(kc + 1) * 512], in0=t1[:], in1=t2[:],
                                                       op=ALU.subtract), r=["s5t1", "s5t2"], w=["zz"])
                k.op("dve", lambda e: e.tensor_tensor(out=t1[:], in0=ps[1][:, :], in1=er, op=ALU.mult),
                     r=[("ps", 1), "EVt"], w=["s5t1"])
                k.op("dve", lambda e: e.tensor_tensor(out=t2[:], in0=ps[0][:, :], in1=ei, op=ALU.mult),
                     r=[("ps", 0), "EVt"], w=["s5t2"])
                k.op("pool", lambda e: e.tensor_tensor(out=zz[:, 1, kc * 512:(kc + 1) * 512], in0=t1[:], in1=t2[:],
                                                       op=ALU.add), r=["s5t1", "s5t2"], w=["zz"])
            if CUT[0] <= 16 and tt >= 1:
                return
            for g4 in range(4):
                for ri in range(2):
                    b = 2 + ri
                    for q in range(4):
                        ct = g4 * 4 + q
                        k.op("pe", lambda e: e.matmul(ps[b][:, q * 128:(q + 1) * 128], lhsT=zz[:, ri, ct * 128:(ct + 1) * 128],
                                                      rhs=c.triU_b[:], start=True, stop=True),
                             r=["zz", "triU_b"], w=[("ps", b)])
                cr = ps[2][:, :].rearrange("p (a b) -> p a b", a=4)
                ci = ps[3][:, :].rearrange("p (a b) -> p a b", a=4)
                etr = ETb[:, 0, g4 * 4:(g4 + 1) * 4, :]
                eti = ETb[:, 1, g4 * 4:(g4 + 1) * 4, :]
                t1v = t1[:].rearrange("p (a b) -> p a b", a=4)
                t2v = t2[:].rearrange("p (a b) -> p a b", a=4)
                wrv = wr[:].rearrange("p (a b) -> p a b", a=4)
                wiv = wi_[:].rearrange("p (a b) -> p a b", a=4)
                if tt == 0:
                    k.op("act", lambda e: e.activation(out=wrv, in_=cr, func=AF.Copy), r=[("ps", 2)], w=["wr"])
                    k.op("act", lambda e: e.activation(out=wiv, in_=ci, func=AF.Copy), r=[("ps", 3)], w=["wi_"])
                else:
                    k.op("dve", lambda e: e.tensor_tensor(out=wrv, in0=cr, in1=cz[:, 0, g4 * 4:(g4 + 1) * 4].unsqueeze(2).to_broadcast([128, 4, 128]),
                                                          op=ALU.add), r=[("ps", 2), "cz_re"], w=["wr"])
                    k.op("dve", lambda e: e.tensor_tensor(out=wiv, in0=ci, in1=cz[:, 1, g4 * 4:(g4 + 1) * 4].unsqueeze(2).to_broadcast([128, 4, 128]),
                                                          op=ALU.add), r=[("ps", 3), "cz_im"], w=["wi_"])
                k.op("act", lambda e: e.activation(out=lastc[:, 0, g4 * 4:(g4 + 1) * 4], in_=wrv[:, :, 127], func=AF.Copy),
                     r=["wr"], w=["lastc"])
                k.op("act", lambda e: e.activation(out=lastc[:, 1, g4 * 4:(g4 + 1) * 4], in_=wiv[:, :, 127], func=AF.Copy),
                     r=["wi_"], w=["lastc"])
                k.op("dve", lambda e: e.tensor_tensor(out=t1v, in0=wrv, in1=etr, op=ALU.mult), r=["wr", "ETb"], w=["s5t1"])
                k.op("pool", lambda e: e.tensor_tensor(out=t2v, in0=wiv, in1=eti, op=ALU.mult), r=["wi_", "ETb"], w=["s5t2"])
                k.op("dve", lambda e: e.tensor_tensor(out=sT[:, 0, g4 * 4:(g4 + 1) * 4, :], in0=t1v, in1=t2v, op=ALU.subtract),
                     r=["s5t1", "s5t2"], w=["sT"])
                k.op("dve", lambda e: e.tensor_tensor(out=t1v, in0=wiv, in1=etr, op=ALU.mult), r=["wi_", "ETb"], w=["s5t1"])
                k.op("pool", lambda e: e.tensor_tensor(out=t2v, in0=wrv, in1=eti, op=ALU.mult), r=["wr", "ETb"], w=["s5t2"])
                k.op("dve", lambda e: e.tensor_tensor(out=sT[:, 1, g4 * 4:(g4 + 1) * 4, :], in0=t1v, in1=t2v, op=ALU.add),
                     r=["s5t1", "s5t2"], w=["sT"])
            if tt < NT - 1:
                cmul(k, "dve", "dve", cz2[:, 0, :], cz2[:, 1, :], lastc[:, 0, :], lastc[:, 1, :], a128[:, 0, :], a128[:, 1, :],
                     (tmpc[:, 0, :], tmpc[:, 1, :]), ["lastc", "a128"], "cz2", "s5tmpc")
                k.op("dve", lambda e: e.tensor_copy(out=cz[:], in_=cz2[:]), r=["cz2_re", "cz2_im"], w=["cz_re", "cz_im"])
            if CUT[0] <= 19 and tt >= 1:
                return
            for kc in range(4):
                n = 0
                for q in range(4):
                    ct = kc * 4 + q
                    for ri in range(2):
                        k.op("pe", lambda e: e.matmul(ps[5][:, kc * 128:(kc + 1) * 128], lhsT=cm[:, ri, ct, :],
                                                      rhs=sT[:, ri, ct, :], start=(n == 0), stop=(n == 7)),
                             r=["cm", "sT"], w=[("ps", 5)])
                        n += 1
            if CUT[0] <= 19.3 and tt >= 1:
                return
            for kc in range(4):
                k.op("dve", lambda e: e.tensor_scalar(out=ypre[:, kc, :], in0=uT[:, kc, tsl(tt)], scalar1=dcol[:, kc:kc + 1],
                                                      scalar2=None, op0=ALU.mult), r=[("uT", kc), "dcol"], w=["ypre"])
                k.op("dve", lambda e: e.tensor_tensor(out=ypre[:, kc, :], in0=ypre[:, kc, :], in1=ps[5][:, kc * 128:(kc + 1) * 128],
                                                      op=ALU.add), r=["ypre", ("ps", 5)], w=["ypre"])
            if CUT[0] <= 19.6 and tt >= 1:
                return
            for kc in range(4):
                if GELU_FN[0] is None:
                    k.op("dve", lambda e: e.tensor_copy(out=yT[:, kc, tsl(tt)], in_=ypre[:, kc, :]), r=["ypre"], w=[("yT", kc)])
                else:
                    k.op("act", lambda e: e.activation(out=yT[:, kc, tsl(tt)], in_=ypre[:, kc, :], func=GELU_FN[0]), r=["ypre"],
                         w=[("yT", kc)])
        for nm, tl, kk in (("ypre", ypre, ["ypre"]), ("zz", zz, ["zz"]), ("sT", sT, ["sT"]), ("ETb", ETb, ["ETb"]), ("EVt", EVt, ["EVt"]),
                           ("wb", wb, ["wb"]), ("cm", cm, ["cm"]), ("a128", a128, ["a128"]), ("lastc", lastc, ["lastc"])):
            if nm in c.dbg:
                ap = tl[:]
                if len(ap.shape) == 3:
                    ap = ap.rearrange("p a b -> p (a b)")
                elif len(ap.shape) == 4:
                    ap = ap.rearrange("p a b c -> p (a b c)")
                k.dma("sp", c.dbg[nm], ap, r=kk, w=["dbg_" + nm])
        if CUT[0] <= 20:
            return
        sg = sb("sg", [128, 4, 512], BF16, p2)
        yk = [("yT", i) for i in range(4)]
        n = 0
        for tb in range(4):
            for c2 in range(4):
                b = 6 + n % 2
                n += 1
                for ch in range(4):
                    k.op("pe", lambda e: e.matmul(ps[b][:, :], lhsT=wglu[:, ch, c2 * 128:(c2 + 1) * 128],
                                                  rhs=yT[:, ch, tb * 512:(tb + 1) * 512], start=(ch == 0), stop=(ch == 3)),
                         r=["wglu"] + yk, w=[("ps", b)])
                k.op("act", lambda e: e.activation(out=sg[:, c2, :], in_=ps[b][:, :], func=AF.Sigmoid), r=[("ps", b)],
                     w=[("sg", c2)])
            for c2 in range(4):
                k.op("dve", lambda e: e.tensor_tensor(out=yT[:, c2, tb * 512:(tb + 1) * 512], in0=yT[:, c2, tb * 512:(tb + 1) * 512],
                                                      in1=sg[:, c2, :], op=ALU.mult), r=[("yT", c2), ("sg", c2)], w=[("yT", c2)])
        k.barrier()


def peer_inputs(c):
    c.wq_d = c.din("peer_w_q", [2, D, 2048])
    c.keys_d = c.din("peer_sub_keys", [2, 8, 2, 128, 128])
    c.u_d = c.din("peer_u", [2, 16384, D])
    c.v_d = c.din("peer_v", [2, 16384, D])
    c.ut_scr = c.nc.dram_tensor("ut_scr", [8, 128, 16384], BF16, kind="Internal").ap()
    c.vb_scr = c.nc.dram_tensor("vb_scr", [16384, D], BF16, kind="Internal").ap()


NEG = -1.0
PEER_EG = [32]
NPROD = 6
PEER_ACT_HEADS = [5]
PEER_PROD = [['act', 'pool', 'pool', 'act', 'pool', 'pool', 'act', 'pool']]


def peer(c, li):
    nc, k, sb, ps, h = c.nc, c.k, c.sb, c.ps, c.h
    wq_d, keys_d, u_d, v_d = c.wq_d[li], c.keys_d[li], c.u_d[li], c.v_d[li]
    utv = c.ut_scr.rearrange("dc d e -> d dc e")
    with ExitStack() as pp:
        usts = [sb("pust%d" % i, [128, 4, D], BF16, pp) for i in range(2)]
        uts = [sb("puts%d" % i, [128, 8, 512], BF16, pp) for i in range(2)]
        vbs_ = [sb("pvb%d" % i, [128, 4, D], BF16, pp) for i in range(2)]
        for eg in range(32):
            i = eg % 2
            uk, tk, vk_ = "pust%d" % i, "puts%d" % i, "pvb%d" % i
            k.dma("pool", usts[i][:], u_d[eg * 512:(eg + 1) * 512, :].rearrange("(a p) d -> p a d", p=128), w=[uk])
            k.dma("pool", vbs_[i][:], v_d[eg * 512:(eg + 1) * 512, :].rearrange("(a p) d -> p a d", p=128), w=[vk_])
            for dc in range(8):
                b = (eg * 8 + dc) % 4
                for a in range(4):
                    k.op("pe", lambda e: e.transpose(out=c.pbf[b][:, a * 128:(a + 1) * 128], in_=usts[i][:, a, dc * 128:(dc + 1) * 128],
                                                     identity=c.ident[:]), r=[uk, "ident"], w=[("ps", b)])
                k.op("act", lambda e: e.activation(out=uts[i][:, dc, :], in_=c.pbf[b][:, 0:512], func=AF.Copy), r=[("ps", b)],
                     w=[tk])
            k.dma("sp", utv[:, :, eg * 512:(eg + 1) * 512], uts[i][:], r=[tk], w=[("utscr", eg)])
            k.dma("sp", c.vb_scr[eg * 512:(eg + 1) * 512, :].rearrange("(a p) d -> p a d", p=128), vbs_[i][:], r=[vk_],
                  w=[("vbscr", eg)])
        k.barrier()
    with ExitStack() as ph:
        hnT = sb("hnT", [128, 8, L], BF16, ph)
        rmsnorm_T(c, c.norm_ffn_g[li, :], hnT, "hnT")
        allhn = [("hnT", tt) for tt in range(NT)]
        e_all = sb("e_all", [128, 4, 16, 128], F32, ph)
        phi = sb("phi", [128, 4, 8], F32, ph)
        mx = sb("pmx", [128, 16], F32, ph)
        t16 = sb("t16", [128, 16, 16], F32, ph)
        tmpb = sb("tmpb", [128, 128], F32, ph)
        cand = sb("cand", [128, 256], F32, ph)
        cand2 = sb("cand2", [128, 256], F32, ph)
        c16 = sb("c16", [128, 16], F32, ph)
        zs = sb("zs", [128, 8], F32, ph)
        for tg in range(4):
            with ExitStack() as p1:
                kT = sb("kT", [128, 16, 128], BF16, p1)
                with ExitStack() as p0:
                    kst = sb("kst", [128, 16, 128], F32, p0)
                    k.dma("sp", kst[:], keys_d.rearrange("h c n d -> n (h c) d"), w=["kst"])
                    for g in range(4):
                        b = g % 2
                        for q in range(4):
                            k.op("pe", lambda e: e.transpose(out=ps[b][:, q * 128:(q + 1) * 128], in_=kst[:, g * 4 + q, :],
                                                             identity=c.identf[:]), r=["kst", "identf"], w=[("ps", b)])
                        k.op("act", lambda e: e.activation(out=kT[:, g * 4:(g + 1) * 4, :].rearrange("p a b -> p (a b)"), in_=ps[b][:, :],
                                                           func=AF.Copy), r=[("ps", b)], w=["kT"])
                    k.barrier()

                wq = sb("wq", [128, 8, 2048], BF16, p1)
                qT = sb("qT", [128, 16, 512], BF16, p1)
                for ch in range(8):
                    k.dma("pool", wq[:, ch, :], wq_d[ch * 128:(ch + 1) * 128, :], w=[("wq", ch)])
                allwq = [("wq", ch) for ch in range(8)]
                for blk in range(16):
                    b = blk % 2
                    for ch in range(8):
                        k.op("pe", lambda e: e.matmul(ps[b][:, :], lhsT=wq[:, ch, blk * 128:(blk + 1) * 128],
                                                      rhs=hnT[:, ch, tg * 512:(tg + 1) * 512], start=(ch == 0), stop=(ch == 7)),
                             r=allwq + allhn, w=[("ps", b)])
                    k.op("act", lambda e: e.activation(out=qT[:, blk, :], in_=ps[b][:, :], func=AF.Copy), r=[("ps", b)],
                         w=[("qT", blk)])
                for tt in range(4):
                    for blk in range(16):
                        b = 2 + blk // 4
                        k.op("pe", lambda e: e.matmul(ps[b][:, (blk % 4) * 128:(blk % 4 + 1) * 128], lhsT=qT[:, blk, tt * 128:(tt + 1) * 128],
                                                      rhs=kT[:, blk, :], start=True, stop=True), r=[("qT", blk), "kT"], w=[("ps", b)])
                    for b4 in range(4):
                        k.op("dve", lambda e: e.tensor_reduce(out=mx[:, b4 * 4:(b4 + 1) * 4],
                                                              in_=ps[2 + b4][:, :].rearrange("p (a n) -> p a n", a=4),
                                                              axis=AX.X, op=ALU.max), r=[("ps", 2 + b4)], w=["pmx"])
                    k.op("dve", lambda e: e.tensor_scalar(out=mx[:], in0=mx[:], scalar1=-1.0, scalar2=None, op0=ALU.mult),
                         r=["pmx"], w=["pmx"])
                    for blk in range(16):
                        b = 2 + blk // 4
                        k.op("act", lambda e: e.activation(out=e_all[:, tt, blk, :], in_=ps[b][:, (blk % 4) * 128:(blk % 4 + 1) * 128],
                                                           func=AF.Exp, bias=mx[:, blk:blk + 1], scale=1.0),
                             r=[("ps", b), "pmx"], w=[("e_all", tt, blk)])
                    for blk in range(16):
                        ek = ("e_all", tt, blk)
                        k.op("dve", lambda e: e.max(out=t16[:, blk, 0:8], in_=e_all[:, tt, blk, :]), r=[ek], w=["t16"])
                        k.op("dve", lambda e: e.match_replace(out=tmpb[:], in_to_replace=t16[:, blk, 0:8],
                                                              in_values=e_all[:, tt, blk, :], imm_value=NEG),
                             r=[ek, "t16"], w=["tmpb"])
                        k.op("dve", lambda e: e.max(out=t16[:, blk, 8:16], in_=tmpb[:]), r=["tmpb"], w=["t16"])
                    for hd in range(8):
                        k.op("dve", lambda e: e.tensor_tensor(
                            out=cand[:].rearrange("p (i j) -> p i j", i=16),
                            in0=t16[:, 2 * hd, :].unsqueeze(2).to_broadcast([128, 16, 16]),
                            in1=t16[:, 2 * hd + 1, :].unsqueeze(1).to_broadcast([128, 16, 16]), op=ALU.mult),
                             r=["t16"], w=["cand"])
                        k.op("dve", lambda e: e.max(out=c16[:, 0:8], in_=cand[:]), r=["cand"], w=["c16"])
                        k.op("dve", lambda e: e.match_replace(out=cand2[:], in_to_replace=c16[:, 0:8], in_values=cand[:],
                                                              imm_value=NEG), r=["cand", "c16"], w=["cand2"])
                        k.op("dve", lambda e: e.max(out=c16[:, 8:16], in_=cand2[:]), r=["cand2"], w=["c16"])
                        k.op("dve", lambda e: e.tensor_scalar(out=phi[:, tt, hd:hd + 1], in0=c16[:, 15:16], scalar1=1.0 - 2e-6, scalar2=None,
                                                              op0=ALU.mult), r=["c16"], w=["phi"])
                        k.op("dve", lambda e: e.tensor_reduce(out=zs[:, hd:hd + 1], in_=c16[:], axis=AX.X, op=ALU.add),
                             r=["c16"], w=["zs"])
                    k.op("dve", lambda e: e.reciprocal(out=zs[:], in_=zs[:]), r=["zs"], w=["zs"])
                    k.op("dve", lambda e: e.tensor_tensor(out=phi[:, tt, :], in0=phi[:, tt, :], in1=zs[:], op=ALU.mult),
                         r=["phi", "zs"], w=["phi"])
                    for hd in range(8):
                        k.op("dve", lambda e: e.tensor_scalar(out=e_all[:, tt, 2 * hd, :], in0=e_all[:, tt, 2 * hd, :],
                                                              scalar1=zs[:, hd:hd + 1], scalar2=None, op0=ALU.mult),
                             r=["zs", ("e_all", tt, 2 * hd)], w=[("e_all", tt, 2 * hd)])
                k.barrier()
            with ExitStack() as p2:
                utsb = sb("utsb", [128, 8, 512], BF16, p2)
                vbs = [sb("vb%d" % i, [128, 4, D], BF16, p2) for i in range(2)]
                gels = [sb("gel%d" % i, [128, 4, 512], BF16, p2) for i in range(2)]
                Gs = [sb("G%d" % i, [128, 8, 512], BF16, p2) for i in range(2)]
                prod = [sb("prod%d" % i, [128, 512], F32, p2) for i in range(NPROD)]
                HsTs = [sb("HsT%d" % i, [128, 4, 128], BF16, p2) for i in range(2)]
                NEG_ = PEER_EG[0]
                steps = [(eg, tt) for eg in range(NEG_) for tt in range(4)]
                npr = [0]

                def load(eg):
                    k.dma("sp", utsb[:], utv[:, :, eg * 512:(eg + 1) * 512], r=[("utscr", eg)], w=["utsb"])
                    k.dma("sp", vbs[eg % 2][:], c.vb_scr[eg * 512:(eg + 1) * 512, :].rearrange("(a p) d -> p a d", p=128),
                          r=[("vbscr", eg)], w=["vb%d" % (eg % 2)])

                def hpart(eg, a):
                    b = a % 2
                    for dc in range(8):
                        k.op("pe", lambda e: e.matmul(ps[b][:, :], lhsT=utsb[:, dc, a * 128:(a + 1) * 128],
                                                      rhs=hnT[:, dc, tg * 512:(tg + 1) * 512], start=(dc == 0), stop=(dc == 7)),
                             r=["utsb"] + allhn, w=[("ps", b)])
                    k.op("act", lambda e: e.activation(out=gels[eg % 2][:, a, :], in_=ps[b][:, :], func=AF.Gelu), r=[("ps", b)],
                         w=[("gel%d" % (eg % 2), a)])

                def stage_a(s_):
                    eg, tt = steps[s_]
                    G, gk = Gs[s_ % 2], "G%d" % (s_ % 2)
                    for hd in range(8):
                        pr = prod[npr[0] % NPROD]
                        pk = "prod%d" % (npr[0] % NPROD)
                        npr[0] += 1
                        pe_ = PEER_PROD[0][hd]
                        if pe_ == 'act':
                            for a in range(4):
                                k.op("act", lambda e: e.activation(out=pr[:, a * 128:(a + 1) * 128], in_=e_all[:, tt, 2 * hd + 1, :],
                                                                   func=AF.Copy, scale=e_all[:, tt, 2 * hd, eg * 4 + a:eg * 4 + a + 1]),
                                     r=[("e_all", tt, 2 * hd), ("e_all", tt, 2 * hd + 1)], w=[pk])
                        else:
                            k.op(pe_, lambda e: e.tensor_tensor(
                                out=pr[:].rearrange("p (a n) -> p a n", a=4),
                                in0=e_all[:, tt, 2 * hd, eg * 4:(eg + 1) * 4].unsqueeze(2).to_broadcast([128, 4, 128]),
                                in1=e_all[:, tt, 2 * hd + 1, :].unsqueeze(1).to_broadcast([128, 4, 128]), op=ALU.mult),
                                 r=[("e_all", tt, 2 * hd), ("e_all", tt, 2 * hd + 1)], w=[pk])
                        k.op("dve", lambda e: e.scalar_tensor_tensor(out=G[:, hd, :], in0=pr[:], scalar=phi[:, tt, hd:hd + 1],
                                                                     in1=pr[:], op0=ALU.is_ge, op1=ALU.mult),
                             r=[pk, "phi"], w=[(gk, hd)])

                def stage_b(s_):
                    eg, tt = steps[s_]
                    G, gk, gb = Gs[s_ % 2], "G%d" % (s_ % 2), 4 + s_ % 2
                    for a in range(4):
                        for hd in range(8):
                            k.op("pe", lambda e: e.matmul(ps[gb][:, a * 128:(a + 1) * 128], lhsT=G[:, hd, a * 128:(a + 1) * 128],
                                                          rhs=c.ident[:], start=(hd == 0), stop=(hd == 7)),
                                 r=[(gk, hd), "ident"], w=[("ps", gb)])

                def stage_c(s_):
                    eg, tt = steps[s_]
                    gb = 4 + s_ % 2
                    k.op("dve", lambda e: e.tensor_tensor(out=HsTs[s_ % 2][:], in0=gels[eg % 2][:, :, tt * 128:(tt + 1) * 128],
                                                          in1=ps[gb][:, :].rearrange("p (a n) -> p a n", a=4), op=ALU.mult),
                         r=[("gel%d" % (eg % 2), a) for a in range(4)] + [("ps", gb)], w=["HsT%d" % (s_ % 2)])

                def obank(s_, dh):
                    return (2 + dh) if s_ % 2 == 0 else (6 + dh)

                def stage_d(s_):
                    eg, tt = steps[s_]
                    for dh in range(2):
                        ob = obank(s_, dh)
                        for a in range(4):
                            k.op("pe", lambda e: e.matmul(ps[ob][:, :], lhsT=HsTs[s_ % 2][:, a, :], rhs=vbs[eg % 2][:, a, dh * 512:(dh + 1) * 512],
                                                          start=(a == 0), stop=(a == 3)), r=["HsT%d" % (s_ % 2), "vb%d" % (eg % 2)],
                                 w=[("ps", ob)])

                def stage_e(s_):
                    eg, tt = steps[s_]
                    tok = tg * 4 + tt
                    for dh in range(2):
                        ob = obank(s_, dh)
                        k.op("dve", lambda e: e.tensor_tensor(out=h[:, tok, dh * 512:(dh + 1) * 512],
                                                              in0=h[:, tok, dh * 512:(dh + 1) * 512], in1=ps[ob][:, :],
                                                              op=ALU.add), r=[("h", tok), ("ps", ob)], w=[("h", tok)])

                if steps:
                    load(0)
                    for a in range(4):
                        hpart(0, a)
                    stage_a(0)
                for s_ in range(len(steps)):
                    eg, tt = steps[s_]
                    if tt == 0 and eg + 1 < NEG_:
                        load(eg + 1)
                    if s_ + 1 < len(steps):
                        stage_a(s_ + 1)
                    stage_b(s_)
                    stage_c(s_)
                    if eg + 1 < NEG_:
                        hpart(eg + 1, tt)
                    stage_d(s_)
                    if s_ >= 1:
                        stage_e(s_ - 1)
                if steps:
                    stage_e(len(steps) - 1)
                k.barrier()


RW_HP = [8]
RW_TB = [4]
RW_NQ = [4]


def rwkv_mixer(c, li):
    nc, k, sb, ps, pbf, h = c.nc, c.k, c.sb, c.ps, c.pbf, c.h
    din = c.din
    mu_d = din("o_mu", [6, D])
    wr_d, wk_d, wv_d = din("o_w_r", [D, D]), din("o_w_k", [D, D]), din("o_w_v", [D, D])
    w0_d = din("o_w0", [1, D])
    ww1_d, ww2_d = din("o_w_w1", [D, 64]), din("o_w_w2", [64, D])
    a0_d = din("o_a0", [1, D])
    wa1_d, wa2_d = din("o_w_a1", [D, 64]), din("o_w_a2", [64, D])
    wg1_d, wg2_d = din("o_w_g1", [D, 128]), din("o_w_g2", [128, D])
    kk_d, ka_d, rk_d = din("o_k_k", [1, D]), din("o_k_a", [1, D]), din("o_r_k", [1, D])
    lg_d, lb_d = din("o_lnx_g", [1, D]), din("o_lnx_b", [1, D])
    wo_d = din("o_w_o", [D, D])

    def mm(out, lhsT, rhs, r, w, start=True, stop=True):
        k.op("pe", lambda e: e.matmul(out, lhsT=lhsT, rhs=rhs, start=start, stop=stop), r=r, w=w)

    def tt_(eng, out, in0, in1, op, r, w):
        k.op(eng, lambda e: e.tensor_tensor(out=out, in0=in0, in1=in1, op=op), r=r, w=w)

    def ts_(eng, out, in0, s1, s2, op0, op1, r, w):
        k.op(eng, lambda e: e.tensor_scalar(out=out, in0=in0, scalar1=s1, scalar2=s2, op0=op0, op1=op1), r=r, w=w)

    def act(out, in_, fn, r, w, **kw):
        k.op("act", lambda e: e.activation(out=out, in_=in_, func=fn, **kw), r=r, w=w)

    with ExitStack() as ph:
        xp = sb("xp", [128, 8, L + 2], BF16, ph)
        k.op("pool", lambda e: e.memset(xp[:, :, 0:1], 0.0), w=["xp0"])
        rmsnorm_T(c, c.norm_mix_g[li, :], xp, "xp", off=1)
        allx = [("xp", tt) for tt in range(NT)] + ["xp0"]
        mjt = sb("mjt", [128, 128], F32, ph)
        mtj = sb("mtj", [128, 128], F32, ph)
        k.op("pool", lambda e: e.affine_select(out=mjt[:], in_=c.ones_f[:], pattern=[[1, 128]], compare_op=ALU.is_gt,
                                                fill=0.0, base=0, channel_multiplier=-1), r=["ones_f"], w=["mjt"])
        k.op("pool", lambda e: e.affine_select(out=mtj[:], in_=c.ones_f[:], pattern=[[-1, 128]], compare_op=ALU.is_gt,
                                                fill=0.0, base=0, channel_multiplier=1), r=["ones_f"], w=["mtj"])
        hm = sb("hm", [128, 2], F32, ph)
        bones = sb("bones", [128, 128], BF16, ph)
        cmask = sb("cmask", [128, 2, 128], F32, ph)
        dsel = sb("dsel", [128, 2, 128], F32, ph)
        k.op("pool", lambda e: e.memset(hm[:], 0.0), w=["hm"])
        k.op("pool", lambda e: e.memset(hm[0:64, 0:1], 1.0), r=["hm"], w=["hm"])
        k.op("pool", lambda e: e.memset(hm[64:128, 1:2], 1.0), r=["hm"], w=["hm"])
        k.op("pool", lambda e: e.memset(bones[:], 0.0), w=["bones"])
        k.op("pool", lambda e: e.memset(bones[0:64, 0:64], 1.0), r=["bones"], w=["bones"])
        k.op("pool", lambda e: e.memset(bones[64:128, 64:128], 1.0), r=["bones"], w=["bones"])
        k.op("pool", lambda e: e.memset(cmask[:].rearrange("p a b -> p (a b)"), 0.0), w=["cmask"])
        k.op("pool", lambda e: e.memset(cmask[:, 0, 0:64], 1.0), r=["cmask"], w=["cmask"])
        k.op("pool", lambda e: e.memset(cmask[:, 1, 64:128], 1.0), r=["cmask"], w=["cmask"])
        for hl in range(2):
            ts_("dve", dsel[:, hl, :], c.identf[:], hm[:, hl:hl + 1], None, ALU.mult, ALU.bypass, ["identf", "hm"], ["dsel"])
        chm = sb("chm", [128, 512], BF16, ph)
        k.op("pool", lambda e: e.memset(chm[:], 1.0), w=["chm"])
        k.op("pool", lambda e: e.memset(chm[:].rearrange("p (a b) -> p a b", a=4)[:, :, 0:1], 0.0), r=["chm"], w=["chm"])
        prm = sb("rprm", [128, 7, 8], F32, ph)
        for i, pd in enumerate((w0_d, a0_d, kk_d, ka_d, rk_d, lg_d, lb_d)):
            k.dma("sp", prm[:, i, :], pd[0, :].rearrange("(dc p) -> p dc", p=128), w=["rprm"], allow_slow_non_contiguous=True)
        mucol = sb("mucol", [128, 6, 8], F32, ph)
        k.dma("sp", mucol[:], mu_d.rearrange("i (dc p) -> p i dc", p=128), w=["mucol"], allow_slow_non_contiguous=True)
        MU = dict(r=0, w=1, k=2, v=3, a=4, g=5)

        def load_split(name, wd, cols, ncol, mui, st):
            w0 = sb(name + "0", [128, 8, ncol], BF16, st)
            wm = sb(name + "m", [128, 8, ncol], BF16, st)
            wp = sb(name + "p", [128, 8, ncol], BF16, st)
            k.dma("pool", w0[:], wd[:, cols].rearrange("(dc p) n -> p dc n", p=128), w=[name + "0"])
            tt_("dve", wm[:], w0[:], mucol[:, mui, :].unsqueeze(2).to_broadcast([128, 8, ncol]), ALU.mult,
                [name + "0", "mucol"], [name + "m"])
            tt_("dve", wp[:], w0[:], wm[:], ALU.subtract, [name + "0", name + "m"], [name + "p"])
            return wp, wm

        ww2 = sb("ww2", [128, D], BF16, ph)
        wa2 = sb("wa2", [128, D], BF16, ph)
        wg2 = sb("wg2r", [128, D], BF16, ph)
        k.dma("pool", ww2[0:64, :], ww2_d[:, :], w=["ww2"])
        k.dma("pool", wa2[0:64, :], wa2_d[:, :], w=["wa2"])
        k.dma("pool", wg2[:], wg2_d[:, :], w=["wg2r"])
        tw1 = sb("tw1", [128, L], BF16, ph)
        ta1 = sb("ta1", [128, L], BF16, ph)
        tg1 = sb("tg1", [128, L], BF16, ph)

        def proj(out_ps, wp, wm, c0, m, t0, n, keys):
            for dc in range(8):
                mm(out_ps, wp[:, dc, c0:c0 + m], xp[:, dc, 1 + t0:1 + t0 + n], allx + keys, [("ps", 0)], start=(dc == 0), stop=False)
            for dc in range(8):
                mm(out_ps, wm[:, dc, c0:c0 + m], xp[:, dc, t0:t0 + n], allx + keys, [("ps", 0)], start=False, stop=(dc == 7))

        with ExitStack() as p0:
            w1p, w1m = load_split("ww1", ww1_d, slice(0, 64), 64, MU["w"], p0)
            a1p, a1m = load_split("wa1", wa1_d, slice(0, 64), 64, MU["a"], p0)
            g1p, g1m = load_split("wg1", wg1_d, slice(0, 128), 128, MU["g"], p0)
            for tb in range(4):
                t0 = tb * 512
                proj(ps[0][0:64, :], w1p, w1m, 0, 64, t0, 512, ["ww1p", "ww1m"])
                act(tw1[0:64, t0:t0 + 512], ps[0][0:64, :], AF.Tanh, [("ps", 0)], ["tw1"])
                proj(ps[0][0:64, :], a1p, a1m, 0, 64, t0, 512, ["wa1p", "wa1m"])
                act(ta1[0:64, t0:t0 + 512], ps[0][0:64, :], AF.Copy, [("ps", 0)], ["ta1"])
                proj(ps[0][:, :], g1p, g1m, 0, 128, t0, 512, ["wg1p", "wg1m"])
                act(tg1[:, t0:t0 + 512], ps[0][:, :], AF.Sigmoid, [("ps", 0)], ["tg1"])
            k.barrier()
        W = 512
        f32t = lambda n, st: sb(n, [128, W], F32, st)
        b16t = lambda n, st: sb(n, [128, W], BF16, st)
        for hp in range(RW_HP[0]):
            with ExitStack() as p1:
                cols = slice(hp * 128, (hp + 1) * 128)
                wrp, wrm = load_split("wr", wr_d, cols, 128, MU["r"], p1)
                wkp, wkm = load_split("wk", wk_d, cols, 128, MU["k"], p1)
                wvp, wvm = load_split("wv", wv_d, cols, 128, MU["v"], p1)
                wo = sb("wo_hp", [128, D], BF16, p1)
                k.dma("pool", wo[:], wo_d[hp * 128:(hp + 1) * 128, :], w=["wo_hp"])
                pc = lambda i: prm[:, i, hp:hp + 1]
                Hb = sb("Hb", [128, 64], BF16, p1)
                k.op("pool", lambda e: e.memset(Hb[:], 0.0), w=["Hb"])
                r_b, k_b, v_b, g_b = b16t("r_b", p1), b16t("k_b", p1), b16t("v_b", p1), b16t("g_b", p1)
                vtok = sb("vtok", [128, 4, 128], BF16, p1)
                lw, cs, asig = f32t("lw", p1), f32t("cs", p1), f32t("asig", p1)
                e1, e2, e3, e4 = f32t("e1", p1), f32t("e2", p1), f32t("e3", p1), f32t("e4", p1)
                kkn, kmod, b_ = f32t("kkn", p1), f32t("kmod", p1), f32t("bb", p1)
                sq = b16t("sq", p1)
                rt, rz0, rz1, az0, az1 = b16t("rt", p1), b16t("rz0", p1), b16t("rz1", p1), b16t("az0", p1), b16t("az1", p1)
                at_, bt, kt, bh, kh = b16t("at", p1), b16t("bt", p1), b16t("kt", p1), b16t("bh", p1), b16t("kh", p1)
                bv = f32t("bv", p1)
                pcl = sb("pcl", [128, 4], F32, p1)
                rz, az = (rz0, rz1), (az0, az1)
                Xs = [sb("X%d" % i, [128, 256], BF16, p1) for i in range(2)]
                MNs = [[sb("MN%d_%d" % (j, i), [128, 256], BF16, p1) for i in range(2)] for j in range(2)]
                cats = [sb("cat%d" % i, [128, 256], BF16, p1) for i in range(2)]
                mrks = [sb("mrk%d" % i, [128, 128], F32, p1) for i in range(2)]
                btok = sb("btok", [128, 2, 128], BF16, p1)
                bz = sb("bz", [128, 2, 128], BF16, p1)
                kz = sb("kz", [128, 2, 128], BF16, p1)
                RGMF = [[sb("%s%d" % (n, i), [128, 128], BF16, p1) for n in ("RpT", "GT", "MpT", "FT")] for i in range(2)]
                gst = sb("gst", [128, 8], F32, p1)
                ysq = sb("ysq", [128, 128], F32, p1)
                yn = sb("yn", [128, 128], BF16, p1)
                z1 = sb("z1", [128, 128], F32, p1)
                ygT = sb("ygT", [128, 128], BF16, p1)
                pending_tail = [None]
                for tb in range(RW_TB[0]):
                    t0 = tb * W
                    if pending_tail[0] is not None:
                        for _ in pending_tail[0]:
                            pass
                        pending_tail[0] = None
                    proj(ps[0][:, :], wrp, wrm, 0, 128, t0, W, ["wrp", "wrm"])
                    act(r_b[:], ps[0][:, :], AF.Copy, [("ps", 0)], ["r_b"])
                    proj(ps[0][:, :], wkp, wkm, 0, 128, t0, W, ["wkp", "wkm"])
                    act(k_b[:], ps[0][:, :], AF.Copy, [("ps", 0)], ["k_b"])
                    proj(ps[0][:, :], wvp, wvm, 0, 128, t0, W, ["wvp", "wvm"])
                    act(v_b[:], ps[0][:, :], AF.Copy, [("ps", 0)], ["v_b"])
                    for q in range(4):
                        tq = t0 + q * 128
                        for dc in range(8):
                            mm(ps[6][:, q * 128:(q + 1) * 128], xp[:, dc, 1 + tq:1 + tq + 128], wvp[:, dc, :], allx + ["wvp"],
                               [("ps", 6)], start=(dc == 0), stop=False)
                        for dc in range(8):
                            mm(ps[6][:, q * 128:(q + 1) * 128], xp[:, dc, tq:tq + 128], wvm[:, dc, :], allx + ["wvm"],
                               [("ps", 6)], start=False, stop=(dc == 7))
                    k.op("dve", lambda e: e.tensor_copy(out=vtok[:].rearrange("p a b -> p (a b)"), in_=ps[6][:, :]),
                         r=[("ps", 6)], w=["vtok"])
                    mm(ps[0][:, :], ww2[0:64, cols], tw1[0:64, t0:t0 + W], ["ww2", "tw1"], [("ps", 0)])
                    act(lw[:], ps[0][:, :], AF.Sigmoid, [("ps", 0), "rprm"], ["lw"], bias=pc(0), scale=1.0)
                    ts_("dve", lw[:], lw[:], -0.6065306597126334, None, ALU.mult, ALU.bypass, ["lw"], ["lw"])
                    mm(ps[0][:, :], wa2[0:64, cols], ta1[0:64, t0:t0 + W], ["wa2", "ta1"], [("ps", 0)])
                    act(asig[:], ps[0][:, :], AF.Sigmoid, [("ps", 0), "rprm"], ["asig"], bias=pc(1), scale=1.0)
                    mm(ps[0][:, :], wg2[:, cols], tg1[:, t0:t0 + W], ["wg2r", "tg1"], [("ps", 0)])
                    act(g_b[:], ps[0][:, :], AF.Copy, [("ps", 0)], ["g_b"])
                    k.op("dve", lambda e: e.tensor_tensor_scan(out=cs[:], data0=chm[:], data1=lw[:], initial=0.0,
                                                               op0=ALU.mult, op1=ALU.add), r=["chm", "lw"], w=["cs"])
                    cs3 = cs[:].rearrange("p (a b) -> p a b", a=4)
                    csl = cs3[:, :, 127:128].to_broadcast([128, 4, 128])
                    act(e1[:], cs[:], AF.Exp, ["cs"], ["e1"])
                    act(e2[:], cs[:], AF.Exp, ["cs"], ["e2"], scale=-1.0)
                    tt_("dve", e3[:], cs[:], lw[:], ALU.subtract, ["cs", "lw"], ["e3"])
                    act(e3[:], e3[:], AF.Exp, ["e3"], ["e3"])
                    tt_("dve", e4[:].rearrange("p (a b) -> p a b", a=4), csl, cs3, ALU.subtract, ["cs"], ["e4"])
                    act(e4[:], e4[:], AF.Exp, ["e4"], ["e4"])
                    act(pcl[:], cs3[:, :, 127], AF.Exp, ["cs"], ["pcl"])
                    ts_("dve", kkn[:], k_b[:], pc(2), None, ALU.mult, ALU.bypass, ["k_b", "rprm"], ["kkn"])
                    tt_("dve", sq[:], kkn[:], kkn[:], ALU.mult, ["kkn"], ["sq"])
                    mm(ps[0][:, :], bones[:], sq[:], ["bones", "sq"], [("ps", 0)])
                    act(kmod[:], ps[0][:, :], AF.Sqrt, [("ps", 0)], ["kmod"])
                    ts_("dve", kmod[:], kmod[:], 1e-12, None, ALU.max, ALU.bypass, ["kmod"], ["kmod"])
                    k.op("dve", lambda e: e.reciprocal(out=kmod[:], in_=kmod[:]), r=["kmod"], w=["kmod"])
                    tt_("dve", kkn[:], kkn[:], kmod[:], ALU.mult, ["kkn", "kmod"], ["kkn"])
                    ts_("dve", kmod[:], asig[:], -1.0, pc(3), ALU.add, ALU.mult, ["asig", "rprm"], ["kmod"])
                    ts_("dve", kmod[:], kmod[:], 1.0, None, ALU.add, ALU.bypass, ["kmod"], ["kmod"])
                    tt_("dve", kmod[:], kmod[:], k_b[:], ALU.mult, ["kmod", "k_b"], ["kmod"])
                    tt_("dve", b_[:], kkn[:], asig[:], ALU.mult, ["kkn", "asig"], ["bb"])
                    tt_("dve", rt[:], r_b[:], e1[:], ALU.mult, ["r_b", "e1"], ["rt"])
                    tt_("pool", kt[:], kmod[:], e2[:], ALU.mult, ["kmod", "e2"], ["kt"])
                    tt_("dve", bt[:], b_[:], e2[:], ALU.mult, ["bb", "e2"], ["bt"])
                    tt_("pool", kh[:], kmod[:], e4[:], ALU.mult, ["kmod", "e4"], ["kh"])
                    tt_("dve", bh[:], b_[:], e4[:], ALU.mult, ["bb", "e4"], ["bh"])
                    tt_("pool", e3[:], kkn[:], e3[:], ALU.mult, ["kkn", "e3"], ["e3"])
                    ts_("dve", at_[:], e3[:], -1.0, None, ALU.mult, ALU.bypass, ["e3"], ["at"])
                    for hl in range(2):
                        ts_("dve", rz[hl][:], rt[:], hm[:, hl:hl + 1], None, ALU.mult, ALU.bypass, ["rt", "hm"], ["rz%d" % hl])
                        ts_("pool", az[hl][:], at_[:], hm[:, hl:hl + 1], None, ALU.mult, ALU.bypass, ["at", "hm"], ["az%d" % hl])
                    tt_("dve", e1[:], r_b[:], kmod[:], ALU.mult, ["r_b", "kmod", "e1", "rt"], ["e1"])
                    ts_("dve", sq[:], e1[:], pc(4), None, ALU.mult, ALU.bypass, ["e1", "rprm", "sq"], ["sq"])
                    mm(ps[0][:, :], bones[:], sq[:], ["bones", "sq"], [("ps", 0)])
                    tt_("dve", bv[:], ps[0][:, :], v_b[:], ALU.mult, [("ps", 0), "v_b"], ["bv"])
                    for q in range(RW_NQ[0]):
                        csl_ = slice(q * 128, (q + 1) * 128)
                        tok = tb * 4 + q
                        k.op("pe", lambda e: e.transpose(out=pbf[1][:, 0:128], in_=bh[:, csl_], identity=c.ident[:]),
                             r=["bh", "ident"], w=[("ps", 1)])
                        k.op("pe", lambda e: e.transpose(out=pbf[1][:, 128:256], in_=kh[:, csl_], identity=c.ident[:]),
                             r=["kh", "ident"], w=[("ps", 1)])
                        for hl in range(2):
                            tt_("dve", bz[:, hl, :], pbf[1][:, 0:128], cmask[:, hl, :], ALU.mult, [("ps", 1), "cmask"], ["bz"])
                            tt_("dve", kz[:, hl, :], pbf[1][:, 128:256], cmask[:, hl, :], ALU.mult, [("ps", 1), "cmask"], ["kz"])
                        def head_seq(hl):
                            azc, rzc = az[hl][:, csl_], rz[hl][:, csl_]
                            azk, rzk = "az%d" % hl, "rz%d" % hl
                            bX, bY = 2 + 2 * hl, 3 + 2 * hl
                            kX, kY = ("ps", bX), ("ps", bY)
                            X, MN, cat, mrk = Xs[hl], MNs[hl], cats[hl], mrks[hl]
                            RpT, GT, MpT, FT = RGMF[hl]
                            xk, ck, mk = "X%d" % hl, "cat%d" % hl, "mrk%d" % hl
                            rk_, gk_, mpk, fk = ("RpT%d" % hl, "GT%d" % hl, "MpT%d" % hl, "FT%d" % hl)
                            tp = pbf[1][:, 256 + hl * 128:384 + hl * 128]
                            mm(ps[bX][:, 0:128], bt[:, csl_], azc, ["bt", azk], [kX])
                            mm(ps[bX][:, 128:256], azc, bt[:, csl_], ["bt", azk], [kX])
                            mm(ps[bX][:, 256:384], azc, kt[:, csl_], ["kt", azk], [kX])
                            k.op("pe", lambda e: e.transpose(out=tp, in_=azc, identity=c.ident[:]), r=[azk, "ident"], w=[("ps", 1)])
                            yield
                            tt_("dve", MN[0][:, 0:128], ps[bX][:, 0:128], mjt[:], ALU.mult, [kX, "mjt"], ["MN0_%d" % hl])
                            tt_("dve", MN[0][:, 128:256], ps[bX][:, 128:256], mtj[:], ALU.mult, [kX, "mtj"], ["MN0_%d" % hl])
                            tt_("dve", X[:, 128:256], ps[bX][:, 256:384], mtj[:], ALU.mult, [kX, "mtj"], [xk])
                            act(X[:, 0:128], tp, AF.Copy, [("ps", 1)], [xk])
                            yield
                            for i in range(7):
                                cur, nxt = MN[i % 2], MN[(i + 1) % 2]
                                ck_, nk = "MN%d_%d" % (i % 2, hl), "MN%d_%d" % ((i + 1) % 2, hl)
                                mm(ps[bY][:, 0:256], cur[:, 0:128], X[:], [ck_, xk], [kY])
                                if i < 6:
                                    mm(ps[bY][:, 256:384], cur[:, 128:256], cur[:, 0:128], [ck_], [kY])
                                    mm(ps[bY][:, 384:512], cur[:, 0:128], cur[:, 128:256], [ck_], [kY])
                                yield
                                if i < 6:
                                    act(nxt[:], ps[bY][:, 256:512], AF.Copy, [kY], [nk, kY])
                                tt_("dve", X[:], X[:], ps[bY][:, 0:256], ALU.add, [xk, kY], [xk, kY])
                                yield
                            mm(ps[bY][:, 0:128], bt[:, csl_], rzc, ["bt", rzk], [kY])
                            mm(ps[bY][:, 128:256], kt[:, csl_], rzc, ["kt", rzk], [kY])
                            yield
                            tt_("dve", cat[:, 0:128], ps[bY][:, 0:128], c.triU_f[:], ALU.mult, [kY, "triU_f"], [ck])
                            tt_("dve", mrk[:], ps[bY][:, 128:256], c.triU_f[:], ALU.mult, [kY, "triU_f"], [mk])
                            k.op("pool", lambda e: e.tensor_copy(out=cat[:, 128:256], in_=bz[:, hl, :]), r=["bz"], w=[ck])
                            yield
                            mm(ps[bX][:, 0:256], X[:, 0:128], cat[:], [xk, ck], [kX])
                            mm(ps[bX][:, 256:512], X[:, 128:256], cat[:], [xk, ck], [kX])
                            yield
                            tt_("dve", RpT[:], ps[bX][:, 0:128], rzc, ALU.add, [kX, rzk], [rk_])
                            k.op("dve", lambda e: e.scalar_tensor_tensor(out=GT[:], in0=dsel[:, hl, :], scalar=pcl[:, q:q + 1],
                                                                         in1=ps[bX][:, 128:256], op0=ALU.mult, op1=ALU.add),
                                 r=["dsel", "pcl", kX], w=[gk_])
                            tt_("dve", MpT[:], ps[bX][:, 256:384], mrk[:], ALU.add, [kX, mk], [mpk])
                            tt_("dve", FT[:], ps[bX][:, 384:512], kz[:, hl, :], ALU.add, [kX, "kz"], [fk])
                            yield
                            vh = vtok[:, q, hl * 64:(hl + 1) * 64]
                            mm(ps[7][:, hl * 64:(hl + 1) * 64], RpT[:], Hb[:], [rk_, "Hb"], [("ps", 7)], start=True, stop=False)
                            mm(ps[7][:, hl * 64:(hl + 1) * 64], MpT[:], vh, [mpk, "vtok"], [("ps", 7)], start=False, stop=True)
                            mm(ps[0][:, 0:64], GT[:], Hb[:], [gk_, "Hb"], [("ps", 0)], start=(hl == 0), stop=False)
                            mm(ps[0][:, 0:64], FT[:], vh, [fk, "vtok"], [("ps", 0)], start=False, stop=(hl == 1))
                            yield

                        gens = [head_seq(0), head_seq(1)]
                        if pending_tail[0] is not None:
                            gens.append(pending_tail[0])
                            pending_tail[0] = None
                        alive = [True] * len(gens)
                        while any(alive):
                            for gi in range(len(gens)):
                                if alive[gi]:
                                    try:
                                        next(gens[gi])
                                    except StopIteration:
                                        alive[gi] = False
                        act(Hb[:], ps[0][:, 0:64], AF.Copy, [("ps", 0)], ["Hb"])

                        def tail_seq(q=q, csl_=csl_, tok=tok):
                            yield
                            y3 = ps[7][:, 0:128].rearrange("p (a b) -> p a b", a=2)
                            yield
                            k.op("dve", lambda e: e.tensor_reduce(out=gst[:, 0:2], in_=y3, axis=AX.X, op=ALU.add), r=[("ps", 7)], w=["gst"])
                            yield
                            act(ysq[:], ps[7][:, 0:128], AF.Square, [("ps", 7)], ["ysq", ("ps", 7)])
                            yield
                            k.op("dve", lambda e: e.tensor_reduce(out=gst[:, 2:4], in_=ysq[:].rearrange("p (a b) -> p a b", a=2),
                                                                  axis=AX.X, op=ALU.add), r=["ysq"], w=["gst"])
                            yield
                            ts_("dve", gst[:, 0:4], gst[:, 0:4], 1.0 / 64, None, ALU.mult, ALU.bypass, ["gst"], ["gst"])
                            yield
                            tt_("dve", gst[:, 4:6], gst[:, 0:2], gst[:, 0:2], ALU.mult, ["gst"], ["gst"])
                            yield
                            tt_("dve", gst[:, 4:6], gst[:, 2:4], gst[:, 4:6], ALU.subtract, ["gst"], ["gst"])
                            yield
                            ts_("dve", gst[:, 4:6], gst[:, 4:6], 64e-5, None, ALU.add, ALU.bypass, ["gst"], ["gst"])
                            yield
                            act(gst[:, 4:6], gst[:, 4:6], AF.Sqrt, ["gst"], ["gst"])
                            yield
                            k.op("dve", lambda e: e.reciprocal(out=gst[:, 6:8], in_=gst[:, 4:6]), r=["gst"], w=["gst"])
                            yield
                            for hl in range(2):
                                ts_("dve", yn[:, hl * 64:(hl + 1) * 64], ps[7][:, hl * 64:(hl + 1) * 64], gst[:, hl:hl + 1],
                                    gst[:, 6 + hl:7 + hl], ALU.subtract, ALU.mult, [("ps", 7), "gst"], ["yn"])
                            yield
                            k.op("pe", lambda e: e.transpose(out=pbf[1][:, 512:640], in_=yn[:], identity=c.ident[:]),
                                 r=["yn", "ident"], w=[("ps", 1)])
                            yield
                            ts_("dve", z1[:], pbf[1][:, 512:640], pc(5), pc(6), ALU.mult, ALU.add, [("ps", 1), "rprm"], ["z1"])
                            yield
                            tt_("dve", z1[:], z1[:], bv[:, csl_], ALU.add, ["z1", "bv"], ["z1"])
                            yield
                            tt_("dve", ygT[:], z1[:], g_b[:, csl_], ALU.mult, ["z1", "g_b"], ["ygT"])
                            yield
                            for dh in range(2):
                                mm(ps[6][:, :], ygT[:], wo[:, dh * 512:(dh + 1) * 512], ["ygT", "wo_hp"], [("ps", 6)])
                                tt_("dve", h[:, tok, dh * 512:(dh + 1) * 512], h[:, tok, dh * 512:(dh + 1) * 512], ps[6][:, :], ALU.add,
                                    [("h", tok), ("ps", 6)], [("h", tok)])
                            yield

                        pending_tail[0] = tail_seq()

                if pending_tail[0] is not None:
                    for _ in pending_tail[0]:
                        pass
                    pending_tail[0] = None
                k.barrier()
        k.barrier()


def build(nc, stages="all", dbg=None):
    es = ExitStack()
    with es:
        dbgaps = {}
        for name, shape in (dbg or {}).items():
            dbgaps[name] = nc.dram_tensor("dbg_" + name, list(shape), BF16 if name in ("uT", "yT", "vk", "zz", "sT", "ETb", "EVt", "wb", "cm") else F32,
                                          kind="ExternalOutput").ap()
        c = setup_common(nc, es, dbgaps)
        peer_inputs(c)
        st = set(stages.split(","))
        if "all" in st:
            st = {"even", "peer0", "rwkv", "peer1"}
        if "even" in st:
            even_mixer(c, 0)
        if "peer0" in st:
            peer(c, 0)
        if "rwkv" in st:
            rwkv_mixer(c, 1)
        if "peer1" in st:
            peer(c, 1)
        if "h" in dbgaps:
            for tt in range(NT):
                c.k.dma("sp", dbgaps["h"][tsl(tt), :], c.h[:, tt, :], r=[("h", tt)], w=["dbg_h"])
        final_norm_store(c)
        print("instr counts", c.k.nins, "sems", c.k.nsem)
    return nc


PARAMS = ["norm_mix_g", "norm_ffn_g", "final_g", "e_w_in", "e_w_out", "s5_a_re", "s5_a_im", "s5_log_dt", "s5_b_re", "s5_b_im",
          "s5_c_re", "s5_c_im", "s5_d", "s5_w_glu", "gla_w_g2", "gla_b_g2", "gla_norm_g",
          "peer_w_q", "peer_sub_keys", "peer_u", "peer_v",
          "o_mu", "o_w_r", "o_w_k", "o_w_v", "o_w0", "o_w_w1", "o_w_w2", "o_a0", "o_w_a1", "o_w_a2", "o_w_g1", "o_w_g2",
          "o_k_k", "o_k_a", "o_r_k", "o_lnx_g", "o_lnx_b", "o_w_o"]


def core_inputs(inputs, b):
    m = {"x": np.ascontiguousarray(inputs["x"][b])}
    for n in PARAMS:
        a = np.asarray(inputs[n])
        if n in ("norm_mix_g", "norm_ffn_g", "peer_w_q", "peer_sub_keys", "peer_u", "peer_v"):
            pass
        elif n == "final_g":
            a = a.reshape(1, D)
        elif n in ("s5_d", "gla_b_g2", "gla_norm_g", "o_w0", "o_a0", "o_k_k", "o_k_a", "o_r_k", "o_lnx_g", "o_lnx_b"):
            a = a.reshape(1, -1)
        else:
            a = a[0]
        m[n] = np.ascontiguousarray(a)
    return m


def kernel(**inputs):
    n = 8
    nc = bass.Bass("TRN2", target_bir_lowering=False)
    build(nc)
    in_maps = [core_inputs(inputs, b) for b in range(n)]
    res = run_bass_kernel_spmd(nc, in_maps, core_ids=list(range(n)))
    return np.stack([r["out"] for r in res.results], axis=0)
```

```python
import numpy as np
from contextlib import ExitStack
import concourse.bass as bass
import concourse.mybir as mybir
from concourse.bass_utils import run_bass_kernel_spmd

F32 = mybir.dt.float32
BF16 = mybir.dt.bfloat16
U32 = mybir.dt.uint32
AF = mybir.ActivationFunctionType
ALU = mybir.AluOpType
AX = mybir.AxisListType

L = 2048
D = 1024
NT = L // 128
EPS = 1e-6
ENGS = ("pe", "dve", "act", "pool", "sp")


class MK:
    ROT = 1 << 30

    def __init__(self, nc, es):
        self.nc = nc
        self.es = es
        self.eng = dict(pe=nc.tensor, dve=nc.vector, act=nc.scalar, pool=nc.gpsimd, sp=nc.sync)
        self.esem = {}
        self.prev_ep = {}
        self.ecnt = {e: 0 for e in ENGS}
        self.seen = {e: {} for e in ENGS}
        self.lastw = {}
        self.readers = {}
        self.dsem = {}
        self.free_dsem = {}
        self.ndsem = 0
        self.nsem = 0
        self.nins = {e: 0 for e in ENGS}
        self.spare = [self._newsem("spare%d" % i) for i in range(6)]
        for e in ENGS:
            self._rot(e)

    def _newsem(self, name):
        self.nsem += 1
        return self.es.enter_context(self.nc.semaphore(name))

    def _rot(self, e):
        if e in self.esem and self.esem[e][2] > 0:
            self.prev_ep[e] = self.esem[e][:3]
        ep = self.esem[e][3] + 1 if e in self.esem else 0
        name = "s_%s_%d" % (e, ep)
        sem = self.spare.pop() if (ep > 0 and self.spare) else self._newsem(name)
        self.esem[e] = (name, sem, 0, ep)

    def _deps(self, r, w):
        d = {}

        def add(p):
            name, sem, c = p
            if name not in d or d[name][1] < c:
                d[name] = (sem, c)

        for k in r:
            if k in self.lastw:
                add(self.lastw[k])
        for k in w:
            if k in self.lastw:
                add(self.lastw[k])
            for n, (s, c) in self.readers.get(k, {}).items():
                add((n, s, c))
        return d

    def _wait(self, e, d):
        E = self.eng[e]
        seen = self.seen[e]
        for name, (sem, c) in d.items():
            if seen.get(name, 0) >= c:
                continue
            E.wait_ge(sem, c)
            seen[name] = c

    def _record(self, p, r, w):
        name, sem, c = p
        for k in w:
            self.lastw[k] = p
            self.readers[k] = {}
        for k in r:
            rd = self.readers.setdefault(k, {})
            if name not in rd or rd[name][1] < c:
                rd[name] = (sem, c)

    def op(self, e, fn, r=(), w=()):
        w = list(w) + [x for x in r if isinstance(x, tuple) and x and x[0] == "ps" and x not in w]
        d = self._deps(r, w)
        if e == "pe":
            d = {n: v for n, v in d.items() if not n.startswith("s_pe_")}
        self._wait(e, d)
        name, sem, cnt, ep = self.esem[e]
        if cnt >= self.ROT:
            self._rot(e)
            name, sem, cnt, ep = self.esem[e]
        ins = fn(self.eng[e])
        cnt += 1
        self.esem[e] = (name, sem, cnt, ep)
        ins.then_inc(sem, 1)
        self.nins[e] += 1
        self._record((name, sem, cnt), r, w)
        return ins

    def dma(self, e, out, in_, r=(), w=(), **kw):
        d = self._deps(r, w)
        self._wait(e, d)
        key = w[0] if len(w) else r[0]
        skey = ("dma", e, key)
        if skey not in self.dsem:
            if self.free_dsem.get(e):
                self.dsem[skey] = self.free_dsem[e].pop()
            else:
                name = "d%d" % self.ndsem
                self.ndsem += 1
                self.dsem[skey] = [name, self._newsem(name), 0]
        ent = self.dsem[skey]
        ins = self.eng[e].dma_start(out=out, in_=in_, **kw)
        ent[2] += 16
        ins.then_inc(ent[1], 16)
        self.nins[e] += 1
        self._record((ent[0], ent[1], ent[2]), r, w)
        return ins

    def barrier(self):
        d = {}
        for e in ENGS:
            name, sem, cnt, ep = self.esem[e]
            if cnt > 0:
                d[name] = (sem, cnt)
            elif e in self.prev_ep:
                pn, psem, pcnt = self.prev_ep[e]
                d[pn] = (psem, pcnt)
        for ent in self.dsem.values():
            if ent[2] > 0:
                d[ent[0]] = (ent[1], ent[2])
        for e in ENGS:
            dd = d
            if e == "pe":
                dd = {n: v for n, v in d.items() if not n.startswith("s_pe_")}
            self._wait(e, dd)
        for skey, ent in self.dsem.items():
            self.free_dsem.setdefault(skey[1], []).append(ent)
        self.dsem = {}


def tsl(tt):
    return slice(tt * 128, (tt + 1) * 128)


class Ctx:
    pass


SKIP = set()
CUT = [99.0]
GELU_FN = [AF.Gelu]
NTT = [NT]
NOINJ = [False]
INJV = [0]


def setup_common(nc, es, dbg):
    c = Ctx()
    c.nc = nc
    c.es = es
    c.dbg = dbg
    k = c.k = MK(nc, es)
    E = es.enter_context

    def dram_in(name, shape):
        return nc.dram_tensor(name, list(shape), F32, kind="ExternalInput").ap()

    c.din = dram_in
    c.x_d = dram_in("x", [L, D])
    c.out_d = nc.dram_tensor("out", [L, D], F32, kind="ExternalOutput").ap()
    c.norm_mix_g = dram_in("norm_mix_g", [2, D])
    c.norm_ffn_g = dram_in("norm_ffn_g", [2, D])
    c.final_g = dram_in("final_g", [1, D])

    used = {}

    def sb(name, shape, dt, st=None):
        n = used.get(name, 0)
        used[name] = n + 1
        nm = name if n == 0 else "%s_%d" % (name, n)
        return (st or es).enter_context(nc.sbuf_tensor(nm, list(shape), dt))

    c.sb = sb
    c.h = sb("h", [128, NT, D], F32)
    c.gbc = sb("gbc", [128, D], F32)
    c.ident = sb("ident", [128, 128], BF16)
    c.identf = sb("identf", [128, 128], F32)
    c.ones_f = sb("ones_f", [128, 128], F32)
    c.ones_b = sb("ones_b", [128, 128], BF16)
    c.onecol = sb("onecol", [128, 1], F32)
    c.ss = sb("ss", [128, NT], F32)
    c.rstd = sb("rstd", [128, NT], F32)
    c.junk = sb("junk", [128, D], BF16)
    c.xs = [sb("xs%d" % i, [128, D], BF16) for i in range(2)]
    c.ps = [E(nc.psum_tensor("ps%d" % i, [128, 512], F32)) for i in range(8)]
    c.pbf = [c.ps[i][:].bitcast(BF16) for i in range(8)]
    c.triU_f = sb("triU_f", [128, 128], F32)
    c.triU_b = sb("triU_b", [128, 128], BF16)

    k.op("pool", lambda e: e.memset(c.identf[:], 0.0), w=["identf"])
    k.op("pool", lambda e: e.affine_select(out=c.identf[:], in_=c.identf[:], pattern=[[-1, 128]],
                                            compare_op=ALU.not_equal, fill=1.0, base=0, channel_multiplier=1),
         r=["identf"], w=["identf"])
    k.op("dve", lambda e: e.tensor_copy(out=c.ident[:], in_=c.identf[:]), r=["identf"], w=["ident"])
    k.op("pool", lambda e: e.memset(c.ones_f[:], 1.0), w=["ones_f"])
    k.op("pool", lambda e: e.memset(c.ones_b[:], 1.0), w=["ones_b"])
    k.op("pool", lambda e: e.memset(c.onecol[:], 1.0), w=["onecol"])
    k.op("pool", lambda e: e.affine_select(out=c.triU_f[:], in_=c.ones_f[:], pattern=[[1, 128]],
                                            compare_op=ALU.is_ge, fill=0.0, base=0, channel_multiplier=-1),
         r=["ones_f"], w=["triU_f"])
    k.op("dve", lambda e: e.tensor_copy(out=c.triU_b[:], in_=c.triU_f[:]), r=["triU_f"], w=["triU_b"])
    for tt in range(NT):
        k.dma("sp", c.h[:, tt, :], c.x_d[tsl(tt), :], w=[("h", tt)])
    return c


def rms_stats(c):
    k = c.k
    for tt in range(NT):
        k.op("act", lambda e: e.activation(out=c.junk[:], in_=c.h[:, tt, :], func=AF.Square,
                                           accum_out=c.ss[:, tt:tt + 1]),
             r=[("h", tt)], w=["junk", ("ss", tt)])
    allss = [("ss", tt) for tt in range(NT)]
    k.op("dve", lambda e: e.tensor_scalar(out=c.rstd[:], in0=c.ss[:], scalar1=1.0 / D, scalar2=EPS,
                                          op0=ALU.mult, op1=ALU.add), r=allss, w=["rstd"])
    k.op("act", lambda e: e.activation(out=c.rstd[:], in_=c.rstd[:], func=AF.Sqrt), r=["rstd"], w=["rstd"])
    k.op("dve", lambda e: e.reciprocal(out=c.rstd[:], in_=c.rstd[:]), r=["rstd"], w=["rstd"])


def rmsnorm_T(c, g_ap, xT, tag, off=0):
    k = c.k
    k.dma("sp", c.gbc[:], g_ap.partition_broadcast(128), w=["gbc"])
    rms_stats(c)
    for tt in range(NT):
        xb = c.xs[tt % 2]
        xk = ("xs", tt % 2)
        k.op("dve", lambda e: e.scalar_tensor_tensor(out=xb[:], in0=c.h[:, tt, :], scalar=c.rstd[:, tt:tt + 1],
                                                     in1=c.gbc[:], op0=ALU.mult, op1=ALU.mult),
             r=[("h", tt), "rstd", "gbc"], w=[xk])
        b = 6 + tt % 2
        pst = c.pbf[b]
        pk = ("ps", b)
        for ch in range(8):
            k.op("pe", lambda e: e.transpose(out=pst[:, ch * 128:(ch + 1) * 128], in_=xb[:, ch * 128:(ch + 1) * 128],
                                             identity=c.ident[:]),
                 r=[xk, "ident"], w=[pk])
        k.op("act", lambda e: e.activation(out=xT[:, :, off + tt * 128:off + (tt + 1) * 128],
                                           in_=pst.rearrange("p (c t) -> p c t", c=8), func=AF.Copy),
             r=[pk], w=[(tag, tt)])


def final_norm_store(c):
    k = c.k
    k.dma("sp", c.gbc[:], c.final_g[0, :].partition_broadcast(128), w=["gbc"])
    rms_stats(c)
    for tt in range(NT):
        k.op("dve", lambda e: e.scalar_tensor_tensor(out=c.h[:, tt, :], in0=c.h[:, tt, :], scalar=c.rstd[:, tt:tt + 1],
                                                     in1=c.gbc[:], op0=ALU.mult, op1=ALU.mult),
             r=[("h", tt), "rstd", "gbc"], w=[("h", tt)])
        k.dma("sp", c.out_d[tsl(tt), :], c.h[:, tt, :], r=[("h", tt)], w=[("out", tt)])
    k._wait("sp", k._deps([("out", tt) for tt in range(NT)], []))


def cmul(k, eng_a, eng_b, o_re, o_im, a_re, a_im, b_re, b_im, t, rk, wk, tk):
    k.op(eng_a, lambda e: e.tensor_tensor(out=t[0], in0=a_re, in1=b_re, op=ALU.mult), r=rk, w=[tk + "0"])
    k.op(eng_a, lambda e: e.tensor_tensor(out=t[1], in0=a_im, in1=b_im, op=ALU.mult), r=rk, w=[tk + "1"])
    k.op(eng_b, lambda e: e.tensor_tensor(out=o_re, in0=t[0], in1=t[1], op=ALU.subtract),
         r=[tk + "0", tk + "1"], w=[wk + "_re"])
    k.op(eng_a, lambda e: e.tensor_tensor(out=t[0], in0=a_re, in1=b_im, op=ALU.mult), r=rk, w=[tk + "0"])
    k.op(eng_a, lambda e: e.tensor_tensor(out=t[1], in0=a_im, in1=b_re, op=ALU.mult), r=rk, w=[tk + "1"])
    k.op(eng_b, lambda e: e.tensor_tensor(out=o_im, in0=t[0], in1=t[1], op=ALU.add),
         r=[tk + "0", tk + "1"], w=[wk + "_im"])


def even_mixer(c, li):
    nc, k, sb, ps, pbf, h = c.nc, c.k, c.sb, c.ps, c.pbf, c.h
    din = c.din
    w_in_d = din("e_w_in", [D, 2064])
    w_out_d = din("e_w_out", [D, D])
    a_re_d = din("s5_a_re", [32, 64])
    a_im_d = din("s5_a_im", [32, 64])
    ldt_d = din("s5_log_dt", [32, 64])
    b_re_d = din("s5_b_re", [32, 64, 16])
    b_im_d = din("s5_b_im", [32, 64, 16])
    c_re_d = din("s5_c_re", [32, 16, 64])
    c_im_d = din("s5_c_im", [32, 16, 64])
    d_d = din("s5_d", [1, 512])
    wglu_d = din("s5_w_glu", [512, 512])
    wg2_d = din("gla_w_g2", [16, 256])
    bg2_d = din("gla_b_g2", [1, 256])
    gng_d = din("gla_norm_g", [1, 512])

    with ExitStack() as ph:
        xy = sb("xy", [128, 8, L], BF16, ph)
        uT = sb("uT", [128, 4, L], BF16, ph)
        xT = xy
        yT = xy
        with ExitStack() as pg:
            qkT = sb("qkT", [128, 4, L], BF16, pg)
            rT = sb("rT", [128, 4, L], BF16, pg)
            glowT = sb("glowT", [16, L], BF16, pg)
            vk = sb("vk", [128, NT, 768], BF16, pg)
            with ExitStack() as p1:
                wi = sb("wi", [128, 8, 1040], BF16, p1)
                rmsnorm_T(c, c.norm_mix_g[li, :], xT, "xT")
                allx = [("xT", tt) for tt in range(NT)]
                allw = [("wi", ch) for ch in range(8)]
                n = 0
                for piece in range(2):
                    cb = piece * 1024
                    ncol = 1024 if piece == 0 else 1040
                    for ch in range(8):
                        k.dma("pool", wi[:, ch, 0:ncol], w_in_d[ch * 128:(ch + 1) * 128, cb:cb + ncol], w=[("wi", ch)])
                    if piece == 0:
                        chunks = ([(i * 128, 128, uT, i, AF.Copy) for i in range(4)] +
                                  [(512 + i * 128, 128, qkT, i, AF.Copy) for i in range(4)])
                    else:
                        chunks = ([(1552 + i * 128, 128, rT, i, AF.Silu) for i in range(4)] +
                                  [(1536, 16, None, 0, AF.Copy)])
                    for (c0, m, dst, di, fn) in chunks:
                        for tb in range(4):
                            b = n % 4
                            n += 1
                            for ch in range(8):
                                k.op("pe", lambda e: e.matmul(ps[b][0:m, :], lhsT=wi[:, ch, c0 - cb:c0 - cb + m],
                                                              rhs=xT[:, ch, tb * 512:(tb + 1) * 512],
                                                              start=(ch == 0), stop=(ch == 7)),
                                     r=allx + allw, w=[("ps", b)])
                            if dst is None:
                                k.op("act", lambda e: e.activation(out=glowT[:, tb * 512:(tb + 1) * 512], in_=ps[b][0:16, :],
                                                                   func=AF.Copy), r=[("ps", b)], w=["glowT"])
                            else:
                                k.op("act", lambda e: e.activation(out=dst[:, di, tb * 512:(tb + 1) * 512], in_=ps[b][:, :],
                                                                   func=fn), r=[("ps", b)], w=[(dst.name, di)])
                    for tt in range(NT):
                        b0 = 4 + tt % 4
                        if piece == 0:
                            for ch in range(8):
                                k.op("pe", lambda e: e.matmul(ps[b0][:, 0:256], lhsT=xT[:, ch, tsl(tt)], rhs=wi[:, ch, 768:1024],
                                                              start=(ch == 0), stop=(ch == 7)), r=allx + allw, w=[("ps", b0)])
                            k.op("dve", lambda e: e.tensor_copy(out=vk[:, tt, 512:768], in_=ps[b0][:, 0:256]), r=[("ps", b0)],
                                 w=[("vk", tt)])
                        else:
                            for ch in range(8):
                                k.op("pe", lambda e: e.matmul(ps[b0][:, :], lhsT=xT[:, ch, tsl(tt)], rhs=wi[:, ch, 0:512],
                                                              start=(ch == 0), stop=(ch == 7)), r=allx + allw, w=[("ps", b0)])
                            k.op("dve", lambda e: e.tensor_copy(out=vk[:, tt, 0:512], in_=ps[b0][:, :]), r=[("ps", b0)],
                                 w=[("vk", tt)])
                k.barrier()
            if "uT" in c.dbg:
                for i in range(4):
                    k.dma("sp", c.dbg["uT"][i], uT[:, i, :], r=[("uT", i)], w=["dbg_uT"])
            if "vk" in c.dbg:
                k.dma("sp", c.dbg["vk"], vk[:, 3, :], r=[("vk", 3)], w=["dbg_vk"])
            if 'gla' not in SKIP:
                gla(c, pg, qkT, rT, glowT, vk, yT, wg2_d, bg2_d, gng_d)
            k.barrier()
        if 's5' not in SKIP:
            s5(c, ph, uT, yT, a_re_d, a_im_d, ldt_d, b_re_d, b_im_d, c_re_d, c_im_d, d_d, wglu_d)
        k.barrier()
        if "yT" in c.dbg:
            for i in range(8):
                k.dma("sp", c.dbg["yT"][i], yT[:, i, :], r=[("yT", i)], w=["dbg_yT"])
        with ExitStack() as p3:
            wo = sb("wo", [128, 8, D], BF16, p3)
            for ch in range(8):
                k.dma("pool", wo[:, ch, :], w_out_d[ch * 128:(ch + 1) * 128, :], w=[("wo", ch)])
            ally = [("yT", i) for i in range(8)]
            allw = [("wo", ch) for ch in range(8)]
            for tt in range(NT):
                for hf in range(2):
                    b = (tt * 2 + hf) % 4
                    for ch in range(8):
                        k.op("pe", lambda e: e.matmul(ps[b][:, :], lhsT=yT[:, ch, tsl(tt)],
                                                      rhs=wo[:, ch, hf * 512:(hf + 1) * 512],
                                                      start=(ch == 0), stop=(ch == 7)), r=ally + allw, w=[("ps", b)])
                    k.op("dve", lambda e: e.tensor_tensor(out=h[:, tt, hf * 512:(hf + 1) * 512],
                                                          in0=h[:, tt, hf * 512:(hf + 1) * 512], in1=ps[b][:, :],
                                                          op=ALU.add), r=[("ps", b), ("h", tt)], w=[("h", tt)])
            k.barrier()


def gla(c, ph, qkT, rT, glowT, vk, yT, wg2_d, bg2_d, gng_d):
    nc, k, sb, ps, pbf = c.nc, c.k, c.sb, c.ps, c.pbf
    with ExitStack() as p2:
        wg2 = sb("wg2", [16, 256], BF16, p2)
        bg2 = sb("bg2", [1, 256], BF16, p2)
        gng = sb("gng", [128, 4], F32, p2)
        triUs = sb("triUs", [128, 128], F32, p2)
        triRs = sb("triRs", [128, 128], F32, p2)
        lp = sb("lp", [128, 256], F32, p2)
        eend = sb("eend", [128, 256], F32, p2)
        ebT = sb("ebT", [128, 2, 128], F32, p2)
        enbT = sb("enbT", [128, 2, 128], F32, p2)
        kend = sb("kend", [128, 256], BF16, p2)
        qd = sb("qd", [128, 2, 128], BF16, p2)
        kd = sb("kd", [128, 2, 128], BF16, p2)
        qdz = sb("qdz", [128, 4, 128], BF16, p2)
        hmask = sb("hmask", [128, 2], F32, p2)
        attT = sb("attT", [128, 4, 128], BF16, p2)
        S = sb("S", [128, 2, 128], F32, p2)
        Sb = sb("Sb", [128, 2, 128], BF16, p2)
        ssq = sb("ssq", [128, 4], F32, p2)
        rso = sb("rso", [128, 4], F32, p2)
        on = sb("on", [128, 4, 128], BF16, p2)
        k.dma("pool", wg2[:], wg2_d[:, :], w=["wg2"])
        k.dma("pool", bg2[:], bg2_d[:, :], w=["bg2"])
        k.dma("sp", gng[:], gng_d[0, :].rearrange("(h v) -> v h", v=128), w=["gng"], allow_slow_non_contiguous=True)
        k.op("dve", lambda e: e.tensor_scalar(out=triUs[:], in0=c.triU_f[:], scalar1=-1.0 / 16, scalar2=None,
                                              op0=ALU.mult), r=["triU_f"], w=["triUs"])
        k.op("dve", lambda e: e.tensor_scalar(out=triRs[:], in0=c.triU_f[:], scalar1=1.0 / 16, scalar2=-1.0 / 16,
                                              op0=ALU.mult, op1=ALU.add), r=["triU_f"], w=["triRs"])
        k.op("pool", lambda e: e.memset(hmask[:], 0.0), w=["hmask"])
        k.op("pool", lambda e: e.memset(hmask[0:64, 0:1], 1.0), r=["hmask"], w=["hmask"])
        k.op("pool", lambda e: e.memset(hmask[64:128, 1:2], 1.0), r=["hmask"], w=["hmask"])
        k.op("pool", lambda e: e.memset(S[:], 0.0), w=["S"])
        k.op("pool", lambda e: e.memset(Sb[:], 0.0), w=["Sb"])
        for tt in range(NT):
            k.op("pe", lambda e: e.matmul(ps[0][:, 0:256], lhsT=glowT[:, tsl(tt)], rhs=wg2[:, :], start=True, stop=False),
                 r=["glowT", "wg2"], w=[("ps", 0)])
            k.op("pe", lambda e: e.matmul(ps[0][:, 0:256], lhsT=c.ones_b[0:1, :], rhs=bg2[:, :], start=False, stop=True),
                 r=["ones_b", "bg2"], w=[("ps", 0)])
            if CUT[0] <= 1:
                break
            k.op("act", lambda e: e.activation(out=lp[:], in_=ps[0][:, 0:256], func=AF.Exp, scale=-1.0),
                 r=[("ps", 0)], w=["lp"])
            k.op("act", lambda e: e.activation(out=lp[:], in_=lp[:], func=AF.Ln, bias=c.onecol[:], scale=1.0),
                 r=["lp", "onecol"], w=["lp"])
            if CUT[0] <= 2:
                break
            k.op("pe", lambda e: e.matmul(ps[1][:, 0:256], lhsT=triRs[:], rhs=lp[:], start=True, stop=True),
                 r=["triRs", "lp"], w=[("ps", 1)])
            for hf in range(2):
                k.op("pe", lambda e: e.matmul(ps[1][:, 256 + hf * 128:256 + (hf + 1) * 128],
                                              lhsT=lp[:, hf * 128:(hf + 1) * 128], rhs=triUs[:], start=True, stop=True),
                     r=["triUs", "lp"], w=[("ps", 1)])
            k.op("act", lambda e: e.activation(out=eend[:], in_=ps[1][:, 0:256], func=AF.Exp), r=[("ps", 1)], w=["eend"])
            k.op("act", lambda e: e.activation(out=ebT[:].rearrange("p a b -> p (a b)"), in_=ps[1][:, 256:512], func=AF.Exp),
                 r=[("ps", 1)], w=["ebT"])
            k.op("act", lambda e: e.activation(out=enbT[:].rearrange("p a b -> p (a b)"), in_=ps[1][:, 256:512], func=AF.Exp,
                                               scale=-1.0), r=[("ps", 1)], w=["enbT"])
            if CUT[0] <= 3:
                break
            k.op("dve", lambda e: e.tensor_tensor(out=kend[:], in0=vk[:, tt, 512:768], in1=eend[:], op=ALU.mult),
                 r=[("vk", tt), "eend"], w=["kend"])
            k.op("dve", lambda e: e.scalar_tensor_tensor(out=qd[:], in0=qkT[:, 0:2, tsl(tt)], scalar=0.125, in1=ebT[:],
                                                         op0=ALU.mult, op1=ALU.mult),
                 r=[("qkT", 0), ("qkT", 1), "ebT"], w=["qd"])
            k.op("dve", lambda e: e.tensor_tensor(out=kd[:], in0=qkT[:, 2:4, tsl(tt)], in1=enbT[:], op=ALU.mult),
                 r=[("qkT", 2), ("qkT", 3), "enbT"], w=["kd"])
            if CUT[0] <= 5:
                break
            for hd in range(4):
                pr = hd // 2
                k.op("dve", lambda e: e.tensor_scalar(out=qdz[:, hd, :], in0=qd[:, pr, :], scalar1=hmask[:, hd % 2:hd % 2 + 1],
                                                      scalar2=None, op0=ALU.mult), r=["qd", "hmask"], w=["qdz"])
            for hd in range(4):
                pr = hd // 2
                k.op("pe", lambda e: e.matmul(ps[2][:, hd * 128:(hd + 1) * 128], lhsT=kd[:, pr, :],
                                              rhs=qdz[:, hd, :], start=True, stop=True),
                     r=["kd", "qdz"], w=[("ps", 2)])
            if CUT[0] <= 5.3:
                break
            k.op("dve", lambda e: e.tensor_tensor(out=attT[:], in0=ps[2][:, :].rearrange("p (a b) -> p a b", a=4),
                                                  in1=c.triU_f[:].unsqueeze(1).to_broadcast([128, 4, 128]), op=ALU.mult),
                 r=[("ps", 2), "triU_f"], w=["attT"])
            if CUT[0] <= 5.6:
                break
            for hd in range(4):
                pr, p0 = hd // 2, (hd % 2) * 64
                k.op("pe", lambda e: e.matmul(ps[3][:, hd * 128:(hd + 1) * 128], lhsT=attT[:, hd, :],
                                              rhs=vk[:, tt, hd * 128:(hd + 1) * 128], start=True, stop=False),
                     r=["attT", ("vk", tt)], w=[("ps", 3)])
                k.op("pe", lambda e: e.matmul(ps[3][:, hd * 128:(hd + 1) * 128], lhsT=qdz[:, hd, :],
                                              rhs=Sb[:, pr, :], start=False, stop=True),
                     r=["qdz", "Sb"], w=[("ps", 3)])
            if CUT[0] <= 6:
                break
            for pr in range(2):
                k.op("pe", lambda e: e.matmul(ps[4][:, pr * 256:(pr + 1) * 256], lhsT=kend[:, pr * 128:(pr + 1) * 128],
                                              rhs=vk[:, tt, pr * 256:(pr + 1) * 256], start=True, stop=True),
                     r=["kend", ("vk", tt)], w=[("ps", 4)])
            for hd in range(4):
                pr, hf, p0 = hd // 2, hd % 2, (hd % 2) * 64
                k.op("dve", lambda e: e.scalar_tensor_tensor(
                    out=S[p0:p0 + 64, pr, :], in0=S[p0:p0 + 64, pr, :], scalar=ebT[p0:p0 + 64, pr, 127:128],
                    in1=ps[4][p0:p0 + 64, pr * 256 + hf * 128:pr * 256 + (hf + 1) * 128], op0=ALU.mult, op1=ALU.add),
                     r=["S", "ebT", ("ps", 4)], w=["S"])
            k.op("dve", lambda e: e.tensor_copy(out=Sb[:], in_=S[:]), r=["S"], w=["Sb"])
            if CUT[0] <= 7:
                break
            for hd in range(4):
                k.op("act", lambda e: e.activation(out=c.junk[:, 0:128], in_=ps[3][:, hd * 128:(hd + 1) * 128],
                                                   func=AF.Square, accum_out=ssq[:, hd:hd + 1]),
                     r=[("ps", 3)], w=["junk", "ssq"])
            k.op("dve", lambda e: e.tensor_scalar(out=rso[:], in0=ssq[:], scalar1=1.0 / 128, scalar2=EPS,
                                                  op0=ALU.mult, op1=ALU.add), r=["ssq"], w=["rso"])
            k.op("act", lambda e: e.activation(out=rso[:], in_=rso[:], func=AF.Sqrt), r=["rso"], w=["rso"])
            k.op("dve", lambda e: e.reciprocal(out=rso[:], in_=rso[:]), r=["rso"], w=["rso"])
            k.op("dve", lambda e: e.tensor_tensor(out=on[:], in0=ps[3][:, :].rearrange("p (a b) -> p a b", a=4),
                                                  in1=rso[:].unsqueeze(2).to_broadcast([128, 4, 128]), op=ALU.mult),
                 r=[("ps", 3), "rso"], w=["on"])
            for hd in range(4):
                k.op("pe", lambda e: e.transpose(out=pbf[5][:, hd * 128:(hd + 1) * 128], in_=on[:, hd, :],
                                                 identity=c.ident[:]), r=["on", "ident"], w=[("ps", 5)])
            for hd in range(4):
                k.op("dve", lambda e: e.scalar_tensor_tensor(
                    out=yT[:, 4 + hd, tsl(tt)], in0=pbf[5][:, hd * 128:(hd + 1) * 128], scalar=gng[:, hd:hd + 1],
                    in1=rT[:, hd, tsl(tt)], op0=ALU.mult, op1=ALU.mult),
                     r=[("ps", 5), "gng", ("rT", hd)], w=[("yT", 4 + hd)])


def s5(c, ph, uT, yT, a_re_d, a_im_d, ldt_d, b_re_d, b_im_d, c_re_d, c_im_d, d_d, wglu_d):
    nc, k, sb, ps, pbf = c.nc, c.k, c.sb, c.ps, c.pbf
    with ExitStack() as p2:
        wb = sb("wb", [128, 2, 4, 512], BF16, p2)
        cm = sb("cm", [128, 2, 16, 128], BF16, p2)
        with ExitStack() as pa:
            wbs = sb("wbs", [128, 2, 4, 512], F32, pa)
            cms = sb("cms", [128, 2, 16, 128], F32, pa)
            k.op("pool", lambda e: e.memset(wbs[:].rearrange("p a b c -> p (a b c)"), 0.0), w=["wbs"])
            k.op("pool", lambda e: e.memset(cms[:].rearrange("p a b c -> p (a b c)"), 0.0), w=["cms"])
            for ri, bd in enumerate((b_re_d, b_im_d)):
                for g in range(32):
                    kc, g8 = g // 8, g % 8
                    k.dma("sp", wbs[g8 * 16:(g8 + 1) * 16, ri, kc, g8 * 64:(g8 + 1) * 64],
                          bd[g].rearrange("p c -> c p"), r=[], w=["wbs"], allow_slow_non_contiguous=True)
            for ri, cd in enumerate((c_re_d, c_im_d)):
                for g in range(32):
                    ct, gl = g // 2, g % 2
                    g8 = g % 8
                    k.dma("sp", cms[gl * 64:(gl + 1) * 64, ri, ct, g8 * 16:(g8 + 1) * 16],
                          cd[g].rearrange("c p -> p c"), r=[], w=["cms"], allow_slow_non_contiguous=True)
            k.op("act", lambda e: e.activation(out=wb[:].rearrange("p a b c -> p (a b c)"),
                                               in_=wbs[:].rearrange("p a b c -> p (a b c)"), func=AF.Copy),
                 r=["wbs"], w=["wb"])
            k.op("act", lambda e: e.activation(out=cm[:, 0].rearrange("p b c -> p (b c)"),
                                               in_=cms[:, 0].rearrange("p b c -> p (b c)"), func=AF.Copy),
                 r=["cms"], w=["cm"])
            k.op("act", lambda e: e.activation(out=cm[:, 1].rearrange("p b c -> p (b c)"),
                                               in_=cms[:, 1].rearrange("p b c -> p (b c)"), func=AF.Copy, scale=-1.0),
                 r=["cms"], w=["cm"])
            k.barrier()
        if CUT[0] <= 10:
            return
        dcol = sb("dcol", [128, 4], F32, p2)
        k.dma("sp", dcol[:], d_d[0, :].rearrange("(c p) -> p c", p=128), w=["dcol"], allow_slow_non_contiguous=True)
        wglu = sb("wglu", [128, 4, 512], BF16, p2)
        for ch in range(4):
            k.dma("pool", wglu[:, ch, :], wglu_d[ch * 128:(ch + 1) * 128, :], w=["wglu"])
        ETb = sb("ETb", [128, 2, 16, 128], BF16, p2)
        EVt = sb("EVt", [128, 2, 2048], BF16, p2)
        a128 = sb("a128", [128, 2, 16], F32, p2)
        with ExitStack() as pb:
            prm = sb("prm", [16, 3, 128], F32, pb)
            for i, pd in enumerate((a_re_d, a_im_d, ldt_d)):
                k.dma("sp", prm[:, i, :], pd.rearrange("(ct gl) p -> ct (gl p)", gl=2), w=["prm"])
            for i in range(3):
                k.op("pe", lambda e: e.transpose(out=ps[0][:, i * 16:(i + 1) * 16], in_=prm[:, i, :], identity=c.identf[0:16, 0:16]),
                     r=["prm", "identf"], w=[("ps", 0)])
            if CUT[0] <= 10.5:
                return
            P = sb("P", [128, 24, 16], F32, pb)
            AR, AI, DT, MAG, TH, S_, C_, T0, T1, RM, FRE, FIM, NR, DEN, ABR, ABI, AVR, AVI = range(18)
            k.op("dve", lambda e: e.tensor_copy(out=P[:, 0:3, :], in_=ps[0][:, 0:48].rearrange("p (a b) -> p a b", a=3)),
                 r=[("ps", 0)], w=["P"])

            def tt_(o, a, b, op, eng="dve"):
                k.op(eng, lambda e: e.tensor_tensor(out=P[:, o, :], in0=P[:, a, :], in1=P[:, b, :], op=op), r=["P"], w=["P"])

            def act_(o, a, fn, scale=1.0):
                k.op("act", lambda e: e.activation(out=P[:, o, :], in_=P[:, a, :], func=fn, scale=scale), r=["P"], w=["P"])

            def ts_(o, a, s1, s2, op0, op1):
                k.op("dve", lambda e: e.tensor_scalar(out=P[:, o, :], in0=P[:, a, :], scalar1=s1, scalar2=s2, op0=op0, op1=op1),
                     r=["P"], w=["P"])

            if CUT[0] <= 11:
                return
            act_(DT, DT, AF.Exp)
            tt_(T0, DT, AR, ALU.mult)
            act_(MAG, T0, AF.Exp)
            act_(RM, T0, AF.Exp, scale=-1.0)
            tt_(TH, DT, AI, ALU.mult)
            act_(T0, TH, AF.Sin, scale=1.0 / 16)
            tt_(T0, T0, T0, ALU.mult)
            ts_(C_, T0, -2.0, 1.0, ALU.mult, ALU.add)
            act_(S_, TH, AF.Sin, scale=1.0 / 8)
            for _ in range(3):
                tt_(T0, C_, C_, ALU.mult)
                tt_(T1, S_, S_, ALU.mult)
                tt_(S_, S_, C_, ALU.mult)
                ts_(S_, S_, 2.0, None, ALU.mult, ALU.bypass)
                tt_(C_, T0, T1, ALU.subtract)
            tt_(ABR, MAG, C_, ALU.mult)
            tt_(ABI, MAG, S_, ALU.mult)
            tt_(AVR, RM, C_, ALU.mult)
            tt_(AVI, RM, S_, ALU.mult)
            ts_(AVI, AVI, -1.0, None, ALU.mult, ALU.bypass)
            ts_(NR, ABR, -1.0, None, ALU.add, ALU.bypass)
            tt_(T0, AR, AR, ALU.mult)
            tt_(T1, AI, AI, ALU.mult)
            tt_(DEN, T0, T1, ALU.add)
            k.op("dve", lambda e: e.reciprocal(out=P[:, DEN, :], in_=P[:, DEN, :]), r=["P"], w=["P"])
            tt_(T0, NR, AR, ALU.mult)
            tt_(T1, ABI, AI, ALU.mult)
            tt_(T0, T0, T1, ALU.add)
            tt_(FRE, T0, DEN, ALU.mult)
            tt_(T0, ABI, AR, ALU.mult)
            tt_(T1, NR, AI, ALU.mult)
            tt_(T0, T0, T1, ALU.subtract)
            tt_(FIM, T0, DEN, ALU.mult)
            if CUT[0] <= 12:
                return
            ET = sb("ET", [128, 2, 16, 128], F32, pb)
            EV = sb("EV", [128, 2, 16, 128], F32, pb)
            tmp = sb("s5tmp", [128, 2, 16, 64], F32, pb)
            pw = sb("s5pw", [128, 2, 16], F32, pb)
            pw2 = sb("s5pw2", [128, 2, 16], F32, pb)
            for (tab, br, bi, i0r, i0i, name) in ((ET, ABR, ABI, None, None, "ET"), (EV, AVR, AVI, FRE, FIM, "EV")):
                if i0r is None:
                    k.op("pool", lambda e: e.memset(tab[:, 0, :, 0:1], 1.0), w=[name])
                    k.op("pool", lambda e: e.memset(tab[:, 1, :, 0:1], 0.0), w=[name])
                else:
                    k.op("dve", lambda e: e.tensor_copy(out=tab[:, 0, :, 0:1], in_=P[:, i0r, :].unsqueeze(2)), r=["P"], w=[name])
                    k.op("dve", lambda e: e.tensor_copy(out=tab[:, 1, :, 0:1], in_=P[:, i0i, :].unsqueeze(2)), r=["P"], w=[name])
                k.op("dve", lambda e: e.tensor_copy(out=pw[:, 0, :], in_=P[:, br, :]), r=["P"], w=["pw"])
                k.op("dve", lambda e: e.tensor_copy(out=pw[:, 1, :], in_=P[:, bi, :]), r=["P"], w=["pw"])
                m = 1
                while m <= 128:
                    if m < 128:
                        bre = pw[:, 0, :].unsqueeze(2).to_broadcast([128, 16, m])
                        bim = pw[:, 1, :].unsqueeze(2).to_broadcast([128, 16, m])
                        cmul(k, "dve", "dve", tab[:, 0, :, m:2 * m], tab[:, 1, :, m:2 * m],
                             tab[:, 0, :, 0:m], tab[:, 1, :, 0:m], bre, bim,
                             (tmp[:, 0, :, 0:m], tmp[:, 1, :, 0:m]), [name, name + "_re", name + "_im", "pw"], name, "s5tmp")
                    elif name == "ET":
                        k.op("dve", lambda e: e.tensor_copy(out=a128[:], in_=pw[:]), r=["pw"], w=["a128"])
                    cmul(k, "dve", "dve", pw2[:, 0, :], pw2[:, 1, :], pw[:, 0, :], pw[:, 1, :], pw[:, 0, :], pw[:, 1, :],
                         (tmp[:, 0, :, 0], tmp[:, 1, :, 0]), ["pw"], "pw2", "s5tmp")
                    k.op("dve", lambda e: e.tensor_copy(out=pw[:], in_=pw2[:]), r=["pw2_re", "pw2_im"], w=["pw"])
                    m *= 2
            if CUT[0] <= 13:
                return
            n = 0
            for ri in range(2):
                for g4 in range(4):
                    b = n % 2
                    n += 1
                    for q in range(4):
                        ct = g4 * 4 + q
                        k.op("pe", lambda e: e.transpose(out=ps[b][:, q * 128:(q + 1) * 128], in_=EV[:, ri, ct, :],
                                                         identity=c.identf[:]), r=["EV", "EV_re", "EV_im", "identf"], w=[("ps", b)])
                    k.op("act", lambda e: e.activation(out=EVt[:, ri, g4 * 512:(g4 + 1) * 512], in_=ps[b][:, :], func=AF.Copy),
                         r=[("ps", b)], w=["EVt"])

            k.op("act", lambda e: e.activation(out=ETb[:].rearrange("p a b c -> p (a b c)"),
                                               in_=ET[:].rearrange("p a b c -> p (a b c)"), func=AF.Copy),
                 r=["ET", "ET_re", "ET_im"], w=["ETb"])
            k.barrier()
        if CUT[0] <= 14:
            return
        tmpc = sb("s5tmpc", [128, 2, 16], F32, p2)
        zz = sb("zz", [128, 2, 2048], BF16, p2)
        t1 = sb("s5t1", [128, 512], F32, p2)
        t2 = sb("s5t2", [128, 512], F32, p2)
        sT = sb("sT", [128, 2, 16, 128], BF16, p2)
        lastc = sb("lastc", [128, 2, 16], F32, p2)
        cz = sb("cz", [128, 2, 16], F32, p2)
        cz2 = sb("cz2", [128, 2, 16], F32, p2)
        wr = sb("s5wr", [128, 512], F32, p2)
        wi_ = sb("s5wi", [128, 512], F32, p2)
        ypre = sb("ypre", [128, 4, 128], F32, p2)
        if CUT[0] <= 15:
            return
        for tt in range(NTT[0]):
            for kc in range(4):
                for ri in range(2):
                    k.op("pe", lambda e: e.matmul(ps[ri][:, :], lhsT=uT[:, kc, tsl(tt)], rhs=wb[:, ri, kc, :],
                                                  start=True, stop=True), r=[("uT", kc), "wb"], w=[("ps", ri)])
                er = EVt[:, 0, kc * 512:(kc + 1) * 512]
                ei = EVt[:, 1, kc * 512:(kc + 1) * 512]
                k.op("dve", lambda e: e.tensor_tensor(out=t1[:], in0=ps[0][:, :], in1=er, op=ALU.mult),
                     r=[("ps", 0), "EVt"], w=["s5t1"])
                k.op("dve", lambda e: e.tensor_tensor(out=t2[:], in0=ps[1][:, :], in1=ei, op=ALU.mult),
                     r=[("ps", 1), "EVt"], w=["s5t2"])
                k.op("pool", lambda e: e.tensor_tensor(out=zz[:, 0, kc * 512:(kc + 1) * 512], in0=t1[:], in1=t2[:],
                                                       op=ALU.subtract), r=["s5t1", "s5t2"], w=["zz"])
                k.op("dve", lambda e: e.tensor_tensor(out=t1[:], in0=ps[1][:, :], in1=er, op=ALU.mult),
                     r=[("ps", 1), "EVt"], w=["s5t1"])
                k.op("dve", lambda e: e.tensor_tensor(out=t2[:], in0=ps[0][:, :], in1=ei, op=ALU.mult),
                     r=[("ps", 0), "EVt"], w=["s5t2"])
                k.op("pool", lambda e: e.tensor_tensor(out=zz[:, 1, kc * 512:(kc + 1) * 512], in0=t1[:], in1=t2[:],
                                                       op=ALU.add), r=["s5t1", "s5t2"], w=["zz"])
            if CUT[0] <= 16 and tt >= 1:
                return
            for g4 in range(4):
                for ri in range(2):
                    b = 2 + ri
                    for q in range(4):
                        ct = g4 * 4 + q
                        k.op("pe", lambda e: e.matmul(ps[b][:, q * 128:(q + 1) * 128], lhsT=zz[:, ri, ct * 128:(ct + 1) * 128],
                                                      rhs=c.triU_b[:], start=True, stop=True),
                             r=["zz", "triU_b"], w=[("ps", b)])
                cr = ps[2][:, :].rearrange("p (a b) -> p a b", a=4)
                ci = ps[3][:, :].rearrange("p (a b) -> p a b", a=4)
                etr = ETb[:, 0, g4 * 4:(g4 + 1) * 4, :]
                eti = ETb[:, 1, g4 * 4:(g4 + 1) * 4, :]
                t1v = t1[:].rearrange("p (a b) -> p a b", a=4)
                t2v = t2[:].rearrange("p (a b) -> p a b", a=4)
                wrv = wr[:].rearrange("p (a b) -> p a b", a=4)
                wiv = wi_[:].rearrange("p (a b) -> p a b", a=4)
                if tt == 0:
                    k.op("act", lambda e: e.activation(out=wrv, in_=cr, func=AF.Copy), r=[("ps", 2)], w=["wr"])
                    k.op("act", lambda e: e.activation(out=wiv, in_=ci, func=AF.Copy), r=[("ps", 3)], w=["wi_"])
                else:
                    k.op("dve", lambda e: e.tensor_tensor(out=wrv, in0=cr, in1=cz[:, 0, g4 * 4:(g4 + 1) * 4].unsqueeze(2).to_broadcast([128, 4, 128]),
                                                          op=ALU.add), r=[("ps", 2), "cz_re"], w=["wr"])
                    k.op("dve", lambda e: e.tensor_tensor(out=wiv, in0=ci, in1=cz[:, 1, g4 * 4:(g4 + 1) * 4].unsqueeze(2).to_broadcast([128, 4, 128]),
                                                          op=ALU.add), r=[("ps", 3), "cz_im"], w=["wi_"])
                k.op("act", lambda e: e.activation(out=lastc[:, 0, g4 * 4:(g4 + 1) * 4], in_=wrv[:, :, 127], func=AF.Copy),
                     r=["wr"], w=["lastc"])
                k.op("act", lambda e: e.activation(out=lastc[:, 1, g4 * 4:(g4 + 1) * 4], in_=wiv[:, :, 127], func=AF.Copy),
                     r=["wi_"], w=["lastc"])
                k.op("dve", lambda e: e.tensor_tensor(out=t1v, in0=wrv, in1=etr, op=ALU.mult), r=["wr", "ETb"], w=["s5t1"])
                k.op("pool", lambda e: e.tensor_tensor(out=t2v, in0=wiv, in1=eti, op=ALU.mult), r=["wi_", "ETb"], w=["s5t2"])
                k.op("dve", lambda e: e.tensor_tensor(out=sT[:, 0, g4 * 4:(g4 + 1) * 4, :], in0=t1v, in1=t2v, op=ALU.subtract),
                     r=["s5t1", "s5t2"], w=["sT"])
                k.op("dve", lambda e: e.tensor_tensor(out=t1v, in0=wiv, in1=etr, op=ALU.mult), r=["wi_", "ETb"], w=["s5t1"])
                k.op("pool", lambda e: e.tensor_tensor(out=t2v, in0=wrv, in1=eti, op=ALU.mult), r=["wr", "ETb"], w=["s5t2"])
                k.op("dve", lambda e: e.tensor_tensor(out=sT[:, 1, g4 * 4:(g4 + 1) * 4, :], in0=t1v, in1=t2v, op=ALU.add),
                     r=["s5t1", "s5t2"], w=["sT"])
            if tt < NT - 1:
                cmul(k, "dve", "dve", cz2[:, 0, :], cz2[:, 1, :], lastc[:, 0, :], lastc[:, 1, :], a128[:, 0, :], a128[:, 1, :],
                     (tmpc[:, 0, :], tmpc[:, 1, :]), ["lastc", "a128"], "cz2", "s5tmpc")
                k.op("dve", lambda e: e.tensor_copy(out=cz[:], in_=cz2[:]), r=["cz2_re", "cz2_im"], w=["cz_re", "cz_im"])
            if CUT[0] <= 19 and tt >= 1:
                return
            for kc in range(4):
                n = 0
                for q in range(4):
                    ct = kc * 4 + q
                    for ri in range(2):
                        k.op("pe", lambda e: e.matmul(ps[5][:, kc * 128:(kc + 1) * 128], lhsT=cm[:, ri, ct, :],
                                                      rhs=sT[:, ri, ct, :], start=(n == 0), stop=(n == 7)),
                             r=["cm", "sT"], w=[("ps", 5)])
                        n += 1
            if CUT[0] <= 19.3 and tt >= 1:
                return
            for kc in range(4):
                k.op("dve", lambda e: e.tensor_scalar(out=ypre[:, kc, :], in0=uT[:, kc, tsl(tt)], scalar1=dcol[:, kc:kc + 1],
                                                      scalar2=None, op0=ALU.mult), r=[("uT", kc), "dcol"], w=["ypre"])
                k.op("dve", lambda e: e.tensor_tensor(out=ypre[:, kc, :], in0=ypre[:, kc, :], in1=ps[5][:, kc * 128:(kc + 1) * 128],
                                                      op=ALU.add), r=["ypre", ("ps", 5)], w=["ypre"])
            if CUT[0] <= 19.6 and tt >= 1:
                return
            for kc in range(4):
                if GELU_FN[0] is None:
                    k.op("dve", lambda e: e.tensor_copy(out=yT[:, kc, tsl(tt)], in_=ypre[:, kc, :]), r=["ypre"], w=[("yT", kc)])
                else:
                    k.op("act", lambda e: e.activation(out=yT[:, kc, tsl(tt)], in_=ypre[:, kc, :], func=GELU_FN[0]), r=["ypre"],
                         w=[("yT", kc)])
        for nm, tl, kk in (("ypre", ypre, ["ypre"]), ("zz", zz, ["zz"]), ("sT", sT, ["sT"]), ("ETb", ETb, ["ETb"]), ("EVt", EVt, ["EVt"]),
                           ("wb", wb, ["wb"]), ("cm", cm, ["cm"]), ("a128", a128, ["a128"]), ("lastc", lastc, ["lastc"])):
            if nm in c.dbg:
                ap = tl[:]
                if len(ap.shape) == 3:
                    ap = ap.rearrange("p a b -> p (a b)")
                elif len(ap.shape) == 4:
                    ap = ap.rearrange("p a b c -> p (a b c)")
                k.dma("sp", c.dbg[nm], ap, r=kk, w=["dbg_" + nm])
        if CUT[0] <= 20:
            return
        sg = sb("sg", [128, 4, 512], BF16, p2)
        yk = [("yT", i) for i in range(4)]
        n = 0
        for tb in range(4):
            for c2 in range(4):
                b = 6 + n % 2
                n += 1
                for ch in range(4):
                    k.op("pe", lambda e: e.matmul(ps[b][:, :], lhsT=wglu[:, ch, c2 * 128:(c2 + 1) * 128],
                                                  rhs=yT[:, ch, tb * 512:(tb + 1) * 512], start=(ch == 0), stop=(ch == 3)),
                         r=["wglu"] + yk, w=[("ps", b)])
                k.op("act", lambda e: e.activation(out=sg[:, c2, :], in_=ps[b][:, :], func=AF.Sigmoid), r=[("ps", b)],
                     w=[("sg", c2)])
            for c2 in range(4):
                k.op("dve", lambda e: e.tensor_tensor(out=yT[:, c2, tb * 512:(tb + 1) * 512], in0=yT[:, c2, tb * 512:(tb + 1) * 512],
                                                      in1=sg[:, c2, :], op=ALU.mult), r=[("yT", c2), ("sg", c2)], w=[("yT", c2)])
        k.barrier()


def peer_inputs(c):
    c.wq_d = c.din("peer_w_q", [2, D, 2048])
    c.keys_d = c.din("peer_sub_keys", [2, 8, 2, 128, 128])
    c.u_d = c.din("peer_u", [2, 16384, D])
    c.v_d = c.din("peer_v", [2, 16384, D])
    c.ut_scr = c.nc.dram_tensor("ut_scr", [8, 128, 16384], BF16, kind="Internal").ap()
    c.vb_scr = c.nc.dram_tensor("vb_scr", [16384, D], BF16, kind="Internal").ap()


NEG = -1.0
PEER_EG = [32]
NPROD = 6
PEER_ACT_HEADS = [5]
PEER_PROD = [['act', 'pool', 'pool', 'act', 'pool', 'pool', 'act', 'pool']]


def peer(c, li):
    nc, k, sb, ps, h = c.nc, c.k, c.sb, c.ps, c.h
    wq_d, keys_d, u_d, v_d = c.wq_d[li], c.keys_d[li], c.u_d[li], c.v_d[li]
    utv = c.ut_scr.rearrange("dc d e -> d dc e")
    def prepass_tiles(pp):
        usts = [sb("pust%d" % i, [128, 2, D], BF16, pp) for i in range(2)]
        uts = [sb("puts%d" % i, [128, 8, 256], BF16, pp) for i in range(2)]
        vbs_ = [sb("pvb%d" % i, [128, 2, D], BF16, pp) for i in range(2)]
        return usts, uts, vbs_

    def prepass_gen(tiles):
        usts, uts, vbs_ = tiles
        for st_ in range(64):
            i = st_ % 2
            e0 = st_ * 256
            uk, tk, vk_ = "pust%d" % i, "puts%d" % i, "pvb%d" % i
            k.dma("pool", usts[i][:], u_d[e0:e0 + 256, :].rearrange("(a p) d -> p a d", p=128), w=[uk])
            k.dma("pool", vbs_[i][:], v_d[e0:e0 + 256, :].rearrange("(a p) d -> p a d", p=128), w=[vk_])
            yield
            for dc in range(8):
                b_ = 6 + dc % 2
                for a_ in range(2):
                    k.op("pe", lambda e: e.transpose(out=c.pbf[b_][:, a_ * 128:(a_ + 1) * 128], in_=usts[i][:, a_, dc * 128:(dc + 1) * 128],
                                                     identity=c.ident[:]), r=[uk, "ident"], w=[("ps", b_)])
                k.op("act", lambda e: e.activation(out=uts[i][:, dc, :], in_=c.pbf[b_][:, 0:256], func=AF.Copy), r=[("ps", b_)],
                     w=[tk])
                if dc % 2 == 1:
                    yield
            k.dma("sp", utv[:, :, e0:e0 + 256], uts[i][:], r=[tk], w=[("utscr", st_ // 2)])
            k.dma("sp", c.vb_scr[e0:e0 + 256, :].rearrange("(a p) d -> p a d", p=128), vbs_[i][:], r=[vk_],
                  w=[("vbscr", st_ // 2)])
            yield

    with ExitStack() as ph:
        hnT = sb("hnT", [128, 8, L], BF16, ph)
        rmsnorm_T(c, c.norm_ffn_g[li, :], hnT, "hnT")
        allhn = [("hnT", tt) for tt in range(NT)]
        e_all = sb("e_all", [128, 4, 16, 128], F32, ph)
        phi = sb("phi", [128, 4, 8], F32, ph)
        mx = sb("pmx", [128, 16], F32, ph)
        t16 = sb("t16", [128, 16, 16], F32, ph)
        tmpb = sb("tmpb", [128, 128], F32, ph)
        cand = sb("cand", [128, 256], F32, ph)
        cand2 = sb("cand2", [128, 256], F32, ph)
        c16 = sb("c16", [128, 16], F32, ph)
        zs = sb("zs", [128, 8], F32, ph)
        pp = ExitStack()
        pgen = [prepass_gen(prepass_tiles(pp))]

        def pump(n=1):
            for _ in range(n):
                if pgen[0] is None:
                    return
                try:
                    next(pgen[0])
                except StopIteration:
                    pgen[0] = None

        for tg in range(4):
            with ExitStack() as p1:
                kT = sb("kT", [128, 16, 128], BF16, p1)
                with ExitStack() as p0:
                    kst = sb("kst", [128, 16, 128], F32, p0)
                    k.dma("sp", kst[:], keys_d.rearrange("h c n d -> n (h c) d"), w=["kst"])
                    for g in range(4):
                        b = g % 2
                        for q in range(4):
                            k.op("pe", lambda e: e.transpose(out=ps[b][:, q * 128:(q + 1) * 128], in_=kst[:, g * 4 + q, :],
                                                             identity=c.identf[:]), r=["kst", "identf"], w=[("ps", b)])
                        k.op("act", lambda e: e.activation(out=kT[:, g * 4:(g + 1) * 4, :].rearrange("p a b -> p (a b)"), in_=ps[b][:, :],
                                                           func=AF.Copy), r=[("ps", b)], w=["kT"])
                    k.barrier()

                wq = sb("wq", [128, 8, 1024], BF16, p1)
                qT = sb("qT", [128, 16, 512], BF16, p1)
                allwq = [("wq", ch) for ch in range(8)]
                for blk in range(16):
                    b = blk % 2
                    if blk % 8 == 0:
                        for ch in range(8):
                            k.dma("pool", wq[:, ch, :], wq_d[ch * 128:(ch + 1) * 128, (blk // 8) * 1024:(blk // 8 + 1) * 1024],
                                  w=[("wq", ch)])
                    bl = blk % 8
                    for ch in range(8):
                        k.op("pe", lambda e: e.matmul(ps[b][:, :], lhsT=wq[:, ch, bl * 128:(bl + 1) * 128],
                                                      rhs=hnT[:, ch, tg * 512:(tg + 1) * 512], start=(ch == 0), stop=(ch == 7)),
                             r=allwq + allhn, w=[("ps", b)])
                    k.op("act", lambda e: e.activation(out=qT[:, blk, :], in_=ps[b][:, :], func=AF.Copy), r=[("ps", b)],
                         w=[("qT", blk)])
                    pump(2)
                for tt in range(4):
                    for blk in range(16):
                        b = 2 + blk // 4
                        k.op("pe", lambda e: e.matmul(ps[b][:, (blk % 4) * 128:(blk % 4 + 1) * 128], lhsT=qT[:, blk, tt * 128:(tt + 1) * 128],
                                                      rhs=kT[:, blk, :], start=True, stop=True), r=[("qT", blk), "kT"], w=[("ps", b)])
                    for b4 in range(4):
                        k.op("dve", lambda e: e.tensor_reduce(out=mx[:, b4 * 4:(b4 + 1) * 4],
                                                              in_=ps[2 + b4][:, :].rearrange("p (a n) -> p a n", a=4),
                                                              axis=AX.X, op=ALU.max), r=[("ps", 2 + b4)], w=["pmx"])
                    k.op("dve", lambda e: e.tensor_scalar(out=mx[:], in0=mx[:], scalar1=-1.0, scalar2=None, op0=ALU.mult),
                         r=["pmx"], w=["pmx"])
                    for blk in range(16):
                        b = 2 + blk // 4
                        k.op("act", lambda e: e.activation(out=e_all[:, tt, blk, :], in_=ps[b][:, (blk % 4) * 128:(blk % 4 + 1) * 128],
                                                           func=AF.Exp, bias=mx[:, blk:blk + 1], scale=1.0),
                             r=[("ps", b), "pmx"], w=[("e_all", tt, blk)])
                    for blk in range(16):
                        ek = ("e_all", tt, blk)
                        k.op("dve", lambda e: e.max(out=t16[:, blk, 0:8], in_=e_all[:, tt, blk, :]), r=[ek], w=["t16"])
                        k.op("dve", lambda e: e.match_replace(out=tmpb[:], in_to_replace=t16[:, blk, 0:8],
                                                              in_values=e_all[:, tt, blk, :], imm_value=NEG),
                             r=[ek, "t16"], w=["tmpb"])
                        k.op("dve", lambda e: e.max(out=t16[:, blk, 8:16], in_=tmpb[:]), r=["tmpb"], w=["t16"])
                        pump(1)
                    for hd in range(8):
                        k.op("dve", lambda e: e.tensor_tensor(
                            out=cand[:].rearrange("p (i j) -> p i j", i=16),
                            in0=t16[:, 2 * hd, :].unsqueeze(2).to_broadcast([128, 16, 16]),
                            in1=t16[:, 2 * hd + 1, :].unsqueeze(1).to_broadcast([128, 16, 16]), op=ALU.mult),
                             r=["t16"], w=["cand"])
                        k.op("dve", lambda e: e.max(out=c16[:, 0:8], in_=cand[:]), r=["cand"], w=["c16"])
                        k.op("dve", lambda e: e.match_replace(out=cand2[:], in_to_replace=c16[:, 0:8], in_values=cand[:],
                                                              imm_value=NEG), r=["cand", "c16"], w=["cand2"])
                        k.op("dve", lambda e: e.max(out=c16[:, 8:16], in_=cand2[:]), r=["cand2"], w=["c16"])
                        k.op("dve", lambda e: e.tensor_scalar(out=phi[:, tt, hd:hd + 1], in0=c16[:, 15:16], scalar1=1.0 - 2e-6, scalar2=None,
                                                              op0=ALU.mult), r=["c16"], w=["phi"])
                        k.op("dve", lambda e: e.tensor_reduce(out=zs[:, hd:hd + 1], in_=c16[:], axis=AX.X, op=ALU.add),
                             r=["c16"], w=["zs"])
                        pump(2)
                    k.op("dve", lambda e: e.reciprocal(out=zs[:], in_=zs[:]), r=["zs"], w=["zs"])
                    k.op("dve", lambda e: e.tensor_tensor(out=phi[:, tt, :], in0=phi[:, tt, :], in1=zs[:], op=ALU.mult),
                         r=["phi", "zs"], w=["phi"])
                    for hd in range(8):
                        k.op("dve", lambda e: e.tensor_scalar(out=e_all[:, tt, 2 * hd, :], in0=e_all[:, tt, 2 * hd, :],
                                                              scalar1=zs[:, hd:hd + 1], scalar2=None, op0=ALU.mult),
                             r=["zs", ("e_all", tt, 2 * hd)], w=[("e_all", tt, 2 * hd)])
                pump(100000)
                k.barrier()
            if tg == 0:
                pp.close()
            with ExitStack() as p2:
                utsb = sb("utsb", [128, 8, 512], BF16, p2)
                vbs = [sb("vb%d" % i, [128, 4, D], BF16, p2) for i in range(2)]
                gels = [sb("gel%d" % i, [128, 4, 512], BF16, p2) for i in range(2)]
                Gs = [sb("G%d" % i, [128, 8, 512], BF16, p2) for i in range(2)]
                prod = [sb("prod%d" % i, [128, 512], F32, p2) for i in range(NPROD)]
                HsTs = [sb("HsT%d" % i, [128, 4, 128], BF16, p2) for i in range(2)]
                NEG_ = PEER_EG[0]
                steps = [(eg, tt) for eg in range(NEG_) for tt in range(4)]
                npr = [0]

                def load(eg):
                    k.dma("sp", utsb[:], utv[:, :, eg * 512:(eg + 1) * 512], r=[("utscr", eg)], w=["utsb"])
                    k.dma("sp", vbs[eg % 2][:], c.vb_scr[eg * 512:(eg + 1) * 512, :].rearrange("(a p) d -> p a d", p=128),
                          r=[("vbscr", eg)], w=["vb%d" % (eg % 2)])

                def hpart(eg, a):
                    b = a % 2
                    for dc in range(8):
                        k.op("pe", lambda e: e.matmul(ps[b][:, :], lhsT=utsb[:, dc, a * 128:(a + 1) * 128],
                                                      rhs=hnT[:, dc, tg * 512:(tg + 1) * 512], start=(dc == 0), stop=(dc == 7)),
                             r=["utsb"] + allhn, w=[("ps", b)])
                    k.op("act", lambda e: e.activation(out=gels[eg % 2][:, a, :], in_=ps[b][:, :], func=AF.Gelu), r=[("ps", b)],
                         w=[("gel%d" % (eg % 2), a)])

                def stage_a(s_):
                    eg, tt = steps[s_]
                    G, gk = Gs[s_ % 2], "G%d" % (s_ % 2)
                    for hd in range(8):
                        pr = prod[npr[0] % NPROD]
                        pk = "prod%d" % (npr[0] % NPROD)
                        npr[0] += 1
                        pe_ = PEER_PROD[0][hd]
                        if pe_ == 'act':
                            for a in range(4):
                                k.op("act", lambda e: e.activation(out=pr[:, a * 128:(a + 1) * 128], in_=e_all[:, tt, 2 * hd + 1, :],
                                                                   func=AF.Copy, scale=e_all[:, tt, 2 * hd, eg * 4 + a:eg * 4 + a + 1]),
                                     r=[("e_all", tt, 2 * hd), ("e_all", tt, 2 * hd + 1)], w=[pk])
                        else:
                            k.op(pe_, lambda e: e.tensor_tensor(
                                out=pr[:].rearrange("p (a n) -> p a n", a=4),
                                in0=e_all[:, tt, 2 * hd, eg * 4:(eg + 1) * 4].unsqueeze(2).to_broadcast([128, 4, 128]),
                                in1=e_all[:, tt, 2 * hd + 1, :].unsqueeze(1).to_broadcast([128, 4, 128]), op=ALU.mult),
                                 r=[("e_all", tt, 2 * hd), ("e_all", tt, 2 * hd + 1)], w=[pk])
                        k.op("dve", lambda e: e.scalar_tensor_tensor(out=G[:, hd, :], in0=pr[:], scalar=phi[:, tt, hd:hd + 1],
                                                                     in1=pr[:], op0=ALU.is_ge, op1=ALU.mult),
                             r=[pk, "phi"], w=[(gk, hd)])

                def stage_b(s_):
                    eg, tt = steps[s_]
                    G, gk, gb = Gs[s_ % 2], "G%d" % (s_ % 2), 4 + s_ % 2
                    for a in range(4):
                        for hd in range(8):
                            k.op("pe", lambda e: e.matmul(ps[gb][:, a * 128:(a + 1) * 128], lhsT=G[:, hd, a * 128:(a + 1) * 128],
                                                          rhs=c.ident[:], start=(hd == 0), stop=(hd == 7)),
                                 r=[(gk, hd), "ident"], w=[("ps", gb)])

                def stage_c(s_):
                    eg, tt = steps[s_]
                    gb = 4 + s_ % 2
                    k.op("dve", lambda e: e.tensor_tensor(out=HsTs[s_ % 2][:], in0=gels[eg % 2][:, :, tt * 128:(tt + 1) * 128],
                                                          in1=ps[gb][:, :].rearrange("p (a n) -> p a n", a=4), op=ALU.mult),
                         r=[("gel%d" % (eg % 2), a) for a in range(4)] + [("ps", gb)], w=["HsT%d" % (s_ % 2)])

                def obank(s_, dh):
                    return (2 + dh) if s_ % 2 == 0 else (6 + dh)

                def stage_d(s_):
                    eg, tt = steps[s_]
                    for dh in range(2):
                        ob = obank(s_, dh)
                        for a in range(4):
                            k.op("pe", lambda e: e.matmul(ps[ob][:, :], lhsT=HsTs[s_ % 2][:, a, :], rhs=vbs[eg % 2][:, a, dh * 512:(dh + 1) * 512],
                                                          start=(a == 0), stop=(a == 3)), r=["HsT%d" % (s_ % 2), "vb%d" % (eg % 2)],
                                 w=[("ps", ob)])

                def stage_e(s_):
                    eg, tt = steps[s_]
                    tok = tg * 4 + tt
                    for dh in range(2):
                        ob = obank(s_, dh)
                        k.op("dve", lambda e: e.tensor_tensor(out=h[:, tok, dh * 512:(dh + 1) * 512],
                                                              in0=h[:, tok, dh * 512:(dh + 1) * 512], in1=ps[ob][:, :],
                                                              op=ALU.add), r=[("h", tok), ("ps", ob)], w=[("h", tok)])

                if steps:
                    load(0)
                    for a in range(4):
                        hpart(0, a)
                    stage_a(0)
                for s_ in range(len(steps)):
                    eg, tt = steps[s_]
                    if tt == 0 and eg + 1 < NEG_:
                        load(eg + 1)
                    if s_ + 1 < len(steps):
                        stage_a(s_ + 1)
                    stage_b(s_)
                    stage_c(s_)
                    if eg + 1 < NEG_:
                        hpart(eg + 1, tt)
                    stage_d(s_)
                    if s_ >= 1:
                        stage_e(s_ - 1)
                if steps:
                    stage_e(len(steps) - 1)
                k.barrier()


RW_HP = [8]
RW_TB = [4]
RW_NQ = [4]


def rwkv_mixer(c, li):
    nc, k, sb, ps, pbf, h = c.nc, c.k, c.sb, c.ps, c.pbf, c.h
    din = c.din
    mu_d = din("o_mu", [6, D])
    wr_d, wk_d, wv_d = din("o_w_r", [D, D]), din("o_w_k", [D, D]), din("o_w_v", [D, D])
    w0_d = din("o_w0", [1, D])
    ww1_d, ww2_d = din("o_w_w1", [D, 64]), din("o_w_w2", [64, D])
    a0_d = din("o_a0", [1, D])
    wa1_d, wa2_d = din("o_w_a1", [D, 64]), din("o_w_a2", [64, D])
    wg1_d, wg2_d = din("o_w_g1", [D, 128]), din("o_w_g2", [128, D])
    kk_d, ka_d, rk_d = din("o_k_k", [1, D]), din("o_k_a", [1, D]), din("o_r_k", [1, D])
    lg_d, lb_d = din("o_lnx_g", [1, D]), din("o_lnx_b", [1, D])
    wo_d = din("o_w_o", [D, D])

    def mm(out, lhsT, rhs, r, w, start=True, stop=True):
        k.op("pe", lambda e: e.matmul(out, lhsT=lhsT, rhs=rhs, start=start, stop=stop), r=r, w=w)

    def tt_(eng, out, in0, in1, op, r, w):
        k.op(eng, lambda e: e.tensor_tensor(out=out, in0=in0, in1=in1, op=op), r=r, w=w)

    def ts_(eng, out, in0, s1, s2, op0, op1, r, w):
        k.op(eng, lambda e: e.tensor_scalar(out=out, in0=in0, scalar1=s1, scalar2=s2, op0=op0, op1=op1), r=r, w=w)

    def act(out, in_, fn, r, w, **kw):
        k.op("act", lambda e: e.activation(out=out, in_=in_, func=fn, **kw), r=r, w=w)

    with ExitStack() as ph:
        xp = sb("xp", [128, 8, L + 2], BF16, ph)
        k.op("pool", lambda e: e.memset(xp[:, :, 0:1], 0.0), w=["xp0"])
        rmsnorm_T(c, c.norm_mix_g[li, :], xp, "xp", off=1)
        allx = [("xp", tt) for tt in range(NT)] + ["xp0"]
        mjt = sb("mjt", [128, 128], F32, ph)
        mtj = sb("mtj", [128, 128], F32, ph)
        k.op("pool", lambda e: e.affine_select(out=mjt[:], in_=c.ones_f[:], pattern=[[1, 128]], compare_op=ALU.is_gt,
                                                fill=0.0, base=0, channel_multiplier=-1), r=["ones_f"], w=["mjt"])
        k.op("pool", lambda e: e.affine_select(out=mtj[:], in_=c.ones_f[:], pattern=[[-1, 128]], compare_op=ALU.is_gt,
                                                fill=0.0, base=0, channel_multiplier=1), r=["ones_f"], w=["mtj"])
        hm = sb("hm", [128, 2], F32, ph)
        bones = sb("bones", [128, 128], BF16, ph)
        cmask = sb("cmask", [128, 2, 128], F32, ph)
        dsel = sb("dsel", [128, 2, 128], F32, ph)
        k.op("pool", lambda e: e.memset(hm[:], 0.0), w=["hm"])
        k.op("pool", lambda e: e.memset(hm[0:64, 0:1], 1.0), r=["hm"], w=["hm"])
        k.op("pool", lambda e: e.memset(hm[64:128, 1:2], 1.0), r=["hm"], w=["hm"])
        k.op("pool", lambda e: e.memset(bones[:], 0.0), w=["bones"])
        k.op("pool", lambda e: e.memset(bones[0:64, 0:64], 1.0), r=["bones"], w=["bones"])
        k.op("pool", lambda e: e.memset(bones[64:128, 64:128], 1.0), r=["bones"], w=["bones"])
        k.op("pool", lambda e: e.memset(cmask[:].rearrange("p a b -> p (a b)"), 0.0), w=["cmask"])
        k.op("pool", lambda e: e.memset(cmask[:, 0, 0:64], 1.0), r=["cmask"], w=["cmask"])
        k.op("pool", lambda e: e.memset(cmask[:, 1, 64:128], 1.0), r=["cmask"], w=["cmask"])
        for hl in range(2):
            ts_("dve", dsel[:, hl, :], c.identf[:], hm[:, hl:hl + 1], None, ALU.mult, ALU.bypass, ["identf", "hm"], ["dsel"])
        chm = sb("chm", [128, 512], BF16, ph)
        k.op("pool", lambda e: e.memset(chm[:], 1.0), w=["chm"])
        k.op("pool", lambda e: e.memset(chm[:].rearrange("p (a b) -> p a b", a=4)[:, :, 0:1], 0.0), r=["chm"], w=["chm"])
        prm = sb("rprm", [128, 7, 8], F32, ph)
        for i, pd in enumerate((w0_d, a0_d, kk_d, ka_d, rk_d, lg_d, lb_d)):
            k.dma("sp", prm[:, i, :], pd[0, :].rearrange("(dc p) -> p dc", p=128), w=["rprm"], allow_slow_non_contiguous=True)
        mucol = sb("mucol", [128, 6, 8], F32, ph)
        k.dma("sp", mucol[:], mu_d.rearrange("i (dc p) -> p i dc", p=128), w=["mucol"], allow_slow_non_contiguous=True)
        MU = dict(r=0, w=1, k=2, v=3, a=4, g=5)

        def load_split(name, wd, cols, ncol, mui, st):
            w0 = sb(name + "0", [128, 8, ncol], BF16, st)
            wm = sb(name + "m", [128, 8, ncol], BF16, st)
            wp = sb(name + "p", [128, 8, ncol], BF16, st)
            k.dma("pool", w0[:], wd[:, cols].rearrange("(dc p) n -> p dc n", p=128), w=[name + "0"])
            tt_("dve", wm[:], w0[:], mucol[:, mui, :].unsqueeze(2).to_broadcast([128, 8, ncol]), ALU.mult,
                [name + "0", "mucol"], [name + "m"])
            tt_("dve", wp[:], w0[:], wm[:], ALU.subtract, [name + "0", name + "m"], [name + "p"])
            return wp, wm

        ww2 = sb("ww2", [128, D], BF16, ph)
        wa2 = sb("wa2", [128, D], BF16, ph)
        wg2 = sb("wg2r", [128, D], BF16, ph)
        k.dma("pool", ww2[0:64, :], ww2_d[:, :], w=["ww2"])
        k.dma("pool", wa2[0:64, :], wa2_d[:, :], w=["wa2"])
        k.dma("pool", wg2[:], wg2_d[:, :], w=["wg2r"])
        tw1 = sb("tw1", [128, L], BF16, ph)
        ta1 = sb("ta1", [128, L], BF16, ph)
        tg1 = sb("tg1", [128, L], BF16, ph)

        def proj(out_ps, wp, wm, c0, m, t0, n, keys):
            for dc in range(8):
                mm(out_ps, wp[:, dc, c0:c0 + m], xp[:, dc, 1 + t0:1 + t0 + n], allx + keys, [("ps", 0)], start=(dc == 0), stop=False)
            for dc in range(8):
                mm(out_ps, wm[:, dc, c0:c0 + m], xp[:, dc, t0:t0 + n], allx + keys, [("ps", 0)], start=False, stop=(dc == 7))

        with ExitStack() as p0:
            w1p, w1m = load_split("ww1", ww1_d, slice(0, 64), 64, MU["w"], p0)
            a1p, a1m = load_split("wa1", wa1_d, slice(0, 64), 64, MU["a"], p0)
            g1p, g1m = load_split("wg1", wg1_d, slice(0, 128), 128, MU["g"], p0)
            for tb in range(4):
                t0 = tb * 512
                proj(ps[0][0:64, :], w1p, w1m, 0, 64, t0, 512, ["ww1p", "ww1m"])
                act(tw1[0:64, t0:t0 + 512], ps[0][0:64, :], AF.Tanh, [("ps", 0)], ["tw1"])
                proj(ps[0][0:64, :], a1p, a1m, 0, 64, t0, 512, ["wa1p", "wa1m"])
                act(ta1[0:64, t0:t0 + 512], ps[0][0:64, :], AF.Copy, [("ps", 0)], ["ta1"])
                proj(ps[0][:, :], g1p, g1m, 0, 128, t0, 512, ["wg1p", "wg1m"])
                act(tg1[:, t0:t0 + 512], ps[0][:, :], AF.Sigmoid, [("ps", 0)], ["tg1"])
            k.barrier()
        W = 512
        f32t = lambda n, st: sb(n, [128, W], F32, st)
        b16t = lambda n, st: sb(n, [128, W], BF16, st)
        for hp in range(RW_HP[0]):
            with ExitStack() as p1:
                cols = slice(hp * 128, (hp + 1) * 128)
                wrp, wrm = load_split("wr", wr_d, cols, 128, MU["r"], p1)
                wkp, wkm = load_split("wk", wk_d, cols, 128, MU["k"], p1)
                wvp, wvm = load_split("wv", wv_d, cols, 128, MU["v"], p1)
                wo = sb("wo_hp", [128, D], BF16, p1)
                k.dma("pool", wo[:], wo_d[hp * 128:(hp + 1) * 128, :], w=["wo_hp"])
                pc = lambda i: prm[:, i, hp:hp + 1]
                Hb = sb("Hb", [128, 64], BF16, p1)
                k.op("pool", lambda e: e.memset(Hb[:], 0.0), w=["Hb"])
                r_b, k_b, v_b, g_b = b16t("r_b", p1), b16t("k_b", p1), b16t("v_b", p1), b16t("g_b", p1)
                vtok = sb("vtok", [128, 4, 128], BF16, p1)
                lw, cs, asig = f32t("lw", p1), f32t("cs", p1), f32t("asig", p1)
                e1, e2, e3, e4 = f32t("e1", p1), f32t("e2", p1), f32t("e3", p1), f32t("e4", p1)
                kkn, kmod, b_ = f32t("kkn", p1), f32t("kmod", p1), f32t("bb", p1)
                sq = b16t("sq", p1)
                rt, rz0, rz1, az0, az1 = b16t("rt", p1), b16t("rz0", p1), b16t("rz1", p1), b16t("az0", p1), b16t("az1", p1)
                at_, bt, kt, bh, kh = b16t("at", p1), b16t("bt", p1), b16t("kt", p1), b16t("bh", p1), b16t("kh", p1)
                bv = f32t("bv", p1)
                pcl = sb("pcl", [128, 4], F32, p1)
                rz, az = (rz0, rz1), (az0, az1)
                Xs = [sb("X%d" % i, [128, 256], BF16, p1) for i in range(2)]
                MNs = [[sb("MN%d_%d" % (j, i), [128, 256], BF16, p1) for i in range(2)] for j in range(2)]
                cats = [sb("cat%d" % i, [128, 256], BF16, p1) for i in range(2)]
                mrks = [sb("mrk%d" % i, [128, 128], F32, p1) for i in range(2)]
                btok = sb("btok", [128, 2, 128], BF16, p1)
                bz = sb("bz", [128, 2, 128], BF16, p1)
                kz = sb("kz", [128, 2, 128], BF16, p1)
                RGMF = [[sb("%s%d" % (n, i), [128, 128], BF16, p1) for n in ("RpT", "GT", "MpT", "FT")] for i in range(2)]
                gst = sb("gst", [128, 8], F32, p1)
                ysq = sb("ysq", [128, 128], F32, p1)
                yn = sb("yn", [128, 128], BF16, p1)
                z1 = sb("z1", [128, 128], F32, p1)
                ygT = sb("ygT", [128, 128], BF16, p1)
                pending_tail = [None]
                for tb in range(RW_TB[0]):
                    t0 = tb * W
                    if pending_tail[0] is not None:
                        for _ in pending_tail[0]:
                            pass
                        pending_tail[0] = None
                    proj(ps[0][:, :], wrp, wrm, 0, 128, t0, W, ["wrp", "wrm"])
                    act(r_b[:], ps[0][:, :], AF.Copy, [("ps", 0)], ["r_b"])
                    proj(ps[0][:, :], wkp, wkm, 0, 128, t0, W, ["wkp", "wkm"])
                    act(k_b[:], ps[0][:, :], AF.Copy, [("ps", 0)], ["k_b"])
                    proj(ps[0][:, :], wvp, wvm, 0, 128, t0, W, ["wvp", "wvm"])
                    act(v_b[:], ps[0][:, :], AF.Copy, [("ps", 0)], ["v_b"])
                    for q in range(4):
                        tq = t0 + q * 128
                        for dc in range(8):
                            mm(ps[6][:, q * 128:(q + 1) * 128], xp[:, dc, 1 + tq:1 + tq + 128], wvp[:, dc, :], allx + ["wvp"],
                               [("ps", 6)], start=(dc == 0), stop=False)
                        for dc in range(8):
                            mm(ps[6][:, q * 128:(q + 1) * 128], xp[:, dc, tq:tq + 128], wvm[:, dc, :], allx + ["wvm"],
                               [("ps", 6)], start=False, stop=(dc == 7))
                    k.op("dve", lambda e: e.tensor_copy(out=vtok[:].rearrange("p a b -> p (a b)"), in_=ps[6][:, :]),
                         r=[("ps", 6)], w=["vtok"])
                    mm(ps[0][:, :], ww2[0:64, cols], tw1[0:64, t0:t0 + W], ["ww2", "tw1"], [("ps", 0)])
                    act(lw[:], ps[0][:, :], AF.Sigmoid, [("ps", 0), "rprm"], ["lw"], bias=pc(0), scale=1.0)
                    ts_("dve", lw[:], lw[:], -0.6065306597126334, None, ALU.mult, ALU.bypass, ["lw"], ["lw"])
                    mm(ps[0][:, :], wa2[0:64, cols], ta1[0:64, t0:t0 + W], ["wa2", "ta1"], [("ps", 0)])
                    act(asig[:], ps[0][:, :], AF.Sigmoid, [("ps", 0), "rprm"], ["asig"], bias=pc(1), scale=1.0)
                    mm(ps[0][:, :], wg2[:, cols], tg1[:, t0:t0 + W], ["wg2r", "tg1"], [("ps", 0)])
                    act(g_b[:], ps[0][:, :], AF.Copy, [("ps", 0)], ["g_b"])
                    k.op("dve", lambda e: e.tensor_tensor_scan(out=cs[:], data0=chm[:], data1=lw[:], initial=0.0,
                                                               op0=ALU.mult, op1=ALU.add), r=["chm", "lw"], w=["cs"])
                    cs3 = cs[:].rearrange("p (a b) -> p a b", a=4)
                    csl = cs3[:, :, 127:128].to_broadcast([128, 4, 128])
                    act(e1[:], cs[:], AF.Exp, ["cs"], ["e1"])
                    act(e2[:], cs[:], AF.Exp, ["cs"], ["e2"], scale=-1.0)
                    tt_("dve", e3[:], cs[:], lw[:], ALU.subtract, ["cs", "lw"], ["e3"])
                    act(e3[:], e3[:], AF.Exp, ["e3"], ["e3"])
                    tt_("dve", e4[:].rearrange("p (a b) -> p a b", a=4), csl, cs3, ALU.subtract, ["cs"], ["e4"])
                    act(e4[:], e4[:], AF.Exp, ["e4"], ["e4"])
                    act(pcl[:], cs3[:, :, 127], AF.Exp, ["cs"], ["pcl"])
                    ts_("dve", kkn[:], k_b[:], pc(2), None, ALU.mult, ALU.bypass, ["k_b", "rprm"], ["kkn"])
                    tt_("dve", sq[:], kkn[:], kkn[:], ALU.mult, ["kkn"], ["sq"])
                    mm(ps[0][:, :], bones[:], sq[:], ["bones", "sq"], [("ps", 0)])
                    act(kmod[:], ps[0][:, :], AF.Sqrt, [("ps", 0)], ["kmod"])
                    ts_("dve", kmod[:], kmod[:], 1e-12, None, ALU.max, ALU.bypass, ["kmod"], ["kmod"])
                    k.op("dve", lambda e: e.reciprocal(out=kmod[:], in_=kmod[:]), r=["kmod"], w=["kmod"])
                    tt_("dve", kkn[:], kkn[:], kmod[:], ALU.mult, ["kkn", "kmod"], ["kkn"])
                    ts_("dve", kmod[:], asig[:], -1.0, pc(3), ALU.add, ALU.mult, ["asig", "rprm"], ["kmod"])
                    ts_("dve", kmod[:], kmod[:], 1.0, None, ALU.add, ALU.bypass, ["kmod"], ["kmod"])
                    tt_("dve", kmod[:], kmod[:], k_b[:], ALU.mult, ["kmod", "k_b"], ["kmod"])
                    tt_("dve", b_[:], kkn[:], asig[:], ALU.mult, ["kkn", "asig"], ["bb"])
                    tt_("dve", rt[:], r_b[:], e1[:], ALU.mult, ["r_b", "e1"], ["rt"])
                    tt_("pool", kt[:], kmod[:], e2[:], ALU.mult, ["kmod", "e2"], ["kt"])
                    tt_("dve", bt[:], b_[:], e2[:], ALU.mult, ["bb", "e2"], ["bt"])
                    tt_("pool", kh[:], kmod[:], e4[:], ALU.mult, ["kmod", "e4"], ["kh"])
                    tt_("dve", bh[:], b_[:], e4[:], ALU.mult, ["bb", "e4"], ["bh"])
                    tt_("pool", e3[:], kkn[:], e3[:], ALU.mult, ["kkn", "e3"], ["e3"])
                    ts_("dve", at_[:], e3[:], -1.0, None, ALU.mult, ALU.bypass, ["e3"], ["at"])
                    for hl in range(2):
                        ts_("dve", rz[hl][:], rt[:], hm[:, hl:hl + 1], None, ALU.mult, ALU.bypass, ["rt", "hm"], ["rz%d" % hl])
                        ts_("pool", az[hl][:], at_[:], hm[:, hl:hl + 1], None, ALU.mult, ALU.bypass, ["at", "hm"], ["az%d" % hl])
                    tt_("dve", e1[:], r_b[:], kmod[:], ALU.mult, ["r_b", "kmod", "e1", "rt"], ["e1"])
                    ts_("dve", sq[:], e1[:], pc(4), None, ALU.mult, ALU.bypass, ["e1", "rprm", "sq"], ["sq"])
                    mm(ps[0][:, :], bones[:], sq[:], ["bones", "sq"], [("ps", 0)])
                    tt_("dve", bv[:], ps[0][:, :], v_b[:], ALU.mult, [("ps", 0), "v_b"], ["bv"])
                    for q in range(RW_NQ[0]):
                        csl_ = slice(q * 128, (q + 1) * 128)
                        tok = tb * 4 + q
                        k.op("pe", lambda e: e.transpose(out=pbf[1][:, 0:128], in_=bh[:, csl_], identity=c.ident[:]),
                             r=["bh", "ident"], w=[("ps", 1)])
                        k.op("pe", lambda e: e.transpose(out=pbf[1][:, 128:256], in_=kh[:, csl_], identity=c.ident[:]),
                             r=["kh", "ident"], w=[("ps", 1)])
                        for hl in range(2):
                            tt_("dve", bz[:, hl, :], pbf[1][:, 0:128], cmask[:, hl, :], ALU.mult, [("ps", 1), "cmask"], ["bz"])
                            tt_("dve", kz[:, hl, :], pbf[1][:, 128:256], cmask[:, hl, :], ALU.mult, [("ps", 1), "cmask"], ["kz"])
                        def head_seq(hl):
                            azc, rzc = az[hl][:, csl_], rz[hl][:, csl_]
                            azk, rzk = "az%d" % hl, "rz%d" % hl
                            bX, bY = 2 + 2 * hl, 3 + 2 * hl
                            kX, kY = ("ps", bX), ("ps", bY)
                            X, MN, cat, mrk = Xs[hl], MNs[hl], cats[hl], mrks[hl]
                            RpT, GT, MpT, FT = RGMF[hl]
                            xk, ck, mk = "X%d" % hl, "cat%d" % hl, "mrk%d" % hl
                            rk_, gk_, mpk, fk = ("RpT%d" % hl, "GT%d" % hl, "MpT%d" % hl, "FT%d" % hl)
                            tp = pbf[1][:, 256 + hl * 128:384 + hl * 128]
                            mm(ps[bX][:, 0:128], bt[:, csl_], azc, ["bt", azk], [kX])
                            mm(ps[bX][:, 128:256], azc, bt[:, csl_], ["bt", azk], [kX])
                            mm(ps[bX][:, 256:384], azc, kt[:, csl_], ["kt", azk], [kX])
                            k.op("pe", lambda e: e.transpose(out=tp, in_=azc, identity=c.ident[:]), r=[azk, "ident"], w=[("ps", 1)])
                            yield
                            tt_("dve", MN[0][:, 0:128], ps[bX][:, 0:128], mjt[:], ALU.mult, [kX, "mjt"], ["MN0_%d" % hl])
                            tt_("dve", MN[0][:, 128:256], ps[bX][:, 128:256], mtj[:], ALU.mult, [kX, "mtj"], ["MN0_%d" % hl])
                            tt_("dve", X[:, 128:256], ps[bX][:, 256:384], mtj[:], ALU.mult, [kX, "mtj"], [xk])
                            act(X[:, 0:128], tp, AF.Copy, [("ps", 1)], [xk])
                            yield
                            for i in range(7):
                                cur, nxt = MN[i % 2], MN[(i + 1) % 2]
                                ck_, nk = "MN%d_%d" % (i % 2, hl), "MN%d_%d" % ((i + 1) % 2, hl)
                                mm(ps[bY][:, 0:256], cur[:, 0:128], X[:], [ck_, xk], [kY])
                                if i < 6:
                                    mm(ps[bY][:, 256:384], cur[:, 128:256], cur[:, 0:128], [ck_], [kY])
                                    mm(ps[bY][:, 384:512], cur[:, 0:128], cur[:, 128:256], [ck_], [kY])
                                yield
                                if i < 6:
                                    act(nxt[:], ps[bY][:, 256:512], AF.Copy, [kY], [nk, kY])
                                tt_("dve", X[:], X[:], ps[bY][:, 0:256], ALU.add, [xk, kY], [xk, kY])
                                yield
                            mm(ps[bY][:, 0:128], bt[:, csl_], rzc, ["bt", rzk], [kY])
                            mm(ps[bY][:, 128:256], kt[:, csl_], rzc, ["kt", rzk], [kY])
                            yield
                            tt_("dve", cat[:, 0:128], ps[bY][:, 0:128], c.triU_f[:], ALU.mult, [kY, "triU_f"], [ck])
                            tt_("dve", mrk[:], ps[bY][:, 128:256], c.triU_f[:], ALU.mult, [kY, "triU_f"], [mk])
                            k.op("pool", lambda e: e.tensor_copy(out=cat[:, 128:256], in_=bz[:, hl, :]), r=["bz"], w=[ck])
                            yield
                            mm(ps[bX][:, 0:256], X[:, 0:128], cat[:], [xk, ck], [kX])
                            mm(ps[bX][:, 256:512], X[:, 128:256], cat[:], [xk, ck], [kX])
                            yield
                            tt_("dve", RpT[:], ps[bX][:, 0:128], rzc, ALU.add, [kX, rzk], [rk_])
                            k.op("dve", lambda e: e.scalar_tensor_tensor(out=GT[:], in0=dsel[:, hl, :], scalar=pcl[:, q:q + 1],
                                                                         in1=ps[bX][:, 128:256], op0=ALU.mult, op1=ALU.add),
                                 r=["dsel", "pcl", kX], w=[gk_])
                            tt_("dve", MpT[:], ps[bX][:, 256:384], mrk[:], ALU.add, [kX, mk], [mpk])
                            tt_("dve", FT[:], ps[bX][:, 384:512], kz[:, hl, :], ALU.add, [kX, "kz"], [fk])
                            yield
                            vh = vtok[:, q, hl * 64:(hl + 1) * 64]
                            mm(ps[7][:, hl * 64:(hl + 1) * 64], RpT[:], Hb[:], [rk_, "Hb"], [("ps", 7)], start=True, stop=False)
                            mm(ps[7][:, hl * 64:(hl + 1) * 64], MpT[:], vh, [mpk, "vtok"], [("ps", 7)], start=False, stop=True)
                            mm(ps[0][:, 0:64], GT[:], Hb[:], [gk_, "Hb"], [("ps", 0)], start=(hl == 0), stop=False)
                            mm(ps[0][:, 0:64], FT[:], vh, [fk, "vtok"], [("ps", 0)], start=False, stop=(hl == 1))
                            yield

                        gens = [head_seq(0), head_seq(1)]
                        if pending_tail[0] is not None:
                            gens.append(pending_tail[0])
                            pending_tail[0] = None
                        alive = [True] * len(gens)
                        while any(alive):
                            for gi in range(len(gens)):
                                if alive[gi]:
                                    try:
                                        next(gens[gi])
                                    except StopIteration:
                                        alive[gi] = False
                        act(Hb[:], ps[0][:, 0:64], AF.Copy, [("ps", 0)], ["Hb"])

                        def tail_seq(q=q, csl_=csl_, tok=tok):
                            yield
                            y3 = ps[7][:, 0:128].rearrange("p (a b) -> p a b", a=2)
                            yield
                            k.op("dve", lambda e: e.tensor_reduce(out=gst[:, 0:2], in_=y3, axis=AX.X, op=ALU.add), r=[("ps", 7)], w=["gst"])
                            yield
                            act(ysq[:], ps[7][:, 0:128], AF.Square, [("ps", 7)], ["ysq", ("ps", 7)])
                            yield
                            k.op("dve", lambda e: e.tensor_reduce(out=gst[:, 2:4], in_=ysq[:].rearrange("p (a b) -> p a b", a=2),
                                                                  axis=AX.X, op=ALU.add), r=["ysq"], w=["gst"])
                            yield
                            ts_("dve", gst[:, 0:4], gst[:, 0:4], 1.0 / 64, None, ALU.mult, ALU.bypass, ["gst"], ["gst"])
                            yield
                            tt_("dve", gst[:, 4:6], gst[:, 0:2], gst[:, 0:2], ALU.mult, ["gst"], ["gst"])
                            yield
                            tt_("dve", gst[:, 4:6], gst[:, 2:4], gst[:, 4:6], ALU.subtract, ["gst"], ["gst"])
                            yield
                            ts_("dve", gst[:, 4:6], gst[:, 4:6], 64e-5, None, ALU.add, ALU.bypass, ["gst"], ["gst"])
                            yield
                            act(gst[:, 4:6], gst[:, 4:6], AF.Sqrt, ["gst"], ["gst"])
                            yield
                            k.op("dve", lambda e: e.reciprocal(out=gst[:, 6:8], in_=gst[:, 4:6]), r=["gst"], w=["gst"])
                            yield
                            for hl in range(2):
                                ts_("dve", yn[:, hl * 64:(hl + 1) * 64], ps[7][:, hl * 64:(hl + 1) * 64], gst[:, hl:hl + 1],
                                    gst[:, 6 + hl:7 + hl], ALU.subtract, ALU.mult, [("ps", 7), "gst"], ["yn"])
                            yield
                            k.op("pe", lambda e: e.transpose(out=pbf[1][:, 512:640], in_=yn[:], identity=c.ident[:]),
                                 r=["yn", "ident"], w=[("ps", 1)])
                            yield
                            ts_("dve", z1[:], pbf[1][:, 512:640], pc(5), pc(6), ALU.mult, ALU.add, [("ps", 1), "rprm"], ["z1"])
                            yield
                            tt_("dve", z1[:], z1[:], bv[:, csl_], ALU.add, ["z1", "bv"], ["z1"])
                            yield
                            tt_("dve", ygT[:], z1[:], g_b[:, csl_], ALU.mult, ["z1", "g_b"], ["ygT"])
                            yield
                            for dh in range(2):
                                mm(ps[6][:, :], ygT[:], wo[:, dh * 512:(dh + 1) * 512], ["ygT", "wo_hp"], [("ps", 6)])
                                tt_("dve", h[:, tok, dh * 512:(dh + 1) * 512], h[:, tok, dh * 512:(dh + 1) * 512], ps[6][:, :], ALU.add,
                                    [("h", tok), ("ps", 6)], [("h", tok)])
                            yield

                        pending_tail[0] = tail_seq()

                if pending_tail[0] is not None:
                    for _ in pending_tail[0]:
                        pass
                    pending_tail[0] = None
                k.barrier()
        k.barrier()


def build(nc, stages="all", dbg=None):
    es = ExitStack()
    with es:
        dbgaps = {}
        for name, shape in (dbg or {}).items():
            dbgaps[name] = nc.dram_tensor("dbg_" + name, list(shape), BF16 if name in ("uT", "yT", "vk", "zz", "sT", "ETb", "EVt", "wb", "cm") else F32,
                                          kind="ExternalOutput").ap()
        c = setup_common(nc, es, dbgaps)
        peer_inputs(c)
        st = set(stages.split(","))
        if "all" in st:
            st = {"even", "peer0", "rwkv", "peer1"}
        if "even" in st:
            even_mixer(c, 0)
        if "peer0" in st:
            peer(c, 0)
        if "rwkv" in st:
            rwkv_mixer(c, 1)
        if "peer1" in st:
            peer(c, 1)
        if "h" in dbgaps:
            for tt in range(NT):
                c.k.dma("sp", dbgaps["h"][tsl(tt), :], c.h[:, tt, :], r=[("h", tt)], w=["dbg_h"])
        final_norm_store(c)
        print("instr counts", c.k.nins, "sems", c.k.nsem)
    return nc


PARAMS = ["norm_mix_g", "norm_ffn_g", "final_g", "e_w_in", "e_w_out", "s5_a_re", "s5_a_im", "s5_log_dt", "s5_b_re", "s5_b_im",
          "s5_c_re", "s5_c_im", "s5_d", "s5_w_glu", "gla_w_g2", "gla_b_g2", "gla_norm_g",
          "peer_w_q", "peer_sub_keys", "peer_u", "peer_v",
          "o_mu", "o_w_r", "o_w_k", "o_w_v", "o_w0", "o_w_w1", "o_w_w2", "o_a0", "o_w_a1", "o_w_a2", "o_w_g1", "o_w_g2",
          "o_k_k", "o_k_a", "o_r_k", "o_lnx_g", "o_lnx_b", "o_w_o"]


def core_inputs(inputs, b):
    m = {"x": np.ascontiguousarray(inputs["x"][b])}
    for n in PARAMS:
        a = np.asarray(inputs[n])
        if n in ("norm_mix_g", "norm_ffn_g", "peer_w_q", "peer_sub_keys", "peer_u", "peer_v"):
            pass
        elif n == "final_g":
            a = a.reshape(1, D)
        elif n in ("s5_d", "gla_b_g2", "gla_norm_g", "o_w0", "o_a0", "o_k_k", "o_k_a", "o_r_k", "o_lnx_g", "o_lnx_b"):
            a = a.reshape(1, -1)
        else:
            a = a[0]
        m[n] = np.ascontiguousarray(a)
    return m


def kernel(**inputs):
    n = 8
    nc = bass.Bass("TRN2", target_bir_lowering=False)
    build(nc)
    in_maps = [core_inputs(inputs, b) for b in range(n)]
    res = run_bass_kernel_spmd(nc, in_maps, core_ids=list(range(n)))
    return np.stack([r["out"] for r in res.results], axis=0)
```

```python
import numpy as np
from contextlib import ExitStack
import concourse.bass as bass
import concourse.mybir as mybir
from concourse.bass_utils import run_bass_kernel_spmd

F32 = mybir.dt.float32
BF16 = mybir.dt.bfloat16
U32 = mybir.dt.uint32
AF = mybir.ActivationFunctionType
ALU = mybir.AluOpType
AX = mybir.AxisListType

L = 2048
D = 1024
NT = L // 128
EPS = 1e-6
ENGS = ("pe", "dve", "act", "pool", "sp")


class MK:
    ROT = 1 << 30

    def __init__(self, nc, es):
        self.nc = nc
        self.es = es
        self.eng = dict(pe=nc.tensor, dve=nc.vector, act=nc.scalar, pool=nc.gpsimd, sp=nc.sync)
        self.esem = {}
        self.prev_ep = {}
        self.ecnt = {e: 0 for e in ENGS}
        self.seen = {e: {} for e in ENGS}
        self.lastw = {}
        self.readers = {}
        self.dsem = {}
        self.free_dsem = {}
        self.ndsem = 0
        self.nsem = 0
        self.nins = {e: 0 for e in ENGS}
        self.spare = [self._newsem("spare%d" % i) for i in range(6)]
        for e in ENGS:
            self._rot(e)

    def _newsem(self, name):
        self.nsem += 1
        return self.es.enter_context(self.nc.semaphore(name))

    def _rot(self, e):
        if e in self.esem and self.esem[e][2] > 0:
            self.prev_ep[e] = self.esem[e][:3]
        ep = self.esem[e][3] + 1 if e in self.esem else 0
        name = "s_%s_%d" % (e, ep)
        sem = self.spare.pop() if (ep > 0 and self.spare) else self._newsem(name)
        self.esem[e] = (name, sem, 0, ep)

    def _deps(self, r, w):
        d = {}

        def add(p):
            name, sem, c = p
            if name not in d or d[name][1] < c:
                d[name] = (sem, c)

        for k in r:
            if k in self.lastw:
                add(self.lastw[k])
        for k in w:
            if k in self.lastw:
                add(self.lastw[k])
            for n, (s, c) in self.readers.get(k, {}).items():
                add((n, s, c))
        return d

    def _wait(self, e, d):
        E = self.eng[e]
        seen = self.seen[e]
        for name, (sem, c) in d.items():
            if seen.get(name, 0) >= c:
                continue
            E.wait_ge(sem, c)
            seen[name] = c

    def _record(self, p, r, w):
        name, sem, c = p
        for k in w:
            self.lastw[k] = p
            self.readers[k] = {}
        for k in r:
            rd = self.readers.setdefault(k, {})
            if name not in rd or rd[name][1] < c:
                rd[name] = (sem, c)

    def op(self, e, fn, r=(), w=()):
        w = list(w) + [x for x in r if isinstance(x, tuple) and x and x[0] == "ps" and x not in w]
        d = self._deps(r, w)
        if e == "pe":
            d = {n: v for n, v in d.items() if not n.startswith("s_pe_")}
        self._wait(e, d)
        name, sem, cnt, ep = self.esem[e]
        if cnt >= self.ROT:
            self._rot(e)
            name, sem, cnt, ep = self.esem[e]
        ins = fn(self.eng[e])
        cnt += 1
        self.esem[e] = (name, sem, cnt, ep)
        ins.then_inc(sem, 1)
        self.nins[e] += 1
        self._record((name, sem, cnt), r, w)
        return ins

    def dma(self, e, out, in_, r=(), w=(), **kw):
        d = self._deps(r, w)
        self._wait(e, d)
        key = w[0] if len(w) else r[0]
        skey = ("dma", e, key)
        if skey not in self.dsem:
            if self.free_dsem.get(e):
                self.dsem[skey] = self.free_dsem[e].pop()
            else:
                name = "d%d" % self.ndsem
                self.ndsem += 1
                self.dsem[skey] = [name, self._newsem(name), 0]
        ent = self.dsem[skey]
        ins = self.eng[e].dma_start(out=out, in_=in_, **kw)
        ent[2] += 16
        ins.then_inc(ent[1], 16)
        self.nins[e] += 1
        self._record((ent[0], ent[1], ent[2]), r, w)
        return ins

    def barrier(self):
        d = {}
        for e in ENGS:
            name, sem, cnt, ep = self.esem[e]
            if cnt > 0:
                d[name] = (sem, cnt)
            elif e in self.prev_ep:
                pn, psem, pcnt = self.prev_ep[e]
                d[pn] = (psem, pcnt)
        for ent in self.dsem.values():
            if ent[2] > 0:
                d[ent[0]] = (ent[1], ent[2])
        for e in ENGS:
            dd = d
            if e == "pe":
                dd = {n: v for n, v in d.items() if not n.startswith("s_pe_")}
            self._wait(e, dd)
        for skey, ent in self.dsem.items():
            self.free_dsem.setdefault(skey[1], []).append(ent)
        self.dsem = {}


def tsl(tt):
    return slice(tt * 128, (tt + 1) * 128)


class Ctx:
    pass


SKIP = set()
CUT = [99.0]
GELU_FN = [AF.Gelu]
NTT = [NT]
NOINJ = [False]
INJV = [0]


def setup_common(nc, es, dbg):
    c = Ctx()
    c.nc = nc
    c.es = es
    c.dbg = dbg
    k = c.k = MK(nc, es)
    E = es.enter_context

    def dram_in(name, shape):
        return nc.dram_tensor(name, list(shape), F32, kind="ExternalInput").ap()

    c.din = dram_in
    c.x_d = dram_in("x", [L, D])
    c.out_d = nc.dram_tensor("out", [L, D], F32, kind="ExternalOutput").ap()
    c.norm_mix_g = dram_in("norm_mix_g", [2, D])
    c.norm_ffn_g = dram_in("norm_ffn_g", [2, D])
    c.final_g = dram_in("final_g", [1, D])

    used = {}

    def sb(name, shape, dt, st=None):
        n = used.get(name, 0)
        used[name] = n + 1
        nm = name if n == 0 else "%s_%d" % (name, n)
        return (st or es).enter_context(nc.sbuf_tensor(nm, list(shape), dt))

    c.sb = sb
    c.h = sb("h", [128, NT, D], F32)
    c.gbc = sb("gbc", [128, D], F32)
    c.ident = sb("ident", [128, 128], BF16)
    c.identf = sb("identf", [128, 128], F32)
    c.ones_f = sb("ones_f", [128, 128], F32)
    c.ones_b = sb("ones_b", [128, 128], BF16)
    c.onecol = sb("onecol", [128, 1], F32)
    c.ss = sb("ss", [128, NT], F32)
    c.rstd = sb("rstd", [128, NT], F32)
    c.junk = sb("junk", [128, D], BF16)
    c.xs = [sb("xs%d" % i, [128, D], BF16) for i in range(2)]
    c.ps = [E(nc.psum_tensor("ps%d" % i, [128, 512], F32)) for i in range(8)]
    c.pbf = [c.ps[i][:].bitcast(BF16) for i in range(8)]
    c.triU_f = sb("triU_f", [128, 128], F32)
    c.triU_b = sb("triU_b", [128, 128], BF16)

    k.op("pool", lambda e: e.memset(c.identf[:], 0.0), w=["identf"])
    k.op("pool", lambda e: e.affine_select(out=c.identf[:], in_=c.identf[:], pattern=[[-1, 128]],
                                            compare_op=ALU.not_equal, fill=1.0, base=0, channel_multiplier=1),
         r=["identf"], w=["identf"])
    k.op("dve", lambda e: e.tensor_copy(out=c.ident[:], in_=c.identf[:]), r=["identf"], w=["ident"])
    k.op("pool", lambda e: e.memset(c.ones_f[:], 1.0), w=["ones_f"])
    k.op("pool", lambda e: e.memset(c.ones_b[:], 1.0), w=["ones_b"])
    k.op("pool", lambda e: e.memset(c.onecol[:], 1.0), w=["onecol"])
    k.op("pool", lambda e: e.affine_select(out=c.triU_f[:], in_=c.ones_f[:], pattern=[[1, 128]],
                                            compare_op=ALU.is_ge, fill=0.0, base=0, channel_multiplier=-1),
         r=["ones_f"], w=["triU_f"])
    k.op("dve", lambda e: e.tensor_copy(out=c.triU_b[:], in_=c.triU_f[:]), r=["triU_f"], w=["triU_b"])
    for tt in range(NT):
        k.dma("sp", c.h[:, tt, :], c.x_d[tsl(tt), :], w=[("h", tt)])
    return c


def rms_stats(c):
    k = c.k
    for tt in range(NT):
        k.op("act", lambda e: e.activation(out=c.junk[:], in_=c.h[:, tt, :], func=AF.Square,
                                           accum_out=c.ss[:, tt:tt + 1]),
             r=[("h", tt)], w=["junk", ("ss", tt)])
    allss = [("ss", tt) for tt in range(NT)]
    k.op("dve", lambda e: e.tensor_scalar(out=c.rstd[:], in0=c.ss[:], scalar1=1.0 / D, scalar2=EPS,
                                          op0=ALU.mult, op1=ALU.add), r=allss, w=["rstd"])
    k.op("act", lambda e: e.activation(out=c.rstd[:], in_=c.rstd[:], func=AF.Sqrt), r=["rstd"], w=["rstd"])
    k.op("dve", lambda e: e.reciprocal(out=c.rstd[:], in_=c.rstd[:]), r=["rstd"], w=["rstd"])


def rmsnorm_T(c, g_ap, xT, tag, off=0):
    k = c.k
    k.dma("sp", c.gbc[:], g_ap.partition_broadcast(128), w=["gbc"])
    rms_stats(c)
    for tt in range(NT):
        xb = c.xs[tt % 2]
        xk = ("xs", tt % 2)
        k.op("dve", lambda e: e.scalar_tensor_tensor(out=xb[:], in0=c.h[:, tt, :], scalar=c.rstd[:, tt:tt + 1],
                                                     in1=c.gbc[:], op0=ALU.mult, op1=ALU.mult),
             r=[("h", tt), "rstd", "gbc"], w=[xk])
        b = 6 + tt % 2
        pst = c.pbf[b]
        pk = ("ps", b)
        for ch in range(8):
            k.op("pe", lambda e: e.transpose(out=pst[:, ch * 128:(ch + 1) * 128], in_=xb[:, ch * 128:(ch + 1) * 128],
                                             identity=c.ident[:]),
                 r=[xk, "ident"], w=[pk])
        k.op("act", lambda e: e.activation(out=xT[:, :, off + tt * 128:off + (tt + 1) * 128],
                                           in_=pst.rearrange("p (c t) -> p c t", c=8), func=AF.Copy),
             r=[pk], w=[(tag, tt)])


def final_norm_store(c):
    k = c.k
    k.dma("sp", c.gbc[:], c.final_g[0, :].partition_broadcast(128), w=["gbc"])
    rms_stats(c)
    for tt in range(NT):
        k.op("dve", lambda e: e.scalar_tensor_tensor(out=c.h[:, tt, :], in0=c.h[:, tt, :], scalar=c.rstd[:, tt:tt + 1],
                                                     in1=c.gbc[:], op0=ALU.mult, op1=ALU.mult),
             r=[("h", tt), "rstd", "gbc"], w=[("h", tt)])
        k.dma("sp", c.out_d[tsl(tt), :], c.h[:, tt, :], r=[("h", tt)], w=[("out", tt)])
    k._wait("sp", k._deps([("out", tt) for tt in range(NT)], []))


def cmul(k, eng_a, eng_b, o_re, o_im, a_re, a_im, b_re, b_im, t, rk, wk, tk):
    k.op(eng_a, lambda e: e.tensor_tensor(out=t[0], in0=a_re, in1=b_re, op=ALU.mult), r=rk, w=[tk + "0"])
    k.op(eng_a, lambda e: e.tensor_tensor(out=t[1], in0=a_im, in1=b_im, op=ALU.mult), r=rk, w=[tk + "1"])
    k.op(eng_b, lambda e: e.tensor_tensor(out=o_re, in0=t[0], in1=t[1], op=ALU.subtract),
         r=[tk + "0", tk + "1"], w=[wk + "_re"])
    k.op(eng_a, lambda e: e.tensor_tensor(out=t[0], in0=a_re, in1=b_im, op=ALU.mult), r=rk, w=[tk + "0"])
    k.op(eng_a, lambda e: e.tensor_tensor(out=t[1], in0=a_im, in1=b_re, op=ALU.mult), r=rk, w=[tk + "1"])
    k.op(eng_b, lambda e: e.tensor_tensor(out=o_im, in0=t[0], in1=t[1], op=ALU.add),
         r=[tk + "0", tk + "1"], w=[wk + "_im"])


def even_mixer(c, li):
    nc, k, sb, ps, pbf, h = c.nc, c.k, c.sb, c.ps, c.pbf, c.h
    din = c.din
    w_in_d = din("e_w_in", [D, 2064])
    w_out_d = din("e_w_out", [D, D])
    a_re_d = din("s5_a_re", [32, 64])
    a_im_d = din("s5_a_im", [32, 64])
    ldt_d = din("s5_log_dt", [32, 64])
    b_re_d = din("s5_b_re", [32, 64, 16])
    b_im_d = din("s5_b_im", [32, 64, 16])
    c_re_d = din("s5_c_re", [32, 16, 64])
    c_im_d = din("s5_c_im", [32, 16, 64])
    d_d = din("s5_d", [1, 512])
    wglu_d = din("s5_w_glu", [512, 512])
    wg2_d = din("gla_w_g2", [16, 256])
    bg2_d = din("gla_b_g2", [1, 256])
    gng_d = din("gla_norm_g", [1, 512])

    with ExitStack() as ph:
        xy = sb("xy", [128, 8, L], BF16, ph)
        uT = sb("uT", [128, 4, L], BF16, ph)
        xT = xy
        yT = xy
        with ExitStack() as pg:
            qkT = sb("qkT", [128, 4, L], BF16, pg)
            rT = sb("rT", [128, 4, L], BF16, pg)
            glowT = sb("glowT", [16, L], BF16, pg)
            vk = sb("vk", [128, NT, 768], BF16, pg)
            with ExitStack() as p1:
                wi = sb("wi", [128, 8, 1040], BF16, p1)
                rmsnorm_T(c, c.norm_mix_g[li, :], xT, "xT")
                allx = [("xT", tt) for tt in range(NT)]
                allw = [("wi", ch) for ch in range(8)]
                n = 0
                for piece in range(2):
                    cb = piece * 1024
                    ncol = 1024 if piece == 0 else 1040
                    for ch in range(8):
                        k.dma("pool", wi[:, ch, 0:ncol], w_in_d[ch * 128:(ch + 1) * 128, cb:cb + ncol], w=[("wi", ch)])
                    if piece == 0:
                        chunks = ([(i * 128, 128, uT, i, AF.Copy) for i in range(4)] +
                                  [(512 + i * 128, 128, qkT, i, AF.Copy) for i in range(4)])
                    else:
                        chunks = ([(1552 + i * 128, 128, rT, i, AF.Silu) for i in range(4)] +
                                  [(1536, 16, None, 0, AF.Copy)])
                    for (c0, m, dst, di, fn) in chunks:
                        for tb in range(4):
                            b = n % 4
                            n += 1
                            for ch in range(8):
                                k.op("pe", lambda e: e.matmul(ps[b][0:m, :], lhsT=wi[:, ch, c0 - cb:c0 - cb + m],
                                                              rhs=xT[:, ch, tb * 512:(tb + 1) * 512],
                                                              start=(ch == 0), stop=(ch == 7)),
                                     r=allx + allw, w=[("ps", b)])
                            if dst is None:
                                k.op("act", lambda e: e.activation(out=glowT[:, tb * 512:(tb + 1) * 512], in_=ps[b][0:16, :],
                                                                   func=AF.Copy), r=[("ps", b)], w=["glowT"])
                            else:
                                k.op("act", lambda e: e.activation(out=dst[:, di, tb * 512:(tb + 1) * 512], in_=ps[b][:, :],
                                                                   func=fn), r=[("ps", b)], w=[(dst.name, di)])
                    for tt in range(NT):
                        b0 = 4 + tt % 4
                        if piece == 0:
                            for ch in range(8):
                                k.op("pe", lambda e: e.matmul(ps[b0][:, 0:256], lhsT=xT[:, ch, tsl(tt)], rhs=wi[:, ch, 768:1024],
                                                              start=(ch == 0), stop=(ch == 7)), r=allx + allw, w=[("ps", b0)])
                            k.op("dve", lambda e: e.tensor_copy(out=vk[:, tt, 512:768], in_=ps[b0][:, 0:256]), r=[("ps", b0)],
                                 w=[("vk", tt)])
                        else:
                            for ch in range(8):
                                k.op("pe", lambda e: e.matmul(ps[b0][:, :], lhsT=xT[:, ch, tsl(tt)], rhs=wi[:, ch, 0:512],
                                                              start=(ch == 0), stop=(ch == 7)), r=allx + allw, w=[("ps", b0)])
                            k.op("dve", lambda e: e.tensor_copy(out=vk[:, tt, 0:512], in_=ps[b0][:, :]), r=[("ps", b0)],
                                 w=[("vk", tt)])
                k.barrier()
            if "uT" in c.dbg:
                for i in range(4):
                    k.dma("sp", c.dbg["uT"][i], uT[:, i, :], r=[("uT", i)], w=["dbg_uT"])
            if "vk" in c.dbg:
                k.dma("sp", c.dbg["vk"], vk[:, 3, :], r=[("vk", 3)], w=["dbg_vk"])
            if 'gla' not in SKIP:
                gla(c, pg, qkT, rT, glowT, vk, yT, wg2_d, bg2_d, gng_d)
            k.barrier()
        if 's5' not in SKIP:
            s5(c, ph, uT, yT, a_re_d, a_im_d, ldt_d, b_re_d, b_im_d, c_re_d, c_im_d, d_d, wglu_d)
        k.barrier()
        if "yT" in c.dbg:
            for i in range(8):
                k.dma("sp", c.dbg["yT"][i], yT[:, i, :], r=[("yT", i)], w=["dbg_yT"])
        with ExitStack() as p3:
            wo = sb("wo", [128, 8, D], BF16, p3)
            for ch in range(8):
                k.dma("pool", wo[:, ch, :], w_out_d[ch * 128:(ch + 1) * 128, :], w=[("wo", ch)])
            ally = [("yT", i) for i in range(8)]
            allw = [("wo", ch) for ch in range(8)]
            for tt in range(NT):
                for hf in range(2):
                    b = (tt * 2 + hf) % 4
                    for ch in range(8):
                        k.op("pe", lambda e: e.matmul(ps[b][:, :], lhsT=yT[:, ch, tsl(tt)],
                                                      rhs=wo[:, ch, hf * 512:(hf + 1) * 512],
                                                      start=(ch == 0), stop=(ch == 7)), r=ally + allw, w=[("ps", b)])
                    k.op("dve", lambda e: e.tensor_tensor(out=h[:, tt, hf * 512:(hf + 1) * 512],
                                                          in0=h[:, tt, hf * 512:(hf + 1) * 512], in1=ps[b][:, :],
                                                          op=ALU.add), r=[("ps", b), ("h", tt)], w=[("h", tt)])
            k.barrier()


def gla(c, ph, qkT, rT, glowT, vk, yT, wg2_d, bg2_d, gng_d):
    nc, k, sb, ps, pbf = c.nc, c.k, c.sb, c.ps, c.pbf
    with ExitStack() as p2:
        wg2 = sb("wg2", [16, 256], BF16, p2)
        bg2 = sb("bg2", [1, 256], BF16, p2)
        gng = sb("gng", [128, 4], F32, p2)
        triUs = sb("triUs", [128, 128], F32, p2)
        triRs = sb("triRs", [128, 128], F32, p2)
        lp = sb("lp", [128, 256], F32, p2)
        eend = sb("eend", [128, 256], F32, p2)
        ebT = sb("ebT", [128, 2, 128], F32, p2)
        enbT = sb("enbT", [128, 2, 128], F32, p2)
        kend = sb("kend", [128, 256], BF16, p2)
        qd = sb("qd", [128, 2, 128], BF16, p2)
        kd = sb("kd", [128, 2, 128], BF16, p2)
        qdz = sb("qdz", [128, 4, 128], BF16, p2)
        hmask = sb("hmask", [128, 2], F32, p2)
        attT = sb("attT", [128, 4, 128], BF16, p2)
        S = sb("S", [128, 2, 128], F32, p2)
        Sb = sb("Sb", [128, 2, 128], BF16, p2)
        ssq = sb("ssq", [128, 4], F32, p2)
        rso = sb("rso", [128, 4], F32, p2)
        on = sb("on", [128, 4, 128], BF16, p2)
        k.dma("pool", wg2[:], wg2_d[:, :], w=["wg2"])
        k.dma("pool", bg2[:], bg2_d[:, :], w=["bg2"])
        k.dma("sp", gng[:], gng_d[0, :].rearrange("(h v) -> v h", v=128), w=["gng"], allow_slow_non_contiguous=True)
        k.op("dve", lambda e: e.tensor_scalar(out=triUs[:], in0=c.triU_f[:], scalar1=-1.0 / 16, scalar2=None,
                                              op0=ALU.mult), r=["triU_f"], w=["triUs"])
        k.op("dve", lambda e: e.tensor_scalar(out=triRs[:], in0=c.triU_f[:], scalar1=1.0 / 16, scalar2=-1.0 / 16,
                                              op0=ALU.mult, op1=ALU.add), r=["triU_f"], w=["triRs"])
        k.op("pool", lambda e: e.memset(hmask[:], 0.0), w=["hmask"])
        k.op("pool", lambda e: e.memset(hmask[0:64, 0:1], 1.0), r=["hmask"], w=["hmask"])
        k.op("pool", lambda e: e.memset(hmask[64:128, 1:2], 1.0), r=["hmask"], w=["hmask"])
        k.op("pool", lambda e: e.memset(S[:], 0.0), w=["S"])
        k.op("pool", lambda e: e.memset(Sb[:], 0.0), w=["Sb"])
        for tt in range(NT):
            k.op("pe", lambda e: e.matmul(ps[0][:, 0:256], lhsT=glowT[:, tsl(tt)], rhs=wg2[:, :], start=True, stop=False),
                 r=["glowT", "wg2"], w=[("ps", 0)])
            k.op("pe", lambda e: e.matmul(ps[0][:, 0:256], lhsT=c.ones_b[0:1, :], rhs=bg2[:, :], start=False, stop=True),
                 r=["ones_b", "bg2"], w=[("ps", 0)])
            if CUT[0] <= 1:
                break
            k.op("act", lambda e: e.activation(out=lp[:], in_=ps[0][:, 0:256], func=AF.Exp, scale=-1.0),
                 r=[("ps", 0)], w=["lp"])
            k.op("act", lambda e: e.activation(out=lp[:], in_=lp[:], func=AF.Ln, bias=c.onecol[:], scale=1.0),
                 r=["lp", "onecol"], w=["lp"])
            if CUT[0] <= 2:
                break
            k.op("pe", lambda e: e.matmul(ps[1][:, 0:256], lhsT=triRs[:], rhs=lp[:], start=True, stop=True),
                 r=["triRs", "lp"], w=[("ps", 1)])
            for hf in range(2):
                k.op("pe", lambda e: e.matmul(ps[1][:, 256 + hf * 128:256 + (hf + 1) * 128],
                                              lhsT=lp[:, hf * 128:(hf + 1) * 128], rhs=triUs[:], start=True, stop=True),
                     r=["triUs", "lp"], w=[("ps", 1)])
            k.op("act", lambda e: e.activation(out=eend[:], in_=ps[1][:, 0:256], func=AF.Exp), r=[("ps", 1)], w=["eend"])
            k.op("act", lambda e: e.activation(out=ebT[:].rearrange("p a b -> p (a b)"), in_=ps[1][:, 256:512], func=AF.Exp),
                 r=[("ps", 1)], w=["ebT"])
            k.op("act", lambda e: e.activation(out=enbT[:].rearrange("p a b -> p (a b)"), in_=ps[1][:, 256:512], func=AF.Exp,
                                               scale=-1.0), r=[("ps", 1)], w=["enbT"])
            if CUT[0] <= 3:
                break
            k.op("dve", lambda e: e.tensor_tensor(out=kend[:], in0=vk[:, tt, 512:768], in1=eend[:], op=ALU.mult),
                 r=[("vk", tt), "eend"], w=["kend"])
            k.op("dve", lambda e: e.scalar_tensor_tensor(out=qd[:], in0=qkT[:, 0:2, tsl(tt)], scalar=0.125, in1=ebT[:],
                                                         op0=ALU.mult, op1=ALU.mult),
                 r=[("qkT", 0), ("qkT", 1), "ebT"], w=["qd"])
            k.op("dve", lambda e: e.tensor_tensor(out=kd[:], in0=qkT[:, 2:4, tsl(tt)], in1=enbT[:], op=ALU.mult),
                 r=[("qkT", 2), ("qkT", 3), "enbT"], w=["kd"])
            if CUT[0] <= 5:
                break
            for hd in range(4):
                pr = hd // 2
                k.op("dve", lambda e: e.tensor_scalar(out=qdz[:, hd, :], in0=qd[:, pr, :], scalar1=hmask[:, hd % 2:hd % 2 + 1],
                                                      scalar2=None, op0=ALU.mult), r=["qd", "hmask"], w=["qdz"])
            for hd in range(4):
                pr = hd // 2
                k.op("pe", lambda e: e.matmul(ps[2][:, hd * 128:(hd + 1) * 128], lhsT=kd[:, pr, :],
                                              rhs=qdz[:, hd, :], start=True, stop=True),
                     r=["kd", "qdz"], w=[("ps", 2)])
            if CUT[0] <= 5.3:
                break
            k.op("dve", lambda e: e.tensor_tensor(out=attT[:], in0=ps[2][:, :].rearrange("p (a b) -> p a b", a=4),
                                                  in1=c.triU_f[:].unsqueeze(1).to_broadcast([128, 4, 128]), op=ALU.mult),
                 r=[("ps", 2), "triU_f"], w=["attT"])
            if CUT[0] <= 5.6:
                break
            for hd in range(4):
                pr, p0 = hd // 2, (hd % 2) * 64
                k.op("pe", lambda e: e.matmul(ps[3][:, hd * 128:(hd + 1) * 128], lhsT=attT[:, hd, :],
                                              rhs=vk[:, tt, hd * 128:(hd + 1) * 128], start=True, stop=False),
                     r=["attT", ("vk", tt)], w=[("ps", 3)])
                k.op("pe", lambda e: e.matmul(ps[3][:, hd * 128:(hd + 1) * 128], lhsT=qdz[:, hd, :],
                                              rhs=Sb[:, pr, :], start=False, stop=True),
                     r=["qdz", "Sb"], w=[("ps", 3)])
            if CUT[0] <= 6:
                break
            for pr in range(2):
                k.op("pe", lambda e: e.matmul(ps[4][:, pr * 256:(pr + 1) * 256], lhsT=kend[:, pr * 128:(pr + 1) * 128],
                                              rhs=vk[:, tt, pr * 256:(pr + 1) * 256], start=True, stop=True),
                     r=["kend", ("vk", tt)], w=[("ps", 4)])
            for hd in range(4):
                pr, hf, p0 = hd // 2, hd % 2, (hd % 2) * 64
                k.op("dve", lambda e: e.scalar_tensor_tensor(
                    out=S[p0:p0 + 64, pr, :], in0=S[p0:p0 + 64, pr, :], scalar=ebT[p0:p0 + 64, pr, 127:128],
                    in1=ps[4][p0:p0 + 64, pr * 256 + hf * 128:pr * 256 + (hf + 1) * 128], op0=ALU.mult, op1=ALU.add),
                     r=["S", "ebT", ("ps", 4)], w=["S"])
            k.op("dve", lambda e: e.tensor_copy(out=Sb[:], in_=S[:]), r=["S"], w=["Sb"])
            if CUT[0] <= 7:
                break
            for hd in range(4):
                k.op("act", lambda e: e.activation(out=c.junk[:, 0:128], in_=ps[3][:, hd * 128:(hd + 1) * 128],
                                                   func=AF.Square, accum_out=ssq[:, hd:hd + 1]),
                     r=[("ps", 3)], w=["junk", "ssq"])
            k.op("dve", lambda e: e.tensor_scalar(out=rso[:], in0=ssq[:], scalar1=1.0 / 128, scalar2=EPS,
                                                  op0=ALU.mult, op1=ALU.add), r=["ssq"], w=["rso"])
            k.op("act", lambda e: e.activation(out=rso[:], in_=rso[:], func=AF.Sqrt), r=["rso"], w=["rso"])
            k.op("dve", lambda e: e.reciprocal(out=rso[:], in_=rso[:]), r=["rso"], w=["rso"])
            k.op("dve", lambda e: e.tensor_tensor(out=on[:], in0=ps[3][:, :].rearrange("p (a b) -> p a b", a=4),
                                                  in1=rso[:].unsqueeze(2).to_broadcast([128, 4, 128]), op=ALU.mult),
                 r=[("ps", 3), "rso"], w=["on"])
            for hd in range(4):
                k.op("pe", lambda e: e.transpose(out=pbf[5][:, hd * 128:(hd + 1) * 128], in_=on[:, hd, :],
                                                 identity=c.ident[:]), r=["on", "ident"], w=[("ps", 5)])
            for hd in range(4):
                k.op("dve", lambda e: e.scalar_tensor_tensor(
                    out=yT[:, 4 + hd, tsl(tt)], in0=pbf[5][:, hd * 128:(hd + 1) * 128], scalar=gng[:, hd:hd + 1],
                    in1=rT[:, hd, tsl(tt)], op0=ALU.mult, op1=ALU.mult),
                     r=[("ps", 5), "gng", ("rT", hd)], w=[("yT", 4 + hd)])


def s5(c, ph, uT, yT, a_re_d, a_im_d, ldt_d, b_re_d, b_im_d, c_re_d, c_im_d, d_d, wglu_d):
    nc, k, sb, ps, pbf = c.nc, c.k, c.sb, c.ps, c.pbf
    with ExitStack() as p2:
        wb = sb("wb", [128, 2, 4, 512], BF16, p2)
        cm = sb("cm", [128, 2, 16, 128], BF16, p2)
        with ExitStack() as pa:
            wbs = sb("wbs", [128, 2, 4, 512], F32, pa)
            cms = sb("cms", [128, 2, 16, 128], F32, pa)
            k.op("pool", lambda e: e.memset(wbs[:].rearrange("p a b c -> p (a b c)"), 0.0), w=["wbs"])
            k.op("pool", lambda e: e.memset(cms[:].rearrange("p a b c -> p (a b c)"), 0.0), w=["cms"])
            for ri, bd in enumerate((b_re_d, b_im_d)):
                for g in range(32):
                    kc, g8 = g // 8, g % 8
                    k.dma("sp", wbs[g8 * 16:(g8 + 1) * 16, ri, kc, g8 * 64:(g8 + 1) * 64],
                          bd[g].rearrange("p c -> c p"), r=[], w=["wbs"], allow_slow_non_contiguous=True)
            for ri, cd in enumerate((c_re_d, c_im_d)):
                for g in range(32):
                    ct, gl = g // 2, g % 2
                    g8 = g % 8
                    k.dma("sp", cms[gl * 64:(gl + 1) * 64, ri, ct, g8 * 16:(g8 + 1) * 16],
                          cd[g].rearrange("c p -> p c"), r=[], w=["cms"], allow_slow_non_contiguous=True)
            k.op("act", lambda e: e.activation(out=wb[:].rearrange("p a b c -> p (a b c)"),
                                               in_=wbs[:].rearrange("p a b c -> p (a b c)"), func=AF.Copy),
                 r=["wbs"], w=["wb"])
            k.op("act", lambda e: e.activation(out=cm[:, 0].rearrange("p b c -> p (b c)"),
                                               in_=cms[:, 0].rearrange("p b c -> p (b c)"), func=AF.Copy),
                 r=["cms"], w=["cm"])
            k.op("act", lambda e: e.activation(out=cm[:, 1].rearrange("p b c -> p (b c)"),
                                               in_=cms[:, 1].rearrange("p b c -> p (b c)"), func=AF.Copy, scale=-1.0),
                 r=["cms"], w=["cm"])
            k.barrier()
        if CUT[0] <= 10:
            return
        dcol = sb("dcol", [128, 4], F32, p2)
        k.dma("sp", dcol[:], d_d[0, :].rearrange("(c p) -> p c", p=128), w=["dcol"], allow_slow_non_contiguous=True)
        wglu = sb("wglu", [128, 4, 512], BF16, p2)
        for ch in range(4):
            k.dma("pool", wglu[:, ch, :], wglu_d[ch * 128:(ch + 1) * 128, :], w=["wglu"])
        ETb = sb("ETb", [128, 2, 16, 128], BF16, p2)
        EVt = sb("EVt", [128, 2, 2048], BF16, p2)
        a128 = sb("a128", [128, 2, 16], F32, p2)
        with ExitStack() as pb:
            prm = sb("prm", [16, 3, 128], F32, pb)
            for i, pd in enumerate((a_re_d, a_im_d, ldt_d)):
                k.dma("sp", prm[:, i, :], pd.rearrange("(ct gl) p -> ct (gl p)", gl=2), w=["prm"])
            for i in range(3):
                k.op("pe", lambda e: e.transpose(out=ps[0][:, i * 16:(i + 1) * 16], in_=prm[:, i, :], identity=c.identf[0:16, 0:16]),
                     r=["prm", "identf"], w=[("ps", 0)])
            if CUT[0] <= 10.5:
                return
            P = sb("P", [128, 24, 16], F32, pb)
            AR, AI, DT, MAG, TH, S_, C_, T0, T1, RM, FRE, FIM, NR, DEN, ABR, ABI, AVR, AVI = range(18)
            k.op("dve", lambda e: e.tensor_copy(out=P[:, 0:3, :], in_=ps[0][:, 0:48].rearrange("p (a b) -> p a b", a=3)),
                 r=[("ps", 0)], w=["P"])

            def tt_(o, a, b, op, eng="dve"):
                k.op(eng, lambda e: e.tensor_tensor(out=P[:, o, :], in0=P[:, a, :], in1=P[:, b, :], op=op), r=["P"], w=["P"])

            def act_(o, a, fn, scale=1.0):
                k.op("act", lambda e: e.activation(out=P[:, o, :], in_=P[:, a, :], func=fn, scale=scale), r=["P"], w=["P"])

            def ts_(o, a, s1, s2, op0, op1):
                k.op("dve", lambda e: e.tensor_scalar(out=P[:, o, :], in0=P[:, a, :], scalar1=s1, scalar2=s2, op0=op0, op1=op1),
                     r=["P"], w=["P"])

            if CUT[0] <= 11:
                return
            act_(DT, DT, AF.Exp)
            tt_(T0, DT, AR, ALU.mult)
            act_(MAG, T0, AF.Exp)
            act_(RM, T0, AF.Exp, scale=-1.0)
            tt_(TH, DT, AI, ALU.mult)
            act_(T0, TH, AF.Sin, scale=1.0 / 16)
            tt_(T0, T0, T0, ALU.mult)
            ts_(C_, T0, -2.0, 1.0, ALU.mult, ALU.add)
            act_(S_, TH, AF.Sin, scale=1.0 / 8)
            for _ in range(3):
                tt_(T0, C_, C_, ALU.mult)
                tt_(T1, S_, S_, ALU.mult)
                tt_(S_, S_, C_, ALU.mult)
                ts_(S_, S_, 2.0, None, ALU.mult, ALU.bypass)
                tt_(C_, T0, T1, ALU.subtract)
            tt_(ABR, MAG, C_, ALU.mult)
            tt_(ABI, MAG, S_, ALU.mult)
            tt_(AVR, RM, C_, ALU.mult)
            tt_(AVI, RM, S_, ALU.mult)
            ts_(AVI, AVI, -1.0, None, ALU.mult, ALU.bypass)
            ts_(NR, ABR, -1.0, None, ALU.add, ALU.bypass)
            tt_(T0, AR, AR, ALU.mult)
            tt_(T1, AI, AI, ALU.mult)
            tt_(DEN, T0, T1, ALU.add)
            k.op("dve", lambda e: e.reciprocal(out=P[:, DEN, :], in_=P[:, DEN, :]), r=["P"], w=["P"])
            tt_(T0, NR, AR, ALU.mult)
            tt_(T1, ABI, AI, ALU.mult)
            tt_(T0, T0, T1, ALU.add)
            tt_(FRE, T0, DEN, ALU.mult)
            tt_(T0, ABI, AR, ALU.mult)
            tt_(T1, NR, AI, ALU.mult)
            tt_(T0, T0, T1, ALU.subtract)
            tt_(FIM, T0, DEN, ALU.mult)
            if CUT[0] <= 12:
                return
            ET = sb("ET", [128, 2, 16, 128], F32, pb)
            EV = sb("EV", [128, 2, 16, 128], F32, pb)
            tmp = sb("s5tmp", [128, 2, 16, 64], F32, pb)
            pw = sb("s5pw", [128, 2, 16], F32, pb)
            pw2 = sb("s5pw2", [128, 2, 16], F32, pb)
            for (tab, br, bi, i0r, i0i, name) in ((ET, ABR, ABI, None, None, "ET"), (EV, AVR, AVI, FRE, FIM, "EV")):
                if i0r is None:
                    k.op("pool", lambda e: e.memset(tab[:, 0, :, 0:1], 1.0), w=[name])
                    k.op("pool", lambda e: e.memset(tab[:, 1, :, 0:1], 0.0), w=[name])
                else:
                    k.op("dve", lambda e: e.tensor_copy(out=tab[:, 0, :, 0:1], in_=P[:, i0r, :].unsqueeze(2)), r=["P"], w=[name])
                    k.op("dve", lambda e: e.tensor_copy(out=tab[:, 1, :, 0:1], in_=P[:, i0i, :].unsqueeze(2)), r=["P"], w=[name])
                k.op("dve", lambda e: e.tensor_copy(out=pw[:, 0, :], in_=P[:, br, :]), r=["P"], w=["pw"])
                k.op("dve", lambda e: e.tensor_copy(out=pw[:, 1, :], in_=P[:, bi, :]), r=["P"], w=["pw"])
                m = 1
                while m <= 128:
                    if m < 128:
                        bre = pw[:, 0, :].unsqueeze(2).to_broadcast([128, 16, m])
                        bim = pw[:, 1, :].unsqueeze(2).to_broadcast([128, 16, m])
                        cmul(k, "dve", "dve", tab[:, 0, :, m:2 * m], tab[:, 1, :, m:2 * m],
                             tab[:, 0, :, 0:m], tab[:, 1, :, 0:m], bre, bim,
                             (tmp[:, 0, :, 0:m], tmp[:, 1, :, 0:m]), [name, name + "_re", name + "_im", "pw"], name, "s5tmp")
                    elif name == "ET":
                        k.op("dve", lambda e: e.tensor_copy(out=a128[:], in_=pw[:]), r=["pw"], w=["a128"])
                    cmul(k, "dve", "dve", pw2[:, 0, :], pw2[:, 1, :], pw[:, 0, :], pw[:, 1, :], pw[:, 0, :], pw[:, 1, :],
                         (tmp[:, 0, :, 0], tmp[:, 1, :, 0]), ["pw"], "pw2", "s5tmp")
                    k.op("dve", lambda e: e.tensor_copy(out=pw[:], in_=pw2[:]), r=["pw2_re", "pw2_im"], w=["pw"])
                    m *= 2
            if CUT[0] <= 13:
                return
            n = 0
            for ri in range(2):
                for g4 in range(4):
                    b = n % 2
                    n += 1
                    for q in range(4):
                        ct = g4 * 4 + q
                        k.op("pe", lambda e: e.transpose(out=ps[b][:, q * 128:(q + 1) * 128], in_=EV[:, ri, ct, :],
                                                         identity=c.identf[:]), r=["EV", "EV_re", "EV_im", "identf"], w=[("ps", b)])
                    k.op("act", lambda e: e.activation(out=EVt[:, ri, g4 * 512:(g4 + 1) * 512], in_=ps[b][:, :], func=AF.Copy),
                         r=[("ps", b)], w=["EVt"])

            k.op("act", lambda e: e.activation(out=ETb[:].rearrange("p a b c -> p (a b c)"),
                                               in_=ET[:].rearrange("p a b c -> p (a b c)"), func=AF.Copy),
                 r=["ET", "ET_re", "ET_im"], w=["ETb"])
            k.barrier()
        if CUT[0] <= 14:
            return
        tmpc = sb("s5tmpc", [128, 2, 16], F32, p2)
        zz = sb("zz", [128, 2, 2048], BF16, p2)
        t1 = sb("s5t1", [128, 512], F32, p2)
        t2 = sb("s5t2", [128, 512], F32, p2)
        sT = sb("sT", [128, 2, 16, 128], BF16, p2)
        lastc = sb("lastc", [128, 2, 16], F32, p2)
        cz = sb("cz", [128, 2, 16], F32, p2)
        cz2 = sb("cz2", [128, 2, 16], F32, p2)
        wr = sb("s5wr", [128, 512], F32, p2)
        wi_ = sb("s5wi", [128, 512], F32, p2)
        ypre = sb("ypre", [128, 4, 128], F32, p2)
        if CUT[0] <= 15:
            return
        for tt in range(NTT[0]):
            for kc in range(4):
                for ri in range(2):
                    k.op("pe", lambda e: e.matmul(ps[ri][:, :], lhsT=uT[:, kc, tsl(tt)], rhs=wb[:, ri, kc, :],
                                                  start=True, stop=True), r=[("uT", kc), "wb"], w=[("ps", ri)])
                er = EVt[:, 0, kc * 512:(kc + 1) * 512]
                ei = EVt[:, 1, kc * 512:(kc + 1) * 512]
                k.op("dve", lambda e: e.tensor_tensor(out=t1[:], in0=ps[0][:, :], in1=er, op=ALU.mult),
                     r=[("ps", 0), "EVt"], w=["s5t1"])
                k.op("dve", lambda e: e.tensor_tensor(out=t2[:], in0=ps[1][:, :], in1=ei, op=ALU.mult),
                     r=[("ps", 1), "EVt"], w=["s5t2"])
                k.op("pool", lambda e: e.tensor_tensor(out=zz[:, 0, kc * 512:(kc + 1) * 512], in0=t1[:], in1=t2[:],
                                                       op=ALU.subtract), r=["s5t1", "s5t2"], w=["zz"])
                k.op("dve", lambda e: e.tensor_tensor(out=t1[:], in0=ps[1][:, :], in1=er, op=ALU.mult),
                     r=[("ps", 1), "EVt"], w=["s5t1"])
                k.op("dve", lambda e: e.tensor_tensor(out=t2[:], in0=ps[0][:, :], in1=ei, op=ALU.mult),
                     r=[("ps", 0), "EVt"], w=["s5t2"])
                k.op("pool", lambda e: e.tensor_tensor(out=zz[:, 1, kc * 512:(kc + 1) * 512], in0=t1[:], in1=t2[:],
                                                       op=ALU.add), r=["s5t1", "s5t2"], w=["zz"])
            if CUT[0] <= 16 and tt >= 1:
                return
            for g4 in range(4):
                for ri in range(2):
                    b = 2 + ri
                    for q in range(4):
                        ct = g4 * 4 + q
                        k.op("pe", lambda e: e.matmul(ps[b][:, q * 128:(q + 1) * 128], lhsT=zz[:, ri, ct * 128:(ct + 1) * 128],
                                                      rhs=c.triU_b[:], start=True, stop=True),
                             r=["zz", "triU_b"], w=[("ps", b)])
                cr = ps[2][:, :].rearrange("p (a b) -> p a b", a=4)
                ci = ps[3][:, :].rearrange("p (a b) -> p a b", a=4)
                etr = ETb[:, 0, g4 * 4:(g4 + 1) * 4, :]
                eti = ETb[:, 1, g4 * 4:(g4 + 1) * 4, :]
                t1v = t1[:].rearrange("p (a b) -> p a b", a=4)
                t2v = t2[:].rearrange("p (a b) -> p a b", a=4)
                wrv = wr[:].rearrange("p (a b) -> p a b", a=4)
                wiv = wi_[:].rearrange("p (a b) -> p a b", a=4)
                if tt == 0:
                    k.op("act", lambda e: e.activation(out=wrv, in_=cr, func=AF.Copy), r=[("ps", 2)], w=["wr"])
                    k.op("act", lambda e: e.activation(out=wiv, in_=ci, func=AF.Copy), r=[("ps", 3)], w=["wi_"])
                else:
                    k.op("dve", lambda e: e.tensor_tensor(out=wrv, in0=cr, in1=cz[:, 0, g4 * 4:(g4 + 1) * 4].unsqueeze(2).to_broadcast([128, 4, 128]),
                                                          op=ALU.add), r=[("ps", 2), "cz_re"], w=["wr"])
                    k.op("dve", lambda e: e.tensor_tensor(out=wiv, in0=ci, in1=cz[:, 1, g4 * 4:(g4 + 1) * 4].unsqueeze(2).to_broadcast([128, 4, 128]),
                                                          op=ALU.add), r=[("ps", 3), "cz_im"], w=["wi_"])
                k.op("act", lambda e: e.activation(out=lastc[:, 0, g4 * 4:(g4 + 1) * 4], in_=wrv[:, :, 127], func=AF.Copy),
                     r=["wr"], w=["lastc"])
                k.op("act", lambda e: e.activation(out=lastc[:, 1, g4 * 4:(g4 + 1) * 4], in_=wiv[:, :, 127], func=AF.Copy),
                     r=["wi_"], w=["lastc"])
                k.op("dve", lambda e: e.tensor_tensor(out=t1v, in0=wrv, in1=etr, op=ALU.mult), r=["wr", "ETb"], w=["s5t1"])
                k.op("pool", lambda e: e.tensor_tensor(out=t2v, in0=wiv, in1=eti, op=ALU.mult), r=["wi_", "ETb"], w=["s5t2"])
                k.op("dve", lambda e: e.tensor_tensor(out=sT[:, 0, g4 * 4:(g4 + 1) * 4, :], in0=t1v, in1=t2v, op=ALU.subtract),
                     r=["s5t1", "s5t2"], w=["sT"])
                k.op("dve", lambda e: e.tensor_tensor(out=t1v, in0=wiv, in1=etr, op=ALU.mult), r=["wi_", "ETb"], w=["s5t1"])
                k.op("pool", lambda e: e.tensor_tensor(out=t2v, in0=wrv, in1=eti, op=ALU.mult), r=["wr", "ETb"], w=["s5t2"])
                k.op("dve", lambda e: e.tensor_tensor(out=sT[:, 1, g4 * 4:(g4 + 1) * 4, :], in0=t1v, in1=t2v, op=ALU.add),
                     r=["s5t1", "s5t2"], w=["sT"])
            if tt < NT - 1:
                cmul(k, "dve", "dve", cz2[:, 0, :], cz2[:, 1, :], lastc[:, 0, :], lastc[:, 1, :], a128[:, 0, :], a128[:, 1, :],
                     (tmpc[:, 0, :], tmpc[:, 1, :]), ["lastc", "a128"], "cz2", "s5tmpc")
                k.op("dve", lambda e: e.tensor_copy(out=cz[:], in_=cz2[:]), r=["cz2_re", "cz2_im"], w=["cz_re", "cz_im"])
            if CUT[0] <= 19 and tt >= 1:
                return
            for kc in range(4):
                n = 0
                for q in range(4):
                    ct = kc * 4 + q
                    for ri in range(2):
                        k.op("pe", lambda e: e.matmul(ps[5][:, kc * 128:(kc + 1) * 128], lhsT=cm[:, ri, ct, :],
                                                      rhs=sT[:, ri, ct, :], start=(n == 0), stop=(n == 7)),
                             r=["cm", "sT"], w=[("ps", 5)])
                        n += 1
            if CUT[0] <= 19.3 and tt >= 1:
                return
            for kc in range(4):
                k.op("dve", lambda e: e.tensor_scalar(out=ypre[:, kc, :], in0=uT[:, kc, tsl(tt)], scalar1=dcol[:, kc:kc + 1],
                                                      scalar2=None, op0=ALU.mult), r=[("uT", kc), "dcol"], w=["ypre"])
                k.op("dve", lambda e: e.tensor_tensor(out=ypre[:, kc, :], in0=ypre[:, kc, :], in1=ps[5][:, kc * 128:(kc + 1) * 128],
                                                      op=ALU.add), r=["ypre", ("ps", 5)], w=["ypre"])
            if CUT[0] <= 19.6 and tt >= 1:
                return
            for kc in range(4):
                if GELU_FN[0] is None:
                    k.op("dve", lambda e: e.tensor_copy(out=yT[:, kc, tsl(tt)], in_=ypre[:, kc, :]), r=["ypre"], w=[("yT", kc)])
                else:
                    k.op("act", lambda e: e.activation(out=yT[:, kc, tsl(tt)], in_=ypre[:, kc, :], func=GELU_FN[0]), r=["ypre"],
                         w=[("yT", kc)])
        for nm, tl, kk in (("ypre", ypre, ["ypre"]), ("zz", zz, ["zz"]), ("sT", sT, ["sT"]), ("ETb", ETb, ["ETb"]), ("EVt", EVt, ["EVt"]),
                           ("wb", wb, ["wb"]), ("cm", cm, ["cm"]), ("a128", a128, ["a128"]), ("lastc", lastc, ["lastc"])):
            if nm in c.dbg:
                ap = tl[:]
                if len(ap.shape) == 3:
                    ap = ap.rearrange("p a b -> p (a b)")
                elif len(ap.shape) == 4:
                    ap = ap.rearrange("p a b c -> p (a b c)")
                k.dma("sp", c.dbg[nm], ap, r=kk, w=["dbg_" + nm])
        if CUT[0] <= 20:
            return
        sg = sb("sg", [128, 4, 512], BF16, p2)
        yk = [("yT", i) for i in range(4)]
        n = 0
        for tb in range(4):
            for c2 in range(4):
                b = 6 + n % 2
                n += 1
                for ch in range(4):
                    k.op("pe", lambda e: e.matmul(ps[b][:, :], lhsT=wglu[:, ch, c2 * 128:(c2 + 1) * 128],
                                                  rhs=yT[:, ch, tb * 512:(tb + 1) * 512], start=(ch == 0), stop=(ch == 3)),
                         r=["wglu"] + yk, w=[("ps", b)])
                k.op("act", lambda e: e.activation(out=sg[:, c2, :], in_=ps[b][:, :], func=AF.Sigmoid), r=[("ps", b)],
                     w=[("sg", c2)])
            for c2 in range(4):
                k.op("dve", lambda e: e.tensor_tensor(out=yT[:, c2, tb * 512:(tb + 1) * 512], in0=yT[:, c2, tb * 512:(tb + 1) * 512],
                                                      in1=sg[:, c2, :], op=ALU.mult), r=[("yT", c2), ("sg", c2)], w=[("yT", c2)])
        k.barrier()


def peer_inputs(c):
    c.wq_d = c.din("peer_w_q", [2, D, 2048])
    c.keys_d = c.din("peer_sub_keys", [2, 8, 2, 128, 128])
    c.u_d = c.din("peer_u", [2, 16384, D])
    c.v_d = c.din("peer_v", [2, 16384, D])
    c.ut_scr = c.nc.dram_tensor("ut_scr", [8, 128, 16384], BF16, kind="Internal").ap()
    c.vb_scr = c.nc.dram_tensor("vb_scr", [16384, D], BF16, kind="Internal").ap()


NEG = -1.0
PEER_EG = [32]
NPROD = 6
PEER_ACT_HEADS = [5]
PEER_PROD = [['act', 'pool', 'pool', 'act', 'pool', 'pool', 'act', 'pool']]


def peer(c, li):
    nc, k, sb, ps, h = c.nc, c.k, c.sb, c.ps, c.h
    wq_d, keys_d, u_d, v_d = c.wq_d[li], c.keys_d[li], c.u_d[li], c.v_d[li]
    utv = c.ut_scr.rearrange("dc d e -> d dc e")
    def prepass_tiles(pp):
        usts = [sb("pust%d" % i, [128, 2, D], BF16, pp) for i in range(2)]
        uts = [sb("puts%d" % i, [128, 8, 256], BF16, pp) for i in range(2)]
        vbs_ = [sb("pvb%d" % i, [128, 2, D], BF16, pp) for i in range(2)]
        return usts, uts, vbs_

    def prepass_gen(tiles):
        usts, uts, vbs_ = tiles
        for st_ in range(64):
            i = st_ % 2
            e0 = st_ * 256
            uk, tk, vk_ = "pust%d" % i, "puts%d" % i, "pvb%d" % i
            k.dma("pool", usts[i][:], u_d[e0:e0 + 256, :].rearrange("(a p) d -> p a d", p=128), w=[uk])
            k.dma("pool", vbs_[i][:], v_d[e0:e0 + 256, :].rearrange("(a p) d -> p a d", p=128), w=[vk_])
            yield
            for dc in range(8):
                b_ = 6 + dc % 2
                for a_ in range(2):
                    k.op("pe", lambda e: e.transpose(out=c.pbf[b_][:, a_ * 128:(a_ + 1) * 128], in_=usts[i][:, a_, dc * 128:(dc + 1) * 128],
                                                     identity=c.ident[:]), r=[uk, "ident"], w=[("ps", b_)])
                k.op("act", lambda e: e.activation(out=uts[i][:, dc, :], in_=c.pbf[b_][:, 0:256], func=AF.Copy), r=[("ps", b_)],
                     w=[tk])
                if dc % 2 == 1:
                    yield
            k.dma("sp", utv[:, :, e0:e0 + 256], uts[i][:], r=[tk], w=[("utscr", st_ // 2)])
            k.dma("sp", c.vb_scr[e0:e0 + 256, :].rearrange("(a p) d -> p a d", p=128), vbs_[i][:], r=[vk_],
                  w=[("vbscr", st_ // 2)])
            yield

    with ExitStack() as ph:
        hnT = sb("hnT", [128, 8, L], BF16, ph)
        rmsnorm_T(c, c.norm_ffn_g[li, :], hnT, "hnT")
        allhn = [("hnT", tt) for tt in range(NT)]
        e_all = sb("e_all", [128, 4, 16, 128], F32, ph)
        phi = sb("phi", [128, 4, 8], F32, ph)
        mx = sb("pmx", [128, 16], F32, ph)
        t16 = sb("t16", [128, 16, 16], F32, ph)
        tmpb = sb("tmpb", [128, 128], F32, ph)
        cand = sb("cand", [128, 256], F32, ph)
        cand2 = sb("cand2", [128, 256], F32, ph)
        c16 = sb("c16", [128, 16], F32, ph)
        zs = sb("zs", [128, 8], F32, ph)
        pp = ExitStack()
        pgen = [prepass_gen(prepass_tiles(pp))]

        def pump(n=1):
            for _ in range(n):
                if pgen[0] is None:
                    return
                try:
                    next(pgen[0])
                except StopIteration:
                    pgen[0] = None

        for tg in range(4):
            with ExitStack() as p1:
                kT = sb("kT", [128, 16, 128], BF16, p1)
                with ExitStack() as p0:
                    kst = sb("kst", [128, 16, 128], F32, p0)
                    k.dma("sp", kst[:], keys_d.rearrange("h c n d -> n (h c) d"), w=["kst"])
                    for g in range(4):
                        b = g % 2
                        for q in range(4):
                            k.op("pe", lambda e: e.transpose(out=ps[b][:, q * 128:(q + 1) * 128], in_=kst[:, g * 4 + q, :],
                                                             identity=c.identf[:]), r=["kst", "identf"], w=[("ps", b)])
                        k.op("act", lambda e: e.activation(out=kT[:, g * 4:(g + 1) * 4, :].rearrange("p a b -> p (a b)"), in_=ps[b][:, :],
                                                           func=AF.Copy), r=[("ps", b)], w=["kT"])
                    k.barrier()

                wq = sb("wq", [128, 8, 1024], BF16, p1)
                qT = sb("qT", [128, 16, 512], BF16, p1)
                allwq = [("wq", ch) for ch in range(8)]
                for blk in range(16):
                    b = blk % 2
                    if blk % 8 == 0:
                        for ch in range(8):
                            k.dma("pool", wq[:, ch, :], wq_d[ch * 128:(ch + 1) * 128, (blk // 8) * 1024:(blk // 8 + 1) * 1024],
                                  w=[("wq", ch)])
                    bl = blk % 8
                    for ch in range(8):
                        k.op("pe", lambda e: e.matmul(ps[b][:, :], lhsT=wq[:, ch, bl * 128:(bl + 1) * 128],
                                                      rhs=hnT[:, ch, tg * 512:(tg + 1) * 512], start=(ch == 0), stop=(ch == 7)),
                             r=allwq + allhn, w=[("ps", b)])
                    k.op("act", lambda e: e.activation(out=qT[:, blk, :], in_=ps[b][:, :], func=AF.Copy), r=[("ps", b)],
                         w=[("qT", blk)])
                    pump(2)
                for tt in range(4):
                    for blk in range(16):
                        b = 2 + blk // 4
                        k.op("pe", lambda e: e.matmul(ps[b][:, (blk % 4) * 128:(blk % 4 + 1) * 128], lhsT=qT[:, blk, tt * 128:(tt + 1) * 128],
                                                      rhs=kT[:, blk, :], start=True, stop=True), r=[("qT", blk), "kT"], w=[("ps", b)])
                    for b4 in range(4):
                        k.op("dve", lambda e: e.tensor_reduce(out=mx[:, b4 * 4:(b4 + 1) * 4],
                                                              in_=ps[2 + b4][:, :].rearrange("p (a n) -> p a n", a=4),
                                                              axis=AX.X, op=ALU.max), r=[("ps", 2 + b4)], w=["pmx"])
                    k.op("dve", lambda e: e.tensor_scalar(out=mx[:], in0=mx[:], scalar1=-1.0, scalar2=None, op0=ALU.mult),
                         r=["pmx"], w=["pmx"])
                    for blk in range(16):
                        b = 2 + blk // 4
                        k.op("act", lambda e: e.activation(out=e_all[:, tt, blk, :], in_=ps[b][:, (blk % 4) * 128:(blk % 4 + 1) * 128],
                                                           func=AF.Exp, bias=mx[:, blk:blk + 1], scale=1.0),
                             r=[("ps", b), "pmx"], w=[("e_all", tt, blk)])
                    for blk in range(16):
                        ek = ("e_all", tt, blk)
                        k.op("dve", lambda e: e.max(out=t16[:, blk, 0:8], in_=e_all[:, tt, blk, :]), r=[ek], w=["t16"])
                        k.op("dve", lambda e: e.match_replace(out=tmpb[:], in_to_replace=t16[:, blk, 0:8],
                                                              in_values=e_all[:, tt, blk, :], imm_value=NEG),
                             r=[ek, "t16"], w=["tmpb"])
                        k.op("dve", lambda e: e.max(out=t16[:, blk, 8:16], in_=tmpb[:]), r=["tmpb"], w=["t16"])
                        pump(1)
                    for hd in range(8):
                        k.op("dve", lambda e: e.tensor_tensor(
                            out=cand[:].rearrange("p (i j) -> p i j", i=16),
                            in0=t16[:, 2 * hd, :].unsqueeze(2).to_broadcast([128, 16, 16]),
                            in1=t16[:, 2 * hd + 1, :].unsqueeze(1).to_broadcast([128, 16, 16]), op=ALU.mult),
                             r=["t16"], w=["cand"])
                        k.op("dve", lambda e: e.max(out=c16[:, 0:8], in_=cand[:]), r=["cand"], w=["c16"])
                        k.op("dve", lambda e: e.match_replace(out=cand2[:], in_to_replace=c16[:, 0:8], in_values=cand[:],
                                                              imm_value=NEG), r=["cand", "c16"], w=["cand2"])
                        k.op("dve", lambda e: e.max(out=c16[:, 8:16], in_=cand2[:]), r=["cand2"], w=["c16"])
                        k.op("dve", lambda e: e.tensor_scalar(out=phi[:, tt, hd:hd + 1], in0=c16[:, 15:16], scalar1=1.0 - 2e-6, scalar2=None,
                                                              op0=ALU.mult), r=["c16"], w=["phi"])
                        k.op("dve", lambda e: e.tensor_reduce(out=zs[:, hd:hd + 1], in_=c16[:], axis=AX.X, op=ALU.add),
                             r=["c16"], w=["zs"])
                        pump(2)
                    k.op("dve", lambda e: e.reciprocal(out=zs[:], in_=zs[:]), r=["zs"], w=["zs"])
                    k.op("dve", lambda e: e.tensor_tensor(out=phi[:, tt, :], in0=phi[:, tt, :], in1=zs[:], op=ALU.mult),
                         r=["phi", "zs"], w=["phi"])
                    for hd in range(8):
                        k.op("dve", lambda e: e.tensor_scalar(out=e_all[:, tt, 2 * hd, :], in0=e_all[:, tt, 2 * hd, :],
                                                              scalar1=zs[:, hd:hd + 1], scalar2=None, op0=ALU.mult),
                             r=["zs", ("e_all", tt, 2 * hd)], w=[("e_all", tt, 2 * hd)])
                pump(100000)
                k.barrier()
            if tg == 0:
                pp.close()
            with ExitStack() as p2:
                utsb = sb("utsb", [128, 8, 512], BF16, p2)
                vbs = [sb("vb%d" % i, [128, 4, D], BF16, p2) for i in range(2)]
                gels = [sb("gel%d" % i, [128, 4, 512], BF16, p2) for i in range(2)]
                Gs = [sb("G%d" % i, [128, 8, 512], BF16, p2) for i in range(2)]
                prod = [sb("prod%d" % i, [128, 512], F32, p2) for i in range(NPROD)]
                HsTs = [sb("HsT%d" % i, [128, 4, 128], BF16, p2) for i in range(2)]
                NEG_ = PEER_EG[0]
                steps = [(eg, tt) for eg in range(NEG_) for tt in range(4)]
                npr = [0]

                def load(eg):
                    k.dma("sp", utsb[:], utv[:, :, eg * 512:(eg + 1) * 512], r=[("utscr", eg)], w=["utsb"])
                    k.dma("sp", vbs[eg % 2][:], c.vb_scr[eg * 512:(eg + 1) * 512, :].rearrange("(a p) d -> p a d", p=128),
                          r=[("vbscr", eg)], w=["vb%d" % (eg % 2)])

                def hpart(eg, a):
                    b = a % 2
                    for dc in range(8):
                        k.op("pe", lambda e: e.matmul(ps[b][:, :], lhsT=utsb[:, dc, a * 128:(a + 1) * 128],
                                                      rhs=hnT[:, dc, tg * 512:(tg + 1) * 512], start=(dc == 0), stop=(dc == 7)),
                             r=["utsb"] + allhn, w=[("ps", b)])
                    k.op("act", lambda e: e.activation(out=gels[eg % 2][:, a, :], in_=ps[b][:, :], func=AF.Gelu), r=[("ps", b)],
                         w=[("gel%d" % (eg % 2), a)])

                def stage_a(s_):
                    eg, tt = steps[s_]
                    G, gk = Gs[s_ % 2], "G%d" % (s_ % 2)
                    for hd in range(8):
                        pr = prod[npr[0] % NPROD]
                        pk = "prod%d" % (npr[0] % NPROD)
                        npr[0] += 1
                        pe_ = PEER_PROD[0][hd]
                        if pe_ == 'act':
                            for a in range(4):
                                k.op("act", lambda e: e.activation(out=pr[:, a * 128:(a + 1) * 128], in_=e_all[:, tt, 2 * hd + 1, :],
                                                                   func=AF.Copy, scale=e_all[:, tt, 2 * hd, eg * 4 + a:eg * 4 + a + 1]),
                                     r=[("e_all", tt, 2 * hd), ("e_all", tt, 2 * hd + 1)], w=[pk])
                        else:
                            k.op(pe_, lambda e: e.tensor_tensor(
                                out=pr[:].rearrange("p (a n) -> p a n", a=4),
                                in0=e_all[:, tt, 2 * hd, eg * 4:(eg + 1) * 4].unsqueeze(2).to_broadcast([128, 4, 128]),
                                in1=e_all[:, tt, 2 * hd + 1, :].unsqueeze(1).to_broadcast([128, 4, 128]), op=ALU.mult),
                                 r=[("e_all", tt, 2 * hd), ("e_all", tt, 2 * hd + 1)], w=[pk])
                        k.op("dve", lambda e: e.scalar_tensor_tensor(out=G[:, hd, :], in0=pr[:], scalar=phi[:, tt, hd:hd + 1],
                                                                     in1=pr[:], op0=ALU.is_ge, op1=ALU.mult),
                             r=[pk, "phi"], w=[(gk, hd)])

                def stage_b(s_):
                    eg, tt = steps[s_]
                    G, gk, gb = Gs[s_ % 2], "G%d" % (s_ % 2), 4 + s_ % 2
                    for a in range(4):
                        for hd in range(8):
                            k.op("pe", lambda e: e.matmul(ps[gb][:, a * 128:(a + 1) * 128], lhsT=G[:, hd, a * 128:(a + 1) * 128],
                                                          rhs=c.ident[:], start=(hd == 0), stop=(hd == 7)),
                                 r=[(gk, hd), "ident"], w=[("ps", gb)])

                def stage_c(s_):
                    eg, tt = steps[s_]
                    gb = 4 + s_ % 2
                    k.op("dve", lambda e: e.tensor_tensor(out=HsTs[s_ % 2][:], in0=gels[eg % 2][:, :, tt * 128:(tt + 1) * 128],
                                                          in1=ps[gb][:, :].rearrange("p (a n) -> p a n", a=4), op=ALU.mult),
                         r=[("gel%d" % (eg % 2), a) for a in range(4)] + [("ps", gb)], w=["HsT%d" % (s_ % 2)])

                def obank(s_, dh):
                    return (2 + dh) if s_ % 2 == 0 else (6 + dh)

                def stage_d(s_):
                    eg, tt = steps[s_]
                    for dh in range(2):
                        ob = obank(s_, dh)
                        for a in range(4):
                            k.op("pe", lambda e: e.matmul(ps[ob][:, :], lhsT=HsTs[s_ % 2][:, a, :], rhs=vbs[eg % 2][:, a, dh * 512:(dh + 1) * 512],
                                                          start=(a == 0), stop=(a == 3)), r=["HsT%d" % (s_ % 2), "vb%d" % (eg % 2)],
                                 w=[("ps", ob)])

                def stage_e(s_):
                    eg, tt = steps[s_]
                    tok = tg * 4 + tt
                    for dh in range(2):
                        ob = obank(s_, dh)
                        k.op("dve", lambda e: e.tensor_tensor(out=h[:, tok, dh * 512:(dh + 1) * 512],
                                                              in0=h[:, tok, dh * 512:(dh + 1) * 512], in1=ps[ob][:, :],
                                                              op=ALU.add), r=[("h", tok), ("ps", ob)], w=[("h", tok)])

                if steps:
                    load(0)
                    for a in range(4):
                        hpart(0, a)
                    stage_a(0)
                for s_ in range(len(steps)):
                    eg, tt = steps[s_]
                    if tt == 0 and eg + 1 < NEG_:
                        load(eg + 1)
                    if s_ + 1 < len(steps):
                        stage_a(s_ + 1)
                    stage_b(s_)
                    stage_c(s_)
                    if eg + 1 < NEG_:
                        hpart(eg + 1, tt)
                    stage_d(s_)
                    if s_ >= 1:
                        stage_e(s_ - 1)
                if steps:
                    stage_e(len(steps) - 1)
                k.barrier()


RW_HP = [8]
RW_TB = [4]
RW_NQ = [4]
RW_LAG = [20]


def rwkv_mixer(c, li):
    nc, k, sb, ps, pbf, h = c.nc, c.k, c.sb, c.ps, c.pbf, c.h
    din = c.din
    mu_d = din("o_mu", [6, D])
    wr_d, wk_d, wv_d = din("o_w_r", [D, D]), din("o_w_k", [D, D]), din("o_w_v", [D, D])
    w0_d = din("o_w0", [1, D])
    ww1_d, ww2_d = din("o_w_w1", [D, 64]), din("o_w_w2", [64, D])
    a0_d = din("o_a0", [1, D])
    wa1_d, wa2_d = din("o_w_a1", [D, 64]), din("o_w_a2", [64, D])
    wg1_d, wg2_d = din("o_w_g1", [D, 128]), din("o_w_g2", [128, D])
    kk_d, ka_d, rk_d = din("o_k_k", [1, D]), din("o_k_a", [1, D]), din("o_r_k", [1, D])
    lg_d, lb_d = din("o_lnx_g", [1, D]), din("o_lnx_b", [1, D])
    wo_d = din("o_w_o", [D, D])

    def mm(out, lhsT, rhs, r, w, start=True, stop=True):
        k.op("pe", lambda e: e.matmul(out, lhsT=lhsT, rhs=rhs, start=start, stop=stop), r=r, w=w)

    def tt_(eng, out, in0, in1, op, r, w):
        k.op(eng, lambda e: e.tensor_tensor(out=out, in0=in0, in1=in1, op=op), r=r, w=w)

    def ts_(eng, out, in0, s1, s2, op0, op1, r, w):
        k.op(eng, lambda e: e.tensor_scalar(out=out, in0=in0, scalar1=s1, scalar2=s2, op0=op0, op1=op1), r=r, w=w)

    def act(out, in_, fn, r, w, **kw):
        k.op("act", lambda e: e.activation(out=out, in_=in_, func=fn, **kw), r=r, w=w)

    with ExitStack() as ph:
        xp = sb("xp", [128, 8, L + 2], BF16, ph)
        k.op("pool", lambda e: e.memset(xp[:, :, 0:1], 0.0), w=["xp0"])
        rmsnorm_T(c, c.norm_mix_g[li, :], xp, "xp", off=1)
        allx = [("xp", tt) for tt in range(NT)] + ["xp0"]
        mjt = sb("mjt", [128, 128], F32, ph)
        mtj = sb("mtj", [128, 128], F32, ph)
        k.op("pool", lambda e: e.affine_select(out=mjt[:], in_=c.ones_f[:], pattern=[[1, 128]], compare_op=ALU.is_gt,
                                                fill=0.0, base=0, channel_multiplier=-1), r=["ones_f"], w=["mjt"])
        k.op("pool", lambda e: e.affine_select(out=mtj[:], in_=c.ones_f[:], pattern=[[-1, 128]], compare_op=ALU.is_gt,
                                                fill=0.0, base=0, channel_multiplier=1), r=["ones_f"], w=["mtj"])
        hm = sb("hm", [128, 2], F32, ph)
        bones = sb("bones", [128, 128], BF16, ph)
        cmask = sb("cmask", [128, 2, 128], F32, ph)
        dsel = sb("dsel", [128, 2, 128], F32, ph)
        k.op("pool", lambda e: e.memset(hm[:], 0.0), w=["hm"])
        k.op("pool", lambda e: e.memset(hm[0:64, 0:1], 1.0), r=["hm"], w=["hm"])
        k.op("pool", lambda e: e.memset(hm[64:128, 1:2], 1.0), r=["hm"], w=["hm"])
        k.op("pool", lambda e: e.memset(bones[:], 0.0), w=["bones"])
        k.op("pool", lambda e: e.memset(bones[0:64, 0:64], 1.0), r=["bones"], w=["bones"])
        k.op("pool", lambda e: e.memset(bones[64:128, 64:128], 1.0), r=["bones"], w=["bones"])
        k.op("pool", lambda e: e.memset(cmask[:].rearrange("p a b -> p (a b)"), 0.0), w=["cmask"])
        k.op("pool", lambda e: e.memset(cmask[:, 0, 0:64], 1.0), r=["cmask"], w=["cmask"])
        k.op("pool", lambda e: e.memset(cmask[:, 1, 64:128], 1.0), r=["cmask"], w=["cmask"])
        for hl in range(2):
            ts_("dve", dsel[:, hl, :], c.identf[:], hm[:, hl:hl + 1], None, ALU.mult, ALU.bypass, ["identf", "hm"], ["dsel"])
        chm = sb("chm", [128, 512], BF16, ph)
        k.op("pool", lambda e: e.memset(chm[:], 1.0), w=["chm"])
        k.op("pool", lambda e: e.memset(chm[:].rearrange("p (a b) -> p a b", a=4)[:, :, 0:1], 0.0), r=["chm"], w=["chm"])
        prm = sb("rprm", [128, 7, 8], F32, ph)
        for i, pd in enumerate((w0_d, a0_d, kk_d, ka_d, rk_d, lg_d, lb_d)):
            k.dma("sp", prm[:, i, :], pd[0, :].rearrange("(dc p) -> p dc", p=128), w=["rprm"], allow_slow_non_contiguous=True)
        mucol = sb("mucol", [128, 6, 8], F32, ph)
        k.dma("sp", mucol[:], mu_d.rearrange("i (dc p) -> p i dc", p=128), w=["mucol"], allow_slow_non_contiguous=True)
        MU = dict(r=0, w=1, k=2, v=3, a=4, g=5)

        def load_split(name, wd, cols, ncol, mui, st):
            w0 = sb(name + "0", [128, 8, ncol], BF16, st)
            wm = sb(name + "m", [128, 8, ncol], BF16, st)
            wp = sb(name + "p", [128, 8, ncol], BF16, st)
            k.dma("pool", w0[:], wd[:, cols].rearrange("(dc p) n -> p dc n", p=128), w=[name + "0"])
            tt_("dve", wm[:], w0[:], mucol[:, mui, :].unsqueeze(2).to_broadcast([128, 8, ncol]), ALU.mult,
                [name + "0", "mucol"], [name + "m"])
            tt_("dve", wp[:], w0[:], wm[:], ALU.subtract, [name + "0", name + "m"], [name + "p"])
            return wp, wm

        ww2 = sb("ww2", [128, D], BF16, ph)
        wa2 = sb("wa2", [128, D], BF16, ph)
        wg2 = sb("wg2r", [128, D], BF16, ph)
        k.dma("pool", ww2[0:64, :], ww2_d[:, :], w=["ww2"])
        k.dma("pool", wa2[0:64, :], wa2_d[:, :], w=["wa2"])
        k.dma("pool", wg2[:], wg2_d[:, :], w=["wg2r"])
        tw1 = sb("tw1", [128, L], BF16, ph)
        ta1 = sb("ta1", [128, L], BF16, ph)
        tg1 = sb("tg1", [128, L], BF16, ph)

        def proj(out_ps, wp, wm, c0, m, t0, n, keys):
            for dc in range(8):
                mm(out_ps, wp[:, dc, c0:c0 + m], xp[:, dc, 1 + t0:1 + t0 + n], allx + keys, [("ps", 0)], start=(dc == 0), stop=False)
            for dc in range(8):
                mm(out_ps, wm[:, dc, c0:c0 + m], xp[:, dc, t0:t0 + n], allx + keys, [("ps", 0)], start=False, stop=(dc == 7))

        with ExitStack() as p0:
            w1p, w1m = load_split("ww1", ww1_d, slice(0, 64), 64, MU["w"], p0)
            a1p, a1m = load_split("wa1", wa1_d, slice(0, 64), 64, MU["a"], p0)
            g1p, g1m = load_split("wg1", wg1_d, slice(0, 128), 128, MU["g"], p0)
            for tb in range(4):
                t0 = tb * 512
                proj(ps[0][0:64, :], w1p, w1m, 0, 64, t0, 512, ["ww1p", "ww1m"])
                act(tw1[0:64, t0:t0 + 512], ps[0][0:64, :], AF.Tanh, [("ps", 0)], ["tw1"])
                proj(ps[0][0:64, :], a1p, a1m, 0, 64, t0, 512, ["wa1p", "wa1m"])
                act(ta1[0:64, t0:t0 + 512], ps[0][0:64, :], AF.Copy, [("ps", 0)], ["ta1"])
                proj(ps[0][:, :], g1p, g1m, 0, 128, t0, 512, ["wg1p", "wg1m"])
                act(tg1[:, t0:t0 + 512], ps[0][:, :], AF.Sigmoid, [("ps", 0)], ["tg1"])
            k.barrier()
        W = 512
        f32t = lambda n, st: sb(n, [128, W], F32, st)
        b16t = lambda n, st: sb(n, [128, W], BF16, st)
        for hp in range(RW_HP[0]):
            with ExitStack() as p1:
                cols = slice(hp * 128, (hp + 1) * 128)
                wrp, wrm = load_split("wr", wr_d, cols, 128, MU["r"], p1)
                wkp, wkm = load_split("wk", wk_d, cols, 128, MU["k"], p1)
                wvp, wvm = load_split("wv", wv_d, cols, 128, MU["v"], p1)
                wo = sb("wo_hp", [128, D], BF16, p1)
                k.dma("pool", wo[:], wo_d[hp * 128:(hp + 1) * 128, :], w=["wo_hp"])
                pc = lambda i: prm[:, i, hp:hp + 1]
                Hb = sb("Hb", [128, 64], BF16, p1)
                k.op("pool", lambda e: e.memset(Hb[:], 0.0), w=["Hb"])
                r_b, k_b, v_b, g_b = b16t("r_b", p1), b16t("k_b", p1), b16t("v_b", p1), b16t("g_b", p1)
                vtok = sb("vtok", [128, 4, 128], BF16, p1)
                lw, cs, asig = f32t("lw", p1), f32t("cs", p1), f32t("asig", p1)
                e1, e2, e3, e4 = f32t("e1", p1), f32t("e2", p1), f32t("e3", p1), f32t("e4", p1)
                kkn, kmod, b_ = f32t("kkn", p1), f32t("kmod", p1), f32t("bb", p1)
                sq = b16t("sq", p1)
                rt, rz0, rz1, az0, az1 = b16t("rt", p1), b16t("rz0", p1), b16t("rz1", p1), b16t("az0", p1), b16t("az1", p1)
                at_, bt, kt, bh, kh = b16t("at", p1), b16t("bt", p1), b16t("kt", p1), b16t("bh", p1), b16t("kh", p1)
                bv = f32t("bv", p1)
                pcl = sb("pcl", [128, 4], F32, p1)
                rz, az = (rz0, rz1), (az0, az1)
                Xs = [sb("X%d" % i, [128, 256], BF16, p1) for i in range(4)]
                MNs = [[sb("MN%d_%d" % (j, i), [128, 256], BF16, p1) for i in range(2)] for j in range(4)]
                cats = [sb("cat%d" % i, [128, 256], BF16, p1) for i in range(4)]
                mrks = [sb("mrk%d" % i, [128, 128], F32, p1) for i in range(4)]
                btok = sb("btok", [128, 2, 128], BF16, p1)
                bzs = [sb("bz%d" % i, [128, 2, 128], BF16, p1) for i in range(2)]
                kzs = [sb("kz%d" % i, [128, 2, 128], BF16, p1) for i in range(2)]
                RGMF = [[sb("%s%d" % (n, i), [128, 128], BF16, p1) for n in ("RpT", "GT", "MpT", "FT")] for i in range(4)]
                gst = sb("gst", [128, 8], F32, p1)
                ysq = sb("ysq", [128, 128], F32, p1)
                yn = sb("yn", [128, 128], BF16, p1)
                z1 = sb("z1", [128, 128], F32, p1)
                ygT = sb("ygT", [128, 128], BF16, p1)
                for tb in range(RW_TB[0]):
                    t0 = tb * W
                    proj(ps[0][:, :], wrp, wrm, 0, 128, t0, W, ["wrp", "wrm"])
                    act(r_b[:], ps[0][:, :], AF.Copy, [("ps", 0)], ["r_b"])
                    proj(ps[0][:, :], wkp, wkm, 0, 128, t0, W, ["wkp", "wkm"])
                    act(k_b[:], ps[0][:, :], AF.Copy, [("ps", 0)], ["k_b"])
                    proj(ps[0][:, :], wvp, wvm, 0, 128, t0, W, ["wvp", "wvm"])
                    act(v_b[:], ps[0][:, :], AF.Copy, [("ps", 0)], ["v_b"])
                    for q in range(4):
                        tq = t0 + q * 128
                        for dc in range(8):
                            mm(ps[6][:, q * 128:(q + 1) * 128], xp[:, dc, 1 + tq:1 + tq + 128], wvp[:, dc, :], allx + ["wvp"],
                               [("ps", 6)], start=(dc == 0), stop=False)
                        for dc in range(8):
                            mm(ps[6][:, q * 128:(q + 1) * 128], xp[:, dc, tq:tq + 128], wvm[:, dc, :], allx + ["wvm"],
                               [("ps", 6)], start=False, stop=(dc == 7))
                    k.op("dve", lambda e: e.tensor_copy(out=vtok[:].rearrange("p a b -> p (a b)"), in_=ps[6][:, :]),
                         r=[("ps", 6)], w=["vtok"])
                    mm(ps[0][:, :], ww2[0:64, cols], tw1[0:64, t0:t0 + W], ["ww2", "tw1"], [("ps", 0)])
                    act(lw[:], ps[0][:, :], AF.Sigmoid, [("ps", 0), "rprm"], ["lw"], bias=pc(0), scale=1.0)
                    ts_("dve", lw[:], lw[:], -0.6065306597126334, None, ALU.mult, ALU.bypass, ["lw"], ["lw"])
                    mm(ps[0][:, :], wa2[0:64, cols], ta1[0:64, t0:t0 + W], ["wa2", "ta1"], [("ps", 0)])
                    act(asig[:], ps[0][:, :], AF.Sigmoid, [("ps", 0), "rprm"], ["asig"], bias=pc(1), scale=1.0)
                    mm(ps[0][:, :], wg2[:, cols], tg1[:, t0:t0 + W], ["wg2r", "tg1"], [("ps", 0)])
                    act(g_b[:], ps[0][:, :], AF.Copy, [("ps", 0)], ["g_b"])
                    k.op("dve", lambda e: e.tensor_tensor_scan(out=cs[:], data0=chm[:], data1=lw[:], initial=0.0,
                                                               op0=ALU.mult, op1=ALU.add), r=["chm", "lw"], w=["cs"])
                    cs3 = cs[:].rearrange("p (a b) -> p a b", a=4)
                    csl = cs3[:, :, 127:128].to_broadcast([128, 4, 128])
                    act(e1[:], cs[:], AF.Exp, ["cs"], ["e1"])
                    act(e2[:], cs[:], AF.Exp, ["cs"], ["e2"], scale=-1.0)
                    tt_("dve", e3[:], cs[:], lw[:], ALU.subtract, ["cs", "lw"], ["e3"])
                    act(e3[:], e3[:], AF.Exp, ["e3"], ["e3"])
                    tt_("dve", e4[:].rearrange("p (a b) -> p a b", a=4), csl, cs3, ALU.subtract, ["cs"], ["e4"])
                    act(e4[:], e4[:], AF.Exp, ["e4"], ["e4"])
                    act(pcl[:], cs3[:, :, 127], AF.Exp, ["cs"], ["pcl"])
                    ts_("dve", kkn[:], k_b[:], pc(2), None, ALU.mult, ALU.bypass, ["k_b", "rprm"], ["kkn"])
                    tt_("dve", sq[:], kkn[:], kkn[:], ALU.mult, ["kkn"], ["sq"])
                    mm(ps[0][:, :], bones[:], sq[:], ["bones", "sq"], [("ps", 0)])
                    act(kmod[:], ps[0][:, :], AF.Sqrt, [("ps", 0)], ["kmod"])
                    ts_("dve", kmod[:], kmod[:], 1e-12, None, ALU.max, ALU.bypass, ["kmod"], ["kmod"])
                    k.op("dve", lambda e: e.reciprocal(out=kmod[:], in_=kmod[:]), r=["kmod"], w=["kmod"])
                    tt_("dve", kkn[:], kkn[:], kmod[:], ALU.mult, ["kkn", "kmod"], ["kkn"])
                    ts_("dve", kmod[:], asig[:], -1.0, pc(3), ALU.add, ALU.mult, ["asig", "rprm"], ["kmod"])
                    ts_("dve", kmod[:], kmod[:], 1.0, None, ALU.add, ALU.bypass, ["kmod"], ["kmod"])
                    tt_("dve", kmod[:], kmod[:], k_b[:], ALU.mult, ["kmod", "k_b"], ["kmod"])
                    tt_("dve", b_[:], kkn[:], asig[:], ALU.mult, ["kkn", "asig"], ["bb"])
                    tt_("dve", rt[:], r_b[:], e1[:], ALU.mult, ["r_b", "e1"], ["rt"])
                    tt_("pool", kt[:], kmod[:], e2[:], ALU.mult, ["kmod", "e2"], ["kt"])
                    tt_("dve", bt[:], b_[:], e2[:], ALU.mult, ["bb", "e2"], ["bt"])
                    tt_("pool", kh[:], kmod[:], e4[:], ALU.mult, ["kmod", "e4"], ["kh"])
                    tt_("dve", bh[:], b_[:], e4[:], ALU.mult, ["bb", "e4"], ["bh"])
                    tt_("pool", e3[:], kkn[:], e3[:], ALU.mult, ["kkn", "e3"], ["e3"])
                    ts_("dve", at_[:], e3[:], -1.0, None, ALU.mult, ALU.bypass, ["e3"], ["at"])
                    for hl in range(2):
                        ts_("dve", rz[hl][:], rt[:], hm[:, hl:hl + 1], None, ALU.mult, ALU.bypass, ["rt", "hm"], ["rz%d" % hl])
                        ts_("pool", az[hl][:], at_[:], hm[:, hl:hl + 1], None, ALU.mult, ALU.bypass, ["at", "hm"], ["az%d" % hl])
                    tt_("dve", e1[:], r_b[:], kmod[:], ALU.mult, ["r_b", "kmod", "e1", "rt"], ["e1"])
                    ts_("dve", sq[:], e1[:], pc(4), None, ALU.mult, ALU.bypass, ["e1", "rprm", "sq"], ["sq"])
                    mm(ps[0][:, :], bones[:], sq[:], ["bones", "sq"], [("ps", 0)])
                    tt_("dve", bv[:], ps[0][:, :], v_b[:], ALU.mult, [("ps", 0), "v_b"], ["bv"])
                    def chunk_seq(q):
                        csl_ = slice(q * 128, (q + 1) * 128)
                        tok = tb * 4 + q
                        par = q % 2
                        bz, kz = bzs[par], kzs[par]
                        bzk, kzk = "bz%d" % par, "kz%d" % par
                        bo = par * 256
                        k.op("pe", lambda e: e.transpose(out=pbf[1][:, bo:bo + 128], in_=bh[:, csl_], identity=c.ident[:]),
                             r=["bh", "ident"], w=[("ps", 1)])
                        k.op("pe", lambda e: e.transpose(out=pbf[1][:, bo + 128:bo + 256], in_=kh[:, csl_], identity=c.ident[:]),
                             r=["kh", "ident"], w=[("ps", 1)])
                        yield
                        for hl in range(2):
                            tt_("dve", bz[:, hl, :], pbf[1][:, bo:bo + 128], cmask[:, hl, :], ALU.mult, [("ps", 1), "cmask"], [bzk])
                            tt_("dve", kz[:, hl, :], pbf[1][:, bo + 128:bo + 256], cmask[:, hl, :], ALU.mult, [("ps", 1), "cmask"], [kzk])
                        yield

                        def head_seq(hl):
                            si = par * 2 + hl
                            azc, rzc = az[hl][:, csl_], rz[hl][:, csl_]
                            azk, rzk = "az%d" % hl, "rz%d" % hl
                            bX = 2 + si
                            kX = ("ps", bX)
                            X, MN, cat, mrk = Xs[si], MNs[si], cats[si], mrks[si]
                            RpT, GT, MpT, FT = RGMF[si]
                            xk, ck, mk = "X%d" % si, "cat%d" % si, "mrk%d" % si
                            rk_, gk_, mpk, fk = ("RpT%d" % si, "GT%d" % si, "MpT%d" % si, "FT%d" % si)
                            tp = pbf[1][:, 512 + si * 128:640 + si * 128]
                            mm(ps[bX][:, 0:128], bt[:, csl_], azc, ["bt", azk], [kX])
                            mm(ps[bX][:, 128:256], azc, bt[:, csl_], ["bt", azk], [kX])
                            mm(ps[bX][:, 256:384], azc, kt[:, csl_], ["kt", azk], [kX])
                            k.op("pe", lambda e: e.transpose(out=tp, in_=azc, identity=c.ident[:]), r=[azk, "ident"], w=[("ps", 1)])
                            yield
                            tt_("dve", MN[0][:, 0:128], ps[bX][:, 0:128], mjt[:], ALU.mult, [kX, "mjt"], ["MN0_%d" % si])
                            tt_("dve", MN[0][:, 128:256], ps[bX][:, 128:256], mtj[:], ALU.mult, [kX, "mtj"], ["MN0_%d" % si])
                            tt_("dve", X[:, 128:256], ps[bX][:, 256:384], mtj[:], ALU.mult, [kX, "mtj"], [xk])
                            act(X[:, 0:128], tp, AF.Copy, [("ps", 1)], [xk])
                            yield
                            for i in range(7):
                                cur, nxt = MN[i % 2], MN[(i + 1) % 2]
                                ck_, nk = "MN%d_%d" % (i % 2, si), "MN%d_%d" % ((i + 1) % 2, si)
                                mm(ps[bX][:, 0:256], cur[:, 0:128], X[:], [ck_, xk], [kX])
                                if i < 6:
                                    mm(ps[bX][:, 256:384], cur[:, 128:256], cur[:, 0:128], [ck_], [kX])
                                    mm(ps[bX][:, 384:512], cur[:, 0:128], cur[:, 128:256], [ck_], [kX])
                                yield
                                if i < 6:
                                    act(nxt[:], ps[bX][:, 256:512], AF.Copy, [kX], [nk, kX])
                                tt_("dve", X[:], X[:], ps[bX][:, 0:256], ALU.add, [xk, kX], [xk, kX])
                                yield
                            mm(ps[bX][:, 0:128], bt[:, csl_], rzc, ["bt", rzk], [kX])
                            mm(ps[bX][:, 128:256], kt[:, csl_], rzc, ["kt", rzk], [kX])
                            yield
                            tt_("dve", cat[:, 0:128], ps[bX][:, 0:128], c.triU_f[:], ALU.mult, [kX, "triU_f"], [ck])
                            tt_("dve", mrk[:], ps[bX][:, 128:256], c.triU_f[:], ALU.mult, [kX, "triU_f"], [mk])
                            k.op("pool", lambda e: e.tensor_copy(out=cat[:, 128:256], in_=bz[:, hl, :]), r=[bzk], w=[ck])
                            yield
                            mm(ps[bX][:, 0:256], X[:, 0:128], cat[:], [xk, ck], [kX])
                            mm(ps[bX][:, 256:512], X[:, 128:256], cat[:], [xk, ck], [kX])
                            yield
                            tt_("dve", RpT[:], ps[bX][:, 0:128], rzc, ALU.add, [kX, rzk], [rk_])
                            k.op("dve", lambda e: e.scalar_tensor_tensor(out=GT[:], in0=dsel[:, hl, :], scalar=pcl[:, q:q + 1],
                                                                         in1=ps[bX][:, 128:256], op0=ALU.mult, op1=ALU.add),
                                 r=["dsel", "pcl", kX], w=[gk_])
                            tt_("dve", MpT[:], ps[bX][:, 256:384], mrk[:], ALU.add, [kX, mk], [mpk])
                            tt_("dve", FT[:], ps[bX][:, 384:512], kz[:, hl, :], ALU.add, [kX, kzk], [fk])
                            yield
                            vh = vtok[:, q, hl * 64:(hl + 1) * 64]
                            mm(ps[7][:, hl * 64:(hl + 1) * 64], RpT[:], Hb[:], [rk_, "Hb"], [("ps", 7)], start=True, stop=False)
                            mm(ps[7][:, hl * 64:(hl + 1) * 64], MpT[:], vh, [mpk, "vtok"], [("ps", 7)], start=False, stop=True)
                            mm(ps[0][:, 0:64], GT[:], Hb[:], [gk_, "Hb"], [("ps", 0)], start=(hl == 0), stop=False)
                            mm(ps[0][:, 0:64], FT[:], vh, [fk, "vtok"], [("ps", 0)], start=False, stop=(hl == 1))
                            yield

                        hg = [head_seq(0), head_seq(1)]
                        alive = [True, True]
                        while any(alive):
                            for gi in range(2):
                                if alive[gi]:
                                    try:
                                        next(hg[gi])
                                    except StopIteration:
                                        alive[gi] = False
                            yield
                        act(Hb[:], ps[0][:, 0:64], AF.Copy, [("ps", 0)], ["Hb"])
                        yield
                        y3 = ps[7][:, 0:128].rearrange("p (a b) -> p a b", a=2)
                        k.op("dve", lambda e: e.tensor_reduce(out=gst[:, 0:2], in_=y3, axis=AX.X, op=ALU.add), r=[("ps", 7)], w=["gst"])
                        act(ysq[:], ps[7][:, 0:128], AF.Square, [("ps", 7)], ["ysq", ("ps", 7)])
                        yield
                        k.op("dve", lambda e: e.tensor_reduce(out=gst[:, 2:4], in_=ysq[:].rearrange("p (a b) -> p a b", a=2),
                                                              axis=AX.X, op=ALU.add), r=["ysq"], w=["gst"])
                        ts_("dve", gst[:, 0:4], gst[:, 0:4], 1.0 / 64, None, ALU.mult, ALU.bypass, ["gst"], ["gst"])
                        yield
                        tt_("dve", gst[:, 4:6], gst[:, 0:2], gst[:, 0:2], ALU.mult, ["gst"], ["gst"])
                        tt_("dve", gst[:, 4:6], gst[:, 2:4], gst[:, 4:6], ALU.subtract, ["gst"], ["gst"])
                        ts_("dve", gst[:, 4:6], gst[:, 4:6], 64e-5, None, ALU.add, ALU.bypass, ["gst"], ["gst"])
                        yield
                        act(gst[:, 4:6], gst[:, 4:6], AF.Sqrt, ["gst"], ["gst"])
                        yield
                        k.op("dve", lambda e: e.reciprocal(out=gst[:, 6:8], in_=gst[:, 4:6]), r=["gst"], w=["gst"])
                        for hl in range(2):
                            ts_("dve", yn[:, hl * 64:(hl + 1) * 64], ps[7][:, hl * 64:(hl + 1) * 64], gst[:, hl:hl + 1],
                                gst[:, 6 + hl:7 + hl], ALU.subtract, ALU.mult, [("ps", 7), "gst"], ["yn"])
                        yield
                        k.op("pe", lambda e: e.transpose(out=pbf[6][:, 0:128], in_=yn[:], identity=c.ident[:]),
                             r=["yn", "ident"], w=[("ps", 6)])
                        yield
                        ts_("dve", z1[:], pbf[6][:, 0:128], pc(5), pc(6), ALU.mult, ALU.add, [("ps", 6), "rprm"], ["z1"])
                        tt_("dve", z1[:], z1[:], bv[:, csl_], ALU.add, ["z1", "bv"], ["z1"])
                        tt_("dve", ygT[:], z1[:], g_b[:, csl_], ALU.mult, ["z1", "g_b"], ["ygT"])
                        yield
                        for dh in range(2):
                            mm(ps[6][:, :], ygT[:], wo[:, dh * 512:(dh + 1) * 512], ["ygT", "wo_hp"], [("ps", 6)])
                            yield
                            tt_("dve", h[:, tok, dh * 512:(dh + 1) * 512], h[:, tok, dh * 512:(dh + 1) * 512], ps[6][:, :], ALU.add,
                                [("h", tok), ("ps", 6)], [("h", tok)])
                            yield

                    def drain(g):
                        for _ in g:
                            pass

                    inflight = []
                    for q in range(RW_NQ[0]):
                        if len(inflight) == 2:
                            drain(inflight.pop(0))
                        inflight.append(chunk_seq(q))
                        for _ in range(RW_LAG[0]):
                            for g in list(inflight):
                                try:
                                    next(g)
                                except StopIteration:
                                    inflight.remove(g)
                    for g in inflight:
                        drain(g)
                k.barrier()
        k.barrier()


def build(nc, stages="all", dbg=None):
    es = ExitStack()
    with es:
        dbgaps = {}
        for name, shape in (dbg or {}).items():
            dbgaps[name] = nc.dram_tensor("dbg_" + name, list(shape), BF16 if name in ("uT", "yT", "vk", "zz", "sT", "ETb", "EVt", "wb", "cm") else F32,
                                          kind="ExternalOutput").ap()
        c = setup_common(nc, es, dbgaps)
        peer_inputs(c)
        st = set(stages.split(","))
        if "all" in st:
            st = {"even", "peer0", "rwkv", "peer1"}
        if "even" in st:
            even_mixer(c, 0)
        if "peer0" in st:
            peer(c, 0)
        if "rwkv" in st:
            rwkv_mixer(c, 1)
        if "peer1" in st:
            peer(c, 1)
        if "h" in dbgaps:
            for tt in range(NT):
                c.k.dma("sp", dbgaps["h"][tsl(tt), :], c.h[:, tt, :], r=[("h", tt)], w=["dbg_h"])
        final_norm_store(c)
        print("instr counts", c.k.nins, "sems", c.k.nsem)
    return nc


PARAMS = ["norm_mix_g", "norm_ffn_g", "final_g", "e_w_in", "e_w_out", "s5_a_re", "s5_a_im", "s5_log_dt", "s5_b_re", "s5_b_im",
          "s5_c_re", "s5_c_im", "s5_d", "s5_w_glu", "gla_w_g2", "gla_b_g2", "gla_norm_g",
          "peer_w_q", "peer_sub_keys", "peer_u", "peer_v",
          "o_mu", "o_w_r", "o_w_k", "o_w_v", "o_w0", "o_w_w1", "o_w_w2", "o_a0", "o_w_a1", "o_w_a2", "o_w_g1", "o_w_g2",
          "o_k_k", "o_k_a", "o_r_k", "o_lnx_g", "o_lnx_b", "o_w_o"]


def core_inputs(inputs, b):
    m = {"x": np.ascontiguousarray(inputs["x"][b])}
    for n in PARAMS:
        a = np.asarray(inputs[n])
        if n in ("norm_mix_g", "norm_ffn_g", "peer_w_q", "peer_sub_keys", "peer_u", "peer_v"):
            pass
        elif n == "final_g":
            a = a.reshape(1, D)
        elif n in ("s5_d", "gla_b_g2", "gla_norm_g", "o_w0", "o_a0", "o_k_k", "o_k_a", "o_r_k", "o_lnx_g", "o_lnx_b"):
            a = a.reshape(1, -1)
        else:
            a = a[0]
        m[n] = np.ascontiguousarray(a)
    return m


def kernel(**inputs):
    n = 8
    nc = bass.Bass("TRN2", target_bir_lowering=False)
    build(nc)
    in_maps = [core_inputs(inputs, b) for b in range(n)]
    res = run_bass_kernel_spmd(nc, in_maps, core_ids=list(range(n)))
    return np.stack([r["out"] for r in res.results], axis=0)
```
